# Optimizing a Trainium2 kernel written in Bass

```python
import math
import jax, jax.numpy as jnp
from jax import lax
import numpy as np

D_MODEL = 1024
BATCH = 8
SEQ = 2048
DEPTH = 1
DEC_BATCH = 128
DEC_SEQ = 1
PAST_LEN = 16384
PAGE_SIZE = 128

GLA_HEADS = 4
GLA_KEY = D_MODEL // 2
GLA_VAL = D_MODEL
GLA_DK = GLA_KEY // GLA_HEADS
GLA_DV = GLA_VAL // GLA_HEADS
GLA_LORA = 16
GLA_TAU = 16.0
GLA_CHUNK = 64
RWKV_HEAD = 64
RWKV_WIDTH = D_MODEL
RWKV_HEADS = RWKV_WIDTH // RWKV_HEAD
RWKV_DECAY_LORA = 64
RWKV_AAA_LORA = 64
RWKV_DECAY_SCALE = 0.606531
GLA_SPLITS = (GLA_KEY, GLA_KEY, GLA_VAL, GLA_VAL, GLA_LORA)
RWKV_SPLITS = (RWKV_WIDTH, RWKV_WIDTH, RWKV_WIDTH, RWKV_WIDTH,
               RWKV_DECAY_LORA, RWKV_AAA_LORA)
GLA_COLS = sum(GLA_SPLITS)
RWKV_COLS = sum(RWKV_SPLITS)
GATE_COLS = 2 * D_MODEL
N_IN = GLA_COLS + RWKV_COLS + GATE_COLS
DEEPNORM_ALPHA = (2.0 * DEPTH) ** 0.25
DEEPNORM_BETA = (8.0 * DEPTH) ** -0.25
LN_EPS = 1e-5
GLA_NORM_EPS = 1e-5
RWKV_GN_EPS = 64e-5
L2_EPS = 1e-12

kernel_name = "gla_rwkv7_gated_hybrid_step"


def _split(x, sizes):
    idx = np.cumsum(np.array(sizes))[:-1].tolist()
    return jnp.split(x, idx, axis=-1)


def _layernorm(x, g, b, eps):
    xf = x.astype(jnp.float32)
    mu = jnp.mean(xf, axis=-1, keepdims=True)
    var = jnp.mean(jnp.square(xf - mu), axis=-1, keepdims=True)
    return ((xf - mu) * lax.rsqrt(var + eps) * g + b).astype(x.dtype)


def _to_chunks(t, C):
    B, T, H, Dd = t.shape
    return t.reshape(B, T // C, C, H, Dd).transpose(1, 0, 3, 2, 4)


def _gla_chunked(q, k, v, log_a, S0):
    B, T = q.shape[0], q.shape[1]
    C = min(GLA_CHUNK, T)
    pad = (-T) % C
    if pad:
        pw = ((0, 0), (0, pad), (0, 0), (0, 0))
        q, k, v, log_a = (jnp.pad(t, pw) for t in (q, k, v, log_a))
    qc, kc, vc, gc = (_to_chunks(t, C) for t in (q, k, v, log_a))
    mask = jnp.tril(jnp.ones((C, C), dtype=bool))[:, :, None]

    def step(S, inp):
        qi, ki, vi, gi = inp
        b = jnp.cumsum(gi, axis=2)
        diff = b[:, :, :, None, :] - b[:, :, None, :, :]
        decay = jnp.exp(jnp.where(mask, diff, -jnp.inf))
        A = jnp.einsum('bhid,bhjd,bhijd->bhij', qi, ki, decay)
        o = jnp.einsum('bhij,bhjv->bhiv', A, vi) + jnp.einsum('bhid,bhdv->bhiv', qi * jnp.exp(b), S)
        b_last = b[:, :, -1:, :]
        S_new = jnp.exp(b_last[:, :, 0, :])[..., None] * S + jnp.einsum(
            'bhjd,bhjv->bhdv', ki * jnp.exp(b_last - b), vi)
        return S_new, o

    S, o = lax.scan(step, S0, (qc, kc, vc, gc))
    n = o.shape[0]
    o = o.transpose(1, 0, 3, 2, 4).reshape(B, n * C, GLA_HEADS, GLA_DV)[:, :T]
    return o, S


def _rwkv7_scan(r, w, k, v, kk, a, S0):
    def step(S, inp):
        rt, wt, kt, vt, kkt, at = inp
        sa = jnp.einsum('bhvk,bhk->bhv', S, -kkt)
        S = S * wt[:, :, None, :] + sa[..., None] * (kkt * at)[:, :, None, :] + vt[..., None] * kt[:, :, None, :]
        y = jnp.einsum('bhvk,bhk->bhv', S, rt)
        return S, y

    xs = tuple(jnp.moveaxis(t, 1, 0) for t in (r, w, k, v, kk, a))
    S, ys = lax.scan(step, S0, xs)
    return jnp.moveaxis(ys, 0, 1), S


def _layer(x, S_gla0, S_rwkv0, shift0, w_in, gla_alpha_w2, gla_alpha_b, gla_norm_w,
           rwkv_mu, rwkv_w0, rwkv_w2, rwkv_a0, rwkv_a2, rwkv_k_k, rwkv_k_a, rwkv_r_k,
           rwkv_lnx_w, rwkv_lnx_b, w_up_gla, w_up_rwkv, w_out, ln_g, ln_b):
    B, T, _ = x.shape
    p = jnp.einsum('btd,dn->btn', x, w_in)
    p_gla, p_rwkv, p_gate = _split(p, (GLA_COLS, RWKV_COLS, GATE_COLS))

    q, k, v, g_a, a_lr = _split(p_gla, GLA_SPLITS)
    log_a = jax.nn.log_sigmoid((jnp.einsum('btr,rk->btk', a_lr, gla_alpha_w2) + gla_alpha_b)
                               .astype(jnp.float32)) / GLA_TAU
    hq = lambda t, d: t.astype(jnp.float32).reshape(B, T, GLA_HEADS, d)
    o_gla, S_gla = _gla_chunked(hq(q, GLA_DK) * (GLA_DK ** -0.5), hq(k, GLA_DK), hq(v, GLA_DV),
                                log_a.reshape(B, T, GLA_HEADS, GLA_DK), S_gla0.astype(jnp.float32))
    o_gla = o_gla * lax.rsqrt(jnp.mean(jnp.square(o_gla), axis=-1, keepdims=True) + GLA_NORM_EPS) * gla_norm_w
    o_gla = o_gla.reshape(B, T, GLA_VAL).astype(x.dtype) * jax.nn.silu(g_a)

    prev = jnp.concatenate([shift0[:, None, :].astype(p_rwkv.dtype), p_rwkv[:, :-1]], axis=1)
    pr = p_rwkv + (prev - p_rwkv) * rwkv_mu
    r, kb, vb, g_b, w_lr, aa_lr = _split(pr, RWKV_SPLITS)
    f32 = lambda t: t.astype(jnp.float32)
    w = jnp.exp(-RWKV_DECAY_SCALE * jax.nn.sigmoid(f32(rwkv_w0 + jnp.einsum('btr,rc->btc', jnp.tanh(w_lr), rwkv_w2))))
    a = jax.nn.sigmoid(f32(rwkv_a0 + jnp.einsum('btr,rc->btc', aa_lr, rwkv_a2)))
    kb = f32(kb)
    hr = lambda t: t.reshape(B, T, RWKV_HEADS, RWKV_HEAD)
    kk = hr(kb * rwkv_k_k)
    kk = kk / jnp.maximum(jnp.sqrt(jnp.sum(jnp.square(kk), axis=-1, keepdims=True)), L2_EPS)
    kb = kb * (1.0 + (a - 1.0) * rwkv_k_a)
    rh, kh, vh = hr(f32(r)), hr(kb), hr(f32(vb))
    y, S_rwkv = _rwkv7_scan(rh, hr(w), kh, vh, kk, hr(a), S_rwkv0.astype(jnp.float32))
    mu = jnp.mean(y, axis=-1, keepdims=True)
    var = jnp.mean(jnp.square(y - mu), axis=-1, keepdims=True)
    y = ((y - mu) * lax.rsqrt(var + RWKV_GN_EPS)).reshape(B, T, RWKV_WIDTH) * rwkv_lnx_w + rwkv_lnx_b
    bonus = jnp.sum(rh * kh * rwkv_r_k, axis=-1, keepdims=True) * vh
    y = y + bonus.reshape(B, T, RWKV_WIDTH)
    o_rwkv = y.astype(x.dtype) * jax.nn.silu(g_b)

    gate_a, gate_b = _split(p_gate, (D_MODEL, D_MODEL))
    m = (jax.nn.sigmoid(gate_a) * jnp.einsum('btc,cd->btd', o_gla, w_up_gla)
         + jax.nn.sigmoid(gate_b) * jnp.einsum('btc,cd->btd', o_rwkv, w_up_rwkv))
    out = jnp.einsum('btd,de->bte', m, w_out)
    y_out = _layernorm(DEEPNORM_ALPHA * x + out, ln_g, ln_b, LN_EPS)
    return y_out, S_gla.astype(S_gla0.dtype), S_rwkv.astype(S_rwkv0.dtype), p_rwkv[:, -1].astype(shift0.dtype)


def setup_inputs(seed: int = 0) -> dict:
    key = jax.random.key(seed)
    ks = jax.random.split(key, 26)
    nrm = lambda k, shape, s: s * jax.random.normal(k, shape, jnp.float32)
    L = DEPTH
    return {
        "x_prompt": nrm(ks[0], (BATCH, SEQ, D_MODEL), 1.0),
        "x_sample": nrm(ks[1], (DEC_BATCH, DEC_SEQ, D_MODEL), 1.0),
        "state_gla": nrm(ks[2], (L, DEC_BATCH, GLA_HEADS, GLA_DK, GLA_DV), 0.5),
        "state_rwkv": nrm(ks[3], (L, DEC_BATCH, RWKV_HEADS, RWKV_HEAD, RWKV_HEAD), 0.3),
        "state_rwkv_shift": nrm(ks[4], (L, DEC_BATCH, RWKV_COLS), 1.0),
        "w_in": nrm(ks[5], (L, D_MODEL, N_IN), D_MODEL ** -0.5),
        "gla_alpha_w2": nrm(ks[6], (L, GLA_LORA, GLA_KEY), GLA_LORA ** -0.5),
        "gla_alpha_b": nrm(ks[7], (L, GLA_KEY), 0.5),
        "gla_norm_w": 1.0 + nrm(ks[8], (L, GLA_DV), 0.05),
        "rwkv_mu": jax.random.uniform(ks[9], (L, RWKV_COLS), jnp.float32),
        "rwkv_w0": nrm(ks[10], (L, RWKV_WIDTH), 0.5),
        "rwkv_w2": nrm(ks[11], (L, RWKV_DECAY_LORA, RWKV_WIDTH), RWKV_DECAY_LORA ** -0.5),
        "rwkv_a0": nrm(ks[12], (L, RWKV_WIDTH), 0.5),
        "rwkv_a2": nrm(ks[13], (L, RWKV_AAA_LORA, RWKV_WIDTH), RWKV_AAA_LORA ** -0.5),
        "rwkv_k_k": 0.85 + nrm(ks[14], (L, RWKV_WIDTH), 0.05),
        "rwkv_k_a": 1.0 + nrm(ks[15], (L, RWKV_WIDTH), 0.05),
        "rwkv_r_k": nrm(ks[16], (L, RWKV_HEADS, RWKV_HEAD), 0.1),
        "rwkv_lnx_w": 1.0 + nrm(ks[17], (L, RWKV_WIDTH), 0.05),
        "rwkv_lnx_b": nrm(ks[18], (L, RWKV_WIDTH), 0.02),
        "w_up_gla": nrm(ks[19], (L, GLA_VAL, D_MODEL), DEEPNORM_BETA * GLA_VAL ** -0.5),
        "w_up_rwkv": nrm(ks[20], (L, RWKV_WIDTH, D_MODEL), DEEPNORM_BETA * RWKV_WIDTH ** -0.5),
        "w_out": nrm(ks[21], (L, D_MODEL, D_MODEL), DEEPNORM_BETA * D_MODEL ** -0.5),
        "ln_g": 1.0 + nrm(ks[22], (L, D_MODEL), 0.05),
        "ln_b": nrm(ks[23], (L, D_MODEL), 0.02),
    }


def reference(x_prompt, x_sample, state_gla, state_rwkv, state_rwkv_shift, w_in, gla_alpha_w2,
              gla_alpha_b, gla_norm_w, rwkv_mu, rwkv_w0, rwkv_w2, rwkv_a0, rwkv_a2, rwkv_k_k,
              rwkv_k_a, rwkv_r_k, rwkv_lnx_w, rwkv_lnx_b, w_up_gla, w_up_rwkv, w_out, ln_g, ln_b):
    B = x_prompt.shape[0]
    hp, hs = x_prompt, x_sample
    gla_p, rwkv_p, shift_p, gla_s, rwkv_s, shift_s = [], [], [], [], [], []
    for l in range(DEPTH):
        lp = (w_in[l], gla_alpha_w2[l], gla_alpha_b[l], gla_norm_w[l], rwkv_mu[l], rwkv_w0[l],
              rwkv_w2[l], rwkv_a0[l], rwkv_a2[l], rwkv_k_k[l], rwkv_k_a[l], rwkv_r_k[l],
              rwkv_lnx_w[l], rwkv_lnx_b[l], w_up_gla[l], w_up_rwkv[l], w_out[l], ln_g[l], ln_b[l])
        z_gla = jnp.zeros((B, GLA_HEADS, GLA_DK, GLA_DV), state_gla.dtype)
        z_rwkv = jnp.zeros((B, RWKV_HEADS, RWKV_HEAD, RWKV_HEAD), state_rwkv.dtype)
        z_shift = jnp.zeros((B, RWKV_COLS), state_rwkv_shift.dtype)
        hp, sg, sr, ss = _layer(hp, z_gla, z_rwkv, z_shift, *lp)
        gla_p.append(sg); rwkv_p.append(sr); shift_p.append(ss)
        hs, sg, sr, ss = _layer(hs, state_gla[l], state_rwkv[l], state_rwkv_shift[l], *lp)
        gla_s.append(sg); rwkv_s.append(sr); shift_s.append(ss)
    return (hp, hs, jnp.stack(gla_p), jnp.stack(rwkv_p), jnp.stack(shift_p),
            jnp.stack(gla_s), jnp.stack(rwkv_s), jnp.stack(shift_s))
```

```python
import contextlib
import numpy as np
import concourse.bass as bass
import concourse.mybir as mybir
from concourse.bass_utils import run_bass_kernel_spmd

F32 = mybir.dt.float32
BF16 = mybir.dt.bfloat16
AF = mybir.ActivationFunctionType
ALU = mybir.AluOpType

ENGS = ["pe", "act", "dve", "pool", "sp"]

T = 2048
NS = 16
TA = T + NS
D = 1024
KC = 8
NIN = 9360
GQ, GK, GV, GG, GA = 0, 512, 1024, 2048, 3072
R0 = 3088
RR, RK, RV, RG, RW = R0, R0 + 1024, R0 + 2048, R0 + 3072, R0 + 4096
G0 = R0 + 4224
SEGS = [(0, 512), (512, 512), (1024, 512), (1536, 512), (2048, 16)]
DECAY = 0.606531
RLEVEL = 99
RPAIRS = 8
RSTOP = None
RLOG = []

CV_MU, CV_AB, CV_GNW, CV_W0, CV_A0, CV_KK, CV_KA, CV_RK, CV_LW, CV_LB = 0, 33, 37, 39, 47, 55, 63, 71, 79, 87
NCV = 95
CM_ID, CM_MASK5, CM_MASKU, CM_ISTK, CM_BONES, CM_BAVG, CM_ONES, CM_SMG, CM_SMR, CM_ID16 = (
    0, 128, 448, 576, 640, 768, 896, 1024, 1536, 2048)
NCM = 2064


class Prog:
    EPOCH = 8192
    NDMA = 14

    def __init__(self):
        self.ops = {e: [] for e in ENGS}
        self.cnt = {e: 0 for e in ENGS}
        self.ndma = 0
        self.dma_events = []
        self.last_w = {}
        self.readers = {}
        self.waited = {e: {} for e in ENGS}
        self.semkeys = set()
        self.out_events = []
        self.enabled = True
        self.pending = {e: [] for e in ENGS}
        self.last_ev = {}
        self.know = {e: {} for e in ENGS}
        self.evclock = {}
        self.evidx = {}
        self.nev = 0

    def fence(self):
        evs = list(self.last_ev.values()) + list(self.dma_events[-self.NDMA:])
        for e in ENGS:
            self.pending[e] = list(evs)

    def _resolve(self, eng, cands):
        know = self.know[eng]
        waits = []
        for ev in sorted(cands, key=lambda e: -self.evidx[e]):
            sk, val = ev
            if eng == "pe" and sk[0] == "pe":
                continue
            if know.get(sk, 0) >= val:
                continue
            waits.append(ev)
            for k2, v2 in self.evclock[ev].items():
                if know.get(k2, 0) < v2:
                    know[k2] = v2
        return waits

    def add(self, eng, fn, r=(), w=(), dma=False, is_out=False):
        if not self.enabled:
            return None
        if RSTOP is not None and getattr(self, "counting", False):
            self.nops = getattr(self, "nops", 0) + 1
            if self.nops > RSTOP:
                return None
        xb = [k for k in r if isinstance(k, str) and k.startswith("bank")]
        if xb:
            r = [k for k in r if k not in xb]
            w = list(w) + [k for k in xb if k not in w]
        cands = set()
        if self.pending[eng]:
            cands.update(self.pending[eng])
            self.pending[eng] = []
        for k in r:
            cands.add(self.last_w.get(k))
        for k in w:
            cands.add(self.last_w.get(k))
            cands.update(self.readers.get(k, ()))
        if dma and self.ndma >= self.NDMA:
            cands.add(self.dma_events[self.ndma - self.NDMA])
        cands.discard(None)
        waits = self._resolve(eng, cands)
        clk = dict(self.know[eng])
        if dma:
            j = self.ndma
            self.ndma += 1
            sk = ("dma", j % self.NDMA)
            val = 16 * (j // self.NDMA + 1)
            ev = (sk, val)
            self.dma_events.append(ev)
            inc = 16
            if is_out:
                self.out_events.append(ev)
        else:
            i = self.cnt[eng]
            self.cnt[eng] += 1
            ep = i // self.EPOCH
            sk = (eng, ep)
            ev = (sk, i % self.EPOCH + 1)
            inc = 1
            self.last_ev[eng] = ev
            for e2 in range(ep):
                clk[(eng, e2)] = self.EPOCH
        clk[sk] = max(clk.get(sk, 0), ev[1])
        self.evclock[ev] = clk
        self.evidx[ev] = self.nev
        self.nev += 1
        self.semkeys.add(sk)
        self.ops[eng].append((waits, fn, ev, inc))
        for k in r:
            self.readers.setdefault(k, []).append(ev)
        for k in w:
            self.last_w[k] = ev
            self.readers[k] = []
        return ev

    def finish(self):
        cands = set(self.dma_events[-self.NDMA:]) | set(self.out_events)
        waits = self._resolve("sp", cands)
        self.ops["sp"].append((waits, None, None, 0))

    def emit(self, nc, stack):
        targets = set()
        for e in ENGS:
            for waits, fn, ev, inc in self.ops[e]:
                targets.update(waits)
        real = {}
        used = set()
        for e in ENGS:
            cnt = {}
            for waits, fn, ev, inc in self.ops[e]:
                if ev is None:
                    continue
                if ev[0][0] == "dma":
                    real[ev] = ev[1]
                    used.add(ev[0])
                elif ev in targets:
                    cnt[ev[0]] = cnt.get(ev[0], 0) + 1
                    real[ev] = cnt[ev[0]]
                    used.add(ev[0])
        sems = {}
        for sk in sorted(used, key=str):
            sems[sk] = stack.enter_context(nc.semaphore("s_" + "_".join(str(x) for x in sk)))
        block = stack.enter_context(nc.Block())
        prog = self

        def run(engname):
            def body(eng):
                for waits, fn, ev, inc in prog.ops[engname]:
                    if fn is None:
                        for w_ in waits:
                            eng.wait_ge(sems[w_[0]], real[w_])
                        continue
                    for w_ in waits[:-1]:
                        eng.wait_ge(sems[w_[0]], real[w_])
                    ins = fn(eng)
                    if waits:
                        ins._wait_ge(sems[waits[-1][0]], real[waits[-1]])
                    if ev in real:
                        ins.then_inc(sems[ev[0]], inc)
            return body

        block.tensor(run("pe"))
        block.scalar(run("act"))
        block.vector(run("dve"))
        block.gpsimd(run("pool"))
        block.sync(run("sp"))

    def act(self, out, in_, func, r, w, bias=0.0, scale=1.0, accum=None):
        if accum is None:
            return self.add("act", lambda e: e.activation(out=out, in_=in_, func=func, bias=bias, scale=scale), r, w)
        return self.add("act", lambda e: e.activation(out=out, in_=in_, func=func, bias=bias, scale=scale,
                                                      accum_out=accum), r, w)

    def ts(self, eng, out, in0, s1, s2, op0, op1, r, w):
        return self.add(eng, lambda e: e.tensor_scalar(out=out, in0=in0, scalar1=s1, scalar2=s2, op0=op0, op1=op1), r, w)

    def stt(self, out, in0, scalar, in1, op0, op1, r, w, accum=None):
        if accum is None:
            return self.add("dve", lambda e: e.scalar_tensor_tensor(out=out, in0=in0, scalar=scalar, in1=in1,
                                                                     op0=op0, op1=op1), r, w)
        return self.add("dve", lambda e: e.scalar_tensor_tensor(out=out, in0=in0, scalar=scalar, in1=in1,
                                                                 op0=op0, op1=op1, accum_out=accum), r, w)

    def tt(self, eng, out, in0, in1, op, r, w):
        return self.add(eng, lambda e: e.tensor_tensor(out=out, in0=in0, in1=in1, op=op), r, w)

    def cp(self, eng, out, in_, r, w):
        if eng == "act":
            return self.add("act", lambda e: e.activation(out=out, in_=in_, func=AF.Copy), r, w)
        return self.add(eng, lambda e: e.tensor_copy(out=out, in_=in_), r, w)

    def memset(self, eng, out, val, w):
        return self.add(eng, lambda e: e.memset(out, val), (), w)

    def mm(self, out, lhsT, rhs, start, stop, r, w):
        return self.add("pe", lambda e: e.matmul(out, lhsT=lhsT, rhs=rhs, start=start, stop=stop), r, w)

    def tr(self, out, in_, ident, r, w):
        return self.add("pe", lambda e: e.transpose(out, in_, ident), r, w)

    def dma(self, out, in_, r, w, is_out=False):
        return self.add("sp", lambda e: e.dma_start(out=out, in_=in_), r, w, dma=True, is_out=is_out)

    def scan(self, out, d0, d1, r, w):
        return self.add("dve", lambda e: e.tensor_tensor_scan(out=out, data0=d0, data1=d1, initial=0.0,
                                                               op0=ALU.mult, op1=ALU.add), r, w)


class Arena:
    def __init__(self, t, words):
        self.t = t
        self.words = words
        self.off = 0
        self.marks = []
        self.peak = 0

    def alloc(self, nwords):
        o = self.off
        self.off += nwords
        self.peak = max(self.peak, self.off)
        assert self.off <= self.words, f"SBUF arena overflow {self.off}>{self.words}"
        return o

    def f32(self, n, parts=128):
        o = self.alloc(n)
        return self.t[0:parts, o:o + n]

    def bf16(self, n, parts=128):
        assert n % 2 == 0
        o = self.alloc(n // 2)
        return self.t[0:parts, o:o + n // 2].bitcast(BF16)

    def mark(self):
        self.marks.append(self.off)

    def release(self):
        self.off = self.marks.pop()


def v3(ap, b):
    return ap.rearrange("p (a b) -> p a b", b=b)


def build_nc(phases="0GRF"):
    nc = bass.Bass("TRN2", target_bir_lowering=False)
    di = lambda n, s: nc.dram_tensor(n, list(s), F32, kind="ExternalInput").ap()
    do = lambda n, s: nc.dram_tensor(n, list(s), F32, kind="ExternalOutput").ap()
    x = di("x", (TA, D))
    w_in = di("w_in", (D, NIN))
    alpha_w2 = di("alpha_w2", (16, 512))
    w2a2 = di("w2a2", (128, 1024))
    w_up_gla = di("w_up_gla", (D, D))
    w_up_rwkv = di("w_up_rwkv", (D, D))
    w_out = di("w_out", (D, D))
    lngb = di("lngb", (128, 2048))
    sgla = di("sgla", (NS, 4, 128, 256))
    srwkv = di("srwkv", (NS, 16, 64, 64))
    sshift = di("sshift", (NS, 4224))
    cvec = di("cvec", (128, NCV))
    cmat = di("cmat", (128, NCM))
    y = do("y", (TA, D))
    gla_p = do("gla_p", (4, 128, 256))
    rwkv_p = do("rwkv_p", (16, 64, 64))
    shift_o = do("shift_o", (17, 4224))
    gla_s = do("gla_s", (NS, 4, 128, 256))
    rwkv_s = do("rwkv_s", (NS, 16, 64, 64))

    P = Prog()
    with contextlib.ExitStack() as st:
        WORDS = 53184
        sb = st.enter_context(nc.sbuf_tensor("arena", [128, WORDS], F32))
        ar = Arena(sb, WORDS)
        banks = [st.enter_context(nc.psum_tensor(f"ps{i}", [128, 512], F32)) for i in range(8)]
        bk = lambda i: f"bank{i}"

        CV = ar.f32(NCV)
        CM = ar.f32(NCM)
        P.dma(CV, cvec, [], ["CV"])
        P.dma(CM, cmat, [], ["CM"])
        ID = CM[:, CM_ID:CM_ID + 128]
        MASK5 = CM[:, CM_MASK5:CM_MASK5 + 320]
        MASKU = CM[:, CM_MASKU:CM_MASKU + 128]
        ISTK = CM[:, CM_ISTK:CM_ISTK + 64]
        BONES = CM[:, CM_BONES:CM_BONES + 128]
        BAVG = CM[:, CM_BAVG:CM_BAVG + 128]
        ONES = CM[:, CM_ONES:CM_ONES + 128]
        SMG = CM[:, CM_SMG:CM_SMG + 512]
        SMR = CM[:, CM_SMR:CM_SMR + 512]
        ID16 = CM[0:16, CM_ID16:CM_ID16 + 16]
        IDb = ar.bf16(128)
        ISTKb = ar.bf16(64)
        BONESb = ar.bf16(128)
        ONESb = ar.bf16(128)
        NAB = ar.f32(4)
        EPSC = ar.f32(2)
        P.cp("pool", IDb, ID, ["CM"], ["IDb"])
        P.cp("pool", ISTKb, ISTK, ["CM"], ["ISTKb"])
        P.cp("pool", BONESb, BONES, ["CM"], ["BONESb"])
        P.cp("pool", ONESb, ONES, ["CM"], ["ONESb"])
        P.ts("pool", NAB, CV[:, CV_AB:CV_AB + 4], -1.0, None, ALU.mult, ALU.bypass, ["CV"], ["NAB"])
        P.memset("pool", EPSC[:, 0:1], 1e-5, ["EPSC"])
        P.memset("pool", EPSC[:, 1:2], 64e-5, ["EPSC"])
        cvc = lambda base, j: CV[:, base + j:base + j + 1]

        W2A2 = ar.f32(1024)
        P.dma(W2A2, w2a2, [], ["W2A2"])

        xT = v3(ar.bf16(KC * TA), TA)
        OgT = v3(ar.bf16(KC * TA), TA)
        OrT = v3(ar.bf16(KC * TA), TA)

        WST = v3(ar.f32(4 * 512), 512)
        WBF = [v3(ar.bf16(KC * 512), 512) for _ in range(3)]
        wstate = {"n": 0, "pre": None}
        wqueue = []

        def _issue_load(srcs):
            en = P.enabled
            P.enabled = True
            par = wstate["n"] % 3
            wstate["n"] += 1
            key = f"WBF{par}"
            for half in range(2):
                off = 0
                for s_ in srcs:
                    n = s_.shape[1]
                    P.dma(WST[:, :, off:off + n],
                          s_.rearrange("(kc p) n -> p kc n", p=128)[:, 4 * half:4 * half + 4, :], [], ["WST"])
                    off += n
                for k2 in range(2):
                    P.cp("pool", WBF[par][:, 4 * half + 2 * k2:4 * half + 2 * k2 + 2, 0:off],
                         WST[:, 2 * k2:2 * k2 + 2, 0:off], ["WST"], [key])
            P.enabled = en
            return WBF[par], key

        def load_wgroup(srcs=None):
            if wstate["pre"] is None:
                wstate["pre"] = _issue_load(wqueue.pop(0))
            cur = wstate["pre"]
            wstate["pre"] = _issue_load(wqueue.pop(0)) if wqueue else None
            return cur

        wqueue.append([w_in[:, GA:GA + 16]])
        for h_ in range(4):
            wqueue.append([w_in[:, GQ + h_ * 128:GQ + (h_ + 1) * 128], w_in[:, GK + h_ * 128:GK + (h_ + 1) * 128],
                           w_in[:, GV + h_ * 256:GV + (h_ + 1) * 256]])
            wqueue.append([w_in[:, GG + h_ * 256:GG + (h_ + 1) * 256]])
        wqueue.append([w_in[:, RW:RW + 128]])
        for p_ in range(8):
            wqueue.append([w_in[:, RR + p_ * 128:RR + (p_ + 1) * 128], w_in[:, RK + p_ * 128:RK + (p_ + 1) * 128],
                           w_in[:, RV + p_ * 128:RV + (p_ + 1) * 128], w_in[:, RG + p_ * 128:RG + (p_ + 1) * 128]])
        for dt_ in range(8):
            wqueue.append([w_in[:, G0 + dt_ * 128:G0 + (dt_ + 1) * 128],
                           w_in[:, G0 + 1024 + dt_ * 128:G0 + 1024 + (dt_ + 1) * 128],
                           w_up_gla[:, dt_ * 128:(dt_ + 1) * 128], w_up_rwkv[:, dt_ * 128:(dt_ + 1) * 128]])
        for g_ in range(2):
            wqueue.append([w_out[:, g_ * 512:(g_ + 1) * 512]])

        def proj_fm(wb, wkey, coff, ncols, s0, W, bank_i, c0=0):
            for kc in range(KC):
                P.mm(banks[bank_i][0:ncols, c0:c0 + W], wb[:, kc, coff:coff + ncols], xT[:, kc, s0:s0 + W],
                     kc == 0, kc == KC - 1, [wkey, "xT"], [bk(bank_i)])

        def proj_tm(wb, wkey, coff, ncols, t0, M, bank_i, c0=0):
            for kc in range(KC):
                P.mm(banks[bank_i][0:M, c0:c0 + ncols], xT[:, kc, t0:t0 + M], wb[:, kc, coff:coff + ncols],
                     kc == 0, kc == KC - 1, [wkey, "xT"], [bk(bank_i)])

        P.enabled = "0" in phases
        ar.mark()
        XS = [ar.f32(1024) for _ in range(2)]
        for tt in range(17):
            rows = 128 if tt < 16 else NS
            xs = XS[tt % 2]
            xk = f"XS{tt % 2}"
            P.dma(xs[0:rows, :], x[tt * 128:tt * 128 + rows, :], [], [xk])
            for half in range(2):
                b = banks[half]
                for j in range(4):
                    kc = half * 4 + j
                    P.tr(b[:, j * 128:j * 128 + rows], xs[0:rows, kc * 128:(kc + 1) * 128], ID[0:rows, 0:rows],
                         [xk, "CM"], [bk(half)])
                src = v3(b[:, 0:512], 128)[:, :, 0:rows]
                dst = xT[:, half * 4:half * 4 + 4, tt * 128:tt * 128 + rows]
                P.cp("act" if half == 0 else "dve", dst, src, [bk(half)], ["xT"])
        ar.release()
        P.fence()

        P.enabled = "G" in phases
        ar.mark()
        AW2 = ar.f32(512, parts=16)
        P.dma(AW2, alpha_w2, [], ["AW2"])
        ALR = ar.f32(TA, parts=16)
        SP = ar.f32(512); CSP = ar.f32(512); EB = ar.f32(512); EINV = ar.f32(512)
        QT = ar.bf16(512); KT = ar.bf16(512); QS = ar.f32(16)
        KH = ar.bf16(128); KHT = ar.bf16(128)
        VTK = v3(ar.bf16(4 * 256), 256)
        ATS = ar.bf16(128)
        SG = [ar.f32(256) for _ in range(2)]
        SGB = ar.bf16(256)
        GS = v3(ar.bf16(2 * 512), 512)
        SQ = v3(ar.bf16(2 * 512), 512)
        RSTD = ar.f32(512)
        ON = ar.f32(512)
        KTOK = ar.f32(128, parts=16); VTOK = ar.f32(256, parts=16)
        KM = v3(ar.f32(16 * 128, parts=16), 128)
        SS = v3(ar.f32(4 * 256), 256)
        SN = v3(ar.f32(4 * 256), 256)

        wb, wk = load_wgroup([w_in[:, GA:GA + 16]])
        for (s0, W) in SEGS:
            proj_fm(wb, wk, 0, 16, s0, W, 0)
            P.cp("act", ALR[:, s0:s0 + W], banks[0][0:16, 0:W], [bk(0)], ["ALR"])

        for h in range(4):
            wqkv, kqkv = load_wgroup([w_in[:, GQ + h * 128:GQ + (h + 1) * 128],
                                      w_in[:, GK + h * 128:GK + (h + 1) * 128],
                                      w_in[:, GV + h * 256:GV + (h + 1) * 256]])
            wg, kg = load_wgroup([w_in[:, GG + h * 256:GG + (h + 1) * 256]])
            cur = 0
            P.memset("pool", SG[0], 0.0, ["SG0"])
            P.memset("pool", SGB, 0.0, ["SGB"])
            for si, (s0, W) in enumerate(SEGS):
                samp = si == 4
                P.mm(banks[2][:, 0:W], AW2[:, h * 128:(h + 1) * 128], ALR[:, s0:s0 + W], True, True,
                     ["AW2", "ALR"], [bk(2)])
                P.act(SP[:, 0:W], banks[2][:, 0:W], AF.Exp, [bk(2), "NAB"], ["SP"], bias=NAB[:, h:h + 1], scale=-1.0)
                P.act(SP[:, 0:W], SP[:, 0:W], AF.Ln, ["SP"], ["SP"], bias=1.0)
                if samp:
                    csp = SP
                else:
                    P.scan(CSP[:, 0:W], SMG[:, 0:W], SP[:, 0:W], ["CM", "SP"], ["CSP"])
                    csp = CSP
                ck = "SP" if samp else "CSP"
                P.act(EB[:, 0:W], csp[:, 0:W], AF.Exp, [ck], ["EB"], scale=-1.0 / 16.0)
                P.act(EINV[:, 0:W], csp[:, 0:W], AF.Exp, [ck], ["EINV"], scale=1.0 / 16.0)
                proj_fm(wqkv, kqkv, 0, 128, s0, W, 0)
                if samp:
                    P.ts("dve", QS[:, 0:W], banks[0][:, 0:W], 128.0 ** -0.5, None, ALU.mult, ALU.bypass,
                         [bk(0)], ["QS"])
                else:
                    P.stt(QT[:, 0:W], banks[0][:, 0:W], 128.0 ** -0.5, EB[:, 0:W], ALU.mult, ALU.mult,
                          [bk(0), "EB"], ["QT"])
                    proj_fm(wqkv, kqkv, 128, 128, s0, W, 1)
                    P.tt("dve", KT[:, 0:W], banks[1][:, 0:W], EINV[:, 0:W], ALU.mult, [bk(1), "EINV"], ["KT"])
                for vh in range(2):
                    proj_fm(wg, kg, vh * 128, 128, s0, W, vh)
                    P.act(GS[:, vh, 0:W], banks[vh][:, 0:W], AF.Silu, [bk(vh)], ["GS"])
                if not samp:
                    for t4 in range(4):
                        bi = t4 % 2
                        proj_tm(wqkv, kqkv, 256, 256, s0 + t4 * 128, 128, bi)
                        P.cp("act", VTK[:, t4, :], banks[bi][:, 0:256], [bk(bi)], ["VTK"])
                    for c in range(4):
                        cs = slice(c * 128, (c + 1) * 128)
                        ecol = EB[:, c * 128 + 127:c * 128 + 128]
                        first = (si == 0 and c == 0)
                        P.mm(banks[4][:, 0:128], KT[:, cs], QT[:, cs], True, True, ["KT", "QT"], [bk(4)])
                        P.tt("dve", ATS, banks[4][:, 0:128], MASKU, ALU.mult, [bk(4), "CM"], ["ATS"])
                        P.ts("pool", KH, KT[:, cs], ecol, None, ALU.mult, ALU.bypass, ["KT", "EB"], ["KH"])
                        b4b = banks[4].bitcast(BF16)
                        P.tr(b4b[:, 512:640], KH, IDb, ["KH", "IDb"], [bk(4)])
                        P.cp("act", KHT, b4b[:, 512:640], [bk(4)], ["KHT"])
                        for vh in range(2):
                            vs = slice(vh * 128, (vh + 1) * 128)
                            P.mm(banks[5][:, vh * 128:(vh + 1) * 128], VTK[:, c, vs], ATS, True, first,
                                 ["VTK", "ATS"], [bk(5)])
                            if not first:
                                P.mm(banks[5][:, vh * 128:(vh + 1) * 128], SGB[:, vs], QT[:, cs], False, True,
                                     ["SGB", "QT"], [bk(5)])
                        P.cp("act", OgT[:, 2 * h:2 * h + 2, s0 + c * 128:s0 + (c + 1) * 128],
                             v3(banks[5][:, 0:256], 128), [bk(5)], ["OgT"])
                        P.mm(banks[6][:, 0:256], KHT, VTK[:, c, :], True, True, ["KHT", "VTK"], [bk(6)])
                        nxt = 1 - cur
                        P.stt(SG[nxt], SG[cur], ecol, banks[6][:, 0:256], ALU.mult, ALU.add,
                              [f"SG{cur}", "EB", bk(6)], [f"SG{nxt}"])
                        P.cp("act", SGB, SG[nxt], [f"SG{nxt}"], ["SGB"])
                        cur = nxt
                    if si == 3:
                        P.dma(gla_p[h], SG[cur], [f"SG{cur}"], [], is_out=True)
                else:
                    proj_tm(wqkv, kqkv, 128, 128, T, NS, 4)
                    P.cp("act", KTOK, banks[4][0:16, 0:128], [bk(4)], ["KTOK"])
                    proj_tm(wqkv, kqkv, 256, 256, T, NS, 4)
                    P.cp("act", VTOK, banks[4][0:16, 0:256], [bk(4)], ["VTOK"])
                    P.add("dve", lambda e: e.tensor_tensor(
                        out=KM, in0=KTOK.unsqueeze(1).to_broadcast([16, 16, 128]),
                        in1=ID16.unsqueeze(2).to_broadcast([16, 16, 128]), op=ALU.mult),
                        ["KTOK", "CM"], ["KM"])
                    for sg in range(4):
                        P.dma(SS, sgla[sg * 4:(sg + 1) * 4, h].rearrange("s d v -> d s v"), [], ["SS"])
                        for j in range(4):
                            s = sg * 4 + j
                            P.add("pe", lambda e, s=s: e.matmul(banks[6][:, 0:256], lhsT=KM[:, s, :], rhs=VTOK,
                                                                start=True, stop=True), ["KM", "VTOK"], [bk(6)])
                            P.stt(SN[:, j, :], SS[:, j, :], EB[:, s:s + 1], banks[6][:, 0:256], ALU.mult, ALU.add,
                                  ["SS", "EB", bk(6)], ["SN"])
                            for vh in range(2):
                                P.mm(banks[5][:, vh * 16 + s:vh * 16 + s + 1], SN[:, j, vh * 128:(vh + 1) * 128],
                                     QS[:, s:s + 1], True, True, ["SN", "QS"], [bk(5)])
                        P.dma(gla_s[sg * 4:(sg + 1) * 4, h].rearrange("s d v -> d s v"), SN, ["SN"], [], is_out=True)
                    P.cp("act", OgT[:, 2 * h:2 * h + 2, T:TA], v3(banks[5][:, 0:32], 16), [bk(5)], ["OgT"])
                og = OgT[:, 2 * h:2 * h + 2, s0:s0 + W]
                P.act(SQ[:, :, 0:W], og, AF.Square, ["OgT"], ["SQ"])
                for vh in range(2):
                    P.mm(banks[3][:, 0:W], ONESb, SQ[:, vh, 0:W], vh == 0, vh == 1, ["ONESb", "SQ"], [bk(3)])
                P.act(RSTD[:, 0:W], banks[3][:, 0:W], AF.Ln, [bk(3), "EPSC"], ["RSTD"], bias=EPSC[:, 0:1], scale=1.0 / 256.0)
                P.act(RSTD[:, 0:W], RSTD[:, 0:W], AF.Exp, ["RSTD"], ["RSTD"], scale=-0.5)
                for vh in range(2):
                    P.stt(ON[:, 0:W], OgT[:, 2 * h + vh, s0:s0 + W], cvc(CV_GNW, vh), RSTD[:, 0:W],
                          ALU.mult, ALU.mult, ["OgT", "CV", "RSTD"], ["ON"])
                    P.tt("pool", OgT[:, 2 * h + vh, s0:s0 + W], ON[:, 0:W], GS[:, vh, 0:W], ALU.mult,
                         ["ON", "GS"], ["OgT"])
        ar.release()
        P.fence()

        P.enabled = "R" in phases
        ar.mark()
        SHT = v3(ar.f32(33 * 16), 16)
        WSTf = WST.rearrange("p a b -> p (a b)")
        for tb in (0, 11, 22):
            P.dma(WSTf[0:16, 0:11 * 128], sshift[:, tb * 128:(tb + 11) * 128], [], ["WST"])
            for j in range(11):
                P.tr(banks[2][:, j * 16:(j + 1) * 16], WSTf[0:16, j * 128:(j + 1) * 128], ID[0:16, 0:16],
                     ["WST", "CM"], [bk(2)])
            P.cp("act", SHT[:, tb:tb + 11, :], v3(banks[2][:, 0:176], 16), [bk(2)], ["SHT"])

        LORA = ar.f32(TA)
        RAW = [ar.f32(513) for _ in range(4)]
        DQ = ar.f32(512)
        BB = ar.f32(9 * 512)
        Bp = lambda i: BB[:, i * 512:(i + 1) * 512]
        Rl, kRl = Bp(0), "B0"
        Kl, kKl = Bp(1), "B1"
        Gl, kGl = Bp(2), "B2"
        SIG, kSIG = Bp(3), "B3"
        Aa, kAa = Bp(4), "B4"
        KKr, kKKr = Bp(5), "B5"
        RN, kRN = Bp(6), "B6"
        KKn, kKKn = Bp(7), "B7"
        T1, kT1 = Bp(2), "B2"
        Kp, kKp = Bp(8), "B8"
        RKf, kRKf = Bp(5), "B5"
        BA, kBA = Bp(1), "B1"
        CS, kCS = Bp(4), "B4"
        GAM, kGAM = Bp(5), "B5"
        GIN, kGIN = Bp(6), "B6"
        CSX, kCSX = Bp(2), "B2"
        GEX, kGEX = Bp(3), "B3"
        YS, kYS = Bp(0), "B0"
        YSQ, kYSQ = Bp(1), "B1"
        MU2, kMU2 = Bp(2), "B2"
        VAR, kVAR = Bp(3), "B3"
        YD, kYD = Bp(4), "B4"
        SHO, kSHO = Bp(0)[0:17, :], "B0"
        SR, kSR = v3(BB[:, 5 * 512:7 * 512], 64), ["B5", "B6"]
        SRN, kSRN = v3(BB[:, 2 * 512:4 * 512], 64), ["B2", "B3"]
        SQb = ar.bf16(512)
        OUT = []
        for o_ in range(2):
            arkb = ar.bf16(2048)
            OUT.append(dict(
                ARKB=arkb, AR=arkb[:, 0:1024], AR4=arkb[:, 0:1024].rearrange("p (c q t) -> p c q t", q=2, t=64),
                Kt=arkb[:, 1024:1536], Bt=arkb[:, 1536:2048],
                Vb=ar.bf16(512), Gs=ar.bf16(512), BON=ar.bf16(512), GAMC=ar.f32(8),
                kAR=f"AR{o_}", kKt=f"Kt{o_}", kBt=f"Bt{o_}", kVb=f"Vb{o_}", kGs=f"Gs{o_}", kBON=f"BON{o_}",
                kGAMC=f"GAMC{o_}"))
        DG = OUT[0]["ARKB"].bitcast(F32)[:, 0:640].rearrange("p (s q k) -> p s q k", q=5, k=64)
        kDG = ["AR0", "Kt0"]
        Vf = ar.f32(16)
        CB = [(ar.bf16(320), ar.bf16(320), [ar.bf16(256) for _ in range(2)], ar.bf16(64), ar.f32(64), ar.bf16(64))
              for _ in range(3)]
        Hb = ar.bf16(64); Hf = ar.f32(64)
        XQ = v3(ar.f32(16 * 5), 5)
        TJ = ar.f32(64); T2 = ar.f32(64); SA = ar.f32(2)
        B4K = [bk(4), "b4tm", "b4s"]
        B5K = [bk(5), "b5z", "b5g", "b5r", "b5zv"]

        def lerp(q, ftile, src_bank, W, samp, out_ap, okey):
            raw = RAW[q]
            rk = f"RAW{q}"
            P.cp("act", raw[:, 1:1 + W], banks[src_bank][:, 0:W], [bk(src_bank)], [rk])
            if samp:
                P.tt("pool", DQ[:, 0:W], SHT[:, ftile, :], raw[:, 1:1 + W], ALU.subtract, ["SHT", rk], ["DQ"])
            else:
                P.tt("pool", DQ[:, 0:W], raw[:, 0:W], raw[:, 1:1 + W], ALU.subtract, [rk], ["DQ"])
            P.stt(out_ap, DQ[:, 0:W], cvc(CV_MU, ftile), raw[:, 1:1 + W], ALU.mult, ALU.add, ["DQ", "CV", rk], [okey])
            if not samp:
                P.cp("pool", raw[:, 0:1], raw[:, W:W + 1], [rk], [rk])

        wl, kl = load_wgroup([w_in[:, RW:RW + 128]])
        P.memset("pool", RAW[0][:, 0:1], 0.0, ["RAW0"])
        for si, (s0, W) in enumerate(SEGS):
            proj_fm(wl, kl, 0, 128, s0, W, 0)
            lerp(0, 32, 0, W, si == 4, LORA[:, s0:s0 + W], "LORA")
        P.act(LORA[0:64, :], LORA[0:64, :], AF.Tanh, ["LORA"], ["LORA"])
        proj_tm(wl, kl, 0, 128, T - 1, 17, 2)
        P.cp("act", SHO[:, 0:128], banks[2][0:17, 0:128], [bk(2)], [kSHO])
        P.dma(shift_o[:, 4096:4224], SHO[:, 0:128], [kSHO], [], is_out=True)

        def gn_post(p, s0, W, O):
            P.act(YSQ[:, 0:W], YS[:, 0:W], AF.Square, [kYS], [kYSQ])
            P.mm(banks[2][:, 0:W], BAVG, YS[:, 0:W], True, True, ["CM", kYS], [bk(2)])
            P.mm(banks[3][:, 0:W], BAVG, YSQ[:, 0:W], True, True, ["CM", kYSQ], [bk(3)])
            P.act(MU2[:, 0:W], banks[2][:, 0:W], AF.Square, [bk(2)], [kMU2])
            P.tt("dve", VAR[:, 0:W], banks[3][:, 0:W], MU2[:, 0:W], ALU.subtract, [bk(3), kMU2], [kVAR])
            P.act(VAR[:, 0:W], VAR[:, 0:W], AF.Ln, [kVAR], [kVAR], bias=EPSC[:, 1:2])
            P.act(VAR[:, 0:W], VAR[:, 0:W], AF.Exp, [kVAR], [kVAR], scale=-0.5)
            P.tt("dve", YD[:, 0:W], YS[:, 0:W], banks[2][:, 0:W], ALU.subtract, [kYS, bk(2)], [kYD])
            P.tt("pool", YD[:, 0:W], YD[:, 0:W], VAR[:, 0:W], ALU.mult, [kYD, kVAR], [kYD])
            P.ts("dve", YD[:, 0:W], YD[:, 0:W], cvc(CV_LW, p), cvc(CV_LB, p), ALU.mult, ALU.add, [kYD, "CV"], [kYD])
            P.tt("pool", YD[:, 0:W], YD[:, 0:W], O["BON"][:, 0:W], ALU.add, [kYD, O["kBON"]], [kYD])
            P.tt("dve", OrT[:, p, s0:s0 + W], YD[:, 0:W], O["Gs"][:, 0:W], ALU.mult, [kYD, O["kGs"]], ["OrT"])

        def mm2(out, lhsT, rhs, n0, n1, l0, l1, r0, r1, start, stop, r, w):
            for hh in range(2):
                ps = slice(hh * 64, (hh + 1) * 64)
                P.mm(out[ps, n0:n1], lhsT[ps, l0:l1], rhs[ps, r0:r1], start, stop, r, w)

        if RLEVEL < 2:
            P.enabled = False
        P.counting = True
        def elem_gen(p, wp, kp, si, O):
            s0, W = SEGS[si]
            samp = si == 4
            Vb, Gs, BON = O["Vb"], O["Gs"], O["BON"]
            AR4, Kt, Bt = O["AR4"], O["Kt"], O["Bt"]
            proj_fm(wp, kp, 0, 128, s0, W, 2)
            lerp(0, p, 2, W, samp, Rl[:, 0:W], kRl)
            yield
            proj_fm(wp, kp, 128, 128, s0, W, 2)
            lerp(1, 8 + p, 2, W, samp, Kl[:, 0:W], kKl)
            yield
            proj_fm(wp, kp, 256, 128, s0, W, 2)
            lerp(2, 16 + p, 2, W, samp, Vb[:, 0:W], O["kVb"])
            if samp:
                P.stt(Vf[:, 0:W], DQ[:, 0:W], cvc(CV_MU, 16 + p), RAW[2][:, 1:1 + W], ALU.mult, ALU.add,
                      ["DQ", "CV", "RAW2"], ["Vf"])
            proj_fm(wp, kp, 384, 128, s0, W, 2)
            lerp(3, 24 + p, 2, W, samp, Gl[:, 0:W], kGl)
            P.act(Gs[:, 0:W], Gl[:, 0:W], AF.Silu, [kGl], [O["kGs"]])
            yield
            P.mm(banks[2][:, 0:W], W2A2[0:64, p * 128:(p + 1) * 128], LORA[0:64, s0:s0 + W], True, True,
                 ["W2A2", "LORA"], [bk(2)])
            P.act(SIG[:, 0:W], banks[2][:, 0:W], AF.Sigmoid, [bk(2), "CV"], [kSIG], bias=cvc(CV_W0, p))
            P.mm(banks[2][:, 0:W], W2A2[64:128, p * 128:(p + 1) * 128], LORA[64:128, s0:s0 + W], True, True,
                 ["W2A2", "LORA"], [bk(2)])
            P.act(Aa[:, 0:W], banks[2][:, 0:W], AF.Sigmoid, [bk(2), "CV"], [kAa], bias=cvc(CV_A0, p))
            yield
            P.ts("pool", KKr[:, 0:W], Kl[:, 0:W], cvc(CV_KK, p), None, ALU.mult, ALU.bypass, [kKl, "CV"], [kKKr])
            P.act(SQb[:, 0:W], KKr[:, 0:W], AF.Square, [kKKr], ["SQb"])
            P.mm(banks[2][:, 0:W], BONESb, SQb[:, 0:W], True, True, ["BONESb", "SQb"], [bk(2)])
            P.act(RN[:, 0:W], banks[2][:, 0:W], AF.Ln, [bk(2)], [kRN])
            P.act(RN[:, 0:W], RN[:, 0:W], AF.Exp, [kRN], [kRN], scale=-0.5)
            P.tt("pool", KKn[:, 0:W], KKr[:, 0:W], RN[:, 0:W], ALU.mult, [kKKr, kRN], [kKKn])
            P.ts("dve", T1[:, 0:W], Aa[:, 0:W], -1.0, cvc(CV_KA, p), ALU.add, ALU.mult, [kAa, "CV"], [kT1])
            P.stt(Kp[:, 0:W], T1[:, 0:W], 1.0, Kl[:, 0:W], ALU.add, ALU.mult, [kT1, kKl], [kKp])
            P.tt("pool", RKf[:, 0:W], Rl[:, 0:W], Kp[:, 0:W], ALU.mult, [kRl, kKp], [kRKf])
            P.ts("pool", RKf[:, 0:W], RKf[:, 0:W], cvc(CV_RK, p), None, ALU.mult, ALU.bypass, [kRKf, "CV"], [kRKf])
            P.mm(banks[2][:, 0:W], BONES, RKf[:, 0:W], True, True, ["CM", kRKf], [bk(2)])
            P.tt("dve", BON[:, 0:W], banks[2][:, 0:W], Vb[:, 0:W], ALU.mult, [bk(2), O["kVb"]], [O["kBON"]])
            P.tt("pool", BA[:, 0:W], KKn[:, 0:W], Aa[:, 0:W], ALU.mult, [kKKn, kAa], [kBA])
            yield
            if not samp:
                P.scan(CS[:, 0:W], SMR[:, 0:W], SIG[:, 0:W], ["CM", kSIG], [kCS])
                P.act(GAM[:, 0:W], CS[:, 0:W], AF.Exp, [kCS], [kGAM], scale=-DECAY)
                P.act(GIN[:, 0:W], CS[:, 0:W], AF.Exp, [kCS], [kGIN], scale=DECAY)
                yield
                P.tt("pool", CSX[:, 0:W], CS[:, 0:W], SIG[:, 0:W], ALU.subtract, [kCS, kSIG], [kCSX])
                P.act(GEX[:, 0:W], CSX[:, 0:W], AF.Exp, [kCSX], [kGEX], scale=-DECAY)
                P.cp("pool", O["GAMC"], v3(GAM[:, 0:W], 64)[:, :, 63], [kGAM], [O["kGAMC"]])
                yield
                v64 = lambda a_: v3(a_[:, 0:W], 64)
                P.stt(AR4[:, :, 0, :], v64(KKn), -1.0, v64(GEX), ALU.mult, ALU.mult, [kKKn, kGEX], [O["kAR"]])
                P.tt("dve", AR4[:, :, 1, :], v64(Rl), v64(GAM), ALU.mult, [kRl, kGAM], [O["kAR"]])
                yield
                P.tt("pool", Kt[:, 0:W], Kp[:, 0:W], GIN[:, 0:W], ALU.mult, [kKp, kGIN], [O["kKt"]])
                P.tt("pool", Bt[:, 0:W], BA[:, 0:W], GIN[:, 0:W], ALU.mult, [kBA, kGIN], [O["kBt"]])
            else:
                P.ts("pool", XQ[:, :, 0], KKn[:, 0:W], -1.0, None, ALU.mult, ALU.bypass, [kKKn], ["XQ"])
                P.act(XQ[:, :, 1], SIG[:, 0:W], AF.Exp, [kSIG], ["XQ"], scale=-DECAY)
                P.cp("pool", XQ[:, :, 2], BA[:, 0:W], [kBA], ["XQ"])
                P.cp("pool", XQ[:, :, 3], Kp[:, 0:W], [kKp], ["XQ"])
                P.cp("pool", XQ[:, :, 4], Rl[:, 0:W], [kRl], ["XQ"])

        def chunk_gen(c, par, si, O):
            tb, db = [(4, 5), (6, 7), (0, 1)][par]
            bT, bD = banks[tb], banks[db]
            bTb = bT.bitcast(BF16)
            kT, kD = bk(tb), bk(db)
            TMp, SCp, ZPQp, GA1p, GVGp, RHp = CB[par]
            kTM, kSC, kGA1, kGVG, kRH = f"TM{par}", f"SC{par}", f"GA1{par}", f"GVG{par}", f"RH{par}"
            AR, AR4, Kt, Bt, Vb = O["AR"], O["AR4"], O["Kt"], O["Bt"], O["Vb"]
            kAR, kKt, kBt, kVb = O["kAR"], O["kKt"], O["kBt"], O["kVb"]
            cs = slice(c * 64, (c + 1) * 64)
            last = (si == 3 and c == 7)
            arc = AR[:, c * 128:(c + 1) * 128]
            for hh in range(2):
                ps = slice(hh * 64, (hh + 1) * 64)
                idb = IDb[ps, ps]
                P.tr(bTb[ps, 64:128], AR4[ps, c, 0, :], idb, [kAR, "IDb"], [kT])
                P.tr(bTb[ps, 128:192], Bt[ps, cs], idb, [kBt, "IDb"], [kT])
                P.tr(bTb[ps, 192:256], Kt[ps, cs], idb, [kKt, "IDb"], [kT])
                P.tr(bTb[ps, 256:320], Vb[ps, cs], idb, [kVb, "IDb"], [kT])
            P.cp("act", TMp[:, 64:320], bTb[:, 64:320], [kT], [kTM])
            mm2(bD, Bt, arc, 0, 128, c * 64, (c + 1) * 64, 0, 128, True, True, [kBt, kAR], [kD])
            mm2(bD, Kt, arc, 128, 256, c * 64, (c + 1) * 64, 0, 128, True, True, [kKt, kAR], [kD])
            mm2(bD, arc, Bt, 256, 320, 0, 64, c * 64, (c + 1) * 64, True, True, [kBt, kAR], [kD])
            P.tt("dve", SCp, bD[:, 0:320], MASK5, ALU.mult, [kD, "CM"], [kSC])
            yield
            mm2(bD, SCp, TMp, 448, 512, 128, 192, 256, 320, True, True, [kSC, kTM], [kD])
            P.cp("act", TMp[:, 0:64], bD[:, 448:512], [kD], [kTM])
            yield
            zsrc, zk = TMp, kTM
            psrc, pk, pc = SCp, kSC, 0
            qsrc, qk, qc = SCp, kSC, 256
            for lvl in range(6):
                mm2(bD, psrc, zsrc, 0, 128, pc, pc + 64, 0, 128, True, True, [pk, zk], [kD])
                if lvl < 5:
                    mm2(bT, qsrc, psrc, 0, 64, qc, qc + 64, pc, pc + 64, True, True, [pk, qk], [kT])
                    mm2(bT, psrc, qsrc, 64, 128, pc, pc + 64, qc, qc + 64, True, True, [pk, qk], [kT])
                dst = ZPQp[lvl % 2]
                dzk = f"Z{par}{lvl % 2}"
                dpk = f"PQ{par}{lvl % 2}"
                P.tt("dve", dst[:, 0:128], bD[:, 0:128], zsrc[:, 0:128], ALU.add, [kD, zk], [dzk])
                if lvl < 5:
                    P.cp("act", dst[:, 128:256], bT[:, 0:128], [kT], [dpk])
                zsrc, zk = dst, dzk
                psrc, pk, pc = dst, dpk, 128
                qsrc, qk, qc = dst, dpk, 192
                yield
            Wm, wk_ = zsrc, zk
            gam_col = O["GAMC"][:, c:c + 1]
            kGC = O["kGAMC"]
            mm2(bD, Wm, TMp, 256, 320, 64, 128, 128, 192, True, True, [wk_, kTM], [kD])
            mm2(bD, TMp, Wm, 320, 384, 128, 192, 0, 64, True, False, [wk_, kTM], [kD])
            mm2(bD, TMp, TMp, 320, 384, 192, 256, 256, 320, False, True, [kTM], [kD])
            mm2(bT, ISTKb, arc, 384, 448, 0, 64, 64, 128, True, False, ["ISTKb", kAR], [kT])
            mm2(bT, Wm, SCp, 384, 448, 64, 128, 64, 128, False, True, [wk_, kSC], [kT])
            P.tt("dve", GA1p, bD[:, 256:320], ISTK, ALU.add, [kD, "CM"], [kGA1])
            P.ts("dve", GVGp, bD[:, 320:384], gam_col, None, ALU.mult, ALU.bypass, [kD, kGC], [kGVG])
            P.cp("act", RHp, bT[:, 384:448], [kT], [kRH])
            yield
            b3 = banks[3]
            mm2(b3, Wm, SCp, c * 64, (c + 1) * 64, 0, 64, 64, 128, True, False, [wk_, kSC], [bk(3)])
            mm2(b3, TMp, SCp, c * 64, (c + 1) * 64, 256, 320, 192, 256, False, False, [kTM, kSC], [bk(3)])
            mm2(b3, Hb, RHp, c * 64, (c + 1) * 64, 0, 64, 0, 64, False, True, ["Hb", kRH], [bk(3)])
            mm2(banks[2], GA1p, Hb, 0, 64, 0, 64, 0, 64, True, True, [kGA1, "Hb"], [bk(2)])
            if last:
                P.stt(Hf, banks[2][:, 0:64], gam_col, GVGp, ALU.mult, ALU.add, [bk(2), kGC, kGVG], ["Hf"])
            P.stt(Hb, banks[2][:, 0:64], gam_col, GVGp, ALU.mult, ALU.add, [bk(2), kGC, kGVG], ["Hb"])

        def run_overlapped(chunk_args, extra):
            active = []
            nxt = 0
            free_par = [0, 1, 2]
            since = 99
            ex = extra
            while nxt < len(chunk_args) or active or ex is not None:
                if nxt < len(chunk_args) and free_par and (since >= 3 or not active):
                    pr_ = free_par.pop(0)
                    c_, si_, O_ = chunk_args[nxt]
                    active.append((chunk_gen(c_, pr_, si_, O_), pr_))
                    nxt += 1
                    since = 0
                since += 1
                for (g_, pr_) in list(active):
                    try:
                        next(g_)
                    except StopIteration:
                        active.remove((g_, pr_))
                        free_par.append(pr_)
                if ex is not None:
                    try:
                        next(ex)
                    except StopIteration:
                        ex = None

        for p in range(8):
            wp, kp = load_wgroup()
            for q in range(4):
                P.memset("pool", RAW[q][:, 0:1], 0.0, [f"RAW{q}"])
            P.memset("pool", Hb, 0.0, ["Hb"])
            proj_tm(wp, kp, 0, 512, T - 1, 17, 2)
            P.cp("act", SHO, banks[2][0:17, 0:512], [bk(2)], [kSHO])
            P.dma(shift_o[:, 0:4096].rearrange("t (q f) -> t q f", q=4)[:, :, p * 128:(p + 1) * 128],
                  v3(SHO, 128), [kSHO], [], is_out=True)
            run_overlapped([], elem_gen(p, wp, kp, 0, OUT[0]))
            for si in range(4):
                s0, W = SEGS[si]
                O = OUT[si % 2]
                On = OUT[(si + 1) % 2]
                run_overlapped([(c, si, O) for c in range(8)], elem_gen(p, wp, kp, si + 1, On))
                P.cp("act", YS[:, 0:W], banks[3][:, 0:W], [bk(3)], [kYS])
                if si == 3:
                    P.tr(banks[2][0:64, 64:192], Hf, ID, ["Hf", "CM"], [bk(2)])
                    P.cp("act", T2[0:64, :], banks[2][0:64, 64:128], [bk(2)], ["T2"])
                    P.cp("act", TJ[0:64, :], banks[2][0:64, 128:192], [bk(2)], ["TJ"])
                    P.dma(rwkv_p[2 * p], T2[0:64, :], ["T2"], [], is_out=True)
                    P.dma(rwkv_p[2 * p + 1], TJ[0:64, :], ["TJ"], [], is_out=True)
                gn_post(p, s0, W, O)
            s0, W = SEGS[4]
            O = OUT[0]
            P.dma(SR, srwkv[:, 2 * p:2 * p + 2].rearrange("s h v k -> (h v) s k"), [], kSR)
            for s2 in range(8):
                hs = slice(s2 * 2, s2 * 2 + 2)
                P.add("dve", lambda e, hs=hs: e.tensor_tensor(
                    out=DG, in0=ISTK.unsqueeze(1).unsqueeze(1).to_broadcast([128, 2, 5, 64]),
                    in1=XQ[:, hs, :].unsqueeze(3).to_broadcast([128, 2, 5, 64]), op=ALU.mult),
                    ["CM", "XQ"], kDG)
                for j in range(2):
                    s = s2 * 2 + j
                    bi = 4 + j
                    bb = banks[bi]
                    dgs = DG[:, j].rearrange("p q k -> p (q k)")
                    for hh in range(2):
                        ps = slice(hh * 64, (hh + 1) * 64)
                        P.mm(bb[ps, 0:320], ONES[ps, 0:64], dgs[ps, :], True, True, ["CM"] + kDG, [bk(bi)])
                    P.stt(TJ, SR[:, s, :], 1.0, bb[:, 0:64], ALU.mult, ALU.mult, kSR + [bk(bi)], ["TJ", "SA"],
                          accum=SA[:, 0:1])
                    P.tt("dve", T2, SR[:, s, :], bb[:, 64:128], ALU.mult, kSR + [bk(bi)], ["T2"])
                    P.stt(T2, bb[:, 128:192], SA[:, 0:1], T2, ALU.mult, ALU.add, [bk(bi), "SA", "T2"], ["T2"])
                    P.stt(SRN[:, s, :], bb[:, 192:256], Vf[:, s:s + 1], T2, ALU.mult, ALU.add,
                          [bk(bi), "Vf", "T2"], kSRN)
                    P.stt(TJ, SRN[:, s, :], 1.0, bb[:, 256:320], ALU.mult, ALU.mult, kSRN + [bk(bi)],
                          ["TJ", kYS], accum=YS[:, s:s + 1])
            P.dma(rwkv_s[:, 2 * p:2 * p + 2].rearrange("s h v k -> (h v) s k"), SRN, kSRN, [], is_out=True)
            gn_post(p, s0, W, O)
        ar.release()
        P.fence()

        P.enabled = "F" in phases
        ar.mark()
        MT = v3(ar.bf16(KC * TA), TA)
        FT = [(ar.bf16(512), ar.bf16(512), ar.f32(512)) for _ in range(2)]
        fcnt = 0
        for dt in range(8):
            wf, kf = load_wgroup()
            for (s0, W) in SEGS:
                par = fcnt % 2
                fcnt += 1
                SGA, SGBt, M1 = FT[par]
                kA, kB, kM = f"SGA{par}", f"SGBt{par}", f"M1{par}"
                b0, b1, b2, b3 = [4 * par + i for i in range(4)]
                proj_fm(wf, kf, 0, 128, s0, W, b0)
                proj_fm(wf, kf, 128, 128, s0, W, b1)
                for kc in range(KC):
                    P.mm(banks[b2][:, 0:W], wf[:, kc, 256:384], OgT[:, kc, s0:s0 + W], kc == 0, kc == KC - 1,
                         [kf, "OgT"], [bk(b2)])
                for kc in range(KC):
                    P.mm(banks[b3][:, 0:W], wf[:, kc, 384:512], OrT[:, kc, s0:s0 + W], kc == 0, kc == KC - 1,
                         [kf, "OrT"], [bk(b3)])
                P.act(SGA[:, 0:W], banks[b0][:, 0:W], AF.Sigmoid, [bk(b0)], [kA])
                P.act(SGBt[:, 0:W], banks[b1][:, 0:W], AF.Sigmoid, [bk(b1)], [kB])
                P.tt("dve", M1[:, 0:W], banks[b2][:, 0:W], SGA[:, 0:W], ALU.mult, [bk(b2), kA], [kM])
                P.tt("dve", MT[:, dt, s0:s0 + W], banks[b3][:, 0:W], SGBt[:, 0:W], ALU.mult, [bk(b3), kB], ["MT"])
                P.tt("pool", MT[:, dt, s0:s0 + W], MT[:, dt, s0:s0 + W], M1[:, 0:W], ALU.add, ["MT", kM], ["MT"])
        WO = v3(ar.bf16(KC * 1024), 1024)
        for g in range(2):
            wo, ko = load_wgroup([w_out[:, g * 512:(g + 1) * 512]])
            P.cp("pool", WO[:, :, g * 512:(g + 1) * 512], wo, [ko], ["WO"])
        XA = xT.rearrange("p a b -> p (a b)").bitcast(F32)
        LNGB = XA[:, 0:2048]
        XR = [XA[:, 2048 + i * 1024:2048 + (i + 1) * 1024] for i in range(2)]
        ZT = [XA[:, 4096 + i * 1024:4096 + (i + 1) * 1024] for i in range(2)]
        P.dma(LNGB, lngb, [], ["LNGB", "xT"])
        ST = ar.f32(8)
        alpha = 2.0 ** 0.25
        for tt in range(17):
            rows = 128 if tt < 16 else NS
            t0 = tt * 128
            xr = XR[tt % 2]; xk = f"XR{tt % 2}"
            zt = ZT[tt % 2]; zk = f"ZT{tt % 2}"
            P.dma(xr[0:rows, :], x[t0:t0 + rows, :], [], [xk, "xT"])
            for eh in range(2):
                bi = 2 * (tt % 2) + eh
                for kc in range(KC):
                    P.mm(banks[bi][0:rows, 0:512], MT[:, kc, t0:t0 + rows], WO[:, kc, eh * 512:(eh + 1) * 512],
                         kc == 0, kc == KC - 1, ["MT", "WO"], [bk(bi)])
                P.stt(zt[0:rows, eh * 512:(eh + 1) * 512], xr[0:rows, eh * 512:(eh + 1) * 512], alpha,
                      banks[bi][0:rows, 0:512], ALU.mult, ALU.add, [xk, bk(bi)], [zk, "xT"])
            R_ = slice(0, rows)
            P.act(xr[R_, :], zt[R_, :], AF.Copy, [zk], [xk, "ST"], accum=ST[R_, 0:1])
            P.act(xr[R_, :], zt[R_, :], AF.Square, [zk], [xk, "ST"], accum=ST[R_, 1:2])
            P.ts("pool", ST[R_, 2:3], ST[R_, 0:1], 1.0 / D, None, ALU.mult, ALU.bypass, ["ST"], ["ST"])
            P.tt("pool", ST[R_, 3:4], ST[R_, 2:3], ST[R_, 2:3], ALU.mult, ["ST"], ["ST"])
            P.stt(ST[R_, 4:5], ST[R_, 1:2], 1.0 / D, ST[R_, 3:4], ALU.mult, ALU.subtract, ["ST"], ["ST"])
            P.act(ST[R_, 5:6], ST[R_, 4:5], AF.Ln, ["ST"], ["ST"], bias=EPSC[R_, 0:1])
            P.act(ST[R_, 5:6], ST[R_, 5:6], AF.Exp, ["ST"], ["ST"], scale=-0.5)
            P.ts("dve", zt[R_, :], zt[R_, :], ST[R_, 2:3], ST[R_, 5:6], ALU.subtract, ALU.mult, [zk, "ST"], [zk])
            P.tt("pool", zt[R_, :], zt[R_, :], LNGB[R_, 0:1024], ALU.mult, [zk, "LNGB"], [zk])
            P.tt("dve", zt[R_, :], zt[R_, :], LNGB[R_, 1024:2048], ALU.add, [zk, "LNGB"], [zk])
            P.dma(y[t0:t0 + rows, :], zt[R_, :], [zk], [], is_out=True)
        ar.release()
        P.fence()

        P.enabled = True
        P.counting = False
        P.finish()
        P.emit(nc, st)
    return nc


def _consts():
    cm = np.zeros((128, NCM), np.float32)
    cm[:, CM_ID:CM_ID + 128] = np.eye(128, dtype=np.float32)
    s = np.arange(128)[:, None] % 64
    t = np.arange(64)[None, :]
    strict = (s < t).astype(np.float32)
    incl = (s <= t).astype(np.float32)
    lower = (t < s).astype(np.float32)
    cm[:, CM_MASK5:CM_MASK5 + 320] = np.concatenate([strict, incl, strict, incl, lower], axis=1)
    j = np.arange(128)[:, None]
    i = np.arange(128)[None, :]
    cm[:, CM_MASKU:CM_MASKU + 128] = (j <= i).astype(np.float32)
    cm[:, CM_ISTK:CM_ISTK + 64] = (s == t).astype(np.float32)
    blk = (np.arange(128)[:, None] // 64 == np.arange(128)[None, :] // 64).astype(np.float32)
    cm[:, CM_BONES:CM_BONES + 128] = blk
    cm[:, CM_BAVG:CM_BAVG + 128] = blk / 64.0
    cm[:, CM_ONES:CM_ONES + 128] = 1.0
    smg = np.ones(512, np.float32); smg[::128] = 0
    smr = np.ones(512, np.float32); smr[::64] = 0
    cm[:, CM_SMG:CM_SMG + 512] = smg[None, :]
    cm[:, CM_SMR:CM_SMR + 512] = smr[None, :]
    cm[0:16, CM_ID16:CM_ID16 + 16] = np.eye(16, dtype=np.float32)
    return cm


def _cols(vec):
    return np.ascontiguousarray(np.asarray(vec, np.float32).reshape(-1, 128).T)


_NC_CACHE = {}


def kernel(x_prompt, x_sample, state_gla, state_rwkv, state_rwkv_shift, w_in, gla_alpha_w2,
           gla_alpha_b, gla_norm_w, rwkv_mu, rwkv_w0, rwkv_w2, rwkv_a0, rwkv_a2, rwkv_k_k,
           rwkv_k_a, rwkv_r_k, rwkv_lnx_w, rwkv_lnx_b, w_up_gla, w_up_rwkv, w_out, ln_g, ln_b):
    f = lambda a: np.ascontiguousarray(np.asarray(a, dtype=np.float32))
    x_prompt, x_sample = f(x_prompt), f(x_sample)
    cvec = np.concatenate([_cols(rwkv_mu[0]), _cols(gla_alpha_b[0]), _cols(gla_norm_w[0]), _cols(rwkv_w0[0]),
                           _cols(rwkv_a0[0]), _cols(rwkv_k_k[0]), _cols(rwkv_k_a[0]),
                           _cols(np.asarray(rwkv_r_k[0]).reshape(-1)), _cols(rwkv_lnx_w[0]), _cols(rwkv_lnx_b[0])],
                          axis=1)
    assert cvec.shape == (128, NCV)
    cmat = _consts()
    w2a2 = np.concatenate([f(rwkv_w2[0]), f(rwkv_a2[0])], axis=0)
    lngb = np.concatenate([np.broadcast_to(f(ln_g[0])[None, :], (128, D)),
                           np.broadcast_to(f(ln_b[0])[None, :], (128, D))], axis=1)
    lngb = np.ascontiguousarray(lngb)
    common = dict(w_in=f(w_in[0]), alpha_w2=f(gla_alpha_w2[0]), w2a2=w2a2, w_up_gla=f(w_up_gla[0]),
                  w_up_rwkv=f(w_up_rwkv[0]), w_out=f(w_out[0]), lngb=lngb, cvec=f(cvec), cmat=cmat)
    in_maps = []
    for c in range(8):
        m = dict(common)
        m["x"] = np.ascontiguousarray(np.concatenate([x_prompt[c], x_sample[NS * c:NS * (c + 1), 0]], axis=0))
        m["sgla"] = f(state_gla[0, NS * c:NS * (c + 1)])
        m["srwkv"] = f(state_rwkv[0, NS * c:NS * (c + 1)])
        m["sshift"] = f(state_rwkv_shift[0, NS * c:NS * (c + 1)])
        in_maps.append(m)
    if "nc" not in _NC_CACHE:
        _NC_CACHE["nc"] = build_nc()
    nc = _NC_CACHE["nc"]
    res = run_bass_kernel_spmd(nc, in_maps, core_ids=list(range(8)))
    rs = res.results
    y_prompt = np.stack([rs[c]["y"][0:T] for c in range(8)], axis=0)
    y_sample = np.concatenate([rs[c]["y"][T:TA] for c in range(8)], axis=0)[:, None, :]
    gla_p = np.stack([rs[c]["gla_p"] for c in range(8)], axis=0)[None]
    rwkv_p = np.stack([rs[c]["rwkv_p"] for c in range(8)], axis=0)[None]
    shift_p = np.stack([rs[c]["shift_o"][0] for c in range(8)], axis=0)[None]
    gla_s = np.concatenate([rs[c]["gla_s"] for c in range(8)], axis=0)[None]
    rwkv_s = np.concatenate([rs[c]["rwkv_s"] for c in range(8)], axis=0)[None]
    shift_s = np.concatenate([rs[c]["shift_o"][1:17] for c in range(8)], axis=0)[None]
    outs = (y_prompt, y_sample, gla_p, rwkv_p, shift_p, gla_s, rwkv_s, shift_s)
    return tuple(np.ascontiguousarray(o, dtype=np.float32) for o in outs)
```

```python
import contextlib
import numpy as np
import concourse.bass as bass
import concourse.mybir as mybir
from concourse.bass_utils import run_bass_kernel_spmd

F32 = mybir.dt.float32
BF16 = mybir.dt.bfloat16
AF = mybir.ActivationFunctionType
ALU = mybir.AluOpType

ENGS = ["pe", "act", "dve", "pool", "sp"]

T = 2048
NS = 16
TA = T + NS
D = 1024
KC = 8
NIN = 9360
GQ, GK, GV, GG, GA = 0, 512, 1024, 2048, 3072
R0 = 3088
RR, RK, RV, RG, RW = R0, R0 + 1024, R0 + 2048, R0 + 3072, R0 + 4096
G0 = R0 + 4224
SEGS = [(0, 512), (512, 512), (1024, 512), (1536, 512), (2048, 16)]
DECAY = 0.606531
RLEVEL = 99
RPAIRS = 8
RSTOP = None
RLOG = []

CV_MU, CV_AB, CV_GNW, CV_W0, CV_A0, CV_KK, CV_KA, CV_RK, CV_LW, CV_LB = 0, 33, 37, 39, 47, 55, 63, 71, 79, 87
NCV = 95
CM_ID, CM_MASK5, CM_MASKU, CM_ISTK, CM_BONES, CM_BAVG, CM_ONES, CM_SMG, CM_SMR, CM_ID16 = (
    0, 128, 448, 576, 640, 768, 896, 1024, 1536, 2048)
NCM = 2064


class Prog:
    EPOCH = 8192
    NDMA = 14

    def __init__(self):
        self.ops = {e: [] for e in ENGS}
        self.cnt = {e: 0 for e in ENGS}
        self.ndma = 0
        self.dma_events = []
        self.last_w = {}
        self.readers = {}
        self.waited = {e: {} for e in ENGS}
        self.semkeys = set()
        self.out_events = []
        self.enabled = True
        self.pending = {e: [] for e in ENGS}
        self.last_ev = {}
        self.know = {e: {} for e in ENGS}
        self.evclock = {}
        self.evidx = {}
        self.nev = 0

    def fence(self):
        evs = list(self.last_ev.values()) + list(self.dma_events[-self.NDMA:])
        for e in ENGS:
            self.pending[e] = list(evs)

    def _resolve(self, eng, cands):
        know = self.know[eng]
        waits = []
        for ev in sorted(cands, key=lambda e: -self.evidx[e]):
            sk, val = ev
            if eng == "pe" and sk[0] == "pe":
                continue
            if know.get(sk, 0) >= val:
                continue
            waits.append(ev)
            for k2, v2 in self.evclock[ev].items():
                if know.get(k2, 0) < v2:
                    know[k2] = v2
        return waits

    def add(self, eng, fn, r=(), w=(), dma=False, is_out=False):
        if not self.enabled:
            return None
        if RSTOP is not None and getattr(self, "counting", False):
            self.nops = getattr(self, "nops", 0) + 1
            if self.nops > RSTOP:
                return None
        xb = [k for k in r if isinstance(k, str) and k.startswith("bank")]
        if xb:
            r = [k for k in r if k not in xb]
            w = list(w) + [k for k in xb if k not in w]
        cands = set()
        if self.pending[eng]:
            cands.update(self.pending[eng])
            self.pending[eng] = []
        for k in r:
            cands.add(self.last_w.get(k))
        for k in w:
            cands.add(self.last_w.get(k))
            cands.update(self.readers.get(k, ()))
        if dma and self.ndma >= self.NDMA:
            cands.add(self.dma_events[self.ndma - self.NDMA])
        cands.discard(None)
        waits = self._resolve(eng, cands)
        clk = dict(self.know[eng])
        if dma:
            j = self.ndma
            self.ndma += 1
            sk = ("dma", j % self.NDMA)
            val = 16 * (j // self.NDMA + 1)
            ev = (sk, val)
            self.dma_events.append(ev)
            inc = 16
            if is_out:
                self.out_events.append(ev)
        else:
            i = self.cnt[eng]
            self.cnt[eng] += 1
            ep = i // self.EPOCH
            sk = (eng, ep)
            ev = (sk, i % self.EPOCH + 1)
            inc = 1
            self.last_ev[eng] = ev
            for e2 in range(ep):
                clk[(eng, e2)] = self.EPOCH
        clk[sk] = max(clk.get(sk, 0), ev[1])
        self.evclock[ev] = clk
        self.evidx[ev] = self.nev
        self.nev += 1
        self.semkeys.add(sk)
        self.ops[eng].append((waits, fn, ev, inc))
        for k in r:
            self.readers.setdefault(k, []).append(ev)
        for k in w:
            self.last_w[k] = ev
            self.readers[k] = []
        return ev

    def finish(self):
        cands = set(self.dma_events[-self.NDMA:]) | set(self.out_events)
        waits = self._resolve("sp", cands)
        self.ops["sp"].append((waits, None, None, 0))

    def emit(self, nc, stack):
        targets = set()
        for e in ENGS:
            for waits, fn, ev, inc in self.ops[e]:
                targets.update(waits)
        real = {}
        used = set()
        for e in ENGS:
            cnt = {}
            for waits, fn, ev, inc in self.ops[e]:
                if ev is None:
                    continue
                if ev[0][0] == "dma":
                    real[ev] = ev[1]
                    used.add(ev[0])
                elif ev in targets:
                    cnt[ev[0]] = cnt.get(ev[0], 0) + 1
                    real[ev] = cnt[ev[0]]
                    used.add(ev[0])
        sems = {}
        for sk in sorted(used, key=str):
            sems[sk] = stack.enter_context(nc.semaphore("s_" + "_".join(str(x) for x in sk)))
        block = stack.enter_context(nc.Block())
        prog = self

        def run(engname):
            def body(eng):
                for waits, fn, ev, inc in prog.ops[engname]:
                    if fn is None:
                        for w_ in waits:
                            eng.wait_ge(sems[w_[0]], real[w_])
                        continue
                    for w_ in waits[:-1]:
                        eng.wait_ge(sems[w_[0]], real[w_])
                    ins = fn(eng)
                    if waits:
                        ins._wait_ge(sems[waits[-1][0]], real[waits[-1]])
                    if ev in real:
                        ins.then_inc(sems[ev[0]], inc)
            return body

        block.tensor(run("pe"))
        block.scalar(run("act"))
        block.vector(run("dve"))
        block.gpsimd(run("pool"))
        block.sync(run("sp"))

    def act(self, out, in_, func, r, w, bias=0.0, scale=1.0, accum=None):
        if accum is None:
            return self.add("act", lambda e: e.activation(out=out, in_=in_, func=func, bias=bias, scale=scale), r, w)
        return self.add("act", lambda e: e.activation(out=out, in_=in_, func=func, bias=bias, scale=scale,
                                                      accum_out=accum), r, w)

    def ts(self, eng, out, in0, s1, s2, op0, op1, r, w):
        return self.add(eng, lambda e: e.tensor_scalar(out=out, in0=in0, scalar1=s1, scalar2=s2, op0=op0, op1=op1), r, w)

    def stt(self, out, in0, scalar, in1, op0, op1, r, w, accum=None):
        if accum is None:
            return self.add("dve", lambda e: e.scalar_tensor_tensor(out=out, in0=in0, scalar=scalar, in1=in1,
                                                                     op0=op0, op1=op1), r, w)
        return self.add("dve", lambda e: e.scalar_tensor_tensor(out=out, in0=in0, scalar=scalar, in1=in1,
                                                                 op0=op0, op1=op1, accum_out=accum), r, w)

    def tt(self, eng, out, in0, in1, op, r, w):
        return self.add(eng, lambda e: e.tensor_tensor(out=out, in0=in0, in1=in1, op=op), r, w)

    def cp(self, eng, out, in_, r, w):
        if eng == "act":
            return self.add("act", lambda e: e.activation(out=out, in_=in_, func=AF.Copy), r, w)
        return self.add(eng, lambda e: e.tensor_copy(out=out, in_=in_), r, w)

    def memset(self, eng, out, val, w):
        return self.add(eng, lambda e: e.memset(out, val), (), w)

    def mm(self, out, lhsT, rhs, start, stop, r, w):
        return self.add("pe", lambda e: e.matmul(out, lhsT=lhsT, rhs=rhs, start=start, stop=stop), r, w)

    def tr(self, out, in_, ident, r, w):
        return self.add("pe", lambda e: e.transpose(out, in_, ident), r, w)

    def dma(self, out, in_, r, w, is_out=False):
        return self.add("sp", lambda e: e.dma_start(out=out, in_=in_), r, w, dma=True, is_out=is_out)

    def scan(self, out, d0, d1, r, w):
        return self.add("dve", lambda e: e.tensor_tensor_scan(out=out, data0=d0, data1=d1, initial=0.0,
                                                               op0=ALU.mult, op1=ALU.add), r, w)


class Arena:
    def __init__(self, t, words):
        self.t = t
        self.words = words
        self.off = 0
        self.marks = []
        self.peak = 0

    def alloc(self, nwords):
        o = self.off
        self.off += nwords
        self.peak = max(self.peak, self.off)
        assert self.off <= self.words, f"SBUF arena overflow {self.off}>{self.words}"
        return o

    def f32(self, n, parts=128):
        o = self.alloc(n)
        return self.t[0:parts, o:o + n]

    def bf16(self, n, parts=128):
        assert n % 2 == 0
        o = self.alloc(n // 2)
        return self.t[0:parts, o:o + n // 2].bitcast(BF16)

    def mark(self):
        self.marks.append(self.off)

    def release(self):
        self.off = self.marks.pop()


def v3(ap, b):
    return ap.rearrange("p (a b) -> p a b", b=b)


def build_nc(phases="0GRF"):
    nc = bass.Bass("TRN2", target_bir_lowering=False)
    di = lambda n, s: nc.dram_tensor(n, list(s), F32, kind="ExternalInput").ap()
    do = lambda n, s: nc.dram_tensor(n, list(s), F32, kind="ExternalOutput").ap()
    x = di("x", (TA, D))
    w_in = di("w_in", (D, NIN))
    alpha_w2 = di("alpha_w2", (16, 512))
    w2a2 = di("w2a2", (128, 1024))
    w_up_gla = di("w_up_gla", (D, D))
    w_up_rwkv = di("w_up_rwkv", (D, D))
    w_out = di("w_out", (D, D))
    lngb = di("lngb", (128, 2048))
    sgla = di("sgla", (NS, 4, 128, 256))
    srwkv = di("srwkv", (NS, 16, 64, 64))
    sshift = di("sshift", (NS, 4224))
    cvec = di("cvec", (128, NCV))
    cmat = di("cmat", (128, NCM))
    y = do("y", (TA, D))
    gla_p = do("gla_p", (4, 128, 256))
    rwkv_p = do("rwkv_p", (16, 64, 64))
    shift_o = do("shift_o", (17, 4224))
    gla_s = do("gla_s", (NS, 4, 128, 256))
    rwkv_s = do("rwkv_s", (NS, 16, 64, 64))

    P = Prog()
    with contextlib.ExitStack() as st:
        WORDS = 53184
        sb = st.enter_context(nc.sbuf_tensor("arena", [128, WORDS], F32))
        ar = Arena(sb, WORDS)
        banks = [st.enter_context(nc.psum_tensor(f"ps{i}", [128, 512], F32)) for i in range(8)]
        bk = lambda i: f"bank{i}"

        CV = ar.f32(NCV)
        CM = ar.f32(NCM)
        P.dma(CV, cvec, [], ["CV"])
        P.dma(CM, cmat, [], ["CM"])
        ID = CM[:, CM_ID:CM_ID + 128]
        MASK5 = CM[:, CM_MASK5:CM_MASK5 + 320]
        MASKU = CM[:, CM_MASKU:CM_MASKU + 128]
        ISTK = CM[:, CM_ISTK:CM_ISTK + 64]
        BONES = CM[:, CM_BONES:CM_BONES + 128]
        BAVG = CM[:, CM_BAVG:CM_BAVG + 128]
        ONES = CM[:, CM_ONES:CM_ONES + 128]
        SMG = CM[:, CM_SMG:CM_SMG + 512]
        SMR = CM[:, CM_SMR:CM_SMR + 512]
        ID16 = CM[0:16, CM_ID16:CM_ID16 + 16]
        IDb = ar.bf16(128)
        ISTKb = ar.bf16(64)
        BONESb = ar.bf16(128)
        ONESb = ar.bf16(128)
        NAB = ar.f32(4)
        EPSC = ar.f32(2)
        P.cp("pool", IDb, ID, ["CM"], ["IDb"])
        P.cp("pool", ISTKb, ISTK, ["CM"], ["ISTKb"])
        P.cp("pool", BONESb, BONES, ["CM"], ["BONESb"])
        P.cp("pool", ONESb, ONES, ["CM"], ["ONESb"])
        P.ts("pool", NAB, CV[:, CV_AB:CV_AB + 4], -1.0, None, ALU.mult, ALU.bypass, ["CV"], ["NAB"])
        P.memset("pool", EPSC[:, 0:1], 1e-5, ["EPSC"])
        P.memset("pool", EPSC[:, 1:2], 64e-5, ["EPSC"])
        cvc = lambda base, j: CV[:, base + j:base + j + 1]

        W2A2 = ar.f32(1024)
        P.dma(W2A2, w2a2, [], ["W2A2"])

        xT = v3(ar.bf16(KC * TA), TA)
        OgT = v3(ar.bf16(KC * TA), TA)
        OrT = v3(ar.bf16(KC * TA), TA)

        WST = v3(ar.f32(4 * 512), 512)
        WBF = [v3(ar.bf16(KC * 512), 512) for _ in range(3)]
        wstate = {"n": 0, "pre": None}
        wqueue = []

        def _issue_load(srcs):
            en = P.enabled
            P.enabled = True
            par = wstate["n"] % 3
            wstate["n"] += 1
            key = f"WBF{par}"
            for half in range(2):
                off = 0
                for s_ in srcs:
                    n = s_.shape[1]
                    P.dma(WST[:, :, off:off + n],
                          s_.rearrange("(kc p) n -> p kc n", p=128)[:, 4 * half:4 * half + 4, :], [], ["WST"])
                    off += n
                for k2 in range(2):
                    P.cp("pool", WBF[par][:, 4 * half + 2 * k2:4 * half + 2 * k2 + 2, 0:off],
                         WST[:, 2 * k2:2 * k2 + 2, 0:off], ["WST"], [key])
            P.enabled = en
            return WBF[par], key

        def load_wgroup(srcs=None):
            if wstate["pre"] is None:
                wstate["pre"] = _issue_load(wqueue.pop(0))
            cur = wstate["pre"]
            wstate["pre"] = _issue_load(wqueue.pop(0)) if wqueue else None
            return cur

        wqueue.append([w_in[:, GA:GA + 16]])
        for h_ in range(4):
            wqueue.append([w_in[:, GQ + h_ * 128:GQ + (h_ + 1) * 128], w_in[:, GK + h_ * 128:GK + (h_ + 1) * 128],
                           w_in[:, GV + h_ * 256:GV + (h_ + 1) * 256]])
            wqueue.append([w_in[:, GG + h_ * 256:GG + (h_ + 1) * 256]])
        wqueue.append([w_in[:, RW:RW + 128]])
        for p_ in range(8):
            wqueue.append([w_in[:, RR + p_ * 128:RR + (p_ + 1) * 128], w_in[:, RK + p_ * 128:RK + (p_ + 1) * 128],
                           w_in[:, RV + p_ * 128:RV + (p_ + 1) * 128], w_in[:, RG + p_ * 128:RG + (p_ + 1) * 128]])
        for dt_ in range(8):
            wqueue.append([w_in[:, G0 + dt_ * 128:G0 + (dt_ + 1) * 128],
                           w_in[:, G0 + 1024 + dt_ * 128:G0 + 1024 + (dt_ + 1) * 128],
                           w_up_gla[:, dt_ * 128:(dt_ + 1) * 128], w_up_rwkv[:, dt_ * 128:(dt_ + 1) * 128]])
        for g_ in range(2):
            wqueue.append([w_out[:, g_ * 512:(g_ + 1) * 512]])

        def proj_fm(wb, wkey, coff, ncols, s0, W, bank_i, c0=0):
            for kc in range(KC):
                P.mm(banks[bank_i][0:ncols, c0:c0 + W], wb[:, kc, coff:coff + ncols], xT[:, kc, s0:s0 + W],
                     kc == 0, kc == KC - 1, [wkey, "xT"], [bk(bank_i)])

        def proj_tm(wb, wkey, coff, ncols, t0, M, bank_i, c0=0):
            for kc in range(KC):
                P.mm(banks[bank_i][0:M, c0:c0 + ncols], xT[:, kc, t0:t0 + M], wb[:, kc, coff:coff + ncols],
                     kc == 0, kc == KC - 1, [wkey, "xT"], [bk(bank_i)])

        P.enabled = "0" in phases
        ar.mark()
        XS = [ar.f32(1024) for _ in range(2)]
        for tt in range(17):
            rows = 128 if tt < 16 else NS
            xs = XS[tt % 2]
            xk = f"XS{tt % 2}"
            P.dma(xs[0:rows, :], x[tt * 128:tt * 128 + rows, :], [], [xk])
            for half in range(2):
                b = banks[half]
                for j in range(4):
                    kc = half * 4 + j
                    P.tr(b[:, j * 128:j * 128 + rows], xs[0:rows, kc * 128:(kc + 1) * 128], ID[0:rows, 0:rows],
                         [xk, "CM"], [bk(half)])
                src = v3(b[:, 0:512], 128)[:, :, 0:rows]
                dst = xT[:, half * 4:half * 4 + 4, tt * 128:tt * 128 + rows]
                P.cp("act" if half == 0 else "dve", dst, src, [bk(half)], ["xT"])
        ar.release()
        P.fence()

        P.enabled = "G" in phases
        ar.mark()
        AW2 = ar.f32(512, parts=16)
        P.dma(AW2, alpha_w2, [], ["AW2"])
        ALR = ar.f32(TA, parts=16)
        SP = ar.f32(512); CSP = ar.f32(512); EB = ar.f32(512); EINV = ar.f32(512)
        QT = ar.bf16(512); KT = ar.bf16(512); QS = ar.f32(16)
        KH = ar.bf16(128); KHT = ar.bf16(128)
        VTK = v3(ar.bf16(4 * 256), 256)
        ATS = ar.bf16(128)
        SG = [ar.f32(256) for _ in range(2)]
        SGB = ar.bf16(256)
        GS = v3(ar.bf16(2 * 512), 512)
        SQ = v3(ar.bf16(2 * 512), 512)
        RSTD = ar.f32(512)
        ON = ar.f32(512)
        KTOK = ar.f32(128, parts=16); VTOK = ar.f32(256, parts=16)
        KM = v3(ar.f32(16 * 128, parts=16), 128)
        SS = v3(ar.f32(4 * 256), 256)
        SN = v3(ar.f32(4 * 256), 256)

        wb, wk = load_wgroup([w_in[:, GA:GA + 16]])
        for (s0, W) in SEGS:
            proj_fm(wb, wk, 0, 16, s0, W, 0)
            P.cp("act", ALR[:, s0:s0 + W], banks[0][0:16, 0:W], [bk(0)], ["ALR"])

        for h in range(4):
            wqkv, kqkv = load_wgroup([w_in[:, GQ + h * 128:GQ + (h + 1) * 128],
                                      w_in[:, GK + h * 128:GK + (h + 1) * 128],
                                      w_in[:, GV + h * 256:GV + (h + 1) * 256]])
            wg, kg = load_wgroup([w_in[:, GG + h * 256:GG + (h + 1) * 256]])
            cur = 0
            P.memset("pool", SG[0], 0.0, ["SG0"])
            P.memset("pool", SGB, 0.0, ["SGB"])
            for si, (s0, W) in enumerate(SEGS):
                samp = si == 4
                P.mm(banks[2][:, 0:W], AW2[:, h * 128:(h + 1) * 128], ALR[:, s0:s0 + W], True, True,
                     ["AW2", "ALR"], [bk(2)])
                P.act(SP[:, 0:W], banks[2][:, 0:W], AF.Exp, [bk(2), "NAB"], ["SP"], bias=NAB[:, h:h + 1], scale=-1.0)
                P.act(SP[:, 0:W], SP[:, 0:W], AF.Ln, ["SP"], ["SP"], bias=1.0)
                if samp:
                    csp = SP
                else:
                    P.scan(CSP[:, 0:W], SMG[:, 0:W], SP[:, 0:W], ["CM", "SP"], ["CSP"])
                    csp = CSP
                ck = "SP" if samp else "CSP"
                P.act(EB[:, 0:W], csp[:, 0:W], AF.Exp, [ck], ["EB"], scale=-1.0 / 16.0)
                P.act(EINV[:, 0:W], csp[:, 0:W], AF.Exp, [ck], ["EINV"], scale=1.0 / 16.0)
                proj_fm(wqkv, kqkv, 0, 128, s0, W, 0)
                if samp:
                    P.ts("dve", QS[:, 0:W], banks[0][:, 0:W], 128.0 ** -0.5, None, ALU.mult, ALU.bypass,
                         [bk(0)], ["QS"])
                else:
                    P.stt(QT[:, 0:W], banks[0][:, 0:W], 128.0 ** -0.5, EB[:, 0:W], ALU.mult, ALU.mult,
                          [bk(0), "EB"], ["QT"])
                    proj_fm(wqkv, kqkv, 128, 128, s0, W, 1)
                    P.tt("dve", KT[:, 0:W], banks[1][:, 0:W], EINV[:, 0:W], ALU.mult, [bk(1), "EINV"], ["KT"])
                for vh in range(2):
                    proj_fm(wg, kg, vh * 128, 128, s0, W, vh)
                    P.act(GS[:, vh, 0:W], banks[vh][:, 0:W], AF.Silu, [bk(vh)], ["GS"])
                if not samp:
                    for t4 in range(4):
                        bi = t4 % 2
                        proj_tm(wqkv, kqkv, 256, 256, s0 + t4 * 128, 128, bi)
                        P.cp("act", VTK[:, t4, :], banks[bi][:, 0:256], [bk(bi)], ["VTK"])
                    for c in range(4):
                        cs = slice(c * 128, (c + 1) * 128)
                        ecol = EB[:, c * 128 + 127:c * 128 + 128]
                        first = (si == 0 and c == 0)
                        P.mm(banks[4][:, 0:128], KT[:, cs], QT[:, cs], True, True, ["KT", "QT"], [bk(4)])
                        P.tt("dve", ATS, banks[4][:, 0:128], MASKU, ALU.mult, [bk(4), "CM"], ["ATS"])
                        P.ts("pool", KH, KT[:, cs], ecol, None, ALU.mult, ALU.bypass, ["KT", "EB"], ["KH"])
                        b4b = banks[4].bitcast(BF16)
                        P.tr(b4b[:, 512:640], KH, IDb, ["KH", "IDb"], [bk(4)])
                        P.cp("act", KHT, b4b[:, 512:640], [bk(4)], ["KHT"])
                        for vh in range(2):
                            vs = slice(vh * 128, (vh + 1) * 128)
                            P.mm(banks[5][:, vh * 128:(vh + 1) * 128], VTK[:, c, vs], ATS, True, first,
                                 ["VTK", "ATS"], [bk(5)])
                            if not first:
                                P.mm(banks[5][:, vh * 128:(vh + 1) * 128], SGB[:, vs], QT[:, cs], False, True,
                                     ["SGB", "QT"], [bk(5)])
                        P.cp("act", OgT[:, 2 * h:2 * h + 2, s0 + c * 128:s0 + (c + 1) * 128],
                             v3(banks[5][:, 0:256], 128), [bk(5)], ["OgT"])
                        P.mm(banks[6][:, 0:256], KHT, VTK[:, c, :], True, True, ["KHT", "VTK"], [bk(6)])
                        nxt = 1 - cur
                        P.stt(SG[nxt], SG[cur], ecol, banks[6][:, 0:256], ALU.mult, ALU.add,
                              [f"SG{cur}", "EB", bk(6)], [f"SG{nxt}"])
                        P.cp("act", SGB, SG[nxt], [f"SG{nxt}"], ["SGB"])
                        cur = nxt
                    if si == 3:
                        P.dma(gla_p[h], SG[cur], [f"SG{cur}"], [], is_out=True)
                else:
                    proj_tm(wqkv, kqkv, 128, 128, T, NS, 4)
                    P.cp("act", KTOK, banks[4][0:16, 0:128], [bk(4)], ["KTOK"])
                    proj_tm(wqkv, kqkv, 256, 256, T, NS, 4)
                    P.cp("act", VTOK, banks[4][0:16, 0:256], [bk(4)], ["VTOK"])
                    P.add("dve", lambda e: e.tensor_tensor(
                        out=KM, in0=KTOK.unsqueeze(1).to_broadcast([16, 16, 128]),
                        in1=ID16.unsqueeze(2).to_broadcast([16, 16, 128]), op=ALU.mult),
                        ["KTOK", "CM"], ["KM"])
                    for sg in range(4):
                        P.dma(SS, sgla[sg * 4:(sg + 1) * 4, h].rearrange("s d v -> d s v"), [], ["SS"])
                        for j in range(4):
                            s = sg * 4 + j
                            P.add("pe", lambda e, s=s: e.matmul(banks[6][:, 0:256], lhsT=KM[:, s, :], rhs=VTOK,
                                                                start=True, stop=True), ["KM", "VTOK"], [bk(6)])
                            P.stt(SN[:, j, :], SS[:, j, :], EB[:, s:s + 1], banks[6][:, 0:256], ALU.mult, ALU.add,
                                  ["SS", "EB", bk(6)], ["SN"])
                            for vh in range(2):
                                P.mm(banks[5][:, vh * 16 + s:vh * 16 + s + 1], SN[:, j, vh * 128:(vh + 1) * 128],
                                     QS[:, s:s + 1], True, True, ["SN", "QS"], [bk(5)])
                        P.dma(gla_s[sg * 4:(sg + 1) * 4, h].rearrange("s d v -> d s v"), SN, ["SN"], [], is_out=True)
                    P.cp("act", OgT[:, 2 * h:2 * h + 2, T:TA], v3(banks[5][:, 0:32], 16), [bk(5)], ["OgT"])
                og = OgT[:, 2 * h:2 * h + 2, s0:s0 + W]
                P.act(SQ[:, :, 0:W], og, AF.Square, ["OgT"], ["SQ"])
                for vh in range(2):
                    P.mm(banks[3][:, 0:W], ONESb, SQ[:, vh, 0:W], vh == 0, vh == 1, ["ONESb", "SQ"], [bk(3)])
                P.act(RSTD[:, 0:W], banks[3][:, 0:W], AF.Ln, [bk(3), "EPSC"], ["RSTD"], bias=EPSC[:, 0:1], scale=1.0 / 256.0)
                P.act(RSTD[:, 0:W], RSTD[:, 0:W], AF.Exp, ["RSTD"], ["RSTD"], scale=-0.5)
                for vh in range(2):
                    P.stt(ON[:, 0:W], OgT[:, 2 * h + vh, s0:s0 + W], cvc(CV_GNW, vh), RSTD[:, 0:W],
                          ALU.mult, ALU.mult, ["OgT", "CV", "RSTD"], ["ON"])
                    P.tt("pool", OgT[:, 2 * h + vh, s0:s0 + W], ON[:, 0:W], GS[:, vh, 0:W], ALU.mult,
                         ["ON", "GS"], ["OgT"])
        ar.release()
        P.fence()

        P.enabled = "R" in phases
        ar.mark()
        SHT = v3(ar.f32(33 * 16), 16)
        WSTf = WST.rearrange("p a b -> p (a b)")
        for tb in (0, 11, 22):
            P.dma(WSTf[0:16, 0:11 * 128], sshift[:, tb * 128:(tb + 11) * 128], [], ["WST"])
            for j in range(11):
                P.tr(banks[2][:, j * 16:(j + 1) * 16], WSTf[0:16, j * 128:(j + 1) * 128], ID[0:16, 0:16],
                     ["WST", "CM"], [bk(2)])
            P.cp("act", SHT[:, tb:tb + 11, :], v3(banks[2][:, 0:176], 16), [bk(2)], ["SHT"])

        LORA = ar.f32(TA)
        RAW = [ar.f32(513) for _ in range(4)]
        DQ = ar.f32(512)
        BB = ar.f32(9 * 512)
        Bp = lambda i: BB[:, i * 512:(i + 1) * 512]
        Rl, kRl = Bp(0), "B0"
        Kl, kKl = Bp(1), "B1"
        Gl, kGl = Bp(2), "B2"
        SIG, kSIG = Bp(3), "B3"
        Aa, kAa = Bp(4), "B4"
        KKr, kKKr = Bp(5), "B5"
        RN, kRN = Bp(6), "B6"
        KKn, kKKn = Bp(7), "B7"
        T1, kT1 = Bp(2), "B2"
        Kp, kKp = Bp(8), "B8"
        RKf, kRKf = Bp(5), "B5"
        BA, kBA = Bp(1), "B1"
        CS, kCS = Bp(4), "B4"
        GAM, kGAM = Bp(5), "B5"
        GIN, kGIN = Bp(6), "B6"
        CSX, kCSX = Bp(2), "B2"
        GEX, kGEX = Bp(3), "B3"
        YS, kYS = Bp(0), "B0"
        YSQ, kYSQ = Bp(1), "B1"
        MU2, kMU2 = Bp(2), "B2"
        VAR, kVAR = Bp(3), "B3"
        YD, kYD = Bp(4), "B4"
        SHO, kSHO = Bp(0)[0:17, :], "B0"
        SR, kSR = v3(BB[:, 5 * 512:7 * 512], 64), ["B5", "B6"]
        SRN, kSRN = v3(BB[:, 2 * 512:4 * 512], 64), ["B2", "B3"]
        SQb = ar.bf16(512)
        OUT = []
        for o_ in range(2):
            arkb = ar.bf16(2048)
            OUT.append(dict(
                ARKB=arkb, AR=arkb[:, 0:1024], AR4=arkb[:, 0:1024].rearrange("p (c q t) -> p c q t", q=2, t=64),
                Kt=arkb[:, 1024:1536], Bt=arkb[:, 1536:2048],
                Vb=ar.bf16(512), Gs=ar.bf16(512), BON=ar.bf16(512), GAMC=ar.f32(8),
                kAR=f"AR{o_}", kKt=f"Kt{o_}", kBt=f"Bt{o_}", kVb=f"Vb{o_}", kGs=f"Gs{o_}", kBON=f"BON{o_}",
                kGAMC=f"GAMC{o_}"))
        DG = OUT[0]["ARKB"].bitcast(F32)[:, 0:640].rearrange("p (s q k) -> p s q k", q=5, k=64)
        kDG = ["AR0", "Kt0"]
        Vf = ar.f32(16)
        CB = [(ar.bf16(320), ar.bf16(320), [ar.bf16(256) for _ in range(2)], ar.bf16(64), ar.f32(64), ar.bf16(64))
              for _ in range(3)]
        Hb = ar.bf16(64); Hf = ar.f32(64)
        XQ = v3(ar.f32(16 * 5), 5)
        TJ = ar.f32(64); T2 = ar.f32(64); SA = ar.f32(2)
        B4K = [bk(4), "b4tm", "b4s"]
        B5K = [bk(5), "b5z", "b5g", "b5r", "b5zv"]

        def lerp(q, ftile, src_bank, W, samp, out_ap, okey):
            raw = RAW[q]
            rk = f"RAW{q}"
            P.cp("act", raw[:, 1:1 + W], banks[src_bank][:, 0:W], [bk(src_bank)], [rk])
            if samp:
                P.tt("pool", DQ[:, 0:W], SHT[:, ftile, :], raw[:, 1:1 + W], ALU.subtract, ["SHT", rk], ["DQ"])
            else:
                P.tt("pool", DQ[:, 0:W], raw[:, 0:W], raw[:, 1:1 + W], ALU.subtract, [rk], ["DQ"])
            P.stt(out_ap, DQ[:, 0:W], cvc(CV_MU, ftile), raw[:, 1:1 + W], ALU.mult, ALU.add, ["DQ", "CV", rk], [okey])
            if not samp:
                P.cp("pool", raw[:, 0:1], raw[:, W:W + 1], [rk], [rk])

        wl, kl = load_wgroup([w_in[:, RW:RW + 128]])
        P.memset("pool", RAW[0][:, 0:1], 0.0, ["RAW0"])
        for si, (s0, W) in enumerate(SEGS):
            proj_fm(wl, kl, 0, 128, s0, W, 0)
            lerp(0, 32, 0, W, si == 4, LORA[:, s0:s0 + W], "LORA")
        P.act(LORA[0:64, :], LORA[0:64, :], AF.Tanh, ["LORA"], ["LORA"])
        proj_tm(wl, kl, 0, 128, T - 1, 17, 2)
        P.cp("act", SHO[:, 0:128], banks[2][0:17, 0:128], [bk(2)], [kSHO])
        P.dma(shift_o[:, 4096:4224], SHO[:, 0:128], [kSHO], [], is_out=True)

        def gn_post(p, s0, W, O):
            P.act(YSQ[:, 0:W], YS[:, 0:W], AF.Square, [kYS], [kYSQ])
            P.mm(banks[2][:, 0:W], BAVG, YS[:, 0:W], True, True, ["CM", kYS], [bk(2)])
            P.mm(banks[3][:, 0:W], BAVG, YSQ[:, 0:W], True, True, ["CM", kYSQ], [bk(3)])
            P.act(MU2[:, 0:W], banks[2][:, 0:W], AF.Square, [bk(2)], [kMU2])
            P.tt("dve", VAR[:, 0:W], banks[3][:, 0:W], MU2[:, 0:W], ALU.subtract, [bk(3), kMU2], [kVAR])
            P.act(VAR[:, 0:W], VAR[:, 0:W], AF.Ln, [kVAR], [kVAR], bias=EPSC[:, 1:2])
            P.act(VAR[:, 0:W], VAR[:, 0:W], AF.Exp, [kVAR], [kVAR], scale=-0.5)
            P.tt("dve", YD[:, 0:W], YS[:, 0:W], banks[2][:, 0:W], ALU.subtract, [kYS, bk(2)], [kYD])
            P.tt("pool", YD[:, 0:W], YD[:, 0:W], VAR[:, 0:W], ALU.mult, [kYD, kVAR], [kYD])
            P.ts("dve", YD[:, 0:W], YD[:, 0:W], cvc(CV_LW, p), cvc(CV_LB, p), ALU.mult, ALU.add, [kYD, "CV"], [kYD])
            P.tt("pool", YD[:, 0:W], YD[:, 0:W], O["BON"][:, 0:W], ALU.add, [kYD, O["kBON"]], [kYD])
            P.tt("dve", OrT[:, p, s0:s0 + W], YD[:, 0:W], O["Gs"][:, 0:W], ALU.mult, [kYD, O["kGs"]], ["OrT"])

        def mm2(out, lhsT, rhs, n0, n1, l0, l1, r0, r1, start, stop, r, w):
            for hh in range(2):
                ps = slice(hh * 64, (hh + 1) * 64)
                P.mm(out[ps, n0:n1], lhsT[ps, l0:l1], rhs[ps, r0:r1], start, stop, r, w)

        if RLEVEL < 2:
            P.enabled = False
        P.counting = True
        def elem_gen(p, wp, kp, si, O):
            s0, W = SEGS[si]
            samp = si == 4
            Vb, Gs, BON = O["Vb"], O["Gs"], O["BON"]
            AR4, Kt, Bt = O["AR4"], O["Kt"], O["Bt"]
            proj_fm(wp, kp, 0, 128, s0, W, 2)
            lerp(0, p, 2, W, samp, Rl[:, 0:W], kRl)
            yield
            proj_fm(wp, kp, 128, 128, s0, W, 2)
            lerp(1, 8 + p, 2, W, samp, Kl[:, 0:W], kKl)
            yield
            proj_fm(wp, kp, 256, 128, s0, W, 2)
            lerp(2, 16 + p, 2, W, samp, Vb[:, 0:W], O["kVb"])
            if samp:
                P.stt(Vf[:, 0:W], DQ[:, 0:W], cvc(CV_MU, 16 + p), RAW[2][:, 1:1 + W], ALU.mult, ALU.add,
                      ["DQ", "CV", "RAW2"], ["Vf"])
            yield
            proj_fm(wp, kp, 384, 128, s0, W, 2)
            lerp(3, 24 + p, 2, W, samp, Gl[:, 0:W], kGl)
            P.act(Gs[:, 0:W], Gl[:, 0:W], AF.Silu, [kGl], [O["kGs"]])
            yield
            P.mm(banks[2][:, 0:W], W2A2[0:64, p * 128:(p + 1) * 128], LORA[0:64, s0:s0 + W], True, True,
                 ["W2A2", "LORA"], [bk(2)])
            P.act(SIG[:, 0:W], banks[2][:, 0:W], AF.Sigmoid, [bk(2), "CV"], [kSIG], bias=cvc(CV_W0, p))
            yield
            P.mm(banks[2][:, 0:W], W2A2[64:128, p * 128:(p + 1) * 128], LORA[64:128, s0:s0 + W], True, True,
                 ["W2A2", "LORA"], [bk(2)])
            P.act(Aa[:, 0:W], banks[2][:, 0:W], AF.Sigmoid, [bk(2), "CV"], [kAa], bias=cvc(CV_A0, p))
            yield
            P.ts("pool", KKr[:, 0:W], Kl[:, 0:W], cvc(CV_KK, p), None, ALU.mult, ALU.bypass, [kKl, "CV"], [kKKr])
            P.act(SQb[:, 0:W], KKr[:, 0:W], AF.Square, [kKKr], ["SQb"])
            P.mm(banks[2][:, 0:W], BONESb, SQb[:, 0:W], True, True, ["BONESb", "SQb"], [bk(2)])
            yield
            P.act(RN[:, 0:W], banks[2][:, 0:W], AF.Ln, [bk(2)], [kRN])
            P.act(RN[:, 0:W], RN[:, 0:W], AF.Exp, [kRN], [kRN], scale=-0.5)
            yield
            P.tt("pool", KKn[:, 0:W], KKr[:, 0:W], RN[:, 0:W], ALU.mult, [kKKr, kRN], [kKKn])
            P.ts("dve", T1[:, 0:W], Aa[:, 0:W], -1.0, cvc(CV_KA, p), ALU.add, ALU.mult, [kAa, "CV"], [kT1])
            P.stt(Kp[:, 0:W], T1[:, 0:W], 1.0, Kl[:, 0:W], ALU.add, ALU.mult, [kT1, kKl], [kKp])
            yield
            P.tt("pool", RKf[:, 0:W], Rl[:, 0:W], Kp[:, 0:W], ALU.mult, [kRl, kKp], [kRKf])
            P.ts("pool", RKf[:, 0:W], RKf[:, 0:W], cvc(CV_RK, p), None, ALU.mult, ALU.bypass, [kRKf, "CV"], [kRKf])
            yield
            P.mm(banks[2][:, 0:W], BONES, RKf[:, 0:W], True, True, ["CM", kRKf], [bk(2)])
            P.tt("dve", BON[:, 0:W], banks[2][:, 0:W], Vb[:, 0:W], ALU.mult, [bk(2), O["kVb"]], [O["kBON"]])
            P.tt("pool", BA[:, 0:W], KKn[:, 0:W], Aa[:, 0:W], ALU.mult, [kKKn, kAa], [kBA])
            yield
            if not samp:
                P.scan(CS[:, 0:W], SMR[:, 0:W], SIG[:, 0:W], ["CM", kSIG], [kCS])
                P.act(GAM[:, 0:W], CS[:, 0:W], AF.Exp, [kCS], [kGAM], scale=-DECAY)
                P.act(GIN[:, 0:W], CS[:, 0:W], AF.Exp, [kCS], [kGIN], scale=DECAY)
                yield
                P.tt("pool", CSX[:, 0:W], CS[:, 0:W], SIG[:, 0:W], ALU.subtract, [kCS, kSIG], [kCSX])
                P.act(GEX[:, 0:W], CSX[:, 0:W], AF.Exp, [kCSX], [kGEX], scale=-DECAY)
                P.cp("pool", O["GAMC"], v3(GAM[:, 0:W], 64)[:, :, 63], [kGAM], [O["kGAMC"]])
                yield
                v64 = lambda a_: v3(a_[:, 0:W], 64)
                P.stt(AR4[:, :, 0, :], v64(KKn), -1.0, v64(GEX), ALU.mult, ALU.mult, [kKKn, kGEX], [O["kAR"]])
                P.tt("dve", AR4[:, :, 1, :], v64(Rl), v64(GAM), ALU.mult, [kRl, kGAM], [O["kAR"]])
                yield
                P.tt("pool", Kt[:, 0:W], Kp[:, 0:W], GIN[:, 0:W], ALU.mult, [kKp, kGIN], [O["kKt"]])
                P.tt("pool", Bt[:, 0:W], BA[:, 0:W], GIN[:, 0:W], ALU.mult, [kBA, kGIN], [O["kBt"]])
            else:
                P.ts("pool", XQ[:, :, 0], KKn[:, 0:W], -1.0, None, ALU.mult, ALU.bypass, [kKKn], ["XQ"])
                P.act(XQ[:, :, 1], SIG[:, 0:W], AF.Exp, [kSIG], ["XQ"], scale=-DECAY)
                P.cp("pool", XQ[:, :, 2], BA[:, 0:W], [kBA], ["XQ"])
                P.cp("pool", XQ[:, :, 3], Kp[:, 0:W], [kKp], ["XQ"])
                P.cp("pool", XQ[:, :, 4], Rl[:, 0:W], [kRl], ["XQ"])

        def chunk_gen(c, par, si, O):
            tb, db = [(4, 5), (6, 7), (0, 1)][par]
            bT, bD = banks[tb], banks[db]
            bTb = bT.bitcast(BF16)
            kT, kD = bk(tb), bk(db)
            TMp, SCp, ZPQp, GA1p, GVGp, RHp = CB[par]
            kTM, kSC, kGA1, kGVG, kRH = f"TM{par}", f"SC{par}", f"GA1{par}", f"GVG{par}", f"RH{par}"
            AR, AR4, Kt, Bt, Vb = O["AR"], O["AR4"], O["Kt"], O["Bt"], O["Vb"]
            kAR, kKt, kBt, kVb = O["kAR"], O["kKt"], O["kBt"], O["kVb"]
            cs = slice(c * 64, (c + 1) * 64)
            last = (si == 3 and c == 7)
            arc = AR[:, c * 128:(c + 1) * 128]
            for hh in range(2):
                ps = slice(hh * 64, (hh + 1) * 64)
                idb = IDb[ps, ps]
                P.tr(bTb[ps, 64:128], AR4[ps, c, 0, :], idb, [kAR, "IDb"], [kT])
                P.tr(bTb[ps, 128:192], Bt[ps, cs], idb, [kBt, "IDb"], [kT])
                P.tr(bTb[ps, 192:256], Kt[ps, cs], idb, [kKt, "IDb"], [kT])
                P.tr(bTb[ps, 256:320], Vb[ps, cs], idb, [kVb, "IDb"], [kT])
            P.cp("act", TMp[:, 64:320], bTb[:, 64:320], [kT], [kTM])
            mm2(bD, Bt, arc, 0, 128, c * 64, (c + 1) * 64, 0, 128, True, True, [kBt, kAR], [kD])
            mm2(bD, Kt, arc, 128, 256, c * 64, (c + 1) * 64, 0, 128, True, True, [kKt, kAR], [kD])
            mm2(bD, arc, Bt, 256, 320, 0, 64, c * 64, (c + 1) * 64, True, True, [kBt, kAR], [kD])
            P.tt("dve", SCp, bD[:, 0:320], MASK5, ALU.mult, [kD, "CM"], [kSC])
            yield
            mm2(bD, SCp, TMp, 448, 512, 128, 192, 256, 320, True, True, [kSC, kTM], [kD])
            P.cp("act", TMp[:, 0:64], bD[:, 448:512], [kD], [kTM])
            yield
            zsrc, zk = TMp, kTM
            psrc, pk, pc = SCp, kSC, 0
            qsrc, qk, qc = SCp, kSC, 256
            for lvl in range(6):
                mm2(bD, psrc, zsrc, 0, 128, pc, pc + 64, 0, 128, True, True, [pk, zk], [kD])
                if lvl < 5:
                    mm2(bT, qsrc, psrc, 0, 64, qc, qc + 64, pc, pc + 64, True, True, [pk, qk], [kT])
                    mm2(bT, psrc, qsrc, 64, 128, pc, pc + 64, qc, qc + 64, True, True, [pk, qk], [kT])
                dst = ZPQp[lvl % 2]
                dzk = f"Z{par}{lvl % 2}"
                dpk = f"PQ{par}{lvl % 2}"
                P.tt("dve", dst[:, 0:128], bD[:, 0:128], zsrc[:, 0:128], ALU.add, [kD, zk], [dzk])
                if lvl < 5:
                    P.cp("act", dst[:, 128:256], bT[:, 0:128], [kT], [dpk])
                zsrc, zk = dst, dzk
                psrc, pk, pc = dst, dpk, 128
                qsrc, qk, qc = dst, dpk, 192
                yield
            Wm, wk_ = zsrc, zk
            gam_col = O["GAMC"][:, c:c + 1]
            kGC = O["kGAMC"]
            mm2(bD, Wm, TMp, 256, 320, 64, 128, 128, 192, True, True, [wk_, kTM], [kD])
            mm2(bD, TMp, Wm, 320, 384, 128, 192, 0, 64, True, False, [wk_, kTM], [kD])
            mm2(bD, TMp, TMp, 320, 384, 192, 256, 256, 320, False, True, [kTM], [kD])
            mm2(bT, ISTKb, arc, 384, 448, 0, 64, 64, 128, True, False, ["ISTKb", kAR], [kT])
            mm2(bT, Wm, SCp, 384, 448, 64, 128, 64, 128, False, True, [wk_, kSC], [kT])
            P.tt("dve", GA1p, bD[:, 256:320], ISTK, ALU.add, [kD, "CM"], [kGA1])
            P.ts("dve", GVGp, bD[:, 320:384], gam_col, None, ALU.mult, ALU.bypass, [kD, kGC], [kGVG])
            P.cp("act", RHp, bT[:, 384:448], [kT], [kRH])
            yield
            b3 = banks[3]
            mm2(b3, Wm, SCp, c * 64, (c + 1) * 64, 0, 64, 64, 128, True, False, [wk_, kSC], [bk(3)])
            mm2(b3, TMp, SCp, c * 64, (c + 1) * 64, 256, 320, 192, 256, False, False, [kTM, kSC], [bk(3)])
            mm2(b3, Hb, RHp, c * 64, (c + 1) * 64, 0, 64, 0, 64, False, True, ["Hb", kRH], [bk(3)])
            mm2(banks[2], GA1p, Hb, 0, 64, 0, 64, 0, 64, True, True, [kGA1, "Hb"], [bk(2)])
            if last:
                P.stt(Hf, banks[2][:, 0:64], gam_col, GVGp, ALU.mult, ALU.add, [bk(2), kGC, kGVG], ["Hf"])
            P.stt(Hb, banks[2][:, 0:64], gam_col, GVGp, ALU.mult, ALU.add, [bk(2), kGC, kGVG], ["Hb"])

        def run_overlapped(chunk_args, extra):
            active = []
            nxt = 0
            free_par = [0, 1, 2]
            since = 99
            ex = extra
            while nxt < len(chunk_args) or active or ex is not None:
                if nxt < len(chunk_args) and free_par and (since >= 3 or not active):
                    pr_ = free_par.pop(0)
                    c_, si_, O_ = chunk_args[nxt]
                    active.append((chunk_gen(c_, pr_, si_, O_), pr_))
                    nxt += 1
                    since = 0
                since += 1
                for (g_, pr_) in list(active):
                    try:
                        next(g_)
                    except StopIteration:
                        active.remove((g_, pr_))
                        free_par.append(pr_)
                if ex is not None:
                    try:
                        next(ex)
                    except StopIteration:
                        ex = None

        for p in range(8):
            wp, kp = load_wgroup()
            for q in range(4):
                P.memset("pool", RAW[q][:, 0:1], 0.0, [f"RAW{q}"])
            P.memset("pool", Hb, 0.0, ["Hb"])
            proj_tm(wp, kp, 0, 512, T - 1, 17, 2)
            P.cp("act", SHO, banks[2][0:17, 0:512], [bk(2)], [kSHO])
            P.dma(shift_o[:, 0:4096].rearrange("t (q f) -> t q f", q=4)[:, :, p * 128:(p + 1) * 128],
                  v3(SHO, 128), [kSHO], [], is_out=True)
            run_overlapped([], elem_gen(p, wp, kp, 0, OUT[0]))
            for si in range(4):
                s0, W = SEGS[si]
                O = OUT[si % 2]
                On = OUT[(si + 1) % 2]
                run_overlapped([(c, si, O) for c in range(8)], elem_gen(p, wp, kp, si + 1, On))
                P.cp("act", YS[:, 0:W], banks[3][:, 0:W], [bk(3)], [kYS])
                if si == 3:
                    P.tr(banks[2][0:64, 64:192], Hf, ID, ["Hf", "CM"], [bk(2)])
                    P.cp("act", T2[0:64, :], banks[2][0:64, 64:128], [bk(2)], ["T2"])
                    P.cp("act", TJ[0:64, :], banks[2][0:64, 128:192], [bk(2)], ["TJ"])
                    P.dma(rwkv_p[2 * p], T2[0:64, :], ["T2"], [], is_out=True)
                    P.dma(rwkv_p[2 * p + 1], TJ[0:64, :], ["TJ"], [], is_out=True)
                gn_post(p, s0, W, O)
            s0, W = SEGS[4]
            O = OUT[0]
            P.dma(SR, srwkv[:, 2 * p:2 * p + 2].rearrange("s h v k -> (h v) s k"), [], kSR)
            for s2 in range(8):
                hs = slice(s2 * 2, s2 * 2 + 2)
                P.add("dve", lambda e, hs=hs: e.tensor_tensor(
                    out=DG, in0=ISTK.unsqueeze(1).unsqueeze(1).to_broadcast([128, 2, 5, 64]),
                    in1=XQ[:, hs, :].unsqueeze(3).to_broadcast([128, 2, 5, 64]), op=ALU.mult),
                    ["CM", "XQ"], kDG)
                for j in range(2):
                    s = s2 * 2 + j
                    bi = 4 + j
                    bb = banks[bi]
                    dgs = DG[:, j].rearrange("p q k -> p (q k)")
                    for hh in range(2):
                        ps = slice(hh * 64, (hh + 1) * 64)
                        P.mm(bb[ps, 0:320], ONES[ps, 0:64], dgs[ps, :], True, True, ["CM"] + kDG, [bk(bi)])
                    P.stt(TJ, SR[:, s, :], 1.0, bb[:, 0:64], ALU.mult, ALU.mult, kSR + [bk(bi)], ["TJ", "SA"],
                          accum=SA[:, 0:1])
                    P.tt("dve", T2, SR[:, s, :], bb[:, 64:128], ALU.mult, kSR + [bk(bi)], ["T2"])
                    P.stt(T2, bb[:, 128:192], SA[:, 0:1], T2, ALU.mult, ALU.add, [bk(bi), "SA", "T2"], ["T2"])
                    P.stt(SRN[:, s, :], bb[:, 192:256], Vf[:, s:s + 1], T2, ALU.mult, ALU.add,
                          [bk(bi), "Vf", "T2"], kSRN)
                    P.stt(TJ, SRN[:, s, :], 1.0, bb[:, 256:320], ALU.mult, ALU.mult, kSRN + [bk(bi)],
                          ["TJ", kYS], accum=YS[:, s:s + 1])
            P.dma(rwkv_s[:, 2 * p:2 * p + 2].rearrange("s h v k -> (h v) s k"), SRN, kSRN, [], is_out=True)
            gn_post(p, s0, W, O)
        ar.release()
        P.fence()

        P.enabled = "F" in phases
        ar.mark()
        MT = v3(ar.bf16(KC * TA), TA)
        FT = [(ar.bf16(512), ar.bf16(512), ar.f32(512)) for _ in range(2)]
        fcnt = 0
        for dt in range(8):
            wf, kf = load_wgroup()
            for (s0, W) in SEGS:
                par = fcnt % 2
                fcnt += 1
                SGA, SGBt, M1 = FT[par]
                kA, kB, kM = f"SGA{par}", f"SGBt{par}", f"M1{par}"
                b0, b1, b2, b3 = [4 * par + i for i in range(4)]
                proj_fm(wf, kf, 0, 128, s0, W, b0)
                proj_fm(wf, kf, 128, 128, s0, W, b1)
                for kc in range(KC):
                    P.mm(banks[b2][:, 0:W], wf[:, kc, 256:384], OgT[:, kc, s0:s0 + W], kc == 0, kc == KC - 1,
                         [kf, "OgT"], [bk(b2)])
                for kc in range(KC):
                    P.mm(banks[b3][:, 0:W], wf[:, kc, 384:512], OrT[:, kc, s0:s0 + W], kc == 0, kc == KC - 1,
                         [kf, "OrT"], [bk(b3)])
                P.act(SGA[:, 0:W], banks[b0][:, 0:W], AF.Sigmoid, [bk(b0)], [kA])
                P.act(SGBt[:, 0:W], banks[b1][:, 0:W], AF.Sigmoid, [bk(b1)], [kB])
                P.tt("dve", M1[:, 0:W], banks[b2][:, 0:W], SGA[:, 0:W], ALU.mult, [bk(b2), kA], [kM])
                P.tt("dve", MT[:, dt, s0:s0 + W], banks[b3][:, 0:W], SGBt[:, 0:W], ALU.mult, [bk(b3), kB], ["MT"])
                P.tt("pool", MT[:, dt, s0:s0 + W], MT[:, dt, s0:s0 + W], M1[:, 0:W], ALU.add, ["MT", kM], ["MT"])
        WO = v3(ar.bf16(KC * 1024), 1024)
        for g in range(2):
            wo, ko = load_wgroup([w_out[:, g * 512:(g + 1) * 512]])
            P.cp("pool", WO[:, :, g * 512:(g + 1) * 512], wo, [ko], ["WO"])
        XA = xT.rearrange("p a b -> p (a b)").bitcast(F32)
        LNGB = XA[:, 0:2048]
        XR = [XA[:, 2048 + i * 1024:2048 + (i + 1) * 1024] for i in range(2)]
        ZT = [XA[:, 4096 + i * 1024:4096 + (i + 1) * 1024] for i in range(2)]
        P.dma(LNGB, lngb, [], ["LNGB", "xT"])
        ST = ar.f32(8)
        alpha = 2.0 ** 0.25
        for tt in range(17):
            rows = 128 if tt < 16 else NS
            t0 = tt * 128
            xr = XR[tt % 2]; xk = f"XR{tt % 2}"
            zt = ZT[tt % 2]; zk = f"ZT{tt % 2}"
            P.dma(xr[0:rows, :], x[t0:t0 + rows, :], [], [xk, "xT"])
            for eh in range(2):
                bi = 2 * (tt % 2) + eh
                for kc in range(KC):
                    P.mm(banks[bi][0:rows, 0:512], MT[:, kc, t0:t0 + rows], WO[:, kc, eh * 512:(eh + 1) * 512],
                         kc == 0, kc == KC - 1, ["MT", "WO"], [bk(bi)])
                P.stt(zt[0:rows, eh * 512:(eh + 1) * 512], xr[0:rows, eh * 512:(eh + 1) * 512], alpha,
                      banks[bi][0:rows, 0:512], ALU.mult, ALU.add, [xk, bk(bi)], [zk, "xT"])
            R_ = slice(0, rows)
            P.act(xr[R_, :], zt[R_, :], AF.Copy, [zk], [xk, "ST"], accum=ST[R_, 0:1])
            P.act(xr[R_, :], zt[R_, :], AF.Square, [zk], [xk, "ST"], accum=ST[R_, 1:2])
            P.ts("pool", ST[R_, 2:3], ST[R_, 0:1], 1.0 / D, None, ALU.mult, ALU.bypass, ["ST"], ["ST"])
            P.tt("pool", ST[R_, 3:4], ST[R_, 2:3], ST[R_, 2:3], ALU.mult, ["ST"], ["ST"])
            P.stt(ST[R_, 4:5], ST[R_, 1:2], 1.0 / D, ST[R_, 3:4], ALU.mult, ALU.subtract, ["ST"], ["ST"])
            P.act(ST[R_, 5:6], ST[R_, 4:5], AF.Ln, ["ST"], ["ST"], bias=EPSC[R_, 0:1])
            P.act(ST[R_, 5:6], ST[R_, 5:6], AF.Exp, ["ST"], ["ST"], scale=-0.5)
            P.ts("dve", zt[R_, :], zt[R_, :], ST[R_, 2:3], ST[R_, 5:6], ALU.subtract, ALU.mult, [zk, "ST"], [zk])
            P.tt("pool", zt[R_, :], zt[R_, :], LNGB[R_, 0:1024], ALU.mult, [zk, "LNGB"], [zk])
            P.tt("dve", zt[R_, :], zt[R_, :], LNGB[R_, 1024:2048], ALU.add, [zk, "LNGB"], [zk])
            P.dma(y[t0:t0 + rows, :], zt[R_, :], [zk], [], is_out=True)
        ar.release()
        P.fence()

        P.enabled = True
        P.counting = False
        P.finish()
        P.emit(nc, st)
    return nc


def _consts():
    cm = np.zeros((128, NCM), np.float32)
    cm[:, CM_ID:CM_ID + 128] = np.eye(128, dtype=np.float32)
    s = np.arange(128)[:, None] % 64
    t = np.arange(64)[None, :]
    strict = (s < t).astype(np.float32)
    incl = (s <= t).astype(np.float32)
    lower = (t < s).astype(np.float32)
    cm[:, CM_MASK5:CM_MASK5 + 320] = np.concatenate([strict, incl, strict, incl, lower], axis=1)
    j = np.arange(128)[:, None]
    i = np.arange(128)[None, :]
    cm[:, CM_MASKU:CM_MASKU + 128] = (j <= i).astype(np.float32)
    cm[:, CM_ISTK:CM_ISTK + 64] = (s == t).astype(np.float32)
    blk = (np.arange(128)[:, None] // 64 == np.arange(128)[None, :] // 64).astype(np.float32)
    cm[:, CM_BONES:CM_BONES + 128] = blk
    cm[:, CM_BAVG:CM_BAVG + 128] = blk / 64.0
    cm[:, CM_ONES:CM_ONES + 128] = 1.0
    smg = np.ones(512, np.float32); smg[::128] = 0
    smr = np.ones(512, np.float32); smr[::64] = 0
    cm[:, CM_SMG:CM_SMG + 512] = smg[None, :]
    cm[:, CM_SMR:CM_SMR + 512] = smr[None, :]
    cm[0:16, CM_ID16:CM_ID16 + 16] = np.eye(16, dtype=np.float32)
    return cm


def _cols(vec):
    return np.ascontiguousarray(np.asarray(vec, np.float32).reshape(-1, 128).T)


_NC_CACHE = {}


def kernel(x_prompt, x_sample, state_gla, state_rwkv, state_rwkv_shift, w_in, gla_alpha_w2,
           gla_alpha_b, gla_norm_w, rwkv_mu, rwkv_w0, rwkv_w2, rwkv_a0, rwkv_a2, rwkv_k_k,
           rwkv_k_a, rwkv_r_k, rwkv_lnx_w, rwkv_lnx_b, w_up_gla, w_up_rwkv, w_out, ln_g, ln_b):
    f = lambda a: np.ascontiguousarray(np.asarray(a, dtype=np.float32))
    x_prompt, x_sample = f(x_prompt), f(x_sample)
    cvec = np.concatenate([_cols(rwkv_mu[0]), _cols(gla_alpha_b[0]), _cols(gla_norm_w[0]), _cols(rwkv_w0[0]),
                           _cols(rwkv_a0[0]), _cols(rwkv_k_k[0]), _cols(rwkv_k_a[0]),
                           _cols(np.asarray(rwkv_r_k[0]).reshape(-1)), _cols(rwkv_lnx_w[0]), _cols(rwkv_lnx_b[0])],
                          axis=1)
    assert cvec.shape == (128, NCV)
    cmat = _consts()
    w2a2 = np.concatenate([f(rwkv_w2[0]), f(rwkv_a2[0])], axis=0)
    lngb = np.concatenate([np.broadcast_to(f(ln_g[0])[None, :], (128, D)),
                           np.broadcast_to(f(ln_b[0])[None, :], (128, D))], axis=1)
    lngb = np.ascontiguousarray(lngb)
    common = dict(w_in=f(w_in[0]), alpha_w2=f(gla_alpha_w2[0]), w2a2=w2a2, w_up_gla=f(w_up_gla[0]),
                  w_up_rwkv=f(w_up_rwkv[0]), w_out=f(w_out[0]), lngb=lngb, cvec=f(cvec), cmat=cmat)
    in_maps = []
    for c in range(8):
        m = dict(common)
        m["x"] = np.ascontiguousarray(np.concatenate([x_prompt[c], x_sample[NS * c:NS * (c + 1), 0]], axis=0))
        m["sgla"] = f(state_gla[0, NS * c:NS * (c + 1)])
        m["srwkv"] = f(state_rwkv[0, NS * c:NS * (c + 1)])
        m["sshift"] = f(state_rwkv_shift[0, NS * c:NS * (c + 1)])
        in_maps.append(m)
    if "nc" not in _NC_CACHE:
        _NC_CACHE["nc"] = build_nc()
    nc = _NC_CACHE["nc"]
    res = run_bass_kernel_spmd(nc, in_maps, core_ids=list(range(8)))
    rs = res.results
    y_prompt = np.stack([rs[c]["y"][0:T] for c in range(8)], axis=0)
    y_sample = np.concatenate([rs[c]["y"][T:TA] for c in range(8)], axis=0)[:, None, :]
    gla_p = np.stack([rs[c]["gla_p"] for c in range(8)], axis=0)[None]
    rwkv_p = np.stack([rs[c]["rwkv_p"] for c in range(8)], axis=0)[None]
    shift_p = np.stack([rs[c]["shift_o"][0] for c in range(8)], axis=0)[None]
    gla_s = np.concatenate([rs[c]["gla_s"] for c in range(8)], axis=0)[None]
    rwkv_s = np.concatenate([rs[c]["rwkv_s"] for c in range(8)], axis=0)[None]
    shift_s = np.concatenate([rs[c]["shift_o"][1:17] for c in range(8)], axis=0)[None]
    outs = (y_prompt, y_sample, gla_p, rwkv_p, shift_p, gla_s, rwkv_s, shift_s)
    return tuple(np.ascontiguousarray(o, dtype=np.float32) for o in outs)
```

```python
import contextlib
import numpy as np
import concourse.bass as bass
import concourse.mybir as mybir
from concourse.bass_utils import run_bass_kernel_spmd

F32 = mybir.dt.float32
BF16 = mybir.dt.bfloat16
AF = mybir.ActivationFunctionType
ALU = mybir.AluOpType

ENGS = ["pe", "act", "dve", "pool", "sp"]

T = 2048
NS = 16
TA = T + NS
D = 1024
KC = 8
NIN = 9360
GQ, GK, GV, GG, GA = 0, 512, 1024, 2048, 3072
R0 = 3088
RR, RK, RV, RG, RW = R0, R0 + 1024, R0 + 2048, R0 + 3072, R0 + 4096
G0 = R0 + 4224
SEGS = [(0, 512), (512, 512), (1024, 512), (1536, 512), (2048, 16)]
DECAY = 0.606531
RLEVEL = 99
RPAIRS = 8
RSTOP = None
RLOG = []

CV_MU, CV_AB, CV_GNW, CV_W0, CV_A0, CV_KK, CV_KA, CV_RK, CV_LW, CV_LB = 0, 33, 37, 39, 47, 55, 63, 71, 79, 87
NCV = 95
CM_ID, CM_MASK5, CM_MASKU, CM_ISTK, CM_BONES, CM_BAVG, CM_ONES, CM_SMG, CM_SMR, CM_ID16 = (
    0, 128, 448, 576, 640, 768, 896, 1024, 1536, 2048)
NCM = 2064


class Prog:
    EPOCH = 8192
    NDMA = 14

    def __init__(self):
        self.ops = {e: [] for e in ENGS}
        self.cnt = {e: 0 for e in ENGS}
        self.ndma = 0
        self.dma_events = []
        self.last_w = {}
        self.readers = {}
        self.waited = {e: {} for e in ENGS}
        self.semkeys = set()
        self.out_events = []
        self.enabled = True
        self.pending = {e: [] for e in ENGS}
        self.last_ev = {}
        self.know = {e: {} for e in ENGS}
        self.evclock = {}
        self.evidx = {}
        self.nev = 0

    def fence(self):
        evs = list(self.last_ev.values()) + list(self.dma_events[-self.NDMA:])
        for e in ENGS:
            self.pending[e] = list(evs)

    def _resolve(self, eng, cands):
        know = self.know[eng]
        waits = []
        for ev in sorted(cands, key=lambda e: -self.evidx[e]):
            sk, val = ev
            if eng == "pe" and sk[0] == "pe":
                continue
            if know.get(sk, 0) >= val:
                continue
            waits.append(ev)
            for k2, v2 in self.evclock[ev].items():
                if know.get(k2, 0) < v2:
                    know[k2] = v2
        return waits

    def add(self, eng, fn, r=(), w=(), dma=False, is_out=False):
        if not self.enabled:
            return None
        if RSTOP is not None and getattr(self, "counting", False):
            self.nops = getattr(self, "nops", 0) + 1
            if self.nops > RSTOP:
                return None
        xb = [k for k in r if isinstance(k, str) and k.startswith("bank")]
        if xb:
            r = [k for k in r if k not in xb]
            w = list(w) + [k for k in xb if k not in w]
        cands = set()
        if self.pending[eng]:
            cands.update(self.pending[eng])
            self.pending[eng] = []
        for k in r:
            cands.add(self.last_w.get(k))
        for k in w:
            cands.add(self.last_w.get(k))
            cands.update(self.readers.get(k, ()))
        if dma and self.ndma >= self.NDMA:
            cands.add(self.dma_events[self.ndma - self.NDMA])
        cands.discard(None)
        waits = self._resolve(eng, cands)
        clk = dict(self.know[eng])
        if dma:
            j = self.ndma
            self.ndma += 1
            sk = ("dma", j % self.NDMA)
            val = 16 * (j // self.NDMA + 1)
            ev = (sk, val)
            self.dma_events.append(ev)
            inc = 16
            if is_out:
                self.out_events.append(ev)
        else:
            i = self.cnt[eng]
            self.cnt[eng] += 1
            ep = i // self.EPOCH
            sk = (eng, ep)
            ev = (sk, i % self.EPOCH + 1)
            inc = 1
            self.last_ev[eng] = ev
            for e2 in range(ep):
                clk[(eng, e2)] = self.EPOCH
        clk[sk] = max(clk.get(sk, 0), ev[1])
        self.evclock[ev] = clk
        self.evidx[ev] = self.nev
        self.nev += 1
        self.semkeys.add(sk)
        self.ops[eng].append((waits, fn, ev, inc))
        for k in r:
            self.readers.setdefault(k, []).append(ev)
        for k in w:
            self.last_w[k] = ev
            self.readers[k] = []
        return ev

    def finish(self):
        cands = set(self.dma_events[-self.NDMA:]) | set(self.out_events)
        waits = self._resolve("sp", cands)
        self.ops["sp"].append((waits, None, None, 0))

    def emit(self, nc, stack):
        targets = set()
        for e in ENGS:
            for waits, fn, ev, inc in self.ops[e]:
                targets.update(waits)
        real = {}
        used = set()
        for e in ENGS:
            cnt = {}
            for waits, fn, ev, inc in self.ops[e]:
                if ev is None:
                    continue
                if ev[0][0] == "dma":
                    real[ev] = ev[1]
                    used.add(ev[0])
                elif ev in targets:
                    cnt[ev[0]] = cnt.get(ev[0], 0) + 1
                    real[ev] = cnt[ev[0]]
                    used.add(ev[0])
        sems = {}
        for sk in sorted(used, key=str):
            sems[sk] = stack.enter_context(nc.semaphore("s_" + "_".join(str(x) for x in sk)))
        block = stack.enter_context(nc.Block())
        prog = self

        def run(engname):
            def body(eng):
                for waits, fn, ev, inc in prog.ops[engname]:
                    if fn is None:
                        for w_ in waits:
                            eng.wait_ge(sems[w_[0]], real[w_])
                        continue
                    for w_ in waits[:-1]:
                        eng.wait_ge(sems[w_[0]], real[w_])
                    ins = fn(eng)
                    if waits:
                        ins._wait_ge(sems[waits[-1][0]], real[waits[-1]])
                    if ev in real:
                        ins.then_inc(sems[ev[0]], inc)
            return body

        block.tensor(run("pe"))
        block.scalar(run("act"))
        block.vector(run("dve"))
        block.gpsimd(run("pool"))
        block.sync(run("sp"))

    def act(self, out, in_, func, r, w, bias=0.0, scale=1.0, accum=None):
        if accum is None:
            return self.add("act", lambda e: e.activation(out=out, in_=in_, func=func, bias=bias, scale=scale), r, w)
        return self.add("act", lambda e: e.activation(out=out, in_=in_, func=func, bias=bias, scale=scale,
                                                      accum_out=accum), r, w)

    def ts(self, eng, out, in0, s1, s2, op0, op1, r, w):
        return self.add(eng, lambda e: e.tensor_scalar(out=out, in0=in0, scalar1=s1, scalar2=s2, op0=op0, op1=op1), r, w)

    def stt(self, out, in0, scalar, in1, op0, op1, r, w, accum=None):
        if accum is None:
            return self.add("dve", lambda e: e.scalar_tensor_tensor(out=out, in0=in0, scalar=scalar, in1=in1,
                                                                     op0=op0, op1=op1), r, w)
        return self.add("dve", lambda e: e.scalar_tensor_tensor(out=out, in0=in0, scalar=scalar, in1=in1,
                                                                 op0=op0, op1=op1, accum_out=accum), r, w)

    def tt(self, eng, out, in0, in1, op, r, w):
        return self.add(eng, lambda e: e.tensor_tensor(out=out, in0=in0, in1=in1, op=op), r, w)

    def cp(self, eng, out, in_, r, w):
        if eng == "act":
            return self.add("act", lambda e: e.activation(out=out, in_=in_, func=AF.Copy), r, w)
        return self.add(eng, lambda e: e.tensor_copy(out=out, in_=in_), r, w)

    def memset(self, eng, out, val, w):
        return self.add(eng, lambda e: e.memset(out, val), (), w)

    def mm(self, out, lhsT, rhs, start, stop, r, w):
        return self.add("pe", lambda e: e.matmul(out, lhsT=lhsT, rhs=rhs, start=start, stop=stop), r, w)

    def tr(self, out, in_, ident, r, w):
        return self.add("pe", lambda e: e.transpose(out, in_, ident), r, w)

    def dma(self, out, in_, r, w, is_out=False):
        return self.add("sp", lambda e: e.dma_start(out=out, in_=in_), r, w, dma=True, is_out=is_out)

    def scan(self, out, d0, d1, r, w):
        return self.add("dve", lambda e: e.tensor_tensor_scan(out=out, data0=d0, data1=d1, initial=0.0,
                                                               op0=ALU.mult, op1=ALU.add), r, w)


class Arena:
    def __init__(self, t, words):
        self.t = t
        self.words = words
        self.off = 0
        self.marks = []
        self.peak = 0

    def alloc(self, nwords):
        o = self.off
        self.off += nwords
        self.peak = max(self.peak, self.off)
        assert self.off <= self.words, f"SBUF arena overflow {self.off}>{self.words}"
        return o

    def f32(self, n, parts=128):
        o = self.alloc(n)
        return self.t[0:parts, o:o + n]

    def bf16(self, n, parts=128):
        assert n % 2 == 0
        o = self.alloc(n // 2)
        return self.t[0:parts, o:o + n // 2].bitcast(BF16)

    def mark(self):
        self.marks.append(self.off)

    def release(self):
        self.off = self.marks.pop()


def v3(ap, b):
    return ap.rearrange("p (a b) -> p a b", b=b)


def build_nc(phases="0GRF"):
    nc = bass.Bass("TRN2", target_bir_lowering=False)
    di = lambda n, s: nc.dram_tensor(n, list(s), F32, kind="ExternalInput").ap()
    do = lambda n, s: nc.dram_tensor(n, list(s), F32, kind="ExternalOutput").ap()
    x = di("x", (TA, D))
    w_in = di("w_in", (D, NIN))
    alpha_w2 = di("alpha_w2", (16, 512))
    w2a2 = di("w2a2", (128, 1024))
    w_up_gla = di("w_up_gla", (D, D))
    w_up_rwkv = di("w_up_rwkv", (D, D))
    w_out = di("w_out", (D, D))
    lngb = di("lngb", (128, 2048))
    sgla = di("sgla", (NS, 4, 128, 256))
    srwkv = di("srwkv", (NS, 16, 64, 64))
    sshift = di("sshift", (NS, 4224))
    cvec = di("cvec", (128, NCV))
    cmat = di("cmat", (128, NCM))
    y = do("y", (TA, D))
    gla_p = do("gla_p", (4, 128, 256))
    rwkv_p = do("rwkv_p", (16, 64, 64))
    shift_o = do("shift_o", (17, 4224))
    gla_s = do("gla_s", (NS, 4, 128, 256))
    rwkv_s = do("rwkv_s", (NS, 16, 64, 64))

    P = Prog()
    with contextlib.ExitStack() as st:
        WORDS = 53184
        sb = st.enter_context(nc.sbuf_tensor("arena", [128, WORDS], F32))
        ar = Arena(sb, WORDS)
        banks = [st.enter_context(nc.psum_tensor(f"ps{i}", [128, 512], F32)) for i in range(8)]
        bk = lambda i: f"bank{i}"

        CV = ar.f32(NCV)
        CM = ar.f32(NCM)
        P.dma(CV, cvec, [], ["CV"])
        P.dma(CM, cmat, [], ["CM"])
        ID = CM[:, CM_ID:CM_ID + 128]
        MASK5 = CM[:, CM_MASK5:CM_MASK5 + 320]
        MASKU = CM[:, CM_MASKU:CM_MASKU + 128]
        ISTK = CM[:, CM_ISTK:CM_ISTK + 64]
        BONES = CM[:, CM_BONES:CM_BONES + 128]
        BAVG = CM[:, CM_BAVG:CM_BAVG + 128]
        ONES = CM[:, CM_ONES:CM_ONES + 128]
        SMG = CM[:, CM_SMG:CM_SMG + 512]
        SMR = CM[:, CM_SMR:CM_SMR + 512]
        ID16 = CM[0:16, CM_ID16:CM_ID16 + 16]
        IDb = ar.bf16(128)
        ISTKb = ar.bf16(64)
        BONESb = ar.bf16(128)
        ONESb = ar.bf16(128)
        NAB = ar.f32(4)
        EPSC = ar.f32(2)
        P.cp("pool", IDb, ID, ["CM"], ["IDb"])
        P.cp("pool", ISTKb, ISTK, ["CM"], ["ISTKb"])
        P.cp("pool", BONESb, BONES, ["CM"], ["BONESb"])
        P.cp("pool", ONESb, ONES, ["CM"], ["ONESb"])
        P.ts("pool", NAB, CV[:, CV_AB:CV_AB + 4], -1.0, None, ALU.mult, ALU.bypass, ["CV"], ["NAB"])
        P.memset("pool", EPSC[:, 0:1], 1e-5, ["EPSC"])
        P.memset("pool", EPSC[:, 1:2], 64e-5, ["EPSC"])
        cvc = lambda base, j: CV[:, base + j:base + j + 1]

        W2A2 = ar.f32(1024)
        P.dma(W2A2, w2a2, [], ["W2A2"])

        xT = v3(ar.bf16(KC * TA), TA)
        OgT = v3(ar.bf16(KC * TA), TA)
        OrT = v3(ar.bf16(KC * TA), TA)

        WST = v3(ar.f32(4 * 512), 512)
        WBF = [v3(ar.bf16(KC * 512), 512) for _ in range(3)]
        wstate = {"n": 0, "pre": None}
        wqueue = []

        def _issue_load(srcs):
            en = P.enabled
            P.enabled = True
            par = wstate["n"] % 3
            wstate["n"] += 1
            key = f"WBF{par}"
            for half in range(2):
                off = 0
                for s_ in srcs:
                    n = s_.shape[1]
                    P.dma(WST[:, :, off:off + n],
                          s_.rearrange("(kc p) n -> p kc n", p=128)[:, 4 * half:4 * half + 4, :], [], ["WST"])
                    off += n
                for k2 in range(2):
                    P.cp("pool", WBF[par][:, 4 * half + 2 * k2:4 * half + 2 * k2 + 2, 0:off],
                         WST[:, 2 * k2:2 * k2 + 2, 0:off], ["WST"], [key])
            P.enabled = en
            return WBF[par], key

        def load_wgroup(srcs=None, defer=False):
            flush_prefetch()
            if wstate["pre"] is None:
                wstate["pre"] = _issue_load(wqueue.pop(0))
            cur = wstate["pre"]
            wstate["pre"] = None
            wstate["pend"] = bool(wqueue)
            if not defer:
                flush_prefetch()
            return cur

        def flush_prefetch():
            if wstate.get("pend"):
                wstate["pend"] = False
                wstate["pre"] = _issue_load(wqueue.pop(0))

        wqueue.append([w_in[:, GA:GA + 16]])
        for h_ in range(4):
            wqueue.append([w_in[:, GQ + h_ * 128:GQ + (h_ + 1) * 128], w_in[:, GK + h_ * 128:GK + (h_ + 1) * 128],
                           w_in[:, GV + h_ * 256:GV + (h_ + 1) * 256]])
            wqueue.append([w_in[:, GG + h_ * 256:GG + (h_ + 1) * 256]])
        wqueue.append([w_in[:, RW:RW + 128]])
        for p_ in range(8):
            wqueue.append([w_in[:, RR + p_ * 128:RR + (p_ + 1) * 128], w_in[:, RK + p_ * 128:RK + (p_ + 1) * 128],
                           w_in[:, RV + p_ * 128:RV + (p_ + 1) * 128], w_in[:, RG + p_ * 128:RG + (p_ + 1) * 128]])
        for dt_ in range(8):
            wqueue.append([w_in[:, G0 + dt_ * 128:G0 + (dt_ + 1) * 128],
                           w_in[:, G0 + 1024 + dt_ * 128:G0 + 1024 + (dt_ + 1) * 128],
                           w_up_gla[:, dt_ * 128:(dt_ + 1) * 128], w_up_rwkv[:, dt_ * 128:(dt_ + 1) * 128]])
        for g_ in range(2):
            wqueue.append([w_out[:, g_ * 512:(g_ + 1) * 512]])

        def proj_fm(wb, wkey, coff, ncols, s0, W, bank_i, c0=0):
            for kc in range(KC):
                P.mm(banks[bank_i][0:ncols, c0:c0 + W], wb[:, kc, coff:coff + ncols], xT[:, kc, s0:s0 + W],
                     kc == 0, kc == KC - 1, [wkey, "xT"], [bk(bank_i)])

        def proj_tm(wb, wkey, coff, ncols, t0, M, bank_i, c0=0):
            for kc in range(KC):
                P.mm(banks[bank_i][0:M, c0:c0 + ncols], xT[:, kc, t0:t0 + M], wb[:, kc, coff:coff + ncols],
                     kc == 0, kc == KC - 1, [wkey, "xT"], [bk(bank_i)])

        P.enabled = "0" in phases
        ar.mark()
        XS = [ar.f32(1024) for _ in range(2)]
        for tt in range(17):
            rows = 128 if tt < 16 else NS
            xs = XS[tt % 2]
            xk = f"XS{tt % 2}"
            P.dma(xs[0:rows, :], x[tt * 128:tt * 128 + rows, :], [], [xk])
            for half in range(2):
                b = banks[half]
                for j in range(4):
                    kc = half * 4 + j
                    P.tr(b[:, j * 128:j * 128 + rows], xs[0:rows, kc * 128:(kc + 1) * 128], ID[0:rows, 0:rows],
                         [xk, "CM"], [bk(half)])
                src = v3(b[:, 0:512], 128)[:, :, 0:rows]
                dst = xT[:, half * 4:half * 4 + 4, tt * 128:tt * 128 + rows]
                P.cp("act" if half == 0 else "dve", dst, src, [bk(half)], ["xT"])
        ar.release()
        P.fence()

        P.enabled = "G" in phases
        ar.mark()
        AW2 = ar.f32(512, parts=16)
        P.dma(AW2, alpha_w2, [], ["AW2"])
        ALR = ar.f32(TA, parts=16)
        SP = ar.f32(512); CSP = ar.f32(512); EB = ar.f32(512); EINV = ar.f32(512)
        QT = ar.bf16(512); KT = ar.bf16(512); QS = ar.f32(16)
        KH = ar.bf16(128); KHT = ar.bf16(128)
        VTK = v3(ar.bf16(4 * 256), 256)
        ATS = ar.bf16(128)
        SG = [ar.f32(256) for _ in range(2)]
        SGB = ar.bf16(256)
        GS = v3(ar.bf16(2 * 512), 512)
        SQ = v3(ar.bf16(2 * 512), 512)
        RSTD = ar.f32(512)
        ON = ar.f32(512)
        KTOK = ar.f32(128, parts=16); VTOK = ar.f32(256, parts=16)
        KM = v3(ar.f32(16 * 128, parts=16), 128)
        SS = v3(ar.f32(4 * 256), 256)
        SN = v3(ar.f32(4 * 256), 256)

        wb, wk = load_wgroup([w_in[:, GA:GA + 16]])
        for (s0, W) in SEGS:
            proj_fm(wb, wk, 0, 16, s0, W, 0)
            P.cp("act", ALR[:, s0:s0 + W], banks[0][0:16, 0:W], [bk(0)], ["ALR"])

        for h in range(4):
            wqkv, kqkv = load_wgroup([w_in[:, GQ + h * 128:GQ + (h + 1) * 128],
                                      w_in[:, GK + h * 128:GK + (h + 1) * 128],
                                      w_in[:, GV + h * 256:GV + (h + 1) * 256]])
            wg, kg = load_wgroup([w_in[:, GG + h * 256:GG + (h + 1) * 256]])
            cur = 0
            P.memset("pool", SG[0], 0.0, ["SG0"])
            P.memset("pool", SGB, 0.0, ["SGB"])
            for si, (s0, W) in enumerate(SEGS):
                samp = si == 4
                P.mm(banks[2][:, 0:W], AW2[:, h * 128:(h + 1) * 128], ALR[:, s0:s0 + W], True, True,
                     ["AW2", "ALR"], [bk(2)])
                P.act(SP[:, 0:W], banks[2][:, 0:W], AF.Exp, [bk(2), "NAB"], ["SP"], bias=NAB[:, h:h + 1], scale=-1.0)
                P.act(SP[:, 0:W], SP[:, 0:W], AF.Ln, ["SP"], ["SP"], bias=1.0)
                if samp:
                    csp = SP
                else:
                    P.scan(CSP[:, 0:W], SMG[:, 0:W], SP[:, 0:W], ["CM", "SP"], ["CSP"])
                    csp = CSP
                ck = "SP" if samp else "CSP"
                P.act(EB[:, 0:W], csp[:, 0:W], AF.Exp, [ck], ["EB"], scale=-1.0 / 16.0)
                P.act(EINV[:, 0:W], csp[:, 0:W], AF.Exp, [ck], ["EINV"], scale=1.0 / 16.0)
                proj_fm(wqkv, kqkv, 0, 128, s0, W, 0)
                if samp:
                    P.ts("dve", QS[:, 0:W], banks[0][:, 0:W], 128.0 ** -0.5, None, ALU.mult, ALU.bypass,
                         [bk(0)], ["QS"])
                else:
                    P.stt(QT[:, 0:W], banks[0][:, 0:W], 128.0 ** -0.5, EB[:, 0:W], ALU.mult, ALU.mult,
                          [bk(0), "EB"], ["QT"])
                    proj_fm(wqkv, kqkv, 128, 128, s0, W, 1)
                    P.tt("dve", KT[:, 0:W], banks[1][:, 0:W], EINV[:, 0:W], ALU.mult, [bk(1), "EINV"], ["KT"])
                for vh in range(2):
                    proj_fm(wg, kg, vh * 128, 128, s0, W, vh)
                    P.act(GS[:, vh, 0:W], banks[vh][:, 0:W], AF.Silu, [bk(vh)], ["GS"])
                if not samp:
                    for t4 in range(4):
                        bi = t4 % 2
                        proj_tm(wqkv, kqkv, 256, 256, s0 + t4 * 128, 128, bi)
                        P.cp("act", VTK[:, t4, :], banks[bi][:, 0:256], [bk(bi)], ["VTK"])
                    for c in range(4):
                        cs = slice(c * 128, (c + 1) * 128)
                        ecol = EB[:, c * 128 + 127:c * 128 + 128]
                        first = (si == 0 and c == 0)
                        P.mm(banks[4][:, 0:128], KT[:, cs], QT[:, cs], True, True, ["KT", "QT"], [bk(4)])
                        P.tt("dve", ATS, banks[4][:, 0:128], MASKU, ALU.mult, [bk(4), "CM"], ["ATS"])
                        P.ts("pool", KH, KT[:, cs], ecol, None, ALU.mult, ALU.bypass, ["KT", "EB"], ["KH"])
                        b4b = banks[4].bitcast(BF16)
                        P.tr(b4b[:, 512:640], KH, IDb, ["KH", "IDb"], [bk(4)])
                        P.cp("act", KHT, b4b[:, 512:640], [bk(4)], ["KHT"])
                        for vh in range(2):
                            vs = slice(vh * 128, (vh + 1) * 128)
                            P.mm(banks[5][:, vh * 128:(vh + 1) * 128], VTK[:, c, vs], ATS, True, first,
                                 ["VTK", "ATS"], [bk(5)])
                            if not first:
                                P.mm(banks[5][:, vh * 128:(vh + 1) * 128], SGB[:, vs], QT[:, cs], False, True,
                                     ["SGB", "QT"], [bk(5)])
                        P.cp("act", OgT[:, 2 * h:2 * h + 2, s0 + c * 128:s0 + (c + 1) * 128],
                             v3(banks[5][:, 0:256], 128), [bk(5)], ["OgT"])
                        P.mm(banks[6][:, 0:256], KHT, VTK[:, c, :], True, True, ["KHT", "VTK"], [bk(6)])
                        nxt = 1 - cur
                        P.stt(SG[nxt], SG[cur], ecol, banks[6][:, 0:256], ALU.mult, ALU.add,
                              [f"SG{cur}", "EB", bk(6)], [f"SG{nxt}"])
                        P.cp("act", SGB, SG[nxt], [f"SG{nxt}"], ["SGB"])
                        cur = nxt
                    if si == 3:
                        P.dma(gla_p[h], SG[cur], [f"SG{cur}"], [], is_out=True)
                else:
                    proj_tm(wqkv, kqkv, 128, 128, T, NS, 4)
                    P.cp("act", KTOK, banks[4][0:16, 0:128], [bk(4)], ["KTOK"])
                    proj_tm(wqkv, kqkv, 256, 256, T, NS, 4)
                    P.cp("act", VTOK, banks[4][0:16, 0:256], [bk(4)], ["VTOK"])
                    P.add("dve", lambda e: e.tensor_tensor(
                        out=KM, in0=KTOK.unsqueeze(1).to_broadcast([16, 16, 128]),
                        in1=ID16.unsqueeze(2).to_broadcast([16, 16, 128]), op=ALU.mult),
                        ["KTOK", "CM"], ["KM"])
                    for sg in range(4):
                        P.dma(SS, sgla[sg * 4:(sg + 1) * 4, h].rearrange("s d v -> d s v"), [], ["SS"])
                        for j in range(4):
                            s = sg * 4 + j
                            P.add("pe", lambda e, s=s: e.matmul(banks[6][:, 0:256], lhsT=KM[:, s, :], rhs=VTOK,
                                                                start=True, stop=True), ["KM", "VTOK"], [bk(6)])
                            P.stt(SN[:, j, :], SS[:, j, :], EB[:, s:s + 1], banks[6][:, 0:256], ALU.mult, ALU.add,
                                  ["SS", "EB", bk(6)], ["SN"])
                            for vh in range(2):
                                P.mm(banks[5][:, vh * 16 + s:vh * 16 + s + 1], SN[:, j, vh * 128:(vh + 1) * 128],
                                     QS[:, s:s + 1], True, True, ["SN", "QS"], [bk(5)])
                        P.dma(gla_s[sg * 4:(sg + 1) * 4, h].rearrange("s d v -> d s v"), SN, ["SN"], [], is_out=True)
                    P.cp("act", OgT[:, 2 * h:2 * h + 2, T:TA], v3(banks[5][:, 0:32], 16), [bk(5)], ["OgT"])
                og = OgT[:, 2 * h:2 * h + 2, s0:s0 + W]
                P.act(SQ[:, :, 0:W], og, AF.Square, ["OgT"], ["SQ"])
                for vh in range(2):
                    P.mm(banks[3][:, 0:W], ONESb, SQ[:, vh, 0:W], vh == 0, vh == 1, ["ONESb", "SQ"], [bk(3)])
                P.act(RSTD[:, 0:W], banks[3][:, 0:W], AF.Ln, [bk(3), "EPSC"], ["RSTD"], bias=EPSC[:, 0:1], scale=1.0 / 256.0)
                P.act(RSTD[:, 0:W], RSTD[:, 0:W], AF.Exp, ["RSTD"], ["RSTD"], scale=-0.5)
                for vh in range(2):
                    P.stt(ON[:, 0:W], OgT[:, 2 * h + vh, s0:s0 + W], cvc(CV_GNW, vh), RSTD[:, 0:W],
                          ALU.mult, ALU.mult, ["OgT", "CV", "RSTD"], ["ON"])
                    P.tt("pool", OgT[:, 2 * h + vh, s0:s0 + W], ON[:, 0:W], GS[:, vh, 0:W], ALU.mult,
                         ["ON", "GS"], ["OgT"])
        ar.release()
        P.fence()

        P.enabled = "R" in phases
        ar.mark()
        SHT = v3(ar.f32(33 * 16), 16)
        WSTf = WST.rearrange("p a b -> p (a b)")
        for tb in (0, 11, 22):
            P.dma(WSTf[0:16, 0:11 * 128], sshift[:, tb * 128:(tb + 11) * 128], [], ["WST"])
            for j in range(11):
                P.tr(banks[2][:, j * 16:(j + 1) * 16], WSTf[0:16, j * 128:(j + 1) * 128], ID[0:16, 0:16],
                     ["WST", "CM"], [bk(2)])
            P.cp("act", SHT[:, tb:tb + 11, :], v3(banks[2][:, 0:176], 16), [bk(2)], ["SHT"])

        LORA = ar.f32(TA)
        RAW = [ar.f32(513) for _ in range(4)]
        DQ = ar.f32(512)
        BB = ar.f32(9 * 512)
        Bp = lambda i: BB[:, i * 512:(i + 1) * 512]
        Rl, kRl = Bp(0), "B0"
        Kl, kKl = Bp(1), "B1"
        Gl, kGl = Bp(2), "B2"
        SIG, kSIG = Bp(3), "B3"
        Aa, kAa = Bp(4), "B4"
        KKr, kKKr = Bp(5), "B5"
        RN, kRN = Bp(6), "B6"
        KKn, kKKn = Bp(7), "B7"
        T1, kT1 = Bp(2), "B2"
        Kp, kKp = Bp(8), "B8"
        RKf, kRKf = Bp(5), "B5"
        BA, kBA = Bp(1), "B1"
        CS, kCS = Bp(4), "B4"
        GAM, kGAM = Bp(5), "B5"
        GIN, kGIN = Bp(6), "B6"
        CSX, kCSX = Bp(2), "B2"
        GEX, kGEX = Bp(3), "B3"
        YS, kYS = Bp(0), "B0"
        YSQ, kYSQ = Bp(1), "B1"
        MU2, kMU2 = Bp(2), "B2"
        VAR, kVAR = Bp(3), "B3"
        YD, kYD = Bp(4), "B4"
        SHO, kSHO = Bp(0)[0:17, :], "B0"
        SR, kSR = v3(BB[:, 5 * 512:7 * 512], 64), ["B5", "B6"]
        SRN, kSRN = v3(BB[:, 2 * 512:4 * 512], 64), ["B2", "B3"]
        SQb = ar.bf16(512)
        OUT = []
        for o_ in range(2):
            arkb = ar.bf16(2048)
            OUT.append(dict(
                ARKB=arkb, AR=arkb[:, 0:1024], AR4=arkb[:, 0:1024].rearrange("p (c q t) -> p c q t", q=2, t=64),
                Kt=arkb[:, 1024:1536], Bt=arkb[:, 1536:2048],
                Vb=ar.bf16(512), Gs=ar.bf16(512), BON=ar.bf16(512), GAMC=ar.f32(8),
                kAR=f"AR{o_}", kKt=f"Kt{o_}", kBt=f"Bt{o_}", kVb=f"Vb{o_}", kGs=f"Gs{o_}", kBON=f"BON{o_}",
                kGAMC=f"GAMC{o_}"))
        DG = OUT[0]["ARKB"].bitcast(F32)[:, 0:640].rearrange("p (s q k) -> p s q k", q=5, k=64)
        kDG = ["AR0", "Kt0"]
        Vf = ar.f32(16)
        CB = [(ar.bf16(320), ar.bf16(320), [ar.bf16(256) for _ in range(2)], ar.bf16(64), ar.f32(64), ar.bf16(64))
              for _ in range(3)]
        Hb = ar.bf16(64); Hf = ar.f32(64)
        XQ = v3(ar.f32(16 * 5), 5)
        TJ = ar.f32(64); T2 = ar.f32(64); SA = ar.f32(2)
        B4K = [bk(4), "b4tm", "b4s"]
        B5K = [bk(5), "b5z", "b5g", "b5r", "b5zv"]

        def lerp(q, ftile, src_bank, W, samp, out_ap, okey):
            raw = RAW[q]
            rk = f"RAW{q}"
            P.cp("act", raw[:, 1:1 + W], banks[src_bank][:, 0:W], [bk(src_bank)], [rk])
            if samp:
                P.tt("pool", DQ[:, 0:W], SHT[:, ftile, :], raw[:, 1:1 + W], ALU.subtract, ["SHT", rk], ["DQ"])
            else:
                P.tt("pool", DQ[:, 0:W], raw[:, 0:W], raw[:, 1:1 + W], ALU.subtract, [rk], ["DQ"])
            P.stt(out_ap, DQ[:, 0:W], cvc(CV_MU, ftile), raw[:, 1:1 + W], ALU.mult, ALU.add, ["DQ", "CV", rk], [okey])
            if not samp:
                P.cp("pool", raw[:, 0:1], raw[:, W:W + 1], [rk], [rk])

        wl, kl = load_wgroup([w_in[:, RW:RW + 128]])
        P.memset("pool", RAW[0][:, 0:1], 0.0, ["RAW0"])
        for si, (s0, W) in enumerate(SEGS):
            proj_fm(wl, kl, 0, 128, s0, W, 0)
            lerp(0, 32, 0, W, si == 4, LORA[:, s0:s0 + W], "LORA")
        P.act(LORA[0:64, :], LORA[0:64, :], AF.Tanh, ["LORA"], ["LORA"])
        proj_tm(wl, kl, 0, 128, T - 1, 17, 2)
        P.cp("act", SHO[:, 0:128], banks[2][0:17, 0:128], [bk(2)], [kSHO])
        P.dma(shift_o[:, 4096:4224], SHO[:, 0:128], [kSHO], [], is_out=True)

        def gn_post(p, s0, W, O):
            P.act(YSQ[:, 0:W], YS[:, 0:W], AF.Square, [kYS], [kYSQ])
            P.mm(banks[2][:, 0:W], BAVG, YS[:, 0:W], True, True, ["CM", kYS], [bk(2)])
            P.mm(banks[3][:, 0:W], BAVG, YSQ[:, 0:W], True, True, ["CM", kYSQ], [bk(3)])
            P.act(MU2[:, 0:W], banks[2][:, 0:W], AF.Square, [bk(2)], [kMU2])
            P.tt("dve", VAR[:, 0:W], banks[3][:, 0:W], MU2[:, 0:W], ALU.subtract, [bk(3), kMU2], [kVAR])
            P.act(VAR[:, 0:W], VAR[:, 0:W], AF.Ln, [kVAR], [kVAR], bias=EPSC[:, 1:2])
            P.act(VAR[:, 0:W], VAR[:, 0:W], AF.Exp, [kVAR], [kVAR], scale=-0.5)
            P.tt("dve", YD[:, 0:W], YS[:, 0:W], banks[2][:, 0:W], ALU.subtract, [kYS, bk(2)], [kYD])
            P.tt("pool", YD[:, 0:W], YD[:, 0:W], VAR[:, 0:W], ALU.mult, [kYD, kVAR], [kYD])
            P.ts("dve", YD[:, 0:W], YD[:, 0:W], cvc(CV_LW, p), cvc(CV_LB, p), ALU.mult, ALU.add, [kYD, "CV"], [kYD])
            P.tt("pool", YD[:, 0:W], YD[:, 0:W], O["BON"][:, 0:W], ALU.add, [kYD, O["kBON"]], [kYD])
            P.tt("dve", OrT[:, p, s0:s0 + W], YD[:, 0:W], O["Gs"][:, 0:W], ALU.mult, [kYD, O["kGs"]], ["OrT"])

        def mm2(out, lhsT, rhs, n0, n1, l0, l1, r0, r1, start, stop, r, w):
            for hh in range(2):
                ps = slice(hh * 64, (hh + 1) * 64)
                P.mm(out[ps, n0:n1], lhsT[ps, l0:l1], rhs[ps, r0:r1], start, stop, r, w)

        if RLEVEL < 2:
            P.enabled = False
        P.counting = True
        def elem_gen(p, wp, kp, si, O):
            s0, W = SEGS[si]
            samp = si == 4
            Vb, Gs, BON = O["Vb"], O["Gs"], O["BON"]
            AR4, Kt, Bt = O["AR4"], O["Kt"], O["Bt"]
            proj_fm(wp, kp, 0, 128, s0, W, 2)
            lerp(0, p, 2, W, samp, Rl[:, 0:W], kRl)
            yield
            proj_fm(wp, kp, 128, 128, s0, W, 2)
            lerp(1, 8 + p, 2, W, samp, Kl[:, 0:W], kKl)
            yield
            proj_fm(wp, kp, 256, 128, s0, W, 2)
            lerp(2, 16 + p, 2, W, samp, Vb[:, 0:W], O["kVb"])
            if samp:
                P.stt(Vf[:, 0:W], DQ[:, 0:W], cvc(CV_MU, 16 + p), RAW[2][:, 1:1 + W], ALU.mult, ALU.add,
                      ["DQ", "CV", "RAW2"], ["Vf"])
            yield
            proj_fm(wp, kp, 384, 128, s0, W, 2)
            lerp(3, 24 + p, 2, W, samp, Gl[:, 0:W], kGl)
            P.act(Gs[:, 0:W], Gl[:, 0:W], AF.Silu, [kGl], [O["kGs"]])
            yield
            P.mm(banks[2][:, 0:W], W2A2[0:64, p * 128:(p + 1) * 128], LORA[0:64, s0:s0 + W], True, True,
                 ["W2A2", "LORA"], [bk(2)])
            P.act(SIG[:, 0:W], banks[2][:, 0:W], AF.Sigmoid, [bk(2), "CV"], [kSIG], bias=cvc(CV_W0, p))
            yield
            P.mm(banks[2][:, 0:W], W2A2[64:128, p * 128:(p + 1) * 128], LORA[64:128, s0:s0 + W], True, True,
                 ["W2A2", "LORA"], [bk(2)])
            P.act(Aa[:, 0:W], banks[2][:, 0:W], AF.Sigmoid, [bk(2), "CV"], [kAa], bias=cvc(CV_A0, p))
            yield
            P.ts("pool", KKr[:, 0:W], Kl[:, 0:W], cvc(CV_KK, p), None, ALU.mult, ALU.bypass, [kKl, "CV"], [kKKr])
            P.act(SQb[:, 0:W], KKr[:, 0:W], AF.Square, [kKKr], ["SQb"])
            P.mm(banks[2][:, 0:W], BONESb, SQb[:, 0:W], True, True, ["BONESb", "SQb"], [bk(2)])
            yield
            P.act(RN[:, 0:W], banks[2][:, 0:W], AF.Ln, [bk(2)], [kRN])
            P.act(RN[:, 0:W], RN[:, 0:W], AF.Exp, [kRN], [kRN], scale=-0.5)
            yield
            P.tt("pool", KKn[:, 0:W], KKr[:, 0:W], RN[:, 0:W], ALU.mult, [kKKr, kRN], [kKKn])
            P.ts("dve", T1[:, 0:W], Aa[:, 0:W], -1.0, cvc(CV_KA, p), ALU.add, ALU.mult, [kAa, "CV"], [kT1])
            P.stt(Kp[:, 0:W], T1[:, 0:W], 1.0, Kl[:, 0:W], ALU.add, ALU.mult, [kT1, kKl], [kKp])
            yield
            P.tt("pool", RKf[:, 0:W], Rl[:, 0:W], Kp[:, 0:W], ALU.mult, [kRl, kKp], [kRKf])
            P.ts("pool", RKf[:, 0:W], RKf[:, 0:W], cvc(CV_RK, p), None, ALU.mult, ALU.bypass, [kRKf, "CV"], [kRKf])
            yield
            P.mm(banks[2][:, 0:W], BONES, RKf[:, 0:W], True, True, ["CM", kRKf], [bk(2)])
            P.tt("dve", BON[:, 0:W], banks[2][:, 0:W], Vb[:, 0:W], ALU.mult, [bk(2), O["kVb"]], [O["kBON"]])
            P.tt("pool", BA[:, 0:W], KKn[:, 0:W], Aa[:, 0:W], ALU.mult, [kKKn, kAa], [kBA])
            yield
            if not samp:
                P.scan(CS[:, 0:W], SMR[:, 0:W], SIG[:, 0:W], ["CM", kSIG], [kCS])
                P.act(GAM[:, 0:W], CS[:, 0:W], AF.Exp, [kCS], [kGAM], scale=-DECAY)
                P.act(GIN[:, 0:W], CS[:, 0:W], AF.Exp, [kCS], [kGIN], scale=DECAY)
                yield
                P.tt("pool", CSX[:, 0:W], CS[:, 0:W], SIG[:, 0:W], ALU.subtract, [kCS, kSIG], [kCSX])
                P.act(GEX[:, 0:W], CSX[:, 0:W], AF.Exp, [kCSX], [kGEX], scale=-DECAY)
                P.cp("pool", O["GAMC"], v3(GAM[:, 0:W], 64)[:, :, 63], [kGAM], [O["kGAMC"]])
                yield
                v64 = lambda a_: v3(a_[:, 0:W], 64)
                P.stt(AR4[:, :, 0, :], v64(KKn), -1.0, v64(GEX), ALU.mult, ALU.mult, [kKKn, kGEX], [O["kAR"]])
                P.tt("dve", AR4[:, :, 1, :], v64(Rl), v64(GAM), ALU.mult, [kRl, kGAM], [O["kAR"]])
                yield
                P.tt("pool", Kt[:, 0:W], Kp[:, 0:W], GIN[:, 0:W], ALU.mult, [kKp, kGIN], [O["kKt"]])
                P.tt("pool", Bt[:, 0:W], BA[:, 0:W], GIN[:, 0:W], ALU.mult, [kBA, kGIN], [O["kBt"]])
            else:
                P.ts("pool", XQ[:, :, 0], KKn[:, 0:W], -1.0, None, ALU.mult, ALU.bypass, [kKKn], ["XQ"])
                P.act(XQ[:, :, 1], SIG[:, 0:W], AF.Exp, [kSIG], ["XQ"], scale=-DECAY)
                P.cp("pool", XQ[:, :, 2], BA[:, 0:W], [kBA], ["XQ"])
                P.cp("pool", XQ[:, :, 3], Kp[:, 0:W], [kKp], ["XQ"])
                P.cp("pool", XQ[:, :, 4], Rl[:, 0:W], [kRl], ["XQ"])

        def chunk_gen(c, par, si, O):
            tb, db = [(4, 5), (6, 7), (0, 1)][par]
            bT, bD = banks[tb], banks[db]
            bTb = bT.bitcast(BF16)
            kT, kD = bk(tb), bk(db)
            TMp, SCp, ZPQp, GA1p, GVGp, RHp = CB[par]
            kTM, kSC, kGA1, kGVG, kRH = f"TM{par}", f"SC{par}", f"GA1{par}", f"GVG{par}", f"RH{par}"
            AR, AR4, Kt, Bt, Vb = O["AR"], O["AR4"], O["Kt"], O["Bt"], O["Vb"]
            kAR, kKt, kBt, kVb = O["kAR"], O["kKt"], O["kBt"], O["kVb"]
            cs = slice(c * 64, (c + 1) * 64)
            last = (si == 3 and c == 7)
            arc = AR[:, c * 128:(c + 1) * 128]
            for hh in range(2):
                ps = slice(hh * 64, (hh + 1) * 64)
                idb = IDb[ps, ps]
                P.tr(bTb[ps, 64:128], AR4[ps, c, 0, :], idb, [kAR, "IDb"], [kT])
                P.tr(bTb[ps, 128:192], Bt[ps, cs], idb, [kBt, "IDb"], [kT])
                P.tr(bTb[ps, 192:256], Kt[ps, cs], idb, [kKt, "IDb"], [kT])
                P.tr(bTb[ps, 256:320], Vb[ps, cs], idb, [kVb, "IDb"], [kT])
            P.cp("act", TMp[:, 64:320], bTb[:, 64:320], [kT], [kTM])
            mm2(bD, Bt, arc, 0, 128, c * 64, (c + 1) * 64, 0, 128, True, True, [kBt, kAR], [kD])
            mm2(bD, Kt, arc, 128, 256, c * 64, (c + 1) * 64, 0, 128, True, True, [kKt, kAR], [kD])
            mm2(bD, arc, Bt, 256, 320, 0, 64, c * 64, (c + 1) * 64, True, True, [kBt, kAR], [kD])
            P.tt("dve", SCp, bD[:, 0:320], MASK5, ALU.mult, [kD, "CM"], [kSC])
            yield
            mm2(bD, SCp, TMp, 448, 512, 128, 192, 256, 320, True, True, [kSC, kTM], [kD])
            P.cp("act", TMp[:, 0:64], bD[:, 448:512], [kD], [kTM])
            yield
            zsrc, zk = TMp, kTM
            psrc, pk, pc = SCp, kSC, 0
            qsrc, qk, qc = SCp, kSC, 256
            for lvl in range(6):
                mm2(bD, psrc, zsrc, 0, 128, pc, pc + 64, 0, 128, True, True, [pk, zk], [kD])
                if lvl < 5:
                    mm2(bT, qsrc, psrc, 0, 64, qc, qc + 64, pc, pc + 64, True, True, [pk, qk], [kT])
                    mm2(bT, psrc, qsrc, 64, 128, pc, pc + 64, qc, qc + 64, True, True, [pk, qk], [kT])
                dst = ZPQp[lvl % 2]
                dzk = f"Z{par}{lvl % 2}"
                dpk = f"PQ{par}{lvl % 2}"
                P.tt("dve", dst[:, 0:128], bD[:, 0:128], zsrc[:, 0:128], ALU.add, [kD, zk], [dzk])
                if lvl < 5:
                    P.cp("act", dst[:, 128:256], bT[:, 0:128], [kT], [dpk])
                zsrc, zk = dst, dzk
                psrc, pk, pc = dst, dpk, 128
                qsrc, qk, qc = dst, dpk, 192
                yield
            Wm, wk_ = zsrc, zk
            gam_col = O["GAMC"][:, c:c + 1]
            kGC = O["kGAMC"]
            mm2(bD, Wm, TMp, 256, 320, 64, 128, 128, 192, True, True, [wk_, kTM], [kD])
            mm2(bD, TMp, Wm, 320, 384, 128, 192, 0, 64, True, False, [wk_, kTM], [kD])
            mm2(bD, TMp, TMp, 320, 384, 192, 256, 256, 320, False, True, [kTM], [kD])
            mm2(bT, ISTKb, arc, 384, 448, 0, 64, 64, 128, True, False, ["ISTKb", kAR], [kT])
            mm2(bT, Wm, SCp, 384, 448, 64, 128, 64, 128, False, True, [wk_, kSC], [kT])
            P.tt("dve", GA1p, bD[:, 256:320], ISTK, ALU.add, [kD, "CM"], [kGA1])
            P.ts("dve", GVGp, bD[:, 320:384], gam_col, None, ALU.mult, ALU.bypass, [kD, kGC], [kGVG])
            P.cp("act", RHp, bT[:, 384:448], [kT], [kRH])
            yield
            b3 = banks[3]
            mm2(b3, Wm, SCp, c * 64, (c + 1) * 64, 0, 64, 64, 128, True, False, [wk_, kSC], [bk(3)])
            mm2(b3, TMp, SCp, c * 64, (c + 1) * 64, 256, 320, 192, 256, False, False, [kTM, kSC], [bk(3)])
            mm2(b3, Hb, RHp, c * 64, (c + 1) * 64, 0, 64, 0, 64, False, True, ["Hb", kRH], [bk(3)])
            mm2(banks[2], GA1p, Hb, 0, 64, 0, 64, 0, 64, True, True, [kGA1, "Hb"], [bk(2)])
            if last:
                P.stt(Hf, banks[2][:, 0:64], gam_col, GVGp, ALU.mult, ALU.add, [bk(2), kGC, kGVG], ["Hf"])
            P.stt(Hb, banks[2][:, 0:64], gam_col, GVGp, ALU.mult, ALU.add, [bk(2), kGC, kGVG], ["Hb"])

        def run_overlapped(chunk_args, extra):
            active = []
            nxt = 0
            free_par = [0, 1, 2]
            since = 99
            ex = extra
            while nxt < len(chunk_args) or active or ex is not None:
                if nxt < len(chunk_args) and free_par and (since >= 3 or not active):
                    pr_ = free_par.pop(0)
                    c_, si_, O_ = chunk_args[nxt]
                    active.append((chunk_gen(c_, pr_, si_, O_), pr_))
                    nxt += 1
                    since = 0
                since += 1
                for (g_, pr_) in list(active):
                    try:
                        next(g_)
                    except StopIteration:
                        active.remove((g_, pr_))
                        free_par.append(pr_)
                if ex is not None:
                    try:
                        next(ex)
                    except StopIteration:
                        ex = None

        for p in range(8):
            wp, kp = load_wgroup(defer=True)
            for q in range(4):
                P.memset("pool", RAW[q][:, 0:1], 0.0, [f"RAW{q}"])
            P.memset("pool", Hb, 0.0, ["Hb"])
            proj_tm(wp, kp, 0, 512, T - 1, 17, 2)
            P.cp("act", SHO, banks[2][0:17, 0:512], [bk(2)], [kSHO])
            P.dma(shift_o[:, 0:4096].rearrange("t (q f) -> t q f", q=4)[:, :, p * 128:(p + 1) * 128],
                  v3(SHO, 128), [kSHO], [], is_out=True)
            run_overlapped([], elem_gen(p, wp, kp, 0, OUT[0]))
            flush_prefetch()
            for si in range(4):
                s0, W = SEGS[si]
                O = OUT[si % 2]
                On = OUT[(si + 1) % 2]
                run_overlapped([(c, si, O) for c in range(8)], elem_gen(p, wp, kp, si + 1, On))
                P.cp("act", YS[:, 0:W], banks[3][:, 0:W], [bk(3)], [kYS])
                if si == 3:
                    P.tr(banks[2][0:64, 64:192], Hf, ID, ["Hf", "CM"], [bk(2)])
                    P.cp("act", T2[0:64, :], banks[2][0:64, 64:128], [bk(2)], ["T2"])
                    P.cp("act", TJ[0:64, :], banks[2][0:64, 128:192], [bk(2)], ["TJ"])
                    P.dma(rwkv_p[2 * p], T2[0:64, :], ["T2"], [], is_out=True)
                    P.dma(rwkv_p[2 * p + 1], TJ[0:64, :], ["TJ"], [], is_out=True)
                gn_post(p, s0, W, O)
            s0, W = SEGS[4]
            O = OUT[0]
            P.dma(SR, srwkv[:, 2 * p:2 * p + 2].rearrange("s h v k -> (h v) s k"), [], kSR)
            for s2 in range(8):
                hs = slice(s2 * 2, s2 * 2 + 2)
                P.add("dve", lambda e, hs=hs: e.tensor_tensor(
                    out=DG, in0=ISTK.unsqueeze(1).unsqueeze(1).to_broadcast([128, 2, 5, 64]),
                    in1=XQ[:, hs, :].unsqueeze(3).to_broadcast([128, 2, 5, 64]), op=ALU.mult),
                    ["CM", "XQ"], kDG)
                for j in range(2):
                    s = s2 * 2 + j
                    bi = 4 + j
                    bb = banks[bi]
                    dgs = DG[:, j].rearrange("p q k -> p (q k)")
                    for hh in range(2):
                        ps = slice(hh * 64, (hh + 1) * 64)
                        P.mm(bb[ps, 0:320], ONES[ps, 0:64], dgs[ps, :], True, True, ["CM"] + kDG, [bk(bi)])
                    P.stt(TJ, SR[:, s, :], 1.0, bb[:, 0:64], ALU.mult, ALU.mult, kSR + [bk(bi)], ["TJ", "SA"],
                          accum=SA[:, 0:1])
                    P.tt("dve", T2, SR[:, s, :], bb[:, 64:128], ALU.mult, kSR + [bk(bi)], ["T2"])
                    P.stt(T2, bb[:, 128:192], SA[:, 0:1], T2, ALU.mult, ALU.add, [bk(bi), "SA", "T2"], ["T2"])
                    P.stt(SRN[:, s, :], bb[:, 192:256], Vf[:, s:s + 1], T2, ALU.mult, ALU.add,
                          [bk(bi), "Vf", "T2"], kSRN)
                    P.stt(TJ, SRN[:, s, :], 1.0, bb[:, 256:320], ALU.mult, ALU.mult, kSRN + [bk(bi)],
                          ["TJ", kYS], accum=YS[:, s:s + 1])
            P.dma(rwkv_s[:, 2 * p:2 * p + 2].rearrange("s h v k -> (h v) s k"), SRN, kSRN, [], is_out=True)
            gn_post(p, s0, W, O)
        ar.release()
        P.fence()

        P.enabled = "F" in phases
        ar.mark()
        MT = v3(ar.bf16(KC * TA), TA)
        FT = [(ar.bf16(512), ar.bf16(512), ar.f32(512)) for _ in range(2)]
        fcnt = 0
        for dt in range(8):
            wf, kf = load_wgroup(defer=True)
            for (s0, W) in SEGS:
                if s0 == 512:
                    flush_prefetch()
                par = fcnt % 2
                fcnt += 1
                SGA, SGBt, M1 = FT[par]
                kA, kB, kM = f"SGA{par}", f"SGBt{par}", f"M1{par}"
                b0, b1, b2, b3 = [4 * par + i for i in range(4)]
                proj_fm(wf, kf, 0, 128, s0, W, b0)
                proj_fm(wf, kf, 128, 128, s0, W, b1)
                for kc in range(KC):
                    P.mm(banks[b2][:, 0:W], wf[:, kc, 256:384], OgT[:, kc, s0:s0 + W], kc == 0, kc == KC - 1,
                         [kf, "OgT"], [bk(b2)])
                for kc in range(KC):
                    P.mm(banks[b3][:, 0:W], wf[:, kc, 384:512], OrT[:, kc, s0:s0 + W], kc == 0, kc == KC - 1,
                         [kf, "OrT"], [bk(b3)])
                P.act(SGA[:, 0:W], banks[b0][:, 0:W], AF.Sigmoid, [bk(b0)], [kA])
                P.act(SGBt[:, 0:W], banks[b1][:, 0:W], AF.Sigmoid, [bk(b1)], [kB])
                P.tt("dve", M1[:, 0:W], banks[b2][:, 0:W], SGA[:, 0:W], ALU.mult, [bk(b2), kA], [kM])
                P.tt("dve", MT[:, dt, s0:s0 + W], banks[b3][:, 0:W], SGBt[:, 0:W], ALU.mult, [bk(b3), kB], ["MT"])
                P.tt("pool", MT[:, dt, s0:s0 + W], MT[:, dt, s0:s0 + W], M1[:, 0:W], ALU.add, ["MT", kM], ["MT"])
        WO = v3(ar.bf16(KC * 1024), 1024)
        for g in range(2):
            wo, ko = load_wgroup([w_out[:, g * 512:(g + 1) * 512]])
            P.cp("pool", WO[:, :, g * 512:(g + 1) * 512], wo, [ko], ["WO"])
        XA = xT.rearrange("p a b -> p (a b)").bitcast(F32)
        LNGB = XA[:, 0:2048]
        XR = [XA[:, 2048 + i * 1024:2048 + (i + 1) * 1024] for i in range(2)]
        ZT = [XA[:, 4096 + i * 1024:4096 + (i + 1) * 1024] for i in range(2)]
        P.dma(LNGB, lngb, [], ["LNGB", "xT"])
        ST = ar.f32(8)
        alpha = 2.0 ** 0.25
        for tt in range(17):
            rows = 128 if tt < 16 else NS
            t0 = tt * 128
            xr = XR[tt % 2]; xk = f"XR{tt % 2}"
            zt = ZT[tt % 2]; zk = f"ZT{tt % 2}"
            P.dma(xr[0:rows, :], x[t0:t0 + rows, :], [], [xk, "xT"])
            for eh in range(2):
                bi = 2 * (tt % 2) + eh
                for kc in range(KC):
                    P.mm(banks[bi][0:rows, 0:512], MT[:, kc, t0:t0 + rows], WO[:, kc, eh * 512:(eh + 1) * 512],
                         kc == 0, kc == KC - 1, ["MT", "WO"], [bk(bi)])
                P.stt(zt[0:rows, eh * 512:(eh + 1) * 512], xr[0:rows, eh * 512:(eh + 1) * 512], alpha,
                      banks[bi][0:rows, 0:512], ALU.mult, ALU.add, [xk, bk(bi)], [zk, "xT"])
            R_ = slice(0, rows)
            P.act(xr[R_, :], zt[R_, :], AF.Copy, [zk], [xk, "ST"], accum=ST[R_, 0:1])
            P.act(xr[R_, :], zt[R_, :], AF.Square, [zk], [xk, "ST"], accum=ST[R_, 1:2])
            P.ts("pool", ST[R_, 2:3], ST[R_, 0:1], 1.0 / D, None, ALU.mult, ALU.bypass, ["ST"], ["ST"])
            P.tt("pool", ST[R_, 3:4], ST[R_, 2:3], ST[R_, 2:3], ALU.mult, ["ST"], ["ST"])
            P.stt(ST[R_, 4:5], ST[R_, 1:2], 1.0 / D, ST[R_, 3:4], ALU.mult, ALU.subtract, ["ST"], ["ST"])
            P.act(ST[R_, 5:6], ST[R_, 4:5], AF.Ln, ["ST"], ["ST"], bias=EPSC[R_, 0:1])
            P.act(ST[R_, 5:6], ST[R_, 5:6], AF.Exp, ["ST"], ["ST"], scale=-0.5)
            P.ts("dve", zt[R_, :], zt[R_, :], ST[R_, 2:3], ST[R_, 5:6], ALU.subtract, ALU.mult, [zk, "ST"], [zk])
            P.tt("pool", zt[R_, :], zt[R_, :], LNGB[R_, 0:1024], ALU.mult, [zk, "LNGB"], [zk])
            P.tt("dve", zt[R_, :], zt[R_, :], LNGB[R_, 1024:2048], ALU.add, [zk, "LNGB"], [zk])
            P.dma(y[t0:t0 + rows, :], zt[R_, :], [zk], [], is_out=True)
        ar.release()
        P.fence()

        P.enabled = True
        P.counting = False
        P.finish()
        P.emit(nc, st)
    return nc


def _consts():
    cm = np.zeros((128, NCM), np.float32)
    cm[:, CM_ID:CM_ID + 128] = np.eye(128, dtype=np.float32)
    s = np.arange(128)[:, None] % 64
    t = np.arange(64)[None, :]
    strict = (s < t).astype(np.float32)
    incl = (s <= t).astype(np.float32)
    lower = (t < s).astype(np.float32)
    cm[:, CM_MASK5:CM_MASK5 + 320] = np.concatenate([strict, incl, strict, incl, lower], axis=1)
    j = np.arange(128)[:, None]
    i = np.arange(128)[None, :]
    cm[:, CM_MASKU:CM_MASKU + 128] = (j <= i).astype(np.float32)
    cm[:, CM_ISTK:CM_ISTK + 64] = (s == t).astype(np.float32)
    blk = (np.arange(128)[:, None] // 64 == np.arange(128)[None, :] // 64).astype(np.float32)
    cm[:, CM_BONES:CM_BONES + 128] = blk
    cm[:, CM_BAVG:CM_BAVG + 128] = blk / 64.0
    cm[:, CM_ONES:CM_ONES + 128] = 1.0
    smg = np.ones(512, np.float32); smg[::128] = 0
    smr = np.ones(512, np.float32); smr[::64] = 0
    cm[:, CM_SMG:CM_SMG + 512] = smg[None, :]
    cm[:, CM_SMR:CM_SMR + 512] = smr[None, :]
    cm[0:16, CM_ID16:CM_ID16 + 16] = np.eye(16, dtype=np.float32)
    return cm


def _cols(vec):
    return np.ascontiguousarray(np.asarray(vec, np.float32).reshape(-1, 128).T)


_NC_CACHE = {}


def kernel(x_prompt, x_sample, state_gla, state_rwkv, state_rwkv_shift, w_in, gla_alpha_w2,
           gla_alpha_b, gla_norm_w, rwkv_mu, rwkv_w0, rwkv_w2, rwkv_a0, rwkv_a2, rwkv_k_k,
           rwkv_k_a, rwkv_r_k, rwkv_lnx_w, rwkv_lnx_b, w_up_gla, w_up_rwkv, w_out, ln_g, ln_b):
    f = lambda a: np.ascontiguousarray(np.asarray(a, dtype=np.float32))
    x_prompt, x_sample = f(x_prompt), f(x_sample)
    cvec = np.concatenate([_cols(rwkv_mu[0]), _cols(gla_alpha_b[0]), _cols(gla_norm_w[0]), _cols(rwkv_w0[0]),
                           _cols(rwkv_a0[0]), _cols(rwkv_k_k[0]), _cols(rwkv_k_a[0]),
                           _cols(np.asarray(rwkv_r_k[0]).reshape(-1)), _cols(rwkv_lnx_w[0]), _cols(rwkv_lnx_b[0])],
                          axis=1)
    assert cvec.shape == (128, NCV)
    cmat = _consts()
    w2a2 = np.concatenate([f(rwkv_w2[0]), f(rwkv_a2[0])], axis=0)
    lngb = np.concatenate([np.broadcast_to(f(ln_g[0])[None, :], (128, D)),
                           np.broadcast_to(f(ln_b[0])[None, :], (128, D))], axis=1)
    lngb = np.ascontiguousarray(lngb)
    common = dict(w_in=f(w_in[0]), alpha_w2=f(gla_alpha_w2[0]), w2a2=w2a2, w_up_gla=f(w_up_gla[0]),
                  w_up_rwkv=f(w_up_rwkv[0]), w_out=f(w_out[0]), lngb=lngb, cvec=f(cvec), cmat=cmat)
    in_maps = []
    for c in range(8):
        m = dict(common)
        m["x"] = np.ascontiguousarray(np.concatenate([x_prompt[c], x_sample[NS * c:NS * (c + 1), 0]], axis=0))
        m["sgla"] = f(state_gla[0, NS * c:NS * (c + 1)])
        m["srwkv"] = f(state_rwkv[0, NS * c:NS * (c + 1)])
        m["sshift"] = f(state_rwkv_shift[0, NS * c:NS * (c + 1)])
        in_maps.append(m)
    if "nc" not in _NC_CACHE:
        _NC_CACHE["nc"] = build_nc()
    nc = _NC_CACHE["nc"]
    res = run_bass_kernel_spmd(nc, in_maps, core_ids=list(range(8)))
    rs = res.results
    y_prompt = np.stack([rs[c]["y"][0:T] for c in range(8)], axis=0)
    y_sample = np.concatenate([rs[c]["y"][T:TA] for c in range(8)], axis=0)[:, None, :]
    gla_p = np.stack([rs[c]["gla_p"] for c in range(8)], axis=0)[None]
    rwkv_p = np.stack([rs[c]["rwkv_p"] for c in range(8)], axis=0)[None]
    shift_p = np.stack([rs[c]["shift_o"][0] for c in range(8)], axis=0)[None]
    gla_s = np.concatenate([rs[c]["gla_s"] for c in range(8)], axis=0)[None]
    rwkv_s = np.concatenate([rs[c]["rwkv_s"] for c in range(8)], axis=0)[None]
    shift_s = np.concatenate([rs[c]["shift_o"][1:17] for c in range(8)], axis=0)[None]
    outs = (y_prompt, y_sample, gla_p, rwkv_p, shift_p, gla_s, rwkv_s, shift_s)
    return tuple(np.ascontiguousarray(o, dtype=np.float32) for o in outs)
```

```python
import contextlib
import numpy as np
import concourse.bass as bass
import concourse.mybir as mybir
from concourse.bass_utils import run_bass_kernel_spmd

F32 = mybir.dt.float32
BF16 = mybir.dt.bfloat16
AF = mybir.ActivationFunctionType
ALU = mybir.AluOpType

ENGS = ["pe", "act", "dve", "pool", "sp"]

T = 2048
NS = 16
TA = T + NS
D = 1024
KC = 8
NIN = 9360
GQ, GK, GV, GG, GA = 0, 512, 1024, 2048, 3072
R0 = 3088
RR, RK, RV, RG, RW = R0, R0 + 1024, R0 + 2048, R0 + 3072, R0 + 4096
G0 = R0 + 4224
SEGS = [(0, 512), (512, 512), (1024, 512), (1536, 512), (2048, 16)]
DECAY = 0.606531
RLEVEL = 99
RPAIRS = 8
RSTOP = None
RLOG = []

CV_MU, CV_AB, CV_GNW, CV_W0, CV_A0, CV_KK, CV_KA, CV_RK, CV_LW, CV_LB = 0, 33, 37, 39, 47, 55, 63, 71, 79, 87
NCV = 95
CM_ID, CM_MASK5, CM_MASKU, CM_ISTK, CM_BONES, CM_BAVG, CM_ONES, CM_SMG, CM_SMR, CM_ID16 = (
    0, 128, 448, 576, 640, 768, 896, 1024, 1536, 2048)
NCM = 2064


class Prog:
    EPOCH = 8192
    NDMA = 14

    def __init__(self):
        self.ops = {e: [] for e in ENGS}
        self.cnt = {e: 0 for e in ENGS}
        self.ndma = 0
        self.dma_events = []
        self.last_w = {}
        self.readers = {}
        self.waited = {e: {} for e in ENGS}
        self.semkeys = set()
        self.out_events = []
        self.enabled = True
        self.pending = {e: [] for e in ENGS}
        self.last_ev = {}
        self.know = {e: {} for e in ENGS}
        self.evclock = {}
        self.evidx = {}
        self.nev = 0

    def fence(self):
        evs = list(self.last_ev.values()) + list(self.dma_events[-self.NDMA:])
        for e in ENGS:
            self.pending[e] = list(evs)

    def _resolve(self, eng, cands):
        know = self.know[eng]
        waits = []
        for ev in sorted(cands, key=lambda e: -self.evidx[e]):
            sk, val = ev
            if eng == "pe" and sk[0] == "pe":
                continue
            if know.get(sk, 0) >= val:
                continue
            waits.append(ev)
            for k2, v2 in self.evclock[ev].items():
                if know.get(k2, 0) < v2:
                    know[k2] = v2
        return waits

    def add(self, eng, fn, r=(), w=(), dma=False, is_out=False):
        if not self.enabled:
            return None
        if RSTOP is not None and getattr(self, "counting", False):
            self.nops = getattr(self, "nops", 0) + 1
            if self.nops > RSTOP:
                return None
        xb = [k for k in r if isinstance(k, str) and k.startswith("bank")]
        if xb:
            r = [k for k in r if k not in xb]
            w = list(w) + [k for k in xb if k not in w]
        cands = set()
        if self.pending[eng]:
            cands.update(self.pending[eng])
            self.pending[eng] = []
        for k in r:
            cands.add(self.last_w.get(k))
        for k in w:
            cands.add(self.last_w.get(k))
            cands.update(self.readers.get(k, ()))
        if dma and self.ndma >= self.NDMA:
            cands.add(self.dma_events[self.ndma - self.NDMA])
        cands.discard(None)
        waits = self._resolve(eng, cands)
        clk = dict(self.know[eng])
        if dma:
            j = self.ndma
            self.ndma += 1
            sk = ("dma", j % self.NDMA)
            val = 16 * (j // self.NDMA + 1)
            ev = (sk, val)
            self.dma_events.append(ev)
            inc = 16
            if is_out:
                self.out_events.append(ev)
        else:
            i = self.cnt[eng]
            self.cnt[eng] += 1
            ep = i // self.EPOCH
            sk = (eng, ep)
            ev = (sk, i % self.EPOCH + 1)
            inc = 1
            self.last_ev[eng] = ev
            for e2 in range(ep):
                clk[(eng, e2)] = self.EPOCH
        clk[sk] = max(clk.get(sk, 0), ev[1])
        self.evclock[ev] = clk
        self.evidx[ev] = self.nev
        self.nev += 1
        self.semkeys.add(sk)
        self.ops[eng].append((waits, fn, ev, inc))
        for k in r:
            self.readers.setdefault(k, []).append(ev)
        for k in w:
            self.last_w[k] = ev
            self.readers[k] = []
        return ev

    def finish(self):
        cands = set(self.dma_events[-self.NDMA:]) | set(self.out_events)
        waits = self._resolve("sp", cands)
        self.ops["sp"].append((waits, None, None, 0))

    def emit(self, nc, stack):
        targets = set()
        for e in ENGS:
            for waits, fn, ev, inc in self.ops[e]:
                targets.update(waits)
        real = {}
        used = set()
        for e in ENGS:
            cnt = {}
            for waits, fn, ev, inc in self.ops[e]:
                if ev is None:
                    continue
                if ev[0][0] == "dma":
                    real[ev] = ev[1]
                    used.add(ev[0])
                elif ev in targets:
                    cnt[ev[0]] = cnt.get(ev[0], 0) + 1
                    real[ev] = cnt[ev[0]]
                    used.add(ev[0])
        sems = {}
        for sk in sorted(used, key=str):
            sems[sk] = stack.enter_context(nc.semaphore("s_" + "_".join(str(x) for x in sk)))
        block = stack.enter_context(nc.Block())
        prog = self

        def run(engname):
            def body(eng):
                for waits, fn, ev, inc in prog.ops[engname]:
                    if fn is None:
                        for w_ in waits:
                            eng.wait_ge(sems[w_[0]], real[w_])
                        continue
                    for w_ in waits[:-1]:
                        eng.wait_ge(sems[w_[0]], real[w_])
                    ins = fn(eng)
                    if waits:
                        ins._wait_ge(sems[waits[-1][0]], real[waits[-1]])
                    if ev in real:
                        ins.then_inc(sems[ev[0]], inc)
            return body

        block.tensor(run("pe"))
        block.scalar(run("act"))
        block.vector(run("dve"))
        block.gpsimd(run("pool"))
        block.sync(run("sp"))

    def act(self, out, in_, func, r, w, bias=0.0, scale=1.0, accum=None):
        if accum is None:
            return self.add("act", lambda e: e.activation(out=out, in_=in_, func=func, bias=bias, scale=scale), r, w)
        return self.add("act", lambda e: e.activation(out=out, in_=in_, func=func, bias=bias, scale=scale,
                                                      accum_out=accum), r, w)

    def ts(self, eng, out, in0, s1, s2, op0, op1, r, w):
        return self.add(eng, lambda e: e.tensor_scalar(out=out, in0=in0, scalar1=s1, scalar2=s2, op0=op0, op1=op1), r, w)

    def stt(self, out, in0, scalar, in1, op0, op1, r, w, accum=None):
        if accum is None:
            return self.add("dve", lambda e: e.scalar_tensor_tensor(out=out, in0=in0, scalar=scalar, in1=in1,
                                                                     op0=op0, op1=op1), r, w)
        return self.add("dve", lambda e: e.scalar_tensor_tensor(out=out, in0=in0, scalar=scalar, in1=in1,
                                                                 op0=op0, op1=op1, accum_out=accum), r, w)

    def tt(self, eng, out, in0, in1, op, r, w):
        return self.add(eng, lambda e: e.tensor_tensor(out=out, in0=in0, in1=in1, op=op), r, w)

    def cp(self, eng, out, in_, r, w):
        if eng == "act":
            return self.add("act", lambda e: e.activation(out=out, in_=in_, func=AF.Copy), r, w)
        return self.add(eng, lambda e: e.tensor_copy(out=out, in_=in_), r, w)

    def memset(self, eng, out, val, w):
        return self.add(eng, lambda e: e.memset(out, val), (), w)

    def mm(self, out, lhsT, rhs, start, stop, r, w):
        return self.add("pe", lambda e: e.matmul(out, lhsT=lhsT, rhs=rhs, start=start, stop=stop), r, w)

    def tr(self, out, in_, ident, r, w):
        return self.add("pe", lambda e: e.transpose(out, in_, ident), r, w)

    def dma(self, out, in_, r, w, is_out=False):
        return self.add("sp", lambda e: e.dma_start(out=out, in_=in_), r, w, dma=True, is_out=is_out)

    def scan(self, out, d0, d1, r, w):
        return self.add("dve", lambda e: e.tensor_tensor_scan(out=out, data0=d0, data1=d1, initial=0.0,
                                                               op0=ALU.mult, op1=ALU.add), r, w)


class Arena:
    def __init__(self, t, words):
        self.t = t
        self.words = words
        self.off = 0
        self.marks = []
        self.peak = 0

    def alloc(self, nwords):
        o = self.off
        self.off += nwords
        self.peak = max(self.peak, self.off)
        assert self.off <= self.words, f"SBUF arena overflow {self.off}>{self.words}"
        return o

    def f32(self, n, parts=128):
        o = self.alloc(n)
        return self.t[0:parts, o:o + n]

    def bf16(self, n, parts=128):
        assert n % 2 == 0
        o = self.alloc(n // 2)
        return self.t[0:parts, o:o + n // 2].bitcast(BF16)

    def mark(self):
        self.marks.append(self.off)

    def release(self):
        self.off = self.marks.pop()


def v3(ap, b):
    return ap.rearrange("p (a b) -> p a b", b=b)


def build_nc(phases="0GRF"):
    nc = bass.Bass("TRN2", target_bir_lowering=False)
    di = lambda n, s: nc.dram_tensor(n, list(s), F32, kind="ExternalInput").ap()
    do = lambda n, s: nc.dram_tensor(n, list(s), F32, kind="ExternalOutput").ap()
    x = di("x", (TA, D))
    w_in = di("w_in", (D, NIN))
    alpha_w2 = di("alpha_w2", (16, 512))
    w2a2 = di("w2a2", (128, 1024))
    w_up_gla = di("w_up_gla", (D, D))
    w_up_rwkv = di("w_up_rwkv", (D, D))
    w_out = di("w_out", (D, D))
    lngb = di("lngb", (128, 2048))
    sgla = di("sgla", (NS, 4, 128, 256))
    srwkv = di("srwkv", (NS, 16, 64, 64))
    sshift = di("sshift", (NS, 4224))
    cvec = di("cvec", (128, NCV))
    cmat = di("cmat", (128, NCM))
    y = do("y", (TA, D))
    gla_p = do("gla_p", (4, 128, 256))
    rwkv_p = do("rwkv_p", (16, 64, 64))
    shift_o = do("shift_o", (17, 4224))
    gla_s = do("gla_s", (NS, 4, 128, 256))
    rwkv_s = do("rwkv_s", (NS, 16, 64, 64))

    P = Prog()
    with contextlib.ExitStack() as st:
        WORDS = 53184
        sb = st.enter_context(nc.sbuf_tensor("arena", [128, WORDS], F32))
        ar = Arena(sb, WORDS)
        banks = [st.enter_context(nc.psum_tensor(f"ps{i}", [128, 512], F32)) for i in range(8)]
        bk = lambda i: f"bank{i}"

        CV = ar.f32(NCV)
        CM = ar.f32(NCM)
        P.dma(CV, cvec, [], ["CV"])
        P.dma(CM, cmat, [], ["CM"])
        ID = CM[:, CM_ID:CM_ID + 128]
        MASK5 = CM[:, CM_MASK5:CM_MASK5 + 320]
        MASKU = CM[:, CM_MASKU:CM_MASKU + 128]
        ISTK = CM[:, CM_ISTK:CM_ISTK + 64]
        BONES = CM[:, CM_BONES:CM_BONES + 128]
        BAVG = CM[:, CM_BAVG:CM_BAVG + 128]
        ONES = CM[:, CM_ONES:CM_ONES + 128]
        SMG = CM[:, CM_SMG:CM_SMG + 512]
        SMR = CM[:, CM_SMR:CM_SMR + 512]
        ID16 = CM[0:16, CM_ID16:CM_ID16 + 16]
        IDb = ar.bf16(128)
        ISTKb = ar.bf16(64)
        BONESb = ar.bf16(128)
        ONESb = ar.bf16(128)
        NAB = ar.f32(4)
        EPSC = ar.f32(2)
        P.cp("pool", IDb, ID, ["CM"], ["IDb"])
        P.cp("pool", ISTKb, ISTK, ["CM"], ["ISTKb"])
        P.cp("pool", BONESb, BONES, ["CM"], ["BONESb"])
        P.cp("pool", ONESb, ONES, ["CM"], ["ONESb"])
        P.ts("pool", NAB, CV[:, CV_AB:CV_AB + 4], -1.0, None, ALU.mult, ALU.bypass, ["CV"], ["NAB"])
        P.memset("pool", EPSC[:, 0:1], 1e-5, ["EPSC"])
        P.memset("pool", EPSC[:, 1:2], 64e-5, ["EPSC"])
        cvc = lambda base, j: CV[:, base + j:base + j + 1]

        W2A2 = ar.f32(1024)
        P.dma(W2A2, w2a2, [], ["W2A2"])

        xT = v3(ar.bf16(KC * TA), TA)
        OgT = v3(ar.bf16(KC * TA), TA)
        OrT = v3(ar.bf16(KC * TA), TA)

        WST = v3(ar.f32(4 * 512), 512)
        WBF = [v3(ar.bf16(KC * 512), 512) for _ in range(3)]
        wstate = {"n": 0, "pre": None}
        wqueue = []

        def _issue_load(srcs):
            en = P.enabled
            P.enabled = True
            par = wstate["n"] % 3
            wstate["n"] += 1
            key = f"WBF{par}"
            for half in range(2):
                off = 0
                for s_ in srcs:
                    n = s_.shape[1]
                    P.dma(WST[:, :, off:off + n],
                          s_.rearrange("(kc p) n -> p kc n", p=128)[:, 4 * half:4 * half + 4, :], [], ["WST"])
                    off += n
                for k2 in range(2):
                    P.cp("pool", WBF[par][:, 4 * half + 2 * k2:4 * half + 2 * k2 + 2, 0:off],
                         WST[:, 2 * k2:2 * k2 + 2, 0:off], ["WST"], [key])
            P.enabled = en
            return WBF[par], key

        def load_wgroup(srcs=None, defer=False):
            flush_prefetch()
            if wstate["pre"] is None:
                wstate["pre"] = _issue_load(wqueue.pop(0))
            cur = wstate["pre"]
            wstate["pre"] = None
            wstate["pend"] = bool(wqueue)
            if not defer:
                flush_prefetch()
            return cur

        def flush_prefetch():
            if wstate.get("pend"):
                wstate["pend"] = False
                wstate["pre"] = _issue_load(wqueue.pop(0))

        wqueue.append([w_in[:, GA:GA + 16]])
        for h_ in range(4):
            wqueue.append([w_in[:, GQ + h_ * 128:GQ + (h_ + 1) * 128], w_in[:, GK + h_ * 128:GK + (h_ + 1) * 128],
                           w_in[:, GV + h_ * 256:GV + (h_ + 1) * 256]])
            wqueue.append([w_in[:, GG + h_ * 256:GG + (h_ + 1) * 256]])
        wqueue.append([w_in[:, RW:RW + 128]])
        for p_ in range(8):
            wqueue.append([w_in[:, RR + p_ * 128:RR + (p_ + 1) * 128], w_in[:, RK + p_ * 128:RK + (p_ + 1) * 128],
                           w_in[:, RV + p_ * 128:RV + (p_ + 1) * 128], w_in[:, RG + p_ * 128:RG + (p_ + 1) * 128]])
        for dt_ in range(8):
            wqueue.append([w_in[:, G0 + dt_ * 128:G0 + (dt_ + 1) * 128],
                           w_in[:, G0 + 1024 + dt_ * 128:G0 + 1024 + (dt_ + 1) * 128],
                           w_up_gla[:, dt_ * 128:(dt_ + 1) * 128], w_up_rwkv[:, dt_ * 128:(dt_ + 1) * 128]])
        for g_ in range(2):
            wqueue.append([w_out[:, g_ * 512:(g_ + 1) * 512]])

        def proj_fm(wb, wkey, coff, ncols, s0, W, bank_i, c0=0):
            for kc in range(KC):
                P.mm(banks[bank_i][0:ncols, c0:c0 + W], wb[:, kc, coff:coff + ncols], xT[:, kc, s0:s0 + W],
                     kc == 0, kc == KC - 1, [wkey, "xT"], [bk(bank_i)])

        def proj_tm(wb, wkey, coff, ncols, t0, M, bank_i, c0=0):
            for kc in range(KC):
                P.mm(banks[bank_i][0:M, c0:c0 + ncols], xT[:, kc, t0:t0 + M], wb[:, kc, coff:coff + ncols],
                     kc == 0, kc == KC - 1, [wkey, "xT"], [bk(bank_i)])

        wstate["pre"] = _issue_load(wqueue.pop(0))
        P.enabled = "0" in phases
        ar.mark()
        XS = [ar.f32(1024) for _ in range(2)]
        for tt in range(17):
            rows = 128 if tt < 16 else NS
            xs = XS[tt % 2]
            xk = f"XS{tt % 2}"
            P.dma(xs[0:rows, :], x[tt * 128:tt * 128 + rows, :], [], [xk])
            for half in range(2):
                b = banks[half]
                for j in range(4):
                    kc = half * 4 + j
                    P.tr(b[:, j * 128:j * 128 + rows], xs[0:rows, kc * 128:(kc + 1) * 128], ID[0:rows, 0:rows],
                         [xk, "CM"], [bk(half)])
                src = v3(b[:, 0:512], 128)[:, :, 0:rows]
                dst = xT[:, half * 4:half * 4 + 4, tt * 128:tt * 128 + rows]
                P.cp("act" if half == 0 else "dve", dst, src, [bk(half)], ["xT"])
        ar.release()
        P.fence()

        P.enabled = "G" in phases
        ar.mark()
        AW2 = ar.f32(512, parts=16)
        P.dma(AW2, alpha_w2, [], ["AW2"])
        ALR = ar.f32(TA, parts=16)
        SP = ar.f32(512); CSP = ar.f32(512); EB = ar.f32(512); EINV = ar.f32(512)
        QT = ar.bf16(512); KT = ar.bf16(512); QS = ar.f32(16)
        KH = ar.bf16(128); KHT = ar.bf16(128)
        VTK = v3(ar.bf16(4 * 256), 256)
        ATS = ar.bf16(128)
        SG = [ar.f32(256) for _ in range(2)]
        SGB = ar.bf16(256)
        GS = v3(ar.bf16(2 * 512), 512)
        SQ = v3(ar.bf16(2 * 512), 512)
        RSTD = ar.f32(512)
        ON = ar.f32(512)
        KTOK = ar.f32(128, parts=16); VTOK = ar.f32(256, parts=16)
        KM = v3(ar.f32(16 * 128, parts=16), 128)
        SS = v3(ar.f32(4 * 256), 256)
        SN = v3(ar.f32(4 * 256), 256)

        wb, wk = load_wgroup([w_in[:, GA:GA + 16]])
        for (s0, W) in SEGS:
            proj_fm(wb, wk, 0, 16, s0, W, 0)
            P.cp("act", ALR[:, s0:s0 + W], banks[0][0:16, 0:W], [bk(0)], ["ALR"])

        for h in range(4):
            wqkv, kqkv = load_wgroup([w_in[:, GQ + h * 128:GQ + (h + 1) * 128],
                                      w_in[:, GK + h * 128:GK + (h + 1) * 128],
                                      w_in[:, GV + h * 256:GV + (h + 1) * 256]])
            wg, kg = load_wgroup(defer=True)
            cur = 0
            P.memset("pool", SG[0], 0.0, ["SG0"])
            P.memset("pool", SGB, 0.0, ["SGB"])
            for si, (s0, W) in enumerate(SEGS):
                samp = si == 4
                if si == 1:
                    flush_prefetch()
                P.mm(banks[2][:, 0:W], AW2[:, h * 128:(h + 1) * 128], ALR[:, s0:s0 + W], True, True,
                     ["AW2", "ALR"], [bk(2)])
                P.act(SP[:, 0:W], banks[2][:, 0:W], AF.Exp, [bk(2), "NAB"], ["SP"], bias=NAB[:, h:h + 1], scale=-1.0)
                P.act(SP[:, 0:W], SP[:, 0:W], AF.Ln, ["SP"], ["SP"], bias=1.0)
                if samp:
                    csp = SP
                else:
                    P.scan(CSP[:, 0:W], SMG[:, 0:W], SP[:, 0:W], ["CM", "SP"], ["CSP"])
                    csp = CSP
                ck = "SP" if samp else "CSP"
                P.act(EB[:, 0:W], csp[:, 0:W], AF.Exp, [ck], ["EB"], scale=-1.0 / 16.0)
                P.act(EINV[:, 0:W], csp[:, 0:W], AF.Exp, [ck], ["EINV"], scale=1.0 / 16.0)
                proj_fm(wqkv, kqkv, 0, 128, s0, W, 0)
                if samp:
                    P.ts("dve", QS[:, 0:W], banks[0][:, 0:W], 128.0 ** -0.5, None, ALU.mult, ALU.bypass,
                         [bk(0)], ["QS"])
                else:
                    P.stt(QT[:, 0:W], banks[0][:, 0:W], 128.0 ** -0.5, EB[:, 0:W], ALU.mult, ALU.mult,
                          [bk(0), "EB"], ["QT"])
                    proj_fm(wqkv, kqkv, 128, 128, s0, W, 1)
                    P.tt("dve", KT[:, 0:W], banks[1][:, 0:W], EINV[:, 0:W], ALU.mult, [bk(1), "EINV"], ["KT"])
                if not samp:
                    for t4 in range(4):
                        bi = t4 % 2
                        proj_tm(wqkv, kqkv, 256, 256, s0 + t4 * 128, 128, bi)
                        P.cp("act", VTK[:, t4, :], banks[bi][:, 0:256], [bk(bi)], ["VTK"])
                    for vh in range(2):
                        proj_fm(wg, kg, vh * 128, 128, s0, W, vh)
                        P.act(GS[:, vh, 0:W], banks[vh][:, 0:W], AF.Silu, [bk(vh)], ["GS"])
                    for c in range(4):
                        cs = slice(c * 128, (c + 1) * 128)
                        ecol = EB[:, c * 128 + 127:c * 128 + 128]
                        first = (si == 0 and c == 0)
                        P.mm(banks[4][:, 0:128], KT[:, cs], QT[:, cs], True, True, ["KT", "QT"], [bk(4)])
                        P.tt("dve", ATS, banks[4][:, 0:128], MASKU, ALU.mult, [bk(4), "CM"], ["ATS"])
                        P.ts("pool", KH, KT[:, cs], ecol, None, ALU.mult, ALU.bypass, ["KT", "EB"], ["KH"])
                        b4b = banks[4].bitcast(BF16)
                        P.tr(b4b[:, 512:640], KH, IDb, ["KH", "IDb"], [bk(4)])
                        P.cp("act", KHT, b4b[:, 512:640], [bk(4)], ["KHT"])
                        for vh in range(2):
                            vs = slice(vh * 128, (vh + 1) * 128)
                            P.mm(banks[5][:, vh * 128:(vh + 1) * 128], VTK[:, c, vs], ATS, True, first,
                                 ["VTK", "ATS"], [bk(5)])
                            if not first:
                                P.mm(banks[5][:, vh * 128:(vh + 1) * 128], SGB[:, vs], QT[:, cs], False, True,
                                     ["SGB", "QT"], [bk(5)])
                        P.cp("act", OgT[:, 2 * h:2 * h + 2, s0 + c * 128:s0 + (c + 1) * 128],
                             v3(banks[5][:, 0:256], 128), [bk(5)], ["OgT"])
                        P.mm(banks[6][:, 0:256], KHT, VTK[:, c, :], True, True, ["KHT", "VTK"], [bk(6)])
                        nxt = 1 - cur
                        P.stt(SG[nxt], SG[cur], ecol, banks[6][:, 0:256], ALU.mult, ALU.add,
                              [f"SG{cur}", "EB", bk(6)], [f"SG{nxt}"])
                        P.cp("act", SGB, SG[nxt], [f"SG{nxt}"], ["SGB"])
                        cur = nxt
                    if si == 3:
                        P.dma(gla_p[h], SG[cur], [f"SG{cur}"], [], is_out=True)
                else:
                    for vh in range(2):
                        proj_fm(wg, kg, vh * 128, 128, s0, W, vh)
                        P.act(GS[:, vh, 0:W], banks[vh][:, 0:W], AF.Silu, [bk(vh)], ["GS"])
                    proj_tm(wqkv, kqkv, 128, 128, T, NS, 4)
                    P.cp("act", KTOK, banks[4][0:16, 0:128], [bk(4)], ["KTOK"])
                    proj_tm(wqkv, kqkv, 256, 256, T, NS, 4)
                    P.cp("act", VTOK, banks[4][0:16, 0:256], [bk(4)], ["VTOK"])
                    P.add("dve", lambda e: e.tensor_tensor(
                        out=KM, in0=KTOK.unsqueeze(1).to_broadcast([16, 16, 128]),
                        in1=ID16.unsqueeze(2).to_broadcast([16, 16, 128]), op=ALU.mult),
                        ["KTOK", "CM"], ["KM"])
                    for sg in range(4):
                        P.dma(SS, sgla[sg * 4:(sg + 1) * 4, h].rearrange("s d v -> d s v"), [], ["SS"])
                        for j in range(4):
                            s = sg * 4 + j
                            P.add("pe", lambda e, s=s: e.matmul(banks[6][:, 0:256], lhsT=KM[:, s, :], rhs=VTOK,
                                                                start=True, stop=True), ["KM", "VTOK"], [bk(6)])
                            P.stt(SN[:, j, :], SS[:, j, :], EB[:, s:s + 1], banks[6][:, 0:256], ALU.mult, ALU.add,
                                  ["SS", "EB", bk(6)], ["SN"])
                            for vh in range(2):
                                P.mm(banks[5][:, vh * 16 + s:vh * 16 + s + 1], SN[:, j, vh * 128:(vh + 1) * 128],
                                     QS[:, s:s + 1], True, True, ["SN", "QS"], [bk(5)])
                        P.dma(gla_s[sg * 4:(sg + 1) * 4, h].rearrange("s d v -> d s v"), SN, ["SN"], [], is_out=True)
                    P.cp("act", OgT[:, 2 * h:2 * h + 2, T:TA], v3(banks[5][:, 0:32], 16), [bk(5)], ["OgT"])
                og = OgT[:, 2 * h:2 * h + 2, s0:s0 + W]
                P.act(SQ[:, :, 0:W], og, AF.Square, ["OgT"], ["SQ"])
                for vh in range(2):
                    P.mm(banks[3][:, 0:W], ONESb, SQ[:, vh, 0:W], vh == 0, vh == 1, ["ONESb", "SQ"], [bk(3)])
                P.act(RSTD[:, 0:W], banks[3][:, 0:W], AF.Ln, [bk(3), "EPSC"], ["RSTD"], bias=EPSC[:, 0:1], scale=1.0 / 256.0)
                P.act(RSTD[:, 0:W], RSTD[:, 0:W], AF.Exp, ["RSTD"], ["RSTD"], scale=-0.5)
                for vh in range(2):
                    P.stt(ON[:, 0:W], OgT[:, 2 * h + vh, s0:s0 + W], cvc(CV_GNW, vh), RSTD[:, 0:W],
                          ALU.mult, ALU.mult, ["OgT", "CV", "RSTD"], ["ON"])
                    P.tt("pool", OgT[:, 2 * h + vh, s0:s0 + W], ON[:, 0:W], GS[:, vh, 0:W], ALU.mult,
                         ["ON", "GS"], ["OgT"])
        ar.release()
        P.fence()

        P.enabled = "R" in phases
        ar.mark()
        SHT = v3(ar.f32(33 * 16), 16)
        WSTf = WST.rearrange("p a b -> p (a b)")
        for tb in (0, 11, 22):
            P.dma(WSTf[0:16, 0:11 * 128], sshift[:, tb * 128:(tb + 11) * 128], [], ["WST"])
            for j in range(11):
                P.tr(banks[2][:, j * 16:(j + 1) * 16], WSTf[0:16, j * 128:(j + 1) * 128], ID[0:16, 0:16],
                     ["WST", "CM"], [bk(2)])
            P.cp("act", SHT[:, tb:tb + 11, :], v3(banks[2][:, 0:176], 16), [bk(2)], ["SHT"])

        LORA = ar.f32(TA)
        RAW = [ar.f32(513) for _ in range(4)]
        DQ = ar.f32(512)
        BB = ar.f32(9 * 512)
        Bp = lambda i: BB[:, i * 512:(i + 1) * 512]
        Rl, kRl = Bp(0), "B0"
        Kl, kKl = Bp(1), "B1"
        Gl, kGl = Bp(2), "B2"
        SIG, kSIG = Bp(3), "B3"
        Aa, kAa = Bp(4), "B4"
        KKr, kKKr = Bp(5), "B5"
        RN, kRN = Bp(6), "B6"
        KKn, kKKn = Bp(7), "B7"
        T1, kT1 = Bp(2), "B2"
        Kp, kKp = Bp(8), "B8"
        RKf, kRKf = Bp(5), "B5"
        BA, kBA = Bp(1), "B1"
        CS, kCS = Bp(4), "B4"
        GAM, kGAM = Bp(5), "B5"
        GIN, kGIN = Bp(6), "B6"
        CSX, kCSX = Bp(2), "B2"
        GEX, kGEX = Bp(3), "B3"
        YS, kYS = Bp(0), "B0"
        YSQ, kYSQ = Bp(1), "B1"
        MU2, kMU2 = Bp(2), "B2"
        VAR, kVAR = Bp(3), "B3"
        YD, kYD = Bp(4), "B4"
        SHO, kSHO = Bp(0)[0:17, :], "B0"
        SR, kSR = v3(BB[:, 5 * 512:7 * 512], 64), ["B5", "B6"]
        SRN, kSRN = v3(BB[:, 2 * 512:4 * 512], 64), ["B2", "B3"]
        SQb = ar.bf16(512)
        OUT = []
        for o_ in range(2):
            arkb = ar.bf16(2048)
            OUT.append(dict(
                ARKB=arkb, AR=arkb[:, 0:1024], AR4=arkb[:, 0:1024].rearrange("p (c q t) -> p c q t", q=2, t=64),
                Kt=arkb[:, 1024:1536], Bt=arkb[:, 1536:2048],
                Vb=ar.bf16(512), Gs=ar.bf16(512), BON=ar.bf16(512), GAMC=ar.f32(8),
                kAR=f"AR{o_}", kKt=f"Kt{o_}", kBt=f"Bt{o_}", kVb=f"Vb{o_}", kGs=f"Gs{o_}", kBON=f"BON{o_}",
                kGAMC=f"GAMC{o_}"))
        DG = OUT[0]["ARKB"].bitcast(F32)[:, 0:640].rearrange("p (s q k) -> p s q k", q=5, k=64)
        kDG = ["AR0", "Kt0"]
        Vf = ar.f32(16)
        CB = [(ar.bf16(320), ar.bf16(320), [ar.bf16(256) for _ in range(2)], ar.bf16(64), ar.f32(64), ar.bf16(64))
              for _ in range(3)]
        Hb = ar.bf16(64); Hf = ar.f32(64)
        XQ = v3(ar.f32(16 * 5), 5)
        TJ = ar.f32(64); T2 = ar.f32(64); SA = ar.f32(2)
        B4K = [bk(4), "b4tm", "b4s"]
        B5K = [bk(5), "b5z", "b5g", "b5r", "b5zv"]

        def lerp(q, ftile, src_bank, W, samp, out_ap, okey):
            raw = RAW[q]
            rk = f"RAW{q}"
            P.cp("act", raw[:, 1:1 + W], banks[src_bank][:, 0:W], [bk(src_bank)], [rk])
            if samp:
                P.tt("pool", DQ[:, 0:W], SHT[:, ftile, :], raw[:, 1:1 + W], ALU.subtract, ["SHT", rk], ["DQ"])
            else:
                P.tt("pool", DQ[:, 0:W], raw[:, 0:W], raw[:, 1:1 + W], ALU.subtract, [rk], ["DQ"])
            P.stt(out_ap, DQ[:, 0:W], cvc(CV_MU, ftile), raw[:, 1:1 + W], ALU.mult, ALU.add, ["DQ", "CV", rk], [okey])
            if not samp:
                P.cp("pool", raw[:, 0:1], raw[:, W:W + 1], [rk], [rk])

        wl, kl = load_wgroup([w_in[:, RW:RW + 128]])
        P.memset("pool", RAW[0][:, 0:1], 0.0, ["RAW0"])
        for si, (s0, W) in enumerate(SEGS):
            proj_fm(wl, kl, 0, 128, s0, W, 0)
            lerp(0, 32, 0, W, si == 4, LORA[:, s0:s0 + W], "LORA")
        P.act(LORA[0:64, :], LORA[0:64, :], AF.Tanh, ["LORA"], ["LORA"])
        proj_tm(wl, kl, 0, 128, T - 1, 17, 2)
        P.cp("act", SHO[:, 0:128], banks[2][0:17, 0:128], [bk(2)], [kSHO])
        P.dma(shift_o[:, 4096:4224], SHO[:, 0:128], [kSHO], [], is_out=True)

        def gn_post(p, s0, W, O):
            P.act(YSQ[:, 0:W], YS[:, 0:W], AF.Square, [kYS], [kYSQ])
            P.mm(banks[2][:, 0:W], BAVG, YS[:, 0:W], True, True, ["CM", kYS], [bk(2)])
            P.mm(banks[3][:, 0:W], BAVG, YSQ[:, 0:W], True, True, ["CM", kYSQ], [bk(3)])
            P.act(MU2[:, 0:W], banks[2][:, 0:W], AF.Square, [bk(2)], [kMU2])
            P.tt("dve", VAR[:, 0:W], banks[3][:, 0:W], MU2[:, 0:W], ALU.subtract, [bk(3), kMU2], [kVAR])
            P.act(VAR[:, 0:W], VAR[:, 0:W], AF.Ln, [kVAR], [kVAR], bias=EPSC[:, 1:2])
            P.act(VAR[:, 0:W], VAR[:, 0:W], AF.Exp, [kVAR], [kVAR], scale=-0.5)
            P.tt("dve", YD[:, 0:W], YS[:, 0:W], banks[2][:, 0:W], ALU.subtract, [kYS, bk(2)], [kYD])
            P.tt("pool", YD[:, 0:W], YD[:, 0:W], VAR[:, 0:W], ALU.mult, [kYD, kVAR], [kYD])
            P.ts("dve", YD[:, 0:W], YD[:, 0:W], cvc(CV_LW, p), cvc(CV_LB, p), ALU.mult, ALU.add, [kYD, "CV"], [kYD])
            P.tt("pool", YD[:, 0:W], YD[:, 0:W], O["BON"][:, 0:W], ALU.add, [kYD, O["kBON"]], [kYD])
            P.tt("dve", OrT[:, p, s0:s0 + W], YD[:, 0:W], O["Gs"][:, 0:W], ALU.mult, [kYD, O["kGs"]], ["OrT"])

        def mm2(out, lhsT, rhs, n0, n1, l0, l1, r0, r1, start, stop, r, w):
            for hh in range(2):
                ps = slice(hh * 64, (hh + 1) * 64)
                P.mm(out[ps, n0:n1], lhsT[ps, l0:l1], rhs[ps, r0:r1], start, stop, r, w)

        if RLEVEL < 2:
            P.enabled = False
        P.counting = True
        def elem_gen(p, wp, kp, si, O):
            s0, W = SEGS[si]
            samp = si == 4
            Vb, Gs, BON = O["Vb"], O["Gs"], O["BON"]
            AR4, Kt, Bt = O["AR4"], O["Kt"], O["Bt"]
            proj_fm(wp, kp, 0, 128, s0, W, 2)
            lerp(0, p, 2, W, samp, Rl[:, 0:W], kRl)
            yield
            proj_fm(wp, kp, 128, 128, s0, W, 2)
            lerp(1, 8 + p, 2, W, samp, Kl[:, 0:W], kKl)
            yield
            proj_fm(wp, kp, 256, 128, s0, W, 2)
            lerp(2, 16 + p, 2, W, samp, Vb[:, 0:W], O["kVb"])
            if samp:
                P.stt(Vf[:, 0:W], DQ[:, 0:W], cvc(CV_MU, 16 + p), RAW[2][:, 1:1 + W], ALU.mult, ALU.add,
                      ["DQ", "CV", "RAW2"], ["Vf"])
            yield
            proj_fm(wp, kp, 384, 128, s0, W, 2)
            lerp(3, 24 + p, 2, W, samp, Gl[:, 0:W], kGl)
            P.act(Gs[:, 0:W], Gl[:, 0:W], AF.Silu, [kGl], [O["kGs"]])
            yield
            P.mm(banks[2][:, 0:W], W2A2[0:64, p * 128:(p + 1) * 128], LORA[0:64, s0:s0 + W], True, True,
                 ["W2A2", "LORA"], [bk(2)])
            P.act(SIG[:, 0:W], banks[2][:, 0:W], AF.Sigmoid, [bk(2), "CV"], [kSIG], bias=cvc(CV_W0, p))
            yield
            P.mm(banks[2][:, 0:W], W2A2[64:128, p * 128:(p + 1) * 128], LORA[64:128, s0:s0 + W], True, True,
                 ["W2A2", "LORA"], [bk(2)])
            P.act(Aa[:, 0:W], banks[2][:, 0:W], AF.Sigmoid, [bk(2), "CV"], [kAa], bias=cvc(CV_A0, p))
            yield
            P.ts("pool", KKr[:, 0:W], Kl[:, 0:W], cvc(CV_KK, p), None, ALU.mult, ALU.bypass, [kKl, "CV"], [kKKr])
            P.act(SQb[:, 0:W], KKr[:, 0:W], AF.Square, [kKKr], ["SQb"])
            P.mm(banks[2][:, 0:W], BONESb, SQb[:, 0:W], True, True, ["BONESb", "SQb"], [bk(2)])
            yield
            P.act(RN[:, 0:W], banks[2][:, 0:W], AF.Ln, [bk(2)], [kRN])
            P.act(RN[:, 0:W], RN[:, 0:W], AF.Exp, [kRN], [kRN], scale=-0.5)
            yield
            P.tt("pool", KKn[:, 0:W], KKr[:, 0:W], RN[:, 0:W], ALU.mult, [kKKr, kRN], [kKKn])
            P.ts("dve", T1[:, 0:W], Aa[:, 0:W], -1.0, cvc(CV_KA, p), ALU.add, ALU.mult, [kAa, "CV"], [kT1])
            P.stt(Kp[:, 0:W], T1[:, 0:W], 1.0, Kl[:, 0:W], ALU.add, ALU.mult, [kT1, kKl], [kKp])
            yield
            P.tt("pool", RKf[:, 0:W], Rl[:, 0:W], Kp[:, 0:W], ALU.mult, [kRl, kKp], [kRKf])
            P.ts("pool", RKf[:, 0:W], RKf[:, 0:W], cvc(CV_RK, p), None, ALU.mult, ALU.bypass, [kRKf, "CV"], [kRKf])
            yield
            P.mm(banks[2][:, 0:W], BONES, RKf[:, 0:W], True, True, ["CM", kRKf], [bk(2)])
            P.tt("dve", BON[:, 0:W], banks[2][:, 0:W], Vb[:, 0:W], ALU.mult, [bk(2), O["kVb"]], [O["kBON"]])
            P.tt("pool", BA[:, 0:W], KKn[:, 0:W], Aa[:, 0:W], ALU.mult, [kKKn, kAa], [kBA])
            yield
            if not samp:
                P.scan(CS[:, 0:W], SMR[:, 0:W], SIG[:, 0:W], ["CM", kSIG], [kCS])
                P.act(GAM[:, 0:W], CS[:, 0:W], AF.Exp, [kCS], [kGAM], scale=-DECAY)
                P.act(GIN[:, 0:W], CS[:, 0:W], AF.Exp, [kCS], [kGIN], scale=DECAY)
                yield
                P.tt("pool", CSX[:, 0:W], CS[:, 0:W], SIG[:, 0:W], ALU.subtract, [kCS, kSIG], [kCSX])
                P.act(GEX[:, 0:W], CSX[:, 0:W], AF.Exp, [kCSX], [kGEX], scale=-DECAY)
                P.cp("pool", O["GAMC"], v3(GAM[:, 0:W], 64)[:, :, 63], [kGAM], [O["kGAMC"]])
                yield
                v64 = lambda a_: v3(a_[:, 0:W], 64)
                P.stt(AR4[:, :, 0, :], v64(KKn), -1.0, v64(GEX), ALU.mult, ALU.mult, [kKKn, kGEX], [O["kAR"]])
                P.tt("dve", AR4[:, :, 1, :], v64(Rl), v64(GAM), ALU.mult, [kRl, kGAM], [O["kAR"]])
                yield
                P.tt("pool", Kt[:, 0:W], Kp[:, 0:W], GIN[:, 0:W], ALU.mult, [kKp, kGIN], [O["kKt"]])
                P.tt("pool", Bt[:, 0:W], BA[:, 0:W], GIN[:, 0:W], ALU.mult, [kBA, kGIN], [O["kBt"]])
            else:
                P.ts("pool", XQ[:, :, 0], KKn[:, 0:W], -1.0, None, ALU.mult, ALU.bypass, [kKKn], ["XQ"])
                P.act(XQ[:, :, 1], SIG[:, 0:W], AF.Exp, [kSIG], ["XQ"], scale=-DECAY)
                P.cp("pool", XQ[:, :, 2], BA[:, 0:W], [kBA], ["XQ"])
                P.cp("pool", XQ[:, :, 3], Kp[:, 0:W], [kKp], ["XQ"])
                P.cp("pool", XQ[:, :, 4], Rl[:, 0:W], [kRl], ["XQ"])

        def chunk_gen(c, par, si, O):
            tb, db = [(4, 5), (6, 7), (0, 1)][par]
            bT, bD = banks[tb], banks[db]
            bTb = bT.bitcast(BF16)
            kT, kD = bk(tb), bk(db)
            TMp, SCp, ZPQp, GA1p, GVGp, RHp = CB[par]
            kTM, kSC, kGA1, kGVG, kRH = f"TM{par}", f"SC{par}", f"GA1{par}", f"GVG{par}", f"RH{par}"
            AR, AR4, Kt, Bt, Vb = O["AR"], O["AR4"], O["Kt"], O["Bt"], O["Vb"]
            kAR, kKt, kBt, kVb = O["kAR"], O["kKt"], O["kBt"], O["kVb"]
            cs = slice(c * 64, (c + 1) * 64)
            last = (si == 3 and c == 7)
            arc = AR[:, c * 128:(c + 1) * 128]
            for hh in range(2):
                ps = slice(hh * 64, (hh + 1) * 64)
                idb = IDb[ps, ps]
                P.tr(bTb[ps, 64:128], AR4[ps, c, 0, :], idb, [kAR, "IDb"], [kT])
                P.tr(bTb[ps, 128:192], Bt[ps, cs], idb, [kBt, "IDb"], [kT])
                P.tr(bTb[ps, 192:256], Kt[ps, cs], idb, [kKt, "IDb"], [kT])
                P.tr(bTb[ps, 256:320], Vb[ps, cs], idb, [kVb, "IDb"], [kT])
            P.cp("act", TMp[:, 64:320], bTb[:, 64:320], [kT], [kTM])
            mm2(bD, Bt, arc, 0, 128, c * 64, (c + 1) * 64, 0, 128, True, True, [kBt, kAR], [kD])
            mm2(bD, Kt, arc, 128, 256, c * 64, (c + 1) * 64, 0, 128, True, True, [kKt, kAR], [kD])
            mm2(bD, arc, Bt, 256, 320, 0, 64, c * 64, (c + 1) * 64, True, True, [kBt, kAR], [kD])
            P.tt("dve", SCp, bD[:, 0:320], MASK5, ALU.mult, [kD, "CM"], [kSC])
            yield
            mm2(bD, SCp, TMp, 448, 512, 128, 192, 256, 320, True, True, [kSC, kTM], [kD])
            P.cp("act", TMp[:, 0:64], bD[:, 448:512], [kD], [kTM])
            yield
            zsrc, zk = TMp, kTM
            psrc, pk, pc = SCp, kSC, 0
            qsrc, qk, qc = SCp, kSC, 256
            for lvl in range(6):
                mm2(bD, psrc, zsrc, 0, 128, pc, pc + 64, 0, 128, True, True, [pk, zk], [kD])
                if lvl < 5:
                    mm2(bT, qsrc, psrc, 0, 64, qc, qc + 64, pc, pc + 64, True, True, [pk, qk], [kT])
                    mm2(bT, psrc, qsrc, 64, 128, pc, pc + 64, qc, qc + 64, True, True, [pk, qk], [kT])
                dst = ZPQp[lvl % 2]
                dzk = f"Z{par}{lvl % 2}"
                dpk = f"PQ{par}{lvl % 2}"
                P.tt("dve", dst[:, 0:128], bD[:, 0:128], zsrc[:, 0:128], ALU.add, [kD, zk], [dzk])
                if lvl < 5:
                    P.cp("act", dst[:, 128:256], bT[:, 0:128], [kT], [dpk])
                zsrc, zk = dst, dzk
                psrc, pk, pc = dst, dpk, 128
                qsrc, qk, qc = dst, dpk, 192
                yield
            Wm, wk_ = zsrc, zk
            gam_col = O["GAMC"][:, c:c + 1]
            kGC = O["kGAMC"]
            mm2(bD, Wm, TMp, 256, 320, 64, 128, 128, 192, True, True, [wk_, kTM], [kD])
            mm2(bD, TMp, Wm, 320, 384, 128, 192, 0, 64, True, False, [wk_, kTM], [kD])
            mm2(bD, TMp, TMp, 320, 384, 192, 256, 256, 320, False, True, [kTM], [kD])
            mm2(bT, ISTKb, arc, 384, 448, 0, 64, 64, 128, True, False, ["ISTKb", kAR], [kT])
            mm2(bT, Wm, SCp, 384, 448, 64, 128, 64, 128, False, True, [wk_, kSC], [kT])
            P.tt("dve", GA1p, bD[:, 256:320], ISTK, ALU.add, [kD, "CM"], [kGA1])
            P.ts("dve", GVGp, bD[:, 320:384], gam_col, None, ALU.mult, ALU.bypass, [kD, kGC], [kGVG])
            P.cp("act", RHp, bT[:, 384:448], [kT], [kRH])
            yield
            b3 = banks[3]
            mm2(b3, Wm, SCp, c * 64, (c + 1) * 64, 0, 64, 64, 128, True, False, [wk_, kSC], [bk(3)])
            mm2(b3, TMp, SCp, c * 64, (c + 1) * 64, 256, 320, 192, 256, False, False, [kTM, kSC], [bk(3)])
            mm2(b3, Hb, RHp, c * 64, (c + 1) * 64, 0, 64, 0, 64, False, True, ["Hb", kRH], [bk(3)])
            mm2(banks[2], GA1p, Hb, 0, 64, 0, 64, 0, 64, True, True, [kGA1, "Hb"], [bk(2)])
            if last:
                P.stt(Hf, banks[2][:, 0:64], gam_col, GVGp, ALU.mult, ALU.add, [bk(2), kGC, kGVG], ["Hf"])
            P.stt(Hb, banks[2][:, 0:64], gam_col, GVGp, ALU.mult, ALU.add, [bk(2), kGC, kGVG], ["Hb"])

        def run_overlapped(chunk_args, extra):
            active = []
            nxt = 0
            free_par = [0, 1, 2]
            since = 99
            ex = extra
            while nxt < len(chunk_args) or active or ex is not None:
                if nxt < len(chunk_args) and free_par and (since >= 3 or not active):
                    pr_ = free_par.pop(0)
                    c_, si_, O_ = chunk_args[nxt]
                    active.append((chunk_gen(c_, pr_, si_, O_), pr_))
                    nxt += 1
                    since = 0
                since += 1
                for (g_, pr_) in list(active):
                    try:
                        next(g_)
                    except StopIteration:
                        active.remove((g_, pr_))
                        free_par.append(pr_)
                if ex is not None:
                    try:
                        next(ex)
                    except StopIteration:
                        ex = None

        for p in range(8):
            wp, kp = load_wgroup(defer=True)
            for q in range(4):
                P.memset("pool", RAW[q][:, 0:1], 0.0, [f"RAW{q}"])
            P.memset("pool", Hb, 0.0, ["Hb"])
            proj_tm(wp, kp, 0, 512, T - 1, 17, 2)
            P.cp("act", SHO, banks[2][0:17, 0:512], [bk(2)], [kSHO])
            P.dma(shift_o[:, 0:4096].rearrange("t (q f) -> t q f", q=4)[:, :, p * 128:(p + 1) * 128],
                  v3(SHO, 128), [kSHO], [], is_out=True)
            run_overlapped([], elem_gen(p, wp, kp, 0, OUT[0]))
            flush_prefetch()
            for si in range(4):
                s0, W = SEGS[si]
                O = OUT[si % 2]
                On = OUT[(si + 1) % 2]
                run_overlapped([(c, si, O) for c in range(8)], elem_gen(p, wp, kp, si + 1, On))
                P.cp("act", YS[:, 0:W], banks[3][:, 0:W], [bk(3)], [kYS])
                if si == 3:
                    P.tr(banks[2][0:64, 64:192], Hf, ID, ["Hf", "CM"], [bk(2)])
                    P.cp("act", T2[0:64, :], banks[2][0:64, 64:128], [bk(2)], ["T2"])
                    P.cp("act", TJ[0:64, :], banks[2][0:64, 128:192], [bk(2)], ["TJ"])
                    P.dma(rwkv_p[2 * p], T2[0:64, :], ["T2"], [], is_out=True)
                    P.dma(rwkv_p[2 * p + 1], TJ[0:64, :], ["TJ"], [], is_out=True)
                gn_post(p, s0, W, O)
            s0, W = SEGS[4]
            O = OUT[0]
            P.dma(SR, srwkv[:, 2 * p:2 * p + 2].rearrange("s h v k -> (h v) s k"), [], kSR)
            for s2 in range(8):
                hs = slice(s2 * 2, s2 * 2 + 2)
                P.add("dve", lambda e, hs=hs: e.tensor_tensor(
                    out=DG, in0=ISTK.unsqueeze(1).unsqueeze(1).to_broadcast([128, 2, 5, 64]),
                    in1=XQ[:, hs, :].unsqueeze(3).to_broadcast([128, 2, 5, 64]), op=ALU.mult),
                    ["CM", "XQ"], kDG)
                for j in range(2):
                    s = s2 * 2 + j
                    bi = 4 + j
                    bb = banks[bi]
                    dgs = DG[:, j].rearrange("p q k -> p (q k)")
                    for hh in range(2):
                        ps = slice(hh * 64, (hh + 1) * 64)
                        P.mm(bb[ps, 0:320], ONES[ps, 0:64], dgs[ps, :], True, True, ["CM"] + kDG, [bk(bi)])
                    P.stt(TJ, SR[:, s, :], 1.0, bb[:, 0:64], ALU.mult, ALU.mult, kSR + [bk(bi)], ["TJ", "SA"],
                          accum=SA[:, 0:1])
                    P.tt("dve", T2, SR[:, s, :], bb[:, 64:128], ALU.mult, kSR + [bk(bi)], ["T2"])
                    P.stt(T2, bb[:, 128:192], SA[:, 0:1], T2, ALU.mult, ALU.add, [bk(bi), "SA", "T2"], ["T2"])
                    P.stt(SRN[:, s, :], bb[:, 192:256], Vf[:, s:s + 1], T2, ALU.mult, ALU.add,
                          [bk(bi), "Vf", "T2"], kSRN)
                    P.stt(TJ, SRN[:, s, :], 1.0, bb[:, 256:320], ALU.mult, ALU.mult, kSRN + [bk(bi)],
                          ["TJ", kYS], accum=YS[:, s:s + 1])
            P.dma(rwkv_s[:, 2 * p:2 * p + 2].rearrange("s h v k -> (h v) s k"), SRN, kSRN, [], is_out=True)
            gn_post(p, s0, W, O)
        ar.release()
        P.fence()

        P.enabled = "F" in phases
        ar.mark()
        MT = v3(ar.bf16(KC * TA), TA)
        FT = [(ar.bf16(512), ar.bf16(512), ar.f32(512)) for _ in range(2)]
        fcnt = 0
        for dt in range(8):
            wf, kf = load_wgroup(defer=True)
            for (s0, W) in SEGS:
                if s0 == 512:
                    flush_prefetch()
                par = fcnt % 2
                fcnt += 1
                SGA, SGBt, M1 = FT[par]
                kA, kB, kM = f"SGA{par}", f"SGBt{par}", f"M1{par}"
                b0, b1, b2, b3 = [4 * par + i for i in range(4)]
                proj_fm(wf, kf, 0, 128, s0, W, b0)
                proj_fm(wf, kf, 128, 128, s0, W, b1)
                for kc in range(KC):
                    P.mm(banks[b2][:, 0:W], wf[:, kc, 256:384], OgT[:, kc, s0:s0 + W], kc == 0, kc == KC - 1,
                         [kf, "OgT"], [bk(b2)])
                for kc in range(KC):
                    P.mm(banks[b3][:, 0:W], wf[:, kc, 384:512], OrT[:, kc, s0:s0 + W], kc == 0, kc == KC - 1,
                         [kf, "OrT"], [bk(b3)])
                P.act(SGA[:, 0:W], banks[b0][:, 0:W], AF.Sigmoid, [bk(b0)], [kA])
                P.act(SGBt[:, 0:W], banks[b1][:, 0:W], AF.Sigmoid, [bk(b1)], [kB])
                P.tt("dve", M1[:, 0:W], banks[b2][:, 0:W], SGA[:, 0:W], ALU.mult, [bk(b2), kA], [kM])
                P.tt("dve", MT[:, dt, s0:s0 + W], banks[b3][:, 0:W], SGBt[:, 0:W], ALU.mult, [bk(b3), kB], ["MT"])
                P.tt("pool", MT[:, dt, s0:s0 + W], MT[:, dt, s0:s0 + W], M1[:, 0:W], ALU.add, ["MT", kM], ["MT"])
        WO = v3(ar.bf16(KC * 1024), 1024)
        for g in range(2):
            wo, ko = load_wgroup([w_out[:, g * 512:(g + 1) * 512]])
            P.cp("pool", WO[:, :, g * 512:(g + 1) * 512], wo, [ko], ["WO"])
        XA = xT.rearrange("p a b -> p (a b)").bitcast(F32)
        LNGB = XA[:, 0:2048]
        XR = [XA[:, 2048 + i * 1024:2048 + (i + 1) * 1024] for i in range(2)]
        ZT = [XA[:, 4096 + i * 1024:4096 + (i + 1) * 1024] for i in range(2)]
        P.dma(LNGB, lngb, [], ["LNGB", "xT"])
        ST = ar.f32(8)
        alpha = 2.0 ** 0.25
        for tt in range(17):
            rows = 128 if tt < 16 else NS
            t0 = tt * 128
            xr = XR[tt % 2]; xk = f"XR{tt % 2}"
            zt = ZT[tt % 2]; zk = f"ZT{tt % 2}"
            P.dma(xr[0:rows, :], x[t0:t0 + rows, :], [], [xk, "xT"])
            for eh in range(2):
                bi = 2 * (tt % 2) + eh
                for kc in range(KC):
                    P.mm(banks[bi][0:rows, 0:512], MT[:, kc, t0:t0 + rows], WO[:, kc, eh * 512:(eh + 1) * 512],
                         kc == 0, kc == KC - 1, ["MT", "WO"], [bk(bi)])
                P.stt(zt[0:rows, eh * 512:(eh + 1) * 512], xr[0:rows, eh * 512:(eh + 1) * 512], alpha,
                      banks[bi][0:rows, 0:512], ALU.mult, ALU.add, [xk, bk(bi)], [zk, "xT"])
            R_ = slice(0, rows)
            P.act(xr[R_, :], zt[R_, :], AF.Copy, [zk], [xk, "ST"], accum=ST[R_, 0:1])
            P.act(xr[R_, :], zt[R_, :], AF.Square, [zk], [xk, "ST"], accum=ST[R_, 1:2])
            P.ts("pool", ST[R_, 2:3], ST[R_, 0:1], 1.0 / D, None, ALU.mult, ALU.bypass, ["ST"], ["ST"])
            P.tt("pool", ST[R_, 3:4], ST[R_, 2:3], ST[R_, 2:3], ALU.mult, ["ST"], ["ST"])
            P.stt(ST[R_, 4:5], ST[R_, 1:2], 1.0 / D, ST[R_, 3:4], ALU.mult, ALU.subtract, ["ST"], ["ST"])
            P.act(ST[R_, 5:6], ST[R_, 4:5], AF.Ln, ["ST"], ["ST"], bias=EPSC[R_, 0:1])
            P.act(ST[R_, 5:6], ST[R_, 5:6], AF.Exp, ["ST"], ["ST"], scale=-0.5)
            P.ts("dve", zt[R_, :], zt[R_, :], ST[R_, 2:3], ST[R_, 5:6], ALU.subtract, ALU.mult, [zk, "ST"], [zk])
            P.tt("pool", zt[R_, :], zt[R_, :], LNGB[R_, 0:1024], ALU.mult, [zk, "LNGB"], [zk])
            P.tt("dve", zt[R_, :], zt[R_, :], LNGB[R_, 1024:2048], ALU.add, [zk, "LNGB"], [zk])
            P.dma(y[t0:t0 + rows, :], zt[R_, :], [zk], [], is_out=True)
        ar.release()
        P.fence()

        P.enabled = True
        P.counting = False
        P.finish()
        P.emit(nc, st)
    return nc


def _consts():
    cm = np.zeros((128, NCM), np.float32)
    cm[:, CM_ID:CM_ID + 128] = np.eye(128, dtype=np.float32)
    s = np.arange(128)[:, None] % 64
    t = np.arange(64)[None, :]
    strict = (s < t).astype(np.float32)
    incl = (s <= t).astype(np.float32)
    lower = (t < s).astype(np.float32)
    cm[:, CM_MASK5:CM_MASK5 + 320] = np.concatenate([strict, incl, strict, incl, lower], axis=1)
    j = np.arange(128)[:, None]
    i = np.arange(128)[None, :]
    cm[:, CM_MASKU:CM_MASKU + 128] = (j <= i).astype(np.float32)
    cm[:, CM_ISTK:CM_ISTK + 64] = (s == t).astype(np.float32)
    blk = (np.arange(128)[:, None] // 64 == np.arange(128)[None, :] // 64).astype(np.float32)
    cm[:, CM_BONES:CM_BONES + 128] = blk
    cm[:, CM_BAVG:CM_BAVG + 128] = blk / 64.0
    cm[:, CM_ONES:CM_ONES + 128] = 1.0
    smg = np.ones(512, np.float32); smg[::128] = 0
    smr = np.ones(512, np.float32); smr[::64] = 0
    cm[:, CM_SMG:CM_SMG + 512] = smg[None, :]
    cm[:, CM_SMR:CM_SMR + 512] = smr[None, :]
    cm[0:16, CM_ID16:CM_ID16 + 16] = np.eye(16, dtype=np.float32)
    return cm


def _cols(vec):
    return np.ascontiguousarray(np.asarray(vec, np.float32).reshape(-1, 128).T)


_NC_CACHE = {}


def kernel(x_prompt, x_sample, state_gla, state_rwkv, state_rwkv_shift, w_in, gla_alpha_w2,
           gla_alpha_b, gla_norm_w, rwkv_mu, rwkv_w0, rwkv_w2, rwkv_a0, rwkv_a2, rwkv_k_k,
           rwkv_k_a, rwkv_r_k, rwkv_lnx_w, rwkv_lnx_b, w_up_gla, w_up_rwkv, w_out, ln_g, ln_b):
    f = lambda a: np.ascontiguousarray(np.asarray(a, dtype=np.float32))
    x_prompt, x_sample = f(x_prompt), f(x_sample)
    cvec = np.concatenate([_cols(rwkv_mu[0]), _cols(gla_alpha_b[0]), _cols(gla_norm_w[0]), _cols(rwkv_w0[0]),
                           _cols(rwkv_a0[0]), _cols(rwkv_k_k[0]), _cols(rwkv_k_a[0]),
                           _cols(np.asarray(rwkv_r_k[0]).reshape(-1)), _cols(rwkv_lnx_w[0]), _cols(rwkv_lnx_b[0])],
                          axis=1)
    assert cvec.shape == (128, NCV)
    cmat = _consts()
    w2a2 = np.concatenate([f(rwkv_w2[0]), f(rwkv_a2[0])], axis=0)
    lngb = np.concatenate([np.broadcast_to(f(ln_g[0])[None, :], (128, D)),
                           np.broadcast_to(f(ln_b[0])[None, :], (128, D))], axis=1)
    lngb = np.ascontiguousarray(lngb)
    common = dict(w_in=f(w_in[0]), alpha_w2=f(gla_alpha_w2[0]), w2a2=w2a2, w_up_gla=f(w_up_gla[0]),
                  w_up_rwkv=f(w_up_rwkv[0]), w_out=f(w_out[0]), lngb=lngb, cvec=f(cvec), cmat=cmat)
    in_maps = []
    for c in range(8):
        m = dict(common)
        m["x"] = np.ascontiguousarray(np.concatenate([x_prompt[c], x_sample[NS * c:NS * (c + 1), 0]], axis=0))
        m["sgla"] = f(state_gla[0, NS * c:NS * (c + 1)])
        m["srwkv"] = f(state_rwkv[0, NS * c:NS * (c + 1)])
        m["sshift"] = f(state_rwkv_shift[0, NS * c:NS * (c + 1)])
        in_maps.append(m)
    if "nc" not in _NC_CACHE:
        _NC_CACHE["nc"] = build_nc()
    nc = _NC_CACHE["nc"]
    res = run_bass_kernel_spmd(nc, in_maps, core_ids=list(range(8)))
    rs = res.results
    y_prompt = np.stack([rs[c]["y"][0:T] for c in range(8)], axis=0)
    y_sample = np.concatenate([rs[c]["y"][T:TA] for c in range(8)], axis=0)[:, None, :]
    gla_p = np.stack([rs[c]["gla_p"] for c in range(8)], axis=0)[None]
    rwkv_p = np.stack([rs[c]["rwkv_p"] for c in range(8)], axis=0)[None]
    shift_p = np.stack([rs[c]["shift_o"][0] for c in range(8)], axis=0)[None]
    gla_s = np.concatenate([rs[c]["gla_s"] for c in range(8)], axis=0)[None]
    rwkv_s = np.concatenate([rs[c]["rwkv_s"] for c in range(8)], axis=0)[None]
    shift_s = np.concatenate([rs[c]["shift_o"][1:17] for c in range(8)], axis=0)[None]
    outs = (y_prompt, y_sample, gla_p, rwkv_p, shift_p, gla_s, rwkv_s, shift_s)
    return tuple(np.ascontiguousarray(o, dtype=np.float32) for o in outs)
```

```python
import contextlib
import numpy as np
import concourse.bass as bass
import concourse.mybir as mybir
from concourse.bass_utils import run_bass_kernel_spmd

F32 = mybir.dt.float32
BF16 = mybir.dt.bfloat16
AF = mybir.ActivationFunctionType
ALU = mybir.AluOpType

ENGS = ["pe", "act", "dve", "pool", "sp"]

T = 2048
NS = 16
TA = T + NS
D = 1024
KC = 8
NIN = 9360
GQ, GK, GV, GG, GA = 0, 512, 1024, 2048, 3072
R0 = 3088
RR, RK, RV, RG, RW = R0, R0 + 1024, R0 + 2048, R0 + 3072, R0 + 4096
G0 = R0 + 4224
SEGS = [(0, 512), (512, 512), (1024, 512), (1536, 512), (2048, 16)]
DECAY = 0.606531
RLEVEL = 99
RPAIRS = 8
RSTOP = None
RLOG = []

CV_MU, CV_AB, CV_GNW, CV_W0, CV_A0, CV_KK, CV_KA, CV_RK, CV_LW, CV_LB = 0, 33, 37, 39, 47, 55, 63, 71, 79, 87
NCV = 95
CM_ID, CM_MASK5, CM_MASKU, CM_ISTK, CM_BONES, CM_BAVG, CM_ONES, CM_SMG, CM_SMR, CM_ID16 = (
    0, 128, 448, 576, 640, 768, 896, 1024, 1536, 2048)
NCM = 2064


class Prog:
    EPOCH = 8192
    NDMA = 14

    def __init__(self):
        self.ops = {e: [] for e in ENGS}
        self.cnt = {e: 0 for e in ENGS}
        self.ndma = 0
        self.dma_events = []
        self.last_w = {}
        self.readers = {}
        self.waited = {e: {} for e in ENGS}
        self.semkeys = set()
        self.out_events = []
        self.enabled = True
        self.pending = {e: [] for e in ENGS}
        self.last_ev = {}
        self.know = {e: {} for e in ENGS}
        self.evclock = {}
        self.evidx = {}
        self.nev = 0

    def fence(self):
        evs = list(self.last_ev.values()) + list(self.dma_events[-self.NDMA:])
        for e in ENGS:
            self.pending[e] = list(evs)

    def _resolve(self, eng, cands):
        know = self.know[eng]
        waits = []
        for ev in sorted(cands, key=lambda e: -self.evidx[e]):
            sk, val = ev
            if eng == "pe" and sk[0] == "pe":
                continue
            if know.get(sk, 0) >= val:
                continue
            waits.append(ev)
            for k2, v2 in self.evclock[ev].items():
                if know.get(k2, 0) < v2:
                    know[k2] = v2
        return waits

    def add(self, eng, fn, r=(), w=(), dma=False, is_out=False):
        if not self.enabled:
            return None
        if RSTOP is not None and getattr(self, "counting", False):
            self.nops = getattr(self, "nops", 0) + 1
            if self.nops > RSTOP:
                return None
        xb = [k for k in r if isinstance(k, str) and k.startswith("bank")]
        if xb:
            r = [k for k in r if k not in xb]
            w = list(w) + [k for k in xb if k not in w]
        cands = set()
        if self.pending[eng]:
            cands.update(self.pending[eng])
            self.pending[eng] = []
        for k in r:
            cands.add(self.last_w.get(k))
        for k in w:
            cands.add(self.last_w.get(k))
            cands.update(self.readers.get(k, ()))
        if dma and self.ndma >= self.NDMA:
            cands.add(self.dma_events[self.ndma - self.NDMA])
        cands.discard(None)
        waits = self._resolve(eng, cands)
        clk = dict(self.know[eng])
        if dma:
            j = self.ndma
            self.ndma += 1
            sk = ("dma", j % self.NDMA)
            val = 16 * (j // self.NDMA + 1)
            ev = (sk, val)
            self.dma_events.append(ev)
            inc = 16
            if is_out:
                self.out_events.append(ev)
        else:
            i = self.cnt[eng]
            self.cnt[eng] += 1
            ep = i // self.EPOCH
            sk = (eng, ep)
            ev = (sk, i % self.EPOCH + 1)
            inc = 1
            self.last_ev[eng] = ev
            for e2 in range(ep):
                clk[(eng, e2)] = self.EPOCH
        clk[sk] = max(clk.get(sk, 0), ev[1])
        self.evclock[ev] = clk
        self.evidx[ev] = self.nev
        self.nev += 1
        self.semkeys.add(sk)
        self.ops[eng].append((waits, fn, ev, inc))
        for k in r:
            self.readers.setdefault(k, []).append(ev)
        for k in w:
            self.last_w[k] = ev
            self.readers[k] = []
        return ev

    def finish(self):
        cands = set(self.dma_events[-self.NDMA:]) | set(self.out_events)
        waits = self._resolve("sp", cands)
        self.ops["sp"].append((waits, None, None, 0))

    def emit(self, nc, stack):
        targets = set()
        for e in ENGS:
            for waits, fn, ev, inc in self.ops[e]:
                targets.update(waits)
        real = {}
        used = set()
        for e in ENGS:
            cnt = {}
            for waits, fn, ev, inc in self.ops[e]:
                if ev is None:
                    continue
                if ev[0][0] == "dma":
                    real[ev] = ev[1]
                    used.add(ev[0])
                elif ev in targets:
                    cnt[ev[0]] = cnt.get(ev[0], 0) + 1
                    real[ev] = cnt[ev[0]]
                    used.add(ev[0])
        sems = {}
        for sk in sorted(used, key=str):
            sems[sk] = stack.enter_context(nc.semaphore("s_" + "_".join(str(x) for x in sk)))
        block = stack.enter_context(nc.Block())
        prog = self

        def run(engname):
            def body(eng):
                for waits, fn, ev, inc in prog.ops[engname]:
                    if fn is None:
                        for w_ in waits:
                            eng.wait_ge(sems[w_[0]], real[w_])
                        continue
                    for w_ in waits[:-1]:
                        eng.wait_ge(sems[w_[0]], real[w_])
                    ins = fn(eng)
                    if waits:
                        ins._wait_ge(sems[waits[-1][0]], real[waits[-1]])
                    if ev in real:
                        ins.then_inc(sems[ev[0]], inc)
            return body

        block.tensor(run("pe"))
        block.scalar(run("act"))
        block.vector(run("dve"))
        block.gpsimd(run("pool"))
        block.sync(run("sp"))

    def act(self, out, in_, func, r, w, bias=0.0, scale=1.0, accum=None):
        if accum is None:
            return self.add("act", lambda e: e.activation(out=out, in_=in_, func=func, bias=bias, scale=scale), r, w)
        return self.add("act", lambda e: e.activation(out=out, in_=in_, func=func, bias=bias, scale=scale,
                                                      accum_out=accum), r, w)

    def ts(self, eng, out, in0, s1, s2, op0, op1, r, w):
        return self.add(eng, lambda e: e.tensor_scalar(out=out, in0=in0, scalar1=s1, scalar2=s2, op0=op0, op1=op1), r, w)

    def stt(self, out, in0, scalar, in1, op0, op1, r, w, accum=None):
        if accum is None:
            return self.add("dve", lambda e: e.scalar_tensor_tensor(out=out, in0=in0, scalar=scalar, in1=in1,
                                                                     op0=op0, op1=op1), r, w)
        return self.add("dve", lambda e: e.scalar_tensor_tensor(out=out, in0=in0, scalar=scalar, in1=in1,
                                                                 op0=op0, op1=op1, accum_out=accum), r, w)

    def tt(self, eng, out, in0, in1, op, r, w):
        return self.add(eng, lambda e: e.tensor_tensor(out=out, in0=in0, in1=in1, op=op), r, w)

    def cp(self, eng, out, in_, r, w):
        if eng == "act":
            return self.add("act", lambda e: e.activation(out=out, in_=in_, func=AF.Copy), r, w)
        return self.add(eng, lambda e: e.tensor_copy(out=out, in_=in_), r, w)

    def memset(self, eng, out, val, w):
        return self.add(eng, lambda e: e.memset(out, val), (), w)

    def mm(self, out, lhsT, rhs, start, stop, r, w):
        return self.add("pe", lambda e: e.matmul(out, lhsT=lhsT, rhs=rhs, start=start, stop=stop), r, w)

    def tr(self, out, in_, ident, r, w):
        return self.add("pe", lambda e: e.transpose(out, in_, ident), r, w)

    def dma(self, out, in_, r, w, is_out=False):
        return self.add("sp", lambda e: e.dma_start(out=out, in_=in_), r, w, dma=True, is_out=is_out)

    def scan(self, out, d0, d1, r, w):
        return self.add("dve", lambda e: e.tensor_tensor_scan(out=out, data0=d0, data1=d1, initial=0.0,
                                                               op0=ALU.mult, op1=ALU.add), r, w)


class Arena:
    def __init__(self, t, words):
        self.t = t
        self.words = words
        self.off = 0
        self.marks = []
        self.peak = 0

    def alloc(self, nwords):
        o = self.off
        self.off += nwords
        self.peak = max(self.peak, self.off)
        assert self.off <= self.words, f"SBUF arena overflow {self.off}>{self.words}"
        return o

    def f32(self, n, parts=128):
        o = self.alloc(n)
        return self.t[0:parts, o:o + n]

    def bf16(self, n, parts=128):
        assert n % 2 == 0
        o = self.alloc(n // 2)
        return self.t[0:parts, o:o + n // 2].bitcast(BF16)

    def mark(self):
        self.marks.append(self.off)

    def release(self):
        self.off = self.marks.pop()


def v3(ap, b):
    return ap.rearrange("p (a b) -> p a b", b=b)


def build_nc(phases="0GRF"):
    nc = bass.Bass("TRN2", target_bir_lowering=False)
    di = lambda n, s: nc.dram_tensor(n, list(s), F32, kind="ExternalInput").ap()
    do = lambda n, s: nc.dram_tensor(n, list(s), F32, kind="ExternalOutput").ap()
    x = di("x", (TA, D))
    w_in = di("w_in", (D, NIN))
    alpha_w2 = di("alpha_w2", (16, 512))
    w2a2 = di("w2a2", (128, 1024))
    w_up_gla = di("w_up_gla", (D, D))
    w_up_rwkv = di("w_up_rwkv", (D, D))
    w_out = di("w_out", (D, D))
    lngb = di("lngb", (128, 2048))
    sgla = di("sgla", (NS, 4, 128, 256))
    srwkv = di("srwkv", (NS, 16, 64, 64))
    sshift = di("sshift", (NS, 4224))
    cvec = di("cvec", (128, NCV))
    cmat = di("cmat", (128, NCM))
    y = do("y", (TA, D))
    gla_p = do("gla_p", (4, 128, 256))
    rwkv_p = do("rwkv_p", (16, 64, 64))
    shift_o = do("shift_o", (17, 4224))
    gla_s = do("gla_s", (NS, 4, 128, 256))
    rwkv_s = do("rwkv_s", (NS, 16, 64, 64))

    P = Prog()
    with contextlib.ExitStack() as st:
        WORDS = 53184
        sb = st.enter_context(nc.sbuf_tensor("arena", [128, WORDS], F32))
        ar = Arena(sb, WORDS)
        banks = [st.enter_context(nc.psum_tensor(f"ps{i}", [128, 512], F32)) for i in range(8)]
        bk = lambda i: f"bank{i}"

        CV = ar.f32(NCV)
        CM = ar.f32(NCM)
        P.dma(CV, cvec, [], ["CV"])
        P.dma(CM, cmat, [], ["CM"])
        ID = CM[:, CM_ID:CM_ID + 128]
        MASK5 = CM[:, CM_MASK5:CM_MASK5 + 320]
        MASKU = CM[:, CM_MASKU:CM_MASKU + 128]
        ISTK = CM[:, CM_ISTK:CM_ISTK + 64]
        BONES = CM[:, CM_BONES:CM_BONES + 128]
        BAVG = CM[:, CM_BAVG:CM_BAVG + 128]
        ONES = CM[:, CM_ONES:CM_ONES + 128]
        SMG = CM[:, CM_SMG:CM_SMG + 512]
        SMR = CM[:, CM_SMR:CM_SMR + 512]
        ID16 = CM[0:16, CM_ID16:CM_ID16 + 16]
        IDb = ar.bf16(128)
        ISTKb = ar.bf16(64)
        BONESb = ar.bf16(128)
        ONESb = ar.bf16(128)
        NAB = ar.f32(4)
        EPSC = ar.f32(2)
        P.cp("pool", IDb, ID, ["CM"], ["IDb"])
        P.cp("pool", ISTKb, ISTK, ["CM"], ["ISTKb"])
        P.cp("pool", BONESb, BONES, ["CM"], ["BONESb"])
        P.cp("pool", ONESb, ONES, ["CM"], ["ONESb"])
        P.ts("pool", NAB, CV[:, CV_AB:CV_AB + 4], -1.0, None, ALU.mult, ALU.bypass, ["CV"], ["NAB"])
        P.memset("pool", EPSC[:, 0:1], 1e-5, ["EPSC"])
        P.memset("pool", EPSC[:, 1:2], 64e-5, ["EPSC"])
        cvc = lambda base, j: CV[:, base + j:base + j + 1]

        W2A2 = ar.f32(1024)
        P.dma(W2A2, w2a2, [], ["W2A2"])

        xT = v3(ar.bf16(KC * TA), TA)
        OgT = v3(ar.bf16(KC * TA), TA)
        OrT = v3(ar.bf16(KC * TA), TA)

        WST = v3(ar.f32(4 * 512), 512)
        WBF = [v3(ar.bf16(KC * 512), 512) for _ in range(3)]
        wstate = {"n": 0, "pre": None}
        wqueue = []

        def _issue_load(srcs):
            en = P.enabled
            P.enabled = True
            par = wstate["n"] % 3
            wstate["n"] += 1
            key = f"WBF{par}"
            for half in range(2):
                off = 0
                for s_ in srcs:
                    n = s_.shape[1]
                    P.dma(WST[:, :, off:off + n],
                          s_.rearrange("(kc p) n -> p kc n", p=128)[:, 4 * half:4 * half + 4, :], [], ["WST"])
                    off += n
                for k2 in range(2):
                    P.cp("pool", WBF[par][:, 4 * half + 2 * k2:4 * half + 2 * k2 + 2, 0:off],
                         WST[:, 2 * k2:2 * k2 + 2, 0:off], ["WST"], [key])
            P.enabled = en
            return WBF[par], key

        def load_wgroup(srcs=None, defer=False):
            flush_prefetch()
            if wstate["pre"] is None:
                wstate["pre"] = _issue_load(wqueue.pop(0))
            cur = wstate["pre"]
            wstate["pre"] = None
            wstate["pend"] = bool(wqueue)
            if not defer:
                flush_prefetch()
            return cur

        def flush_prefetch():
            if wstate.get("pend"):
                wstate["pend"] = False
                wstate["pre"] = _issue_load(wqueue.pop(0))

        wqueue.append([w_in[:, GA:GA + 16]])
        for h_ in range(4):
            wqueue.append([w_in[:, GQ + h_ * 128:GQ + (h_ + 1) * 128], w_in[:, GK + h_ * 128:GK + (h_ + 1) * 128],
                           w_in[:, GV + h_ * 256:GV + (h_ + 1) * 256]])
            wqueue.append([w_in[:, GG + h_ * 256:GG + (h_ + 1) * 256]])
        wqueue.append([w_in[:, RW:RW + 128]])
        for p_ in range(8):
            wqueue.append([w_in[:, RR + p_ * 128:RR + (p_ + 1) * 128], w_in[:, RK + p_ * 128:RK + (p_ + 1) * 128],
                           w_in[:, RV + p_ * 128:RV + (p_ + 1) * 128], w_in[:, RG + p_ * 128:RG + (p_ + 1) * 128]])
        for dt_ in range(8):
            wqueue.append([w_in[:, G0 + dt_ * 128:G0 + (dt_ + 1) * 128],
                           w_in[:, G0 + 1024 + dt_ * 128:G0 + 1024 + (dt_ + 1) * 128],
                           w_up_gla[:, dt_ * 128:(dt_ + 1) * 128], w_up_rwkv[:, dt_ * 128:(dt_ + 1) * 128]])
        for g_ in range(2):
            wqueue.append([w_out[:, g_ * 512:(g_ + 1) * 512]])

        def proj_fm(wb, wkey, coff, ncols, s0, W, bank_i, c0=0):
            for kc in range(KC):
                P.mm(banks[bank_i][0:ncols, c0:c0 + W], wb[:, kc, coff:coff + ncols], xT[:, kc, s0:s0 + W],
                     kc == 0, kc == KC - 1, [wkey, "xT"], [bk(bank_i)])

        def proj_tm(wb, wkey, coff, ncols, t0, M, bank_i, c0=0):
            for kc in range(KC):
                P.mm(banks[bank_i][0:M, c0:c0 + ncols], xT[:, kc, t0:t0 + M], wb[:, kc, coff:coff + ncols],
                     kc == 0, kc == KC - 1, [wkey, "xT"], [bk(bank_i)])

        wstate["pre"] = _issue_load(wqueue.pop(0))
        P.enabled = "0" in phases
        ar.mark()
        XS = [ar.f32(1024) for _ in range(4)]
        for tt in range(17):
            rows = 128 if tt < 16 else NS
            xs = XS[tt % 4]
            xk = f"XS{tt % 4}"
            P.dma(xs[0:rows, :], x[tt * 128:tt * 128 + rows, :], [], [xk])
            for half in range(2):
                b = banks[half]
                for j in range(4):
                    kc = half * 4 + j
                    P.tr(b[:, j * 128:j * 128 + rows], xs[0:rows, kc * 128:(kc + 1) * 128], ID[0:rows, 0:rows],
                         [xk, "CM"], [bk(half)])
                src = v3(b[:, 0:512], 128)[:, :, 0:rows]
                dst = xT[:, half * 4:half * 4 + 4, tt * 128:tt * 128 + rows]
                P.cp("act" if half == 0 else "dve", dst, src, [bk(half)], ["xT"])
        ar.release()
        P.fence()

        P.enabled = "G" in phases
        ar.mark()
        AW2 = ar.f32(512, parts=16)
        P.dma(AW2, alpha_w2, [], ["AW2"])
        ALR = ar.f32(TA, parts=16)
        SP = ar.f32(512); CSP = ar.f32(512); EB = ar.f32(512); EINV = ar.f32(512)
        QT = ar.bf16(512); KT = ar.bf16(512); QS = ar.f32(16)
        KH = ar.bf16(128); KHT = ar.bf16(128)
        VTK = v3(ar.bf16(4 * 256), 256)
        ATS = ar.bf16(128)
        SG = [ar.f32(256) for _ in range(2)]
        SGB = ar.bf16(256)
        GS = v3(ar.bf16(2 * 512), 512)
        SQ = v3(ar.bf16(2 * 512), 512)
        RSTD = ar.f32(512)
        ON = ar.f32(512)
        KTOK = ar.f32(128, parts=16); VTOK = ar.f32(256, parts=16)
        KM = v3(ar.f32(16 * 128, parts=16), 128)
        SSb = [v3(ar.f32(4 * 256), 256) for _ in range(2)]
        SNb = [v3(ar.f32(4 * 256), 256) for _ in range(2)]

        wb, wk = load_wgroup([w_in[:, GA:GA + 16]])
        for (s0, W) in SEGS:
            proj_fm(wb, wk, 0, 16, s0, W, 0)
            P.cp("act", ALR[:, s0:s0 + W], banks[0][0:16, 0:W], [bk(0)], ["ALR"])

        for h in range(4):
            wqkv, kqkv = load_wgroup([w_in[:, GQ + h * 128:GQ + (h + 1) * 128],
                                      w_in[:, GK + h * 128:GK + (h + 1) * 128],
                                      w_in[:, GV + h * 256:GV + (h + 1) * 256]])
            wg, kg = load_wgroup(defer=True)
            cur = 0
            P.memset("pool", SG[0], 0.0, ["SG0"])
            P.memset("pool", SGB, 0.0, ["SGB"])
            for si, (s0, W) in enumerate(SEGS):
                samp = si == 4
                if si == 1:
                    flush_prefetch()
                P.mm(banks[2][:, 0:W], AW2[:, h * 128:(h + 1) * 128], ALR[:, s0:s0 + W], True, True,
                     ["AW2", "ALR"], [bk(2)])
                P.act(SP[:, 0:W], banks[2][:, 0:W], AF.Exp, [bk(2), "NAB"], ["SP"], bias=NAB[:, h:h + 1], scale=-1.0)
                P.act(SP[:, 0:W], SP[:, 0:W], AF.Ln, ["SP"], ["SP"], bias=1.0)
                if samp:
                    csp = SP
                else:
                    P.scan(CSP[:, 0:W], SMG[:, 0:W], SP[:, 0:W], ["CM", "SP"], ["CSP"])
                    csp = CSP
                ck = "SP" if samp else "CSP"
                P.act(EB[:, 0:W], csp[:, 0:W], AF.Exp, [ck], ["EB"], scale=-1.0 / 16.0)
                P.act(EINV[:, 0:W], csp[:, 0:W], AF.Exp, [ck], ["EINV"], scale=1.0 / 16.0)
                proj_fm(wqkv, kqkv, 0, 128, s0, W, 0)
                if samp:
                    P.ts("dve", QS[:, 0:W], banks[0][:, 0:W], 128.0 ** -0.5, None, ALU.mult, ALU.bypass,
                         [bk(0)], ["QS"])
                else:
                    P.stt(QT[:, 0:W], banks[0][:, 0:W], 128.0 ** -0.5, EB[:, 0:W], ALU.mult, ALU.mult,
                          [bk(0), "EB"], ["QT"])
                    proj_fm(wqkv, kqkv, 128, 128, s0, W, 1)
                    P.tt("dve", KT[:, 0:W], banks[1][:, 0:W], EINV[:, 0:W], ALU.mult, [bk(1), "EINV"], ["KT"])
                if not samp:
                    for t4 in range(4):
                        bi = t4 % 2
                        proj_tm(wqkv, kqkv, 256, 256, s0 + t4 * 128, 128, bi)
                        P.cp("act", VTK[:, t4, :], banks[bi][:, 0:256], [bk(bi)], ["VTK"])
                    for vh in range(2):
                        proj_fm(wg, kg, vh * 128, 128, s0, W, vh)
                        P.act(GS[:, vh, 0:W], banks[vh][:, 0:W], AF.Silu, [bk(vh)], ["GS"])
                    for c in range(4):
                        cs = slice(c * 128, (c + 1) * 128)
                        ecol = EB[:, c * 128 + 127:c * 128 + 128]
                        first = (si == 0 and c == 0)
                        P.mm(banks[4][:, 0:128], KT[:, cs], QT[:, cs], True, True, ["KT", "QT"], [bk(4)])
                        P.tt("dve", ATS, banks[4][:, 0:128], MASKU, ALU.mult, [bk(4), "CM"], ["ATS"])
                        P.ts("pool", KH, KT[:, cs], ecol, None, ALU.mult, ALU.bypass, ["KT", "EB"], ["KH"])
                        b4b = banks[4].bitcast(BF16)
                        P.tr(b4b[:, 512:640], KH, IDb, ["KH", "IDb"], [bk(4)])
                        P.cp("act", KHT, b4b[:, 512:640], [bk(4)], ["KHT"])
                        for vh in range(2):
                            vs = slice(vh * 128, (vh + 1) * 128)
                            P.mm(banks[5][:, vh * 128:(vh + 1) * 128], VTK[:, c, vs], ATS, True, first,
                                 ["VTK", "ATS"], [bk(5)])
                            if not first:
                                P.mm(banks[5][:, vh * 128:(vh + 1) * 128], SGB[:, vs], QT[:, cs], False, True,
                                     ["SGB", "QT"], [bk(5)])
                        P.cp("act", OgT[:, 2 * h:2 * h + 2, s0 + c * 128:s0 + (c + 1) * 128],
                             v3(banks[5][:, 0:256], 128), [bk(5)], ["OgT"])
                        P.mm(banks[6][:, 0:256], KHT, VTK[:, c, :], True, True, ["KHT", "VTK"], [bk(6)])
                        nxt = 1 - cur
                        P.stt(SG[nxt], SG[cur], ecol, banks[6][:, 0:256], ALU.mult, ALU.add,
                              [f"SG{cur}", "EB", bk(6)], [f"SG{nxt}"])
                        P.cp("act", SGB, SG[nxt], [f"SG{nxt}"], ["SGB"])
                        cur = nxt
                    if si == 3:
                        P.dma(gla_p[h], SG[cur], [f"SG{cur}"], [], is_out=True)
                else:
                    for vh in range(2):
                        proj_fm(wg, kg, vh * 128, 128, s0, W, vh)
                        P.act(GS[:, vh, 0:W], banks[vh][:, 0:W], AF.Silu, [bk(vh)], ["GS"])
                    proj_tm(wqkv, kqkv, 128, 128, T, NS, 4)
                    P.cp("act", KTOK, banks[4][0:16, 0:128], [bk(4)], ["KTOK"])
                    proj_tm(wqkv, kqkv, 256, 256, T, NS, 4)
                    P.cp("act", VTOK, banks[4][0:16, 0:256], [bk(4)], ["VTOK"])
                    P.add("dve", lambda e: e.tensor_tensor(
                        out=KM, in0=KTOK.unsqueeze(1).to_broadcast([16, 16, 128]),
                        in1=ID16.unsqueeze(2).to_broadcast([16, 16, 128]), op=ALU.mult),
                        ["KTOK", "CM"], ["KM"])
                    P.dma(SSb[0], sgla[0:4, h].rearrange("s d v -> d s v"), [], ["SS0"])
                    for sg in range(4):
                        SS, SN = SSb[sg % 2], SNb[sg % 2]
                        kSS, kSN = f"SS{sg % 2}", f"SN{sg % 2}"
                        if sg + 1 < 4:
                            P.dma(SSb[(sg + 1) % 2], sgla[(sg + 1) * 4:(sg + 2) * 4, h].rearrange("s d v -> d s v"),
                                  [], [f"SS{(sg + 1) % 2}"])
                        for j in range(4):
                            s = sg * 4 + j
                            P.add("pe", lambda e, s=s: e.matmul(banks[6][:, 0:256], lhsT=KM[:, s, :], rhs=VTOK,
                                                                start=True, stop=True), ["KM", "VTOK"], [bk(6)])
                            P.stt(SN[:, j, :], SS[:, j, :], EB[:, s:s + 1], banks[6][:, 0:256], ALU.mult, ALU.add,
                                  [kSS, "EB", bk(6)], [kSN])
                            for vh in range(2):
                                P.mm(banks[5][:, vh * 16 + s:vh * 16 + s + 1], SN[:, j, vh * 128:(vh + 1) * 128],
                                     QS[:, s:s + 1], True, True, [kSN, "QS"], [bk(5)])
                        P.dma(gla_s[sg * 4:(sg + 1) * 4, h].rearrange("s d v -> d s v"), SN, [kSN], [], is_out=True)
                    P.cp("act", OgT[:, 2 * h:2 * h + 2, T:TA], v3(banks[5][:, 0:32], 16), [bk(5)], ["OgT"])
                og = OgT[:, 2 * h:2 * h + 2, s0:s0 + W]
                P.act(SQ[:, :, 0:W], og, AF.Square, ["OgT"], ["SQ"])
                for vh in range(2):
                    P.mm(banks[3][:, 0:W], ONESb, SQ[:, vh, 0:W], vh == 0, vh == 1, ["ONESb", "SQ"], [bk(3)])
                P.act(RSTD[:, 0:W], banks[3][:, 0:W], AF.Ln, [bk(3), "EPSC"], ["RSTD"], bias=EPSC[:, 0:1], scale=1.0 / 256.0)
                P.act(RSTD[:, 0:W], RSTD[:, 0:W], AF.Exp, ["RSTD"], ["RSTD"], scale=-0.5)
                for vh in range(2):
                    P.stt(ON[:, 0:W], OgT[:, 2 * h + vh, s0:s0 + W], cvc(CV_GNW, vh), RSTD[:, 0:W],
                          ALU.mult, ALU.mult, ["OgT", "CV", "RSTD"], ["ON"])
                    P.tt("pool", OgT[:, 2 * h + vh, s0:s0 + W], ON[:, 0:W], GS[:, vh, 0:W], ALU.mult,
                         ["ON", "GS"], ["OgT"])
        ar.release()
        P.fence()

        P.enabled = "R" in phases
        ar.mark()
        SHT = v3(ar.f32(33 * 16), 16)
        WSTf = WST.rearrange("p a b -> p (a b)")
        for tb in (0, 11, 22):
            P.dma(WSTf[0:16, 0:11 * 128], sshift[:, tb * 128:(tb + 11) * 128], [], ["WST"])
            for j in range(11):
                P.tr(banks[2][:, j * 16:(j + 1) * 16], WSTf[0:16, j * 128:(j + 1) * 128], ID[0:16, 0:16],
                     ["WST", "CM"], [bk(2)])
            P.cp("act", SHT[:, tb:tb + 11, :], v3(banks[2][:, 0:176], 16), [bk(2)], ["SHT"])

        LORA = ar.f32(TA)
        RAW = [ar.f32(513) for _ in range(4)]
        DQ = ar.f32(512)
        BB = ar.f32(9 * 512)
        Bp = lambda i: BB[:, i * 512:(i + 1) * 512]
        Rl, kRl = Bp(0), "B0"
        Kl, kKl = Bp(1), "B1"
        Gl, kGl = Bp(2), "B2"
        SIG, kSIG = Bp(3), "B3"
        Aa, kAa = Bp(4), "B4"
        KKr, kKKr = Bp(5), "B5"
        RN, kRN = Bp(6), "B6"
        KKn, kKKn = Bp(7), "B7"
        T1, kT1 = Bp(2), "B2"
        Kp, kKp = Bp(8), "B8"
        RKf, kRKf = Bp(5), "B5"
        BA, kBA = Bp(1), "B1"
        CS, kCS = Bp(4), "B4"
        GAM, kGAM = Bp(5), "B5"
        GIN, kGIN = Bp(6), "B6"
        CSX, kCSX = Bp(2), "B2"
        GEX, kGEX = Bp(3), "B3"
        YS, kYS = Bp(0), "B0"
        YSQ, kYSQ = Bp(1), "B1"
        MU2, kMU2 = Bp(2), "B2"
        VAR, kVAR = Bp(3), "B3"
        YD, kYD = Bp(4), "B4"
        SHO, kSHO = Bp(0)[0:17, :], "B0"
        SR, kSR = v3(BB[:, 5 * 512:7 * 512], 64), ["B5", "B6"]
        SRN, kSRN = v3(BB[:, 2 * 512:4 * 512], 64), ["B2", "B3"]
        SQb = ar.bf16(512)
        OUT = []
        for o_ in range(2):
            arkb = ar.bf16(2048)
            OUT.append(dict(
                ARKB=arkb, AR=arkb[:, 0:1024], AR4=arkb[:, 0:1024].rearrange("p (c q t) -> p c q t", q=2, t=64),
                Kt=arkb[:, 1024:1536], Bt=arkb[:, 1536:2048],
                Vb=ar.bf16(512), Gs=ar.bf16(512), BON=ar.bf16(512), GAMC=ar.f32(8),
                kAR=f"AR{o_}", kKt=f"Kt{o_}", kBt=f"Bt{o_}", kVb=f"Vb{o_}", kGs=f"Gs{o_}", kBON=f"BON{o_}",
                kGAMC=f"GAMC{o_}"))
        DG = OUT[0]["ARKB"].bitcast(F32)[:, 0:640].rearrange("p (s q k) -> p s q k", q=5, k=64)
        kDG = ["AR0", "Kt0"]
        Vf = ar.f32(16)
        CB = [(ar.bf16(320), ar.bf16(320), [ar.bf16(256) for _ in range(2)], ar.bf16(64), ar.f32(64), ar.bf16(64))
              for _ in range(3)]
        Hb = ar.bf16(64); Hf = ar.f32(64)
        XQ = v3(ar.f32(16 * 5), 5)
        TJ = ar.f32(64); T2 = ar.f32(64); SA = ar.f32(2)
        B4K = [bk(4), "b4tm", "b4s"]
        B5K = [bk(5), "b5z", "b5g", "b5r", "b5zv"]

        def lerp(q, ftile, src_bank, W, samp, out_ap, okey):
            raw = RAW[q]
            rk = f"RAW{q}"
            P.cp("act", raw[:, 1:1 + W], banks[src_bank][:, 0:W], [bk(src_bank)], [rk])
            if samp:
                P.tt("pool", DQ[:, 0:W], SHT[:, ftile, :], raw[:, 1:1 + W], ALU.subtract, ["SHT", rk], ["DQ"])
            else:
                P.tt("pool", DQ[:, 0:W], raw[:, 0:W], raw[:, 1:1 + W], ALU.subtract, [rk], ["DQ"])
            P.stt(out_ap, DQ[:, 0:W], cvc(CV_MU, ftile), raw[:, 1:1 + W], ALU.mult, ALU.add, ["DQ", "CV", rk], [okey])
            if not samp:
                P.cp("pool", raw[:, 0:1], raw[:, W:W + 1], [rk], [rk])

        wl, kl = load_wgroup([w_in[:, RW:RW + 128]])
        P.memset("pool", RAW[0][:, 0:1], 0.0, ["RAW0"])
        for si, (s0, W) in enumerate(SEGS):
            proj_fm(wl, kl, 0, 128, s0, W, 0)
            lerp(0, 32, 0, W, si == 4, LORA[:, s0:s0 + W], "LORA")
        P.act(LORA[0:64, :], LORA[0:64, :], AF.Tanh, ["LORA"], ["LORA"])
        proj_tm(wl, kl, 0, 128, T - 1, 17, 2)
        P.cp("act", SHO[:, 0:128], banks[2][0:17, 0:128], [bk(2)], [kSHO])
        P.dma(shift_o[:, 4096:4224], SHO[:, 0:128], [kSHO], [], is_out=True)

        def gn_post(p, s0, W, O):
            P.act(YSQ[:, 0:W], YS[:, 0:W], AF.Square, [kYS], [kYSQ])
            P.mm(banks[2][:, 0:W], BAVG, YS[:, 0:W], True, True, ["CM", kYS], [bk(2)])
            P.mm(banks[3][:, 0:W], BAVG, YSQ[:, 0:W], True, True, ["CM", kYSQ], [bk(3)])
            P.act(MU2[:, 0:W], banks[2][:, 0:W], AF.Square, [bk(2)], [kMU2])
            P.tt("dve", VAR[:, 0:W], banks[3][:, 0:W], MU2[:, 0:W], ALU.subtract, [bk(3), kMU2], [kVAR])
            P.act(VAR[:, 0:W], VAR[:, 0:W], AF.Ln, [kVAR], [kVAR], bias=EPSC[:, 1:2])
            P.act(VAR[:, 0:W], VAR[:, 0:W], AF.Exp, [kVAR], [kVAR], scale=-0.5)
            P.tt("dve", YD[:, 0:W], YS[:, 0:W], banks[2][:, 0:W], ALU.subtract, [kYS, bk(2)], [kYD])
            P.tt("pool", YD[:, 0:W], YD[:, 0:W], VAR[:, 0:W], ALU.mult, [kYD, kVAR], [kYD])
            P.ts("dve", YD[:, 0:W], YD[:, 0:W], cvc(CV_LW, p), cvc(CV_LB, p), ALU.mult, ALU.add, [kYD, "CV"], [kYD])
            P.tt("pool", YD[:, 0:W], YD[:, 0:W], O["BON"][:, 0:W], ALU.add, [kYD, O["kBON"]], [kYD])
            P.tt("dve", OrT[:, p, s0:s0 + W], YD[:, 0:W], O["Gs"][:, 0:W], ALU.mult, [kYD, O["kGs"]], ["OrT"])

        def mm2(out, lhsT, rhs, n0, n1, l0, l1, r0, r1, start, stop, r, w):
            for hh in range(2):
                ps = slice(hh * 64, (hh + 1) * 64)
                P.mm(out[ps, n0:n1], lhsT[ps, l0:l1], rhs[ps, r0:r1], start, stop, r, w)

        if RLEVEL < 2:
            P.enabled = False
        P.counting = True
        def elem_gen(p, wp, kp, si, O):
            s0, W = SEGS[si]
            samp = si == 4
            Vb, Gs, BON = O["Vb"], O["Gs"], O["BON"]
            AR4, Kt, Bt = O["AR4"], O["Kt"], O["Bt"]
            proj_fm(wp, kp, 0, 128, s0, W, 2)
            lerp(0, p, 2, W, samp, Rl[:, 0:W], kRl)
            yield
            proj_fm(wp, kp, 128, 128, s0, W, 2)
            lerp(1, 8 + p, 2, W, samp, Kl[:, 0:W], kKl)
            yield
            proj_fm(wp, kp, 256, 128, s0, W, 2)
            lerp(2, 16 + p, 2, W, samp, Vb[:, 0:W], O["kVb"])
            if samp:
                P.stt(Vf[:, 0:W], DQ[:, 0:W], cvc(CV_MU, 16 + p), RAW[2][:, 1:1 + W], ALU.mult, ALU.add,
                      ["DQ", "CV", "RAW2"], ["Vf"])
            yield
            proj_fm(wp, kp, 384, 128, s0, W, 2)
            lerp(3, 24 + p, 2, W, samp, Gl[:, 0:W], kGl)
            P.act(Gs[:, 0:W], Gl[:, 0:W], AF.Silu, [kGl], [O["kGs"]])
            yield
            P.mm(banks[2][:, 0:W], W2A2[0:64, p * 128:(p + 1) * 128], LORA[0:64, s0:s0 + W], True, True,
                 ["W2A2", "LORA"], [bk(2)])
            P.act(SIG[:, 0:W], banks[2][:, 0:W], AF.Sigmoid, [bk(2), "CV"], [kSIG], bias=cvc(CV_W0, p))
            yield
            P.mm(banks[2][:, 0:W], W2A2[64:128, p * 128:(p + 1) * 128], LORA[64:128, s0:s0 + W], True, True,
                 ["W2A2", "LORA"], [bk(2)])
            P.act(Aa[:, 0:W], banks[2][:, 0:W], AF.Sigmoid, [bk(2), "CV"], [kAa], bias=cvc(CV_A0, p))
            yield
            P.ts("pool", KKr[:, 0:W], Kl[:, 0:W], cvc(CV_KK, p), None, ALU.mult, ALU.bypass, [kKl, "CV"], [kKKr])
            P.act(SQb[:, 0:W], KKr[:, 0:W], AF.Square, [kKKr], ["SQb"])
            P.mm(banks[2][:, 0:W], BONESb, SQb[:, 0:W], True, True, ["BONESb", "SQb"], [bk(2)])
            yield
            P.act(RN[:, 0:W], banks[2][:, 0:W], AF.Ln, [bk(2)], [kRN])
            P.act(RN[:, 0:W], RN[:, 0:W], AF.Exp, [kRN], [kRN], scale=-0.5)
            yield
            P.tt("pool", KKn[:, 0:W], KKr[:, 0:W], RN[:, 0:W], ALU.mult, [kKKr, kRN], [kKKn])
            P.ts("dve", T1[:, 0:W], Aa[:, 0:W], -1.0, cvc(CV_KA, p), ALU.add, ALU.mult, [kAa, "CV"], [kT1])
            P.stt(Kp[:, 0:W], T1[:, 0:W], 1.0, Kl[:, 0:W], ALU.add, ALU.mult, [kT1, kKl], [kKp])
            yield
            P.tt("pool", RKf[:, 0:W], Rl[:, 0:W], Kp[:, 0:W], ALU.mult, [kRl, kKp], [kRKf])
            P.ts("pool", RKf[:, 0:W], RKf[:, 0:W], cvc(CV_RK, p), None, ALU.mult, ALU.bypass, [kRKf, "CV"], [kRKf])
            yield
            P.mm(banks[2][:, 0:W], BONES, RKf[:, 0:W], True, True, ["CM", kRKf], [bk(2)])
            P.tt("dve", BON[:, 0:W], banks[2][:, 0:W], Vb[:, 0:W], ALU.mult, [bk(2), O["kVb"]], [O["kBON"]])
            P.tt("pool", BA[:, 0:W], KKn[:, 0:W], Aa[:, 0:W], ALU.mult, [kKKn, kAa], [kBA])
            yield
            if not samp:
                P.scan(CS[:, 0:W], SMR[:, 0:W], SIG[:, 0:W], ["CM", kSIG], [kCS])
                P.act(GAM[:, 0:W], CS[:, 0:W], AF.Exp, [kCS], [kGAM], scale=-DECAY)
                P.act(GIN[:, 0:W], CS[:, 0:W], AF.Exp, [kCS], [kGIN], scale=DECAY)
                yield
                P.tt("pool", CSX[:, 0:W], CS[:, 0:W], SIG[:, 0:W], ALU.subtract, [kCS, kSIG], [kCSX])
                P.act(GEX[:, 0:W], CSX[:, 0:W], AF.Exp, [kCSX], [kGEX], scale=-DECAY)
                P.cp("pool", O["GAMC"], v3(GAM[:, 0:W], 64)[:, :, 63], [kGAM], [O["kGAMC"]])
                yield
                v64 = lambda a_: v3(a_[:, 0:W], 64)
                P.stt(AR4[:, :, 0, :], v64(KKn), -1.0, v64(GEX), ALU.mult, ALU.mult, [kKKn, kGEX], [O["kAR"]])
                P.tt("dve", AR4[:, :, 1, :], v64(Rl), v64(GAM), ALU.mult, [kRl, kGAM], [O["kAR"]])
                yield
                P.tt("pool", Kt[:, 0:W], Kp[:, 0:W], GIN[:, 0:W], ALU.mult, [kKp, kGIN], [O["kKt"]])
                P.tt("pool", Bt[:, 0:W], BA[:, 0:W], GIN[:, 0:W], ALU.mult, [kBA, kGIN], [O["kBt"]])
            else:
                P.ts("pool", XQ[:, :, 0], KKn[:, 0:W], -1.0, None, ALU.mult, ALU.bypass, [kKKn], ["XQ"])
                P.act(XQ[:, :, 1], SIG[:, 0:W], AF.Exp, [kSIG], ["XQ"], scale=-DECAY)
                P.cp("pool", XQ[:, :, 2], BA[:, 0:W], [kBA], ["XQ"])
                P.cp("pool", XQ[:, :, 3], Kp[:, 0:W], [kKp], ["XQ"])
                P.cp("pool", XQ[:, :, 4], Rl[:, 0:W], [kRl], ["XQ"])
                P.dma(SR, srwkv[:, 2 * p:2 * p + 2].rearrange("s h v k -> (h v) s k"), [], kSR)

        def chunk_gen(c, par, si, O):
            tb, db = [(4, 5), (6, 7), (0, 1)][par]
            bT, bD = banks[tb], banks[db]
            bTb = bT.bitcast(BF16)
            kT, kD = bk(tb), bk(db)
            TMp, SCp, ZPQp, GA1p, GVGp, RHp = CB[par]
            kTM, kSC, kGA1, kGVG, kRH = f"TM{par}", f"SC{par}", f"GA1{par}", f"GVG{par}", f"RH{par}"
            AR, AR4, Kt, Bt, Vb = O["AR"], O["AR4"], O["Kt"], O["Bt"], O["Vb"]
            kAR, kKt, kBt, kVb = O["kAR"], O["kKt"], O["kBt"], O["kVb"]
            cs = slice(c * 64, (c + 1) * 64)
            last = (si == 3 and c == 7)
            arc = AR[:, c * 128:(c + 1) * 128]
            for hh in range(2):
                ps = slice(hh * 64, (hh + 1) * 64)
                idb = IDb[ps, ps]
                P.tr(bTb[ps, 64:128], AR4[ps, c, 0, :], idb, [kAR, "IDb"], [kT])
                P.tr(bTb[ps, 128:192], Bt[ps, cs], idb, [kBt, "IDb"], [kT])
                P.tr(bTb[ps, 192:256], Kt[ps, cs], idb, [kKt, "IDb"], [kT])
                P.tr(bTb[ps, 256:320], Vb[ps, cs], idb, [kVb, "IDb"], [kT])
            P.cp("act", TMp[:, 64:320], bTb[:, 64:320], [kT], [kTM])
            mm2(bD, Bt, arc, 0, 128, c * 64, (c + 1) * 64, 0, 128, True, True, [kBt, kAR], [kD])
            mm2(bD, Kt, arc, 128, 256, c * 64, (c + 1) * 64, 0, 128, True, True, [kKt, kAR], [kD])
            mm2(bD, arc, Bt, 256, 320, 0, 64, c * 64, (c + 1) * 64, True, True, [kBt, kAR], [kD])
            P.tt("dve", SCp, bD[:, 0:320], MASK5, ALU.mult, [kD, "CM"], [kSC])
            yield
            mm2(bD, SCp, TMp, 448, 512, 128, 192, 256, 320, True, True, [kSC, kTM], [kD])
            P.cp("act", TMp[:, 0:64], bD[:, 448:512], [kD], [kTM])
            yield
            zsrc, zk = TMp, kTM
            psrc, pk, pc = SCp, kSC, 0
            qsrc, qk, qc = SCp, kSC, 256
            for lvl in range(6):
                mm2(bD, psrc, zsrc, 0, 128, pc, pc + 64, 0, 128, True, True, [pk, zk], [kD])
                if lvl < 5:
                    mm2(bT, qsrc, psrc, 0, 64, qc, qc + 64, pc, pc + 64, True, True, [pk, qk], [kT])
                    mm2(bT, psrc, qsrc, 64, 128, pc, pc + 64, qc, qc + 64, True, True, [pk, qk], [kT])
                dst = ZPQp[lvl % 2]
                dzk = f"Z{par}{lvl % 2}"
                dpk = f"PQ{par}{lvl % 2}"
                P.tt("dve", dst[:, 0:128], bD[:, 0:128], zsrc[:, 0:128], ALU.add, [kD, zk], [dzk])
                if lvl < 5:
                    P.cp("act", dst[:, 128:256], bT[:, 0:128], [kT], [dpk])
                zsrc, zk = dst, dzk
                psrc, pk, pc = dst, dpk, 128
                qsrc, qk, qc = dst, dpk, 192
                yield
            Wm, wk_ = zsrc, zk
            gam_col = O["GAMC"][:, c:c + 1]
            kGC = O["kGAMC"]
            mm2(bD, Wm, TMp, 256, 320, 64, 128, 128, 192, True, True, [wk_, kTM], [kD])
            mm2(bD, TMp, Wm, 320, 384, 128, 192, 0, 64, True, False, [wk_, kTM], [kD])
            mm2(bD, TMp, TMp, 320, 384, 192, 256, 256, 320, False, True, [kTM], [kD])
            mm2(bT, ISTKb, arc, 384, 448, 0, 64, 64, 128, True, False, ["ISTKb", kAR], [kT])
            mm2(bT, Wm, SCp, 384, 448, 64, 128, 64, 128, False, True, [wk_, kSC], [kT])
            P.tt("dve", GA1p, bD[:, 256:320], ISTK, ALU.add, [kD, "CM"], [kGA1])
            P.ts("dve", GVGp, bD[:, 320:384], gam_col, None, ALU.mult, ALU.bypass, [kD, kGC], [kGVG])
            P.cp("act", RHp, bT[:, 384:448], [kT], [kRH])
            yield
            b3 = banks[3]
            mm2(b3, Wm, SCp, c * 64, (c + 1) * 64, 0, 64, 64, 128, True, False, [wk_, kSC], [bk(3)])
            mm2(b3, TMp, SCp, c * 64, (c + 1) * 64, 256, 320, 192, 256, False, False, [kTM, kSC], [bk(3)])
            mm2(b3, Hb, RHp, c * 64, (c + 1) * 64, 0, 64, 0, 64, False, True, ["Hb", kRH], [bk(3)])
            mm2(banks[2], GA1p, Hb, 0, 64, 0, 64, 0, 64, True, True, [kGA1, "Hb"], [bk(2)])
            if last:
                P.stt(Hf, banks[2][:, 0:64], gam_col, GVGp, ALU.mult, ALU.add, [bk(2), kGC, kGVG], ["Hf"])
            P.stt(Hb, banks[2][:, 0:64], gam_col, GVGp, ALU.mult, ALU.add, [bk(2), kGC, kGVG], ["Hb"])

        def run_overlapped(chunk_args, extra):
            active = []
            nxt = 0
            free_par = [0, 1, 2]
            since = 99
            ex = extra
            while nxt < len(chunk_args) or active or ex is not None:
                if nxt < len(chunk_args) and free_par and (since >= 3 or not active):
                    pr_ = free_par.pop(0)
                    c_, si_, O_ = chunk_args[nxt]
                    active.append((chunk_gen(c_, pr_, si_, O_), pr_))
                    nxt += 1
                    since = 0
                since += 1
                for (g_, pr_) in list(active):
                    try:
                        next(g_)
                    except StopIteration:
                        active.remove((g_, pr_))
                        free_par.append(pr_)
                if ex is not None:
                    try:
                        next(ex)
                    except StopIteration:
                        ex = None

        for p in range(8):
            wp, kp = load_wgroup(defer=True)
            for q in range(4):
                P.memset("pool", RAW[q][:, 0:1], 0.0, [f"RAW{q}"])
            P.memset("pool", Hb, 0.0, ["Hb"])
            proj_tm(wp, kp, 0, 512, T - 1, 17, 2)
            P.cp("act", SHO, banks[2][0:17, 0:512], [bk(2)], [kSHO])
            P.dma(shift_o[:, 0:4096].rearrange("t (q f) -> t q f", q=4)[:, :, p * 128:(p + 1) * 128],
                  v3(SHO, 128), [kSHO], [], is_out=True)
            run_overlapped([], elem_gen(p, wp, kp, 0, OUT[0]))
            flush_prefetch()
            for si in range(4):
                s0, W = SEGS[si]
                O = OUT[si % 2]
                On = OUT[(si + 1) % 2]
                run_overlapped([(c, si, O) for c in range(8)], elem_gen(p, wp, kp, si + 1, On))
                P.cp("act", YS[:, 0:W], banks[3][:, 0:W], [bk(3)], [kYS])
                if si == 3:
                    P.tr(banks[2][0:64, 64:192], Hf, ID, ["Hf", "CM"], [bk(2)])
                    P.cp("act", T2[0:64, :], banks[2][0:64, 64:128], [bk(2)], ["T2"])
                    P.cp("act", TJ[0:64, :], banks[2][0:64, 128:192], [bk(2)], ["TJ"])
                    P.dma(rwkv_p[2 * p], T2[0:64, :], ["T2"], [], is_out=True)
                    P.dma(rwkv_p[2 * p + 1], TJ[0:64, :], ["TJ"], [], is_out=True)
                gn_post(p, s0, W, O)
            s0, W = SEGS[4]
            O = OUT[0]
            for s2 in range(8):
                hs = slice(s2 * 2, s2 * 2 + 2)
                P.add("dve", lambda e, hs=hs: e.tensor_tensor(
                    out=DG, in0=ISTK.unsqueeze(1).unsqueeze(1).to_broadcast([128, 2, 5, 64]),
                    in1=XQ[:, hs, :].unsqueeze(3).to_broadcast([128, 2, 5, 64]), op=ALU.mult),
                    ["CM", "XQ"], kDG)
                for j in range(2):
                    s = s2 * 2 + j
                    bi = 4 + j
                    bb = banks[bi]
                    dgs = DG[:, j].rearrange("p q k -> p (q k)")
                    for hh in range(2):
                        ps = slice(hh * 64, (hh + 1) * 64)
                        P.mm(bb[ps, 0:320], ONES[ps, 0:64], dgs[ps, :], True, True, ["CM"] + kDG, [bk(bi)])
                    P.stt(TJ, SR[:, s, :], 1.0, bb[:, 0:64], ALU.mult, ALU.mult, kSR + [bk(bi)], ["TJ", "SA"],
                          accum=SA[:, 0:1])
                    P.tt("dve", T2, SR[:, s, :], bb[:, 64:128], ALU.mult, kSR + [bk(bi)], ["T2"])
                    P.stt(T2, bb[:, 128:192], SA[:, 0:1], T2, ALU.mult, ALU.add, [bk(bi), "SA", "T2"], ["T2"])
                    P.stt(SRN[:, s, :], bb[:, 192:256], Vf[:, s:s + 1], T2, ALU.mult, ALU.add,
                          [bk(bi), "Vf", "T2"], kSRN)
                    P.stt(TJ, SRN[:, s, :], 1.0, bb[:, 256:320], ALU.mult, ALU.mult, kSRN + [bk(bi)],
                          ["TJ", kYS], accum=YS[:, s:s + 1])
            P.dma(rwkv_s[:, 2 * p:2 * p + 2].rearrange("s h v k -> (h v) s k"), SRN, kSRN, [], is_out=True)
            gn_post(p, s0, W, O)
        ar.release()
        P.fence()

        P.enabled = "F" in phases
        ar.mark()
        MT = v3(ar.bf16(KC * TA), TA)
        FT = [(ar.bf16(512), ar.bf16(512), ar.f32(512)) for _ in range(2)]
        fcnt = 0
        for dt in range(8):
            wf, kf = load_wgroup(defer=True)
            for (s0, W) in SEGS:
                if s0 == 512:
                    flush_prefetch()
                par = fcnt % 2
                fcnt += 1
                SGA, SGBt, M1 = FT[par]
                kA, kB, kM = f"SGA{par}", f"SGBt{par}", f"M1{par}"
                b0, b1, b2, b3 = [4 * par + i for i in range(4)]
                proj_fm(wf, kf, 0, 128, s0, W, b0)
                proj_fm(wf, kf, 128, 128, s0, W, b1)
                for kc in range(KC):
                    P.mm(banks[b2][:, 0:W], wf[:, kc, 256:384], OgT[:, kc, s0:s0 + W], kc == 0, kc == KC - 1,
                         [kf, "OgT"], [bk(b2)])
                for kc in range(KC):
                    P.mm(banks[b3][:, 0:W], wf[:, kc, 384:512], OrT[:, kc, s0:s0 + W], kc == 0, kc == KC - 1,
                         [kf, "OrT"], [bk(b3)])
                P.act(SGA[:, 0:W], banks[b0][:, 0:W], AF.Sigmoid, [bk(b0)], [kA])
                P.act(SGBt[:, 0:W], banks[b1][:, 0:W], AF.Sigmoid, [bk(b1)], [kB])
                P.tt("dve", M1[:, 0:W], banks[b2][:, 0:W], SGA[:, 0:W], ALU.mult, [bk(b2), kA], [kM])
                P.tt("dve", MT[:, dt, s0:s0 + W], banks[b3][:, 0:W], SGBt[:, 0:W], ALU.mult, [bk(b3), kB], ["MT"])
                P.tt("pool", MT[:, dt, s0:s0 + W], MT[:, dt, s0:s0 + W], M1[:, 0:W], ALU.add, ["MT", kM], ["MT"])
        WO = v3(ar.bf16(KC * 1024), 1024)
        for g in range(2):
            wo, ko = load_wgroup([w_out[:, g * 512:(g + 1) * 512]])
            P.cp("pool", WO[:, :, g * 512:(g + 1) * 512], wo, [ko], ["WO"])
        XA = xT.rearrange("p a b -> p (a b)").bitcast(F32)
        LNGB = XA[:, 0:2048]
        XR = [XA[:, 2048 + i * 1024:2048 + (i + 1) * 1024] for i in range(2)]
        ZT = [XA[:, 4096 + i * 1024:4096 + (i + 1) * 1024] for i in range(2)]
        P.dma(LNGB, lngb, [], ["LNGB", "xT"])
        ST = ar.f32(8)
        alpha = 2.0 ** 0.25
        for tt in range(17):
            rows = 128 if tt < 16 else NS
            t0 = tt * 128
            xr = XR[tt % 2]; xk = f"XR{tt % 2}"
            zt = ZT[tt % 2]; zk = f"ZT{tt % 2}"
            if tt == 0:
                P.dma(xr[0:rows, :], x[t0:t0 + rows, :], [], [xk, "xT"])
            if tt + 1 < 17:
                rn_ = 128 if tt + 1 < 16 else NS
                P.dma(XR[(tt + 1) % 2][0:rn_, :], x[(tt + 1) * 128:(tt + 1) * 128 + rn_, :], [],
                      [f"XR{(tt + 1) % 2}", "xT"])
            for eh in range(2):
                bi = 2 * (tt % 2) + eh
                for kc in range(KC):
                    P.mm(banks[bi][0:rows, 0:512], MT[:, kc, t0:t0 + rows], WO[:, kc, eh * 512:(eh + 1) * 512],
                         kc == 0, kc == KC - 1, ["MT", "WO"], [bk(bi)])
                P.stt(zt[0:rows, eh * 512:(eh + 1) * 512], xr[0:rows, eh * 512:(eh + 1) * 512], alpha,
                      banks[bi][0:rows, 0:512], ALU.mult, ALU.add, [xk, bk(bi)], [zk, "xT"])
            R_ = slice(0, rows)
            P.act(xr[R_, :], zt[R_, :], AF.Copy, [zk], [xk, "ST"], accum=ST[R_, 0:1])
            P.act(xr[R_, :], zt[R_, :], AF.Square, [zk], [xk, "ST"], accum=ST[R_, 1:2])
            P.ts("pool", ST[R_, 2:3], ST[R_, 0:1], 1.0 / D, None, ALU.mult, ALU.bypass, ["ST"], ["ST"])
            P.tt("pool", ST[R_, 3:4], ST[R_, 2:3], ST[R_, 2:3], ALU.mult, ["ST"], ["ST"])
            P.stt(ST[R_, 4:5], ST[R_, 1:2], 1.0 / D, ST[R_, 3:4], ALU.mult, ALU.subtract, ["ST"], ["ST"])
            P.act(ST[R_, 5:6], ST[R_, 4:5], AF.Ln, ["ST"], ["ST"], bias=EPSC[R_, 0:1])
            P.act(ST[R_, 5:6], ST[R_, 5:6], AF.Exp, ["ST"], ["ST"], scale=-0.5)
            P.ts("dve", zt[R_, :], zt[R_, :], ST[R_, 2:3], ST[R_, 5:6], ALU.subtract, ALU.mult, [zk, "ST"], [zk])
            P.tt("pool", zt[R_, :], zt[R_, :], LNGB[R_, 0:1024], ALU.mult, [zk, "LNGB"], [zk])
            P.tt("dve", zt[R_, :], zt[R_, :], LNGB[R_, 1024:2048], ALU.add, [zk, "LNGB"], [zk])
            P.dma(y[t0:t0 + rows, :], zt[R_, :], [zk], [], is_out=True)
        ar.release()
        P.fence()

        P.enabled = True
        P.counting = False
        P.finish()
        P.emit(nc, st)
    return nc


def _consts():
    cm = np.zeros((128, NCM), np.float32)
    cm[:, CM_ID:CM_ID + 128] = np.eye(128, dtype=np.float32)
    s = np.arange(128)[:, None] % 64
    t = np.arange(64)[None, :]
    strict = (s < t).astype(np.float32)
    incl = (s <= t).astype(np.float32)
    lower = (t < s).astype(np.float32)
    cm[:, CM_MASK5:CM_MASK5 + 320] = np.concatenate([strict, incl, strict, incl, lower], axis=1)
    j = np.arange(128)[:, None]
    i = np.arange(128)[None, :]
    cm[:, CM_MASKU:CM_MASKU + 128] = (j <= i).astype(np.float32)
    cm[:, CM_ISTK:CM_ISTK + 64] = (s == t).astype(np.float32)
    blk = (np.arange(128)[:, None] // 64 == np.arange(128)[None, :] // 64).astype(np.float32)
    cm[:, CM_BONES:CM_BONES + 128] = blk
    cm[:, CM_BAVG:CM_BAVG + 128] = blk / 64.0
    cm[:, CM_ONES:CM_ONES + 128] = 1.0
    smg = np.ones(512, np.float32); smg[::128] = 0
    smr = np.ones(512, np.float32); smr[::64] = 0
    cm[:, CM_SMG:CM_SMG + 512] = smg[None, :]
    cm[:, CM_SMR:CM_SMR + 512] = smr[None, :]
    cm[0:16, CM_ID16:CM_ID16 + 16] = np.eye(16, dtype=np.float32)
    return cm


def _cols(vec):
    return np.ascontiguousarray(np.asarray(vec, np.float32).reshape(-1, 128).T)


_NC_CACHE = {}


def kernel(x_prompt, x_sample, state_gla, state_rwkv, state_rwkv_shift, w_in, gla_alpha_w2,
           gla_alpha_b, gla_norm_w, rwkv_mu, rwkv_w0, rwkv_w2, rwkv_a0, rwkv_a2, rwkv_k_k,
           rwkv_k_a, rwkv_r_k, rwkv_lnx_w, rwkv_lnx_b, w_up_gla, w_up_rwkv, w_out, ln_g, ln_b):
    f = lambda a: np.ascontiguousarray(np.asarray(a, dtype=np.float32))
    x_prompt, x_sample = f(x_prompt), f(x_sample)
    cvec = np.concatenate([_cols(rwkv_mu[0]), _cols(gla_alpha_b[0]), _cols(gla_norm_w[0]), _cols(rwkv_w0[0]),
                           _cols(rwkv_a0[0]), _cols(rwkv_k_k[0]), _cols(rwkv_k_a[0]),
                           _cols(np.asarray(rwkv_r_k[0]).reshape(-1)), _cols(rwkv_lnx_w[0]), _cols(rwkv_lnx_b[0])],
                          axis=1)
    assert cvec.shape == (128, NCV)
    cmat = _consts()
    w2a2 = np.concatenate([f(rwkv_w2[0]), f(rwkv_a2[0])], axis=0)
    lngb = np.concatenate([np.broadcast_to(f(ln_g[0])[None, :], (128, D)),
                           np.broadcast_to(f(ln_b[0])[None, :], (128, D))], axis=1)
    lngb = np.ascontiguousarray(lngb)
    common = dict(w_in=f(w_in[0]), alpha_w2=f(gla_alpha_w2[0]), w2a2=w2a2, w_up_gla=f(w_up_gla[0]),
                  w_up_rwkv=f(w_up_rwkv[0]), w_out=f(w_out[0]), lngb=lngb, cvec=f(cvec), cmat=cmat)
    in_maps = []
    for c in range(8):
        m = dict(common)
        m["x"] = np.ascontiguousarray(np.concatenate([x_prompt[c], x_sample[NS * c:NS * (c + 1), 0]], axis=0))
        m["sgla"] = f(state_gla[0, NS * c:NS * (c + 1)])
        m["srwkv"] = f(state_rwkv[0, NS * c:NS * (c + 1)])
        m["sshift"] = f(state_rwkv_shift[0, NS * c:NS * (c + 1)])
        in_maps.append(m)
    if "nc" not in _NC_CACHE:
        _NC_CACHE["nc"] = build_nc()
    nc = _NC_CACHE["nc"]
    res = run_bass_kernel_spmd(nc, in_maps, core_ids=list(range(8)))
    rs = res.results
    y_prompt = np.stack([rs[c]["y"][0:T] for c in range(8)], axis=0)
    y_sample = np.concatenate([rs[c]["y"][T:TA] for c in range(8)], axis=0)[:, None, :]
    gla_p = np.stack([rs[c]["gla_p"] for c in range(8)], axis=0)[None]
    rwkv_p = np.stack([rs[c]["rwkv_p"] for c in range(8)], axis=0)[None]
    shift_p = np.stack([rs[c]["shift_o"][0] for c in range(8)], axis=0)[None]
    gla_s = np.concatenate([rs[c]["gla_s"] for c in range(8)], axis=0)[None]
    rwkv_s = np.concatenate([rs[c]["rwkv_s"] for c in range(8)], axis=0)[None]
    shift_s = np.concatenate([rs[c]["shift_o"][1:17] for c in range(8)], axis=0)[None]
    outs = (y_prompt, y_sample, gla_p, rwkv_p, shift_p, gla_s, rwkv_s, shift_s)
    return tuple(np.ascontiguousarray(o, dtype=np.float32) for o in outs)
```

```python
import contextlib
import numpy as np
import concourse.bass as bass
import concourse.mybir as mybir
from concourse.bass_utils import run_bass_kernel_spmd

F32 = mybir.dt.float32
BF16 = mybir.dt.bfloat16
AF = mybir.ActivationFunctionType
ALU = mybir.AluOpType

ENGS = ["pe", "act", "dve", "pool", "sp"]

T = 2048
NS = 16
TA = T + NS
D = 1024
KC = 8
NIN = 9360
GQ, GK, GV, GG, GA = 0, 512, 1024, 2048, 3072
R0 = 3088
RR, RK, RV, RG, RW = R0, R0 + 1024, R0 + 2048, R0 + 3072, R0 + 4096
G0 = R0 + 4224
SEGS = [(0, 512), (512, 512), (1024, 512), (1536, 512), (2048, 16)]
DECAY = 0.606531
RLEVEL = 99
RPAIRS = 8
RSTOP = None
RLOG = []

CV_MU, CV_AB, CV_GNW, CV_W0, CV_A0, CV_KK, CV_KA, CV_RK, CV_LW, CV_LB = 0, 33, 37, 39, 47, 55, 63, 71, 79, 87
NCV = 95
CM_ID, CM_MASK5, CM_MASKU, CM_ISTK, CM_BONES, CM_BAVG, CM_ONES, CM_SMG, CM_SMR, CM_ID16 = (
    0, 128, 448, 576, 640, 768, 896, 1024, 1536, 2048)
NCM = 2064


class Prog:
    EPOCH = 8192
    NDMA = 14

    def __init__(self):
        self.ops = {e: [] for e in ENGS}
        self.cnt = {e: 0 for e in ENGS}
        self.ndma = 0
        self.dma_events = []
        self.last_w = {}
        self.readers = {}
        self.waited = {e: {} for e in ENGS}
        self.semkeys = set()
        self.out_events = []
        self.enabled = True
        self.pending = {e: [] for e in ENGS}
        self.last_ev = {}
        self.know = {e: {} for e in ENGS}
        self.evclock = {}
        self.evidx = {}
        self.nev = 0

    def fence(self):
        evs = list(self.last_ev.values()) + list(self.dma_events[-self.NDMA:])
        for e in ENGS:
            self.pending[e] = list(evs)

    def _resolve(self, eng, cands):
        know = self.know[eng]
        waits = []
        for ev in sorted(cands, key=lambda e: -self.evidx[e]):
            sk, val = ev
            if eng == "pe" and sk[0] == "pe":
                continue
            if know.get(sk, 0) >= val:
                continue
            waits.append(ev)
            for k2, v2 in self.evclock[ev].items():
                if know.get(k2, 0) < v2:
                    know[k2] = v2
        return waits

    def add(self, eng, fn, r=(), w=(), dma=False, is_out=False):
        if not self.enabled:
            return None
        if RSTOP is not None and getattr(self, "counting", False):
            self.nops = getattr(self, "nops", 0) + 1
            if self.nops > RSTOP:
                return None
        xb = [k for k in r if isinstance(k, str) and k.startswith("bank")]
        if xb:
            r = [k for k in r if k not in xb]
            w = list(w) + [k for k in xb if k not in w]
        cands = set()
        if self.pending[eng]:
            cands.update(self.pending[eng])
            self.pending[eng] = []
        for k in r:
            cands.add(self.last_w.get(k))
        for k in w:
            cands.add(self.last_w.get(k))
            cands.update(self.readers.get(k, ()))
        if dma and self.ndma >= self.NDMA:
            cands.add(self.dma_events[self.ndma - self.NDMA])
        cands.discard(None)
        waits = self._resolve(eng, cands)
        clk = dict(self.know[eng])
        if dma:
            j = self.ndma
            self.ndma += 1
            sk = ("dma", j % self.NDMA)
            val = 16 * (j // self.NDMA + 1)
            ev = (sk, val)
            self.dma_events.append(ev)
            inc = 16
            if is_out:
                self.out_events.append(ev)
        else:
            i = self.cnt[eng]
            self.cnt[eng] += 1
            ep = i // self.EPOCH
            sk = (eng, ep)
            ev = (sk, i % self.EPOCH + 1)
            inc = 1
            self.last_ev[eng] = ev
            for e2 in range(ep):
                clk[(eng, e2)] = self.EPOCH
        clk[sk] = max(clk.get(sk, 0), ev[1])
        self.evclock[ev] = clk
        self.evidx[ev] = self.nev
        self.nev += 1
        self.semkeys.add(sk)
        self.ops[eng].append((waits, fn, ev, inc))
        for k in r:
            self.readers.setdefault(k, []).append(ev)
        for k in w:
            self.last_w[k] = ev
            self.readers[k] = []
        return ev

    def finish(self):
        cands = set(self.dma_events[-self.NDMA:]) | set(self.out_events)
        waits = self._resolve("sp", cands)
        self.ops["sp"].append((waits, None, None, 0))

    def emit(self, nc, stack):
        targets = set()
        for e in ENGS:
            for waits, fn, ev, inc in self.ops[e]:
                targets.update(waits)
        real = {}
        used = set()
        for e in ENGS:
            cnt = {}
            for waits, fn, ev, inc in self.ops[e]:
                if ev is None:
                    continue
                if ev[0][0] == "dma":
                    real[ev] = ev[1]
                    used.add(ev[0])
                elif ev in targets:
                    cnt[ev[0]] = cnt.get(ev[0], 0) + 1
                    real[ev] = cnt[ev[0]]
                    used.add(ev[0])
        sems = {}
        for sk in sorted(used, key=str):
            sems[sk] = stack.enter_context(nc.semaphore("s_" + "_".join(str(x) for x in sk)))
        block = stack.enter_context(nc.Block())
        prog = self

        def run(engname):
            def body(eng):
                for waits, fn, ev, inc in prog.ops[engname]:
                    if fn is None:
                        for w_ in waits:
                            eng.wait_ge(sems[w_[0]], real[w_])
                        continue
                    for w_ in waits[:-1]:
                        eng.wait_ge(sems[w_[0]], real[w_])
                    ins = fn(eng)
                    if waits:
                        ins._wait_ge(sems[waits[-1][0]], real[waits[-1]])
                    if ev in real:
                        ins.then_inc(sems[ev[0]], inc)
            return body

        block.tensor(run("pe"))
        block.scalar(run("act"))
        block.vector(run("dve"))
        block.gpsimd(run("pool"))
        block.sync(run("sp"))

    def act(self, out, in_, func, r, w, bias=0.0, scale=1.0, accum=None):
        if accum is None:
            return self.add("act", lambda e: e.activation(out=out, in_=in_, func=func, bias=bias, scale=scale), r, w)
        return self.add("act", lambda e: e.activation(out=out, in_=in_, func=func, bias=bias, scale=scale,
                                                      accum_out=accum), r, w)

    def ts(self, eng, out, in0, s1, s2, op0, op1, r, w):
        return self.add(eng, lambda e: e.tensor_scalar(out=out, in0=in0, scalar1=s1, scalar2=s2, op0=op0, op1=op1), r, w)

    def stt(self, out, in0, scalar, in1, op0, op1, r, w, accum=None):
        if accum is None:
            return self.add("dve", lambda e: e.scalar_tensor_tensor(out=out, in0=in0, scalar=scalar, in1=in1,
                                                                     op0=op0, op1=op1), r, w)
        return self.add("dve", lambda e: e.scalar_tensor_tensor(out=out, in0=in0, scalar=scalar, in1=in1,
                                                                 op0=op0, op1=op1, accum_out=accum), r, w)

    def tt(self, eng, out, in0, in1, op, r, w):
        return self.add(eng, lambda e: e.tensor_tensor(out=out, in0=in0, in1=in1, op=op), r, w)

    def cp(self, eng, out, in_, r, w):
        if eng == "act":
            return self.add("act", lambda e: e.activation(out=out, in_=in_, func=AF.Copy), r, w)
        return self.add(eng, lambda e: e.tensor_copy(out=out, in_=in_), r, w)

    def memset(self, eng, out, val, w):
        return self.add(eng, lambda e: e.memset(out, val), (), w)

    def mm(self, out, lhsT, rhs, start, stop, r, w):
        return self.add("pe", lambda e: e.matmul(out, lhsT=lhsT, rhs=rhs, start=start, stop=stop), r, w)

    def tr(self, out, in_, ident, r, w):
        return self.add("pe", lambda e: e.transpose(out, in_, ident), r, w)

    def dma(self, out, in_, r, w, is_out=False):
        return self.add("sp", lambda e: e.dma_start(out=out, in_=in_), r, w, dma=True, is_out=is_out)

    def scan(self, out, d0, d1, r, w):
        return self.add("dve", lambda e: e.tensor_tensor_scan(out=out, data0=d0, data1=d1, initial=0.0,
                                                               op0=ALU.mult, op1=ALU.add), r, w)


class Arena:
    def __init__(self, t, words):
        self.t = t
        self.words = words
        self.off = 0
        self.marks = []
        self.peak = 0

    def alloc(self, nwords):
        o = self.off
        self.off += nwords
        self.peak = max(self.peak, self.off)
        assert self.off <= self.words, f"SBUF arena overflow {self.off}>{self.words}"
        return o

    def f32(self, n, parts=128):
        o = self.alloc(n)
        return self.t[0:parts, o:o + n]

    def bf16(self, n, parts=128):
        assert n % 2 == 0
        o = self.alloc(n // 2)
        return self.t[0:parts, o:o + n // 2].bitcast(BF16)

    def mark(self):
        self.marks.append(self.off)

    def release(self):
        self.off = self.marks.pop()


def v3(ap, b):
    return ap.rearrange("p (a b) -> p a b", b=b)


def build_nc(phases="0GRF"):
    nc = bass.Bass("TRN2", target_bir_lowering=False)
    di = lambda n, s: nc.dram_tensor(n, list(s), F32, kind="ExternalInput").ap()
    do = lambda n, s: nc.dram_tensor(n, list(s), F32, kind="ExternalOutput").ap()
    x = di("x", (TA, D))
    w_in = di("w_in", (D, NIN))
    alpha_w2 = di("alpha_w2", (16, 512))
    w2a2 = di("w2a2", (128, 1024))
    w_up_gla = di("w_up_gla", (D, D))
    w_up_rwkv = di("w_up_rwkv", (D, D))
    w_out = di("w_out", (D, D))
    lngb = di("lngb", (128, 2048))
    sgla = di("sgla", (NS, 4, 128, 256))
    srwkv = di("srwkv", (NS, 16, 64, 64))
    sshift = di("sshift", (NS, 4224))
    cvec = di("cvec", (128, NCV))
    cmat = di("cmat", (128, NCM))
    y = do("y", (TA, D))
    gla_p = do("gla_p", (4, 128, 256))
    rwkv_p = do("rwkv_p", (16, 64, 64))
    shift_o = do("shift_o", (17, 4224))
    gla_s = do("gla_s", (NS, 4, 128, 256))
    rwkv_s = do("rwkv_s", (NS, 16, 64, 64))

    P = Prog()
    with contextlib.ExitStack() as st:
        WORDS = 53184
        sb = st.enter_context(nc.sbuf_tensor("arena", [128, WORDS], F32))
        ar = Arena(sb, WORDS)
        banks = [st.enter_context(nc.psum_tensor(f"ps{i}", [128, 512], F32)) for i in range(8)]
        bk = lambda i: f"bank{i}"

        CV = ar.f32(NCV)
        CM = ar.f32(NCM)
        P.dma(CV, cvec, [], ["CV"])
        P.dma(CM, cmat, [], ["CM"])
        ID = CM[:, CM_ID:CM_ID + 128]
        MASK5 = CM[:, CM_MASK5:CM_MASK5 + 320]
        MASKU = CM[:, CM_MASKU:CM_MASKU + 128]
        ISTK = CM[:, CM_ISTK:CM_ISTK + 64]
        BONES = CM[:, CM_BONES:CM_BONES + 128]
        BAVG = CM[:, CM_BAVG:CM_BAVG + 128]
        ONES = CM[:, CM_ONES:CM_ONES + 128]
        SMG = CM[:, CM_SMG:CM_SMG + 512]
        SMR = CM[:, CM_SMR:CM_SMR + 512]
        ID16 = CM[0:16, CM_ID16:CM_ID16 + 16]
        IDb = ar.bf16(128)
        ISTKb = ar.bf16(64)
        BONESb = ar.bf16(128)
        ONESb = ar.bf16(128)
        NAB = ar.f32(4)
        EPSC = ar.f32(2)
        P.cp("pool", IDb, ID, ["CM"], ["IDb"])
        P.cp("pool", ISTKb, ISTK, ["CM"], ["ISTKb"])
        P.cp("pool", BONESb, BONES, ["CM"], ["BONESb"])
        P.cp("pool", ONESb, ONES, ["CM"], ["ONESb"])
        P.ts("pool", NAB, CV[:, CV_AB:CV_AB + 4], -1.0, None, ALU.mult, ALU.bypass, ["CV"], ["NAB"])
        P.memset("pool", EPSC[:, 0:1], 1e-5, ["EPSC"])
        P.memset("pool", EPSC[:, 1:2], 64e-5, ["EPSC"])
        cvc = lambda base, j: CV[:, base + j:base + j + 1]

        W2A2 = ar.f32(1024)
        P.dma(W2A2, w2a2, [], ["W2A2"])

        xT = v3(ar.bf16(KC * TA), TA)
        OgT = v3(ar.bf16(KC * TA), TA)
        OrT = v3(ar.bf16(KC * TA), TA)

        WST = v3(ar.f32(4 * 512), 512)
        WBF = [v3(ar.bf16(KC * 512), 512) for _ in range(3)]
        wstate = {"n": 0, "pre": None}
        wqueue = []

        def _issue_load(srcs):
            en = P.enabled
            P.enabled = True
            par = wstate["n"] % 3
            wstate["n"] += 1
            key = f"WBF{par}"
            for half in range(2):
                off = 0
                for s_ in srcs:
                    n = s_.shape[1]
                    P.dma(WST[:, :, off:off + n],
                          s_.rearrange("(kc p) n -> p kc n", p=128)[:, 4 * half:4 * half + 4, :], [], ["WST"])
                    off += n
                for k2 in range(2):
                    P.cp("pool", WBF[par][:, 4 * half + 2 * k2:4 * half + 2 * k2 + 2, 0:off],
                         WST[:, 2 * k2:2 * k2 + 2, 0:off], ["WST"], [key])
            P.enabled = en
            return WBF[par], key

        def load_wgroup(srcs=None, defer=False):
            flush_prefetch()
            if wstate["pre"] is None:
                wstate["pre"] = _issue_load(wqueue.pop(0))
            cur = wstate["pre"]
            wstate["pre"] = None
            wstate["pend"] = bool(wqueue)
            if not defer:
                flush_prefetch()
            return cur

        def flush_prefetch():
            if wstate.get("pend"):
                wstate["pend"] = False
                wstate["pre"] = _issue_load(wqueue.pop(0))

        wqueue.append([w_in[:, GA:GA + 16]])
        for h_ in range(4):
            wqueue.append([w_in[:, GQ + h_ * 128:GQ + (h_ + 1) * 128], w_in[:, GK + h_ * 128:GK + (h_ + 1) * 128],
                           w_in[:, GV + h_ * 256:GV + (h_ + 1) * 256]])
            wqueue.append([w_in[:, GG + h_ * 256:GG + (h_ + 1) * 256]])
        wqueue.append([w_in[:, RW:RW + 128]])
        for p_ in range(8):
            wqueue.append([w_in[:, RR + p_ * 128:RR + (p_ + 1) * 128], w_in[:, RK + p_ * 128:RK + (p_ + 1) * 128],
                           w_in[:, RV + p_ * 128:RV + (p_ + 1) * 128], w_in[:, RG + p_ * 128:RG + (p_ + 1) * 128]])
        for dt_ in range(8):
            wqueue.append([w_in[:, G0 + dt_ * 128:G0 + (dt_ + 1) * 128],
                           w_in[:, G0 + 1024 + dt_ * 128:G0 + 1024 + (dt_ + 1) * 128],
                           w_up_gla[:, dt_ * 128:(dt_ + 1) * 128], w_up_rwkv[:, dt_ * 128:(dt_ + 1) * 128]])
        for g_ in range(2):
            wqueue.append([w_out[:, g_ * 512:(g_ + 1) * 512]])

        def proj_fm(wb, wkey, coff, ncols, s0, W, bank_i, c0=0):
            for kc in range(KC):
                P.mm(banks[bank_i][0:ncols, c0:c0 + W], wb[:, kc, coff:coff + ncols], xT[:, kc, s0:s0 + W],
                     kc == 0, kc == KC - 1, [wkey, "xT"], [bk(bank_i)])

        def proj_tm(wb, wkey, coff, ncols, t0, M, bank_i, c0=0):
            for kc in range(KC):
                P.mm(banks[bank_i][0:M, c0:c0 + ncols], xT[:, kc, t0:t0 + M], wb[:, kc, coff:coff + ncols],
                     kc == 0, kc == KC - 1, [wkey, "xT"], [bk(bank_i)])

        wstate["pre"] = _issue_load(wqueue.pop(0))
        P.enabled = "0" in phases
        ar.mark()
        XS = [ar.f32(1024) for _ in range(4)]
        for tt in range(17):
            rows = 128 if tt < 16 else NS
            xs = XS[tt % 4]
            xk = f"XS{tt % 4}"
            P.dma(xs[0:rows, :], x[tt * 128:tt * 128 + rows, :], [], [xk])
            for half in range(2):
                b = banks[half]
                for j in range(4):
                    kc = half * 4 + j
                    P.tr(b[:, j * 128:j * 128 + rows], xs[0:rows, kc * 128:(kc + 1) * 128], ID[0:rows, 0:rows],
                         [xk, "CM"], [bk(half)])
                src = v3(b[:, 0:512], 128)[:, :, 0:rows]
                dst = xT[:, half * 4:half * 4 + 4, tt * 128:tt * 128 + rows]
                P.cp("act" if half == 0 else "dve", dst, src, [bk(half)], ["xT"])
        ar.release()
        P.fence()

        P.enabled = "G" in phases
        ar.mark()
        AW2 = ar.f32(512, parts=16)
        P.dma(AW2, alpha_w2, [], ["AW2"])
        ALR = ar.f32(TA, parts=16)
        SP = ar.f32(512); CSP = ar.f32(512); EB = ar.f32(512); EINV = ar.f32(512)
        QT = ar.bf16(512); KT = ar.bf16(512); QS = ar.f32(16)
        KH = ar.bf16(128); KHT = ar.bf16(128)
        VTK = v3(ar.bf16(4 * 256), 256)
        ATS = ar.bf16(128)
        SG = [ar.f32(256) for _ in range(2)]
        SGB = ar.bf16(256)
        GS = v3(ar.bf16(2 * 512), 512)
        SQ = v3(ar.bf16(2 * 512), 512)
        RSTD = ar.f32(512)
        ON = ar.f32(512)
        KTOK = ar.f32(128, parts=16); VTOK = ar.f32(256, parts=16)
        KM = v3(ar.f32(16 * 128, parts=16), 128)
        SSb = [v3(ar.f32(4 * 256), 256) for _ in range(2)]
        SNb = [v3(ar.f32(4 * 256), 256) for _ in range(2)]

        wb, wk = load_wgroup([w_in[:, GA:GA + 16]])
        for (s0, W) in SEGS:
            proj_fm(wb, wk, 0, 16, s0, W, 0)
            P.cp("act", ALR[:, s0:s0 + W], banks[0][0:16, 0:W], [bk(0)], ["ALR"])

        for h in range(4):
            wqkv, kqkv = load_wgroup([w_in[:, GQ + h * 128:GQ + (h + 1) * 128],
                                      w_in[:, GK + h * 128:GK + (h + 1) * 128],
                                      w_in[:, GV + h * 256:GV + (h + 1) * 256]])
            wg, kg = load_wgroup(defer=True)
            cur = 0
            P.memset("pool", SG[0], 0.0, ["SG0"])
            P.memset("pool", SGB, 0.0, ["SGB"])
            for si, (s0, W) in enumerate(SEGS):
                samp = si == 4
                if si == 1:
                    flush_prefetch()
                P.mm(banks[2][:, 0:W], AW2[:, h * 128:(h + 1) * 128], ALR[:, s0:s0 + W], True, True,
                     ["AW2", "ALR"], [bk(2)])
                P.act(SP[:, 0:W], banks[2][:, 0:W], AF.Exp, [bk(2), "NAB"], ["SP"], bias=NAB[:, h:h + 1], scale=-1.0)
                P.act(SP[:, 0:W], SP[:, 0:W], AF.Ln, ["SP"], ["SP"], bias=1.0)
                if samp:
                    csp = SP
                else:
                    P.scan(CSP[:, 0:W], SMG[:, 0:W], SP[:, 0:W], ["CM", "SP"], ["CSP"])
                    csp = CSP
                ck = "SP" if samp else "CSP"
                P.act(EB[:, 0:W], csp[:, 0:W], AF.Exp, [ck], ["EB"], scale=-1.0 / 16.0)
                P.act(EINV[:, 0:W], csp[:, 0:W], AF.Exp, [ck], ["EINV"], scale=1.0 / 16.0)
                proj_fm(wqkv, kqkv, 0, 128, s0, W, 0)
                if samp:
                    P.ts("dve", QS[:, 0:W], banks[0][:, 0:W], 128.0 ** -0.5, None, ALU.mult, ALU.bypass,
                         [bk(0)], ["QS"])
                else:
                    P.stt(QT[:, 0:W], banks[0][:, 0:W], 128.0 ** -0.5, EB[:, 0:W], ALU.mult, ALU.mult,
                          [bk(0), "EB"], ["QT"])
                    proj_fm(wqkv, kqkv, 128, 128, s0, W, 1)
                    P.tt("dve", KT[:, 0:W], banks[1][:, 0:W], EINV[:, 0:W], ALU.mult, [bk(1), "EINV"], ["KT"])
                if not samp:
                    for t4 in range(4):
                        bi = t4 % 2
                        proj_tm(wqkv, kqkv, 256, 256, s0 + t4 * 128, 128, bi)
                        P.cp("act", VTK[:, t4, :], banks[bi][:, 0:256], [bk(bi)], ["VTK"])
                    for vh in range(2):
                        proj_fm(wg, kg, vh * 128, 128, s0, W, vh)
                        P.act(GS[:, vh, 0:W], banks[vh][:, 0:W], AF.Silu, [bk(vh)], ["GS"])
                    for c in range(4):
                        cs = slice(c * 128, (c + 1) * 128)
                        ecol = EB[:, c * 128 + 127:c * 128 + 128]
                        first = (si == 0 and c == 0)
                        P.mm(banks[4][:, 0:128], KT[:, cs], QT[:, cs], True, True, ["KT", "QT"], [bk(4)])
                        P.tt("dve", ATS, banks[4][:, 0:128], MASKU, ALU.mult, [bk(4), "CM"], ["ATS"])
                        P.ts("pool", KH, KT[:, cs], ecol, None, ALU.mult, ALU.bypass, ["KT", "EB"], ["KH"])
                        b4b = banks[4].bitcast(BF16)
                        P.tr(b4b[:, 512:640], KH, IDb, ["KH", "IDb"], [bk(4)])
                        P.cp("act", KHT, b4b[:, 512:640], [bk(4)], ["KHT"])
                        for vh in range(2):
                            vs = slice(vh * 128, (vh + 1) * 128)
                            P.mm(banks[5][:, vh * 128:(vh + 1) * 128], VTK[:, c, vs], ATS, True, first,
                                 ["VTK", "ATS"], [bk(5)])
                            if not first:
                                P.mm(banks[5][:, vh * 128:(vh + 1) * 128], SGB[:, vs], QT[:, cs], False, True,
                                     ["SGB", "QT"], [bk(5)])
                        P.cp("act", OgT[:, 2 * h:2 * h + 2, s0 + c * 128:s0 + (c + 1) * 128],
                             v3(banks[5][:, 0:256], 128), [bk(5)], ["OgT"])
                        P.mm(banks[6][:, 0:256], KHT, VTK[:, c, :], True, True, ["KHT", "VTK"], [bk(6)])
                        nxt = 1 - cur
                        P.stt(SG[nxt], SG[cur], ecol, banks[6][:, 0:256], ALU.mult, ALU.add,
                              [f"SG{cur}", "EB", bk(6)], [f"SG{nxt}"])
                        P.cp("act", SGB, SG[nxt], [f"SG{nxt}"], ["SGB"])
                        cur = nxt
                    if si == 3:
                        P.dma(gla_p[h], SG[cur], [f"SG{cur}"], [], is_out=True)
                else:
                    for vh in range(2):
                        proj_fm(wg, kg, vh * 128, 128, s0, W, vh)
                        P.act(GS[:, vh, 0:W], banks[vh][:, 0:W], AF.Silu, [bk(vh)], ["GS"])
                    proj_tm(wqkv, kqkv, 128, 128, T, NS, 4)
                    P.cp("act", KTOK, banks[4][0:16, 0:128], [bk(4)], ["KTOK"])
                    proj_tm(wqkv, kqkv, 256, 256, T, NS, 4)
                    P.cp("act", VTOK, banks[4][0:16, 0:256], [bk(4)], ["VTOK"])
                    P.add("dve", lambda e: e.tensor_tensor(
                        out=KM, in0=KTOK.unsqueeze(1).to_broadcast([16, 16, 128]),
                        in1=ID16.unsqueeze(2).to_broadcast([16, 16, 128]), op=ALU.mult),
                        ["KTOK", "CM"], ["KM"])
                    P.dma(SSb[0], sgla[0:4, h].rearrange("s d v -> d s v"), [], ["SS0"])
                    for sg in range(4):
                        SS, SN = SSb[sg % 2], SNb[sg % 2]
                        kSS, kSN = f"SS{sg % 2}", f"SN{sg % 2}"
                        if sg + 1 < 4:
                            P.dma(SSb[(sg + 1) % 2], sgla[(sg + 1) * 4:(sg + 2) * 4, h].rearrange("s d v -> d s v"),
                                  [], [f"SS{(sg + 1) % 2}"])
                        for j in range(4):
                            s = sg * 4 + j
                            P.add("pe", lambda e, s=s: e.matmul(banks[6][:, 0:256], lhsT=KM[:, s, :], rhs=VTOK,
                                                                start=True, stop=True), ["KM", "VTOK"], [bk(6)])
                            P.stt(SN[:, j, :], SS[:, j, :], EB[:, s:s + 1], banks[6][:, 0:256], ALU.mult, ALU.add,
                                  [kSS, "EB", bk(6)], [kSN])
                            for vh in range(2):
                                P.mm(banks[5][:, vh * 16 + s:vh * 16 + s + 1], SN[:, j, vh * 128:(vh + 1) * 128],
                                     QS[:, s:s + 1], True, True, [kSN, "QS"], [bk(5)])
                        P.dma(gla_s[sg * 4:(sg + 1) * 4, h].rearrange("s d v -> d s v"), SN, [kSN], [], is_out=True)
                    P.cp("act", OgT[:, 2 * h:2 * h + 2, T:TA], v3(banks[5][:, 0:32], 16), [bk(5)], ["OgT"])
                og = OgT[:, 2 * h:2 * h + 2, s0:s0 + W]
                P.act(SQ[:, :, 0:W], og, AF.Square, ["OgT"], ["SQ"])
                for vh in range(2):
                    P.mm(banks[3][:, 0:W], ONESb, SQ[:, vh, 0:W], vh == 0, vh == 1, ["ONESb", "SQ"], [bk(3)])
                P.act(RSTD[:, 0:W], banks[3][:, 0:W], AF.Ln, [bk(3), "EPSC"], ["RSTD"], bias=EPSC[:, 0:1], scale=1.0 / 256.0)
                P.act(RSTD[:, 0:W], RSTD[:, 0:W], AF.Exp, ["RSTD"], ["RSTD"], scale=-0.5)
                for vh in range(2):
                    P.stt(ON[:, 0:W], OgT[:, 2 * h + vh, s0:s0 + W], cvc(CV_GNW, vh), RSTD[:, 0:W],
                          ALU.mult, ALU.mult, ["OgT", "CV", "RSTD"], ["ON"])
                    P.tt("pool", OgT[:, 2 * h + vh, s0:s0 + W], ON[:, 0:W], GS[:, vh, 0:W], ALU.mult,
                         ["ON", "GS"], ["OgT"])
        ar.release()
        P.fence()

        P.enabled = "R" in phases
        ar.mark()
        SHT = v3(ar.f32(33 * 16), 16)
        WSTf = WST.rearrange("p a b -> p (a b)")
        for tb in (0, 11, 22):
            P.dma(WSTf[0:16, 0:11 * 128], sshift[:, tb * 128:(tb + 11) * 128], [], ["WST"])
            for j in range(11):
                P.tr(banks[2][:, j * 16:(j + 1) * 16], WSTf[0:16, j * 128:(j + 1) * 128], ID[0:16, 0:16],
                     ["WST", "CM"], [bk(2)])
            P.cp("act", SHT[:, tb:tb + 11, :], v3(banks[2][:, 0:176], 16), [bk(2)], ["SHT"])

        LORA = ar.f32(TA)
        RAW = [ar.f32(513) for _ in range(4)]
        DQ = ar.f32(512)
        BB = ar.f32(9 * 512)
        Bp = lambda i: BB[:, i * 512:(i + 1) * 512]
        Rl, kRl = Bp(0), "B0"
        Kl, kKl = Bp(1), "B1"
        Gl, kGl = Bp(2), "B2"
        SIG, kSIG = Bp(3), "B3"
        Aa, kAa = Bp(4), "B4"
        KKr, kKKr = Bp(5), "B5"
        RN, kRN = Bp(6), "B6"
        KKn, kKKn = Bp(7), "B7"
        T1, kT1 = Bp(2), "B2"
        Kp, kKp = Bp(8), "B8"
        RKf, kRKf = Bp(5), "B5"
        BA, kBA = Bp(1), "B1"
        CS, kCS = Bp(4), "B4"
        GAM, kGAM = Bp(5), "B5"
        GIN, kGIN = Bp(6), "B6"
        CSX, kCSX = Bp(2), "B2"
        GEX, kGEX = Bp(3), "B3"
        YS, kYS = Bp(0), "B0"
        YSQ, kYSQ = Bp(1), "B1"
        MU2, kMU2 = Bp(2), "B2"
        VAR, kVAR = Bp(3), "B3"
        YD, kYD = Bp(4), "B4"
        SHO, kSHO = Bp(0)[0:17, :], "B0"
        SR, kSR = v3(BB[:, 5 * 512:7 * 512], 64), ["B5", "B6"]
        SRN, kSRN = v3(BB[:, 2 * 512:4 * 512], 64), ["B2", "B3"]
        SQb = ar.bf16(512)
        OUT = []
        for o_ in range(2):
            arkb = ar.bf16(2048)
            OUT.append(dict(
                ARKB=arkb, AR=arkb[:, 0:1024], AR4=arkb[:, 0:1024].rearrange("p (c q t) -> p c q t", q=2, t=64),
                Kt=arkb[:, 1024:1536], Bt=arkb[:, 1536:2048],
                Vb=ar.bf16(512), Gs=ar.bf16(512), BON=ar.bf16(512), GAMC=ar.f32(8),
                kAR=f"AR{o_}", kKt=f"Kt{o_}", kBt=f"Bt{o_}", kVb=f"Vb{o_}", kGs=f"Gs{o_}", kBON=f"BON{o_}",
                kGAMC=f"GAMC{o_}"))
        DG = OUT[0]["ARKB"].bitcast(F32)[:, 0:640].rearrange("p (s q k) -> p s q k", q=5, k=64)
        kDG = ["AR0", "Kt0"]
        Vf = ar.f32(16)
        CB = [(ar.bf16(320), ar.bf16(320), [ar.bf16(256) for _ in range(2)], ar.bf16(64), ar.f32(64), ar.bf16(64))
              for _ in range(3)]
        Hb = ar.bf16(64); Hf = ar.f32(64)
        XQ = v3(ar.f32(16 * 5), 5)
        TJ = ar.f32(64); T2 = ar.f32(64); SA = ar.f32(2)
        B4K = [bk(4), "b4tm", "b4s"]
        B5K = [bk(5), "b5z", "b5g", "b5r", "b5zv"]

        def lerp(q, ftile, src_bank, W, samp, out_ap, okey):
            raw = RAW[q]
            rk = f"RAW{q}"
            P.cp("act", raw[:, 1:1 + W], banks[src_bank][:, 0:W], [bk(src_bank)], [rk])
            if samp:
                P.tt("pool", DQ[:, 0:W], SHT[:, ftile, :], raw[:, 1:1 + W], ALU.subtract, ["SHT", rk], ["DQ"])
            else:
                P.tt("pool", DQ[:, 0:W], raw[:, 0:W], raw[:, 1:1 + W], ALU.subtract, [rk], ["DQ"])
            P.stt(out_ap, DQ[:, 0:W], cvc(CV_MU, ftile), raw[:, 1:1 + W], ALU.mult, ALU.add, ["DQ", "CV", rk], [okey])
            if not samp:
                P.cp("pool", raw[:, 0:1], raw[:, W:W + 1], [rk], [rk])

        wl, kl = load_wgroup([w_in[:, RW:RW + 128]])
        P.memset("pool", RAW[0][:, 0:1], 0.0, ["RAW0"])
        for si, (s0, W) in enumerate(SEGS):
            proj_fm(wl, kl, 0, 128, s0, W, 0)
            lerp(0, 32, 0, W, si == 4, LORA[:, s0:s0 + W], "LORA")
        P.act(LORA[0:64, :], LORA[0:64, :], AF.Tanh, ["LORA"], ["LORA"])
        proj_tm(wl, kl, 0, 128, T - 1, 17, 2)
        P.cp("act", SHO[:, 0:128], banks[2][0:17, 0:128], [bk(2)], [kSHO])
        P.dma(shift_o[:, 4096:4224], SHO[:, 0:128], [kSHO], [], is_out=True)

        def gn_post(p, s0, W, O):
            P.act(YSQ[:, 0:W], YS[:, 0:W], AF.Square, [kYS], [kYSQ])
            P.mm(banks[2][:, 0:W], BAVG, YS[:, 0:W], True, True, ["CM", kYS], [bk(2)])
            P.mm(banks[3][:, 0:W], BAVG, YSQ[:, 0:W], True, True, ["CM", kYSQ], [bk(3)])
            P.act(MU2[:, 0:W], banks[2][:, 0:W], AF.Square, [bk(2)], [kMU2])
            P.tt("dve", VAR[:, 0:W], banks[3][:, 0:W], MU2[:, 0:W], ALU.subtract, [bk(3), kMU2], [kVAR])
            P.act(VAR[:, 0:W], VAR[:, 0:W], AF.Ln, [kVAR], [kVAR], bias=EPSC[:, 1:2])
            P.act(VAR[:, 0:W], VAR[:, 0:W], AF.Exp, [kVAR], [kVAR], scale=-0.5)
            P.tt("dve", YD[:, 0:W], YS[:, 0:W], banks[2][:, 0:W], ALU.subtract, [kYS, bk(2)], [kYD])
            P.tt("pool", YD[:, 0:W], YD[:, 0:W], VAR[:, 0:W], ALU.mult, [kYD, kVAR], [kYD])
            P.ts("dve", YD[:, 0:W], YD[:, 0:W], cvc(CV_LW, p), cvc(CV_LB, p), ALU.mult, ALU.add, [kYD, "CV"], [kYD])
            P.tt("pool", YD[:, 0:W], YD[:, 0:W], O["BON"][:, 0:W], ALU.add, [kYD, O["kBON"]], [kYD])
            P.tt("dve", OrT[:, p, s0:s0 + W], YD[:, 0:W], O["Gs"][:, 0:W], ALU.mult, [kYD, O["kGs"]], ["OrT"])

        def mm2(out, lhsT, rhs, n0, n1, l0, l1, r0, r1, start, stop, r, w):
            for hh in range(2):
                ps = slice(hh * 64, (hh + 1) * 64)
                P.mm(out[ps, n0:n1], lhsT[ps, l0:l1], rhs[ps, r0:r1], start, stop, r, w)

        if RLEVEL < 2:
            P.enabled = False
        P.counting = True
        def elem_gen(p, wp, kp, si, O):
            s0, W = SEGS[si]
            samp = si == 4
            Vb, Gs, BON = O["Vb"], O["Gs"], O["BON"]
            AR4, Kt, Bt = O["AR4"], O["Kt"], O["Bt"]
            proj_fm(wp, kp, 0, 128, s0, W, 2)
            lerp(0, p, 2, W, samp, Rl[:, 0:W], kRl)
            yield
            proj_fm(wp, kp, 128, 128, s0, W, 2)
            lerp(1, 8 + p, 2, W, samp, Kl[:, 0:W], kKl)
            yield
            proj_fm(wp, kp, 256, 128, s0, W, 2)
            lerp(2, 16 + p, 2, W, samp, Vb[:, 0:W], O["kVb"])
            if samp:
                P.stt(Vf[:, 0:W], DQ[:, 0:W], cvc(CV_MU, 16 + p), RAW[2][:, 1:1 + W], ALU.mult, ALU.add,
                      ["DQ", "CV", "RAW2"], ["Vf"])
            yield
            proj_fm(wp, kp, 384, 128, s0, W, 2)
            lerp(3, 24 + p, 2, W, samp, Gl[:, 0:W], kGl)
            P.act(Gs[:, 0:W], Gl[:, 0:W], AF.Silu, [kGl], [O["kGs"]])
            yield
            P.mm(banks[2][:, 0:W], W2A2[0:64, p * 128:(p + 1) * 128], LORA[0:64, s0:s0 + W], True, True,
                 ["W2A2", "LORA"], [bk(2)])
            P.act(SIG[:, 0:W], banks[2][:, 0:W], AF.Sigmoid, [bk(2), "CV"], [kSIG], bias=cvc(CV_W0, p))
            yield
            P.mm(banks[2][:, 0:W], W2A2[64:128, p * 128:(p + 1) * 128], LORA[64:128, s0:s0 + W], True, True,
                 ["W2A2", "LORA"], [bk(2)])
            P.act(Aa[:, 0:W], banks[2][:, 0:W], AF.Sigmoid, [bk(2), "CV"], [kAa], bias=cvc(CV_A0, p))
            yield
            P.ts("pool", KKr[:, 0:W], Kl[:, 0:W], cvc(CV_KK, p), None, ALU.mult, ALU.bypass, [kKl, "CV"], [kKKr])
            P.act(SQb[:, 0:W], KKr[:, 0:W], AF.Square, [kKKr], ["SQb"])
            P.mm(banks[2][:, 0:W], BONESb, SQb[:, 0:W], True, True, ["BONESb", "SQb"], [bk(2)])
            yield
            P.act(RN[:, 0:W], banks[2][:, 0:W], AF.Ln, [bk(2)], [kRN])
            P.act(RN[:, 0:W], RN[:, 0:W], AF.Exp, [kRN], [kRN], scale=-0.5)
            yield
            P.tt("pool", KKn[:, 0:W], KKr[:, 0:W], RN[:, 0:W], ALU.mult, [kKKr, kRN], [kKKn])
            P.ts("dve", T1[:, 0:W], Aa[:, 0:W], -1.0, cvc(CV_KA, p), ALU.add, ALU.mult, [kAa, "CV"], [kT1])
            P.stt(Kp[:, 0:W], T1[:, 0:W], 1.0, Kl[:, 0:W], ALU.add, ALU.mult, [kT1, kKl], [kKp])
            yield
            P.tt("pool", RKf[:, 0:W], Rl[:, 0:W], Kp[:, 0:W], ALU.mult, [kRl, kKp], [kRKf])
            P.ts("pool", RKf[:, 0:W], RKf[:, 0:W], cvc(CV_RK, p), None, ALU.mult, ALU.bypass, [kRKf, "CV"], [kRKf])
            yield
            P.mm(banks[2][:, 0:W], BONES, RKf[:, 0:W], True, True, ["CM", kRKf], [bk(2)])
            P.tt("dve", BON[:, 0:W], banks[2][:, 0:W], Vb[:, 0:W], ALU.mult, [bk(2), O["kVb"]], [O["kBON"]])
            P.tt("pool", BA[:, 0:W], KKn[:, 0:W], Aa[:, 0:W], ALU.mult, [kKKn, kAa], [kBA])
            yield
            if not samp:
                P.scan(CS[:, 0:W], SMR[:, 0:W], SIG[:, 0:W], ["CM", kSIG], [kCS])
                P.act(GAM[:, 0:W], CS[:, 0:W], AF.Exp, [kCS], [kGAM], scale=-DECAY)
                P.act(GIN[:, 0:W], CS[:, 0:W], AF.Exp, [kCS], [kGIN], scale=DECAY)
                yield
                P.tt("pool", CSX[:, 0:W], CS[:, 0:W], SIG[:, 0:W], ALU.subtract, [kCS, kSIG], [kCSX])
                P.act(GEX[:, 0:W], CSX[:, 0:W], AF.Exp, [kCSX], [kGEX], scale=-DECAY)
                P.cp("pool", O["GAMC"], v3(GAM[:, 0:W], 64)[:, :, 63], [kGAM], [O["kGAMC"]])
                yield
                v64 = lambda a_: v3(a_[:, 0:W], 64)
                P.stt(AR4[:, :, 0, :], v64(KKn), -1.0, v64(GEX), ALU.mult, ALU.mult, [kKKn, kGEX], [O["kAR"]])
                P.tt("dve", AR4[:, :, 1, :], v64(Rl), v64(GAM), ALU.mult, [kRl, kGAM], [O["kAR"]])
                yield
                P.tt("pool", Kt[:, 0:W], Kp[:, 0:W], GIN[:, 0:W], ALU.mult, [kKp, kGIN], [O["kKt"]])
                P.tt("pool", Bt[:, 0:W], BA[:, 0:W], GIN[:, 0:W], ALU.mult, [kBA, kGIN], [O["kBt"]])
            else:
                P.ts("pool", XQ[:, :, 0], KKn[:, 0:W], -1.0, None, ALU.mult, ALU.bypass, [kKKn], ["XQ"])
                P.act(XQ[:, :, 1], SIG[:, 0:W], AF.Exp, [kSIG], ["XQ"], scale=-DECAY)
                P.cp("pool", XQ[:, :, 2], BA[:, 0:W], [kBA], ["XQ"])
                P.cp("pool", XQ[:, :, 3], Kp[:, 0:W], [kKp], ["XQ"])
                P.cp("pool", XQ[:, :, 4], Rl[:, 0:W], [kRl], ["XQ"])

        def chunk_gen(c, par, si, O):
            tb, db = [(4, 5), (6, 7), (0, 1)][par]
            bT, bD = banks[tb], banks[db]
            bTb = bT.bitcast(BF16)
            kT, kD = bk(tb), bk(db)
            TMp, SCp, ZPQp, GA1p, GVGp, RHp = CB[par]
            kTM, kSC, kGA1, kGVG, kRH = f"TM{par}", f"SC{par}", f"GA1{par}", f"GVG{par}", f"RH{par}"
            AR, AR4, Kt, Bt, Vb = O["AR"], O["AR4"], O["Kt"], O["Bt"], O["Vb"]
            kAR, kKt, kBt, kVb = O["kAR"], O["kKt"], O["kBt"], O["kVb"]
            cs = slice(c * 64, (c + 1) * 64)
            last = (si == 3 and c == 7)
            arc = AR[:, c * 128:(c + 1) * 128]
            for hh in range(2):
                ps = slice(hh * 64, (hh + 1) * 64)
                idb = IDb[ps, ps]
                P.tr(bTb[ps, 64:128], AR4[ps, c, 0, :], idb, [kAR, "IDb"], [kT])
                P.tr(bTb[ps, 128:192], Bt[ps, cs], idb, [kBt, "IDb"], [kT])
                P.tr(bTb[ps, 192:256], Kt[ps, cs], idb, [kKt, "IDb"], [kT])
                P.tr(bTb[ps, 256:320], Vb[ps, cs], idb, [kVb, "IDb"], [kT])
            P.cp("act", TMp[:, 64:320], bTb[:, 64:320], [kT], [kTM])
            mm2(bD, Bt, arc, 0, 128, c * 64, (c + 1) * 64, 0, 128, True, True, [kBt, kAR], [kD])
            mm2(bD, Kt, arc, 128, 256, c * 64, (c + 1) * 64, 0, 128, True, True, [kKt, kAR], [kD])
            mm2(bD, arc, Bt, 256, 320, 0, 64, c * 64, (c + 1) * 64, True, True, [kBt, kAR], [kD])
            P.tt("dve", SCp, bD[:, 0:320], MASK5, ALU.mult, [kD, "CM"], [kSC])
            yield
            mm2(bD, SCp, TMp, 448, 512, 128, 192, 256, 320, True, True, [kSC, kTM], [kD])
            P.cp("act", TMp[:, 0:64], bD[:, 448:512], [kD], [kTM])
            yield
            zsrc, zk = TMp, kTM
            psrc, pk, pc = SCp, kSC, 0
            qsrc, qk, qc = SCp, kSC, 256
            for lvl in range(6):
                mm2(bD, psrc, zsrc, 0, 128, pc, pc + 64, 0, 128, True, True, [pk, zk], [kD])
                if lvl < 5:
                    mm2(bT, qsrc, psrc, 0, 64, qc, qc + 64, pc, pc + 64, True, True, [pk, qk], [kT])
                    mm2(bT, psrc, qsrc, 64, 128, pc, pc + 64, qc, qc + 64, True, True, [pk, qk], [kT])
                dst = ZPQp[lvl % 2]
                dzk = f"Z{par}{lvl % 2}"
                dpk = f"PQ{par}{lvl % 2}"
                P.tt("dve", dst[:, 0:128], bD[:, 0:128], zsrc[:, 0:128], ALU.add, [kD, zk], [dzk])
                if lvl < 5:
                    P.cp("act", dst[:, 128:256], bT[:, 0:128], [kT], [dpk])
                zsrc, zk = dst, dzk
                psrc, pk, pc = dst, dpk, 128
                qsrc, qk, qc = dst, dpk, 192
                yield
            Wm, wk_ = zsrc, zk
            gam_col = O["GAMC"][:, c:c + 1]
            kGC = O["kGAMC"]
            mm2(bD, Wm, TMp, 256, 320, 64, 128, 128, 192, True, True, [wk_, kTM], [kD])
            mm2(bD, TMp, Wm, 320, 384, 128, 192, 0, 64, True, False, [wk_, kTM], [kD])
            mm2(bD, TMp, TMp, 320, 384, 192, 256, 256, 320, False, True, [kTM], [kD])
            mm2(bT, ISTKb, arc, 384, 448, 0, 64, 64, 128, True, False, ["ISTKb", kAR], [kT])
            mm2(bT, Wm, SCp, 384, 448, 64, 128, 64, 128, False, True, [wk_, kSC], [kT])
            P.tt("dve", GA1p, bD[:, 256:320], ISTK, ALU.add, [kD, "CM"], [kGA1])
            P.ts("dve", GVGp, bD[:, 320:384], gam_col, None, ALU.mult, ALU.bypass, [kD, kGC], [kGVG])
            P.cp("act", RHp, bT[:, 384:448], [kT], [kRH])
            yield
            b3 = banks[3]
            mm2(b3, Wm, SCp, c * 64, (c + 1) * 64, 0, 64, 64, 128, True, False, [wk_, kSC], [bk(3)])
            mm2(b3, TMp, SCp, c * 64, (c + 1) * 64, 256, 320, 192, 256, False, False, [kTM, kSC], [bk(3)])
            mm2(b3, Hb, RHp, c * 64, (c + 1) * 64, 0, 64, 0, 64, False, True, ["Hb", kRH], [bk(3)])
            mm2(banks[2], GA1p, Hb, 0, 64, 0, 64, 0, 64, True, True, [kGA1, "Hb"], [bk(2)])
            if last:
                P.stt(Hf, banks[2][:, 0:64], gam_col, GVGp, ALU.mult, ALU.add, [bk(2), kGC, kGVG], ["Hf"])
            P.stt(Hb, banks[2][:, 0:64], gam_col, GVGp, ALU.mult, ALU.add, [bk(2), kGC, kGVG], ["Hb"])

        def run_overlapped(chunk_args, extra):
            active = []
            nxt = 0
            free_par = [0, 1, 2]
            since = 99
            ex = extra
            while nxt < len(chunk_args) or active or ex is not None:
                if nxt < len(chunk_args) and free_par and (since >= 3 or not active):
                    pr_ = free_par.pop(0)
                    c_, si_, O_ = chunk_args[nxt]
                    active.append((chunk_gen(c_, pr_, si_, O_), pr_))
                    nxt += 1
                    since = 0
                since += 1
                for (g_, pr_) in list(active):
                    try:
                        next(g_)
                    except StopIteration:
                        active.remove((g_, pr_))
                        free_par.append(pr_)
                if ex is not None:
                    try:
                        next(ex)
                    except StopIteration:
                        ex = None

        for p in range(8):
            wp, kp = load_wgroup(defer=True)
            for q in range(4):
                P.memset("pool", RAW[q][:, 0:1], 0.0, [f"RAW{q}"])
            P.memset("pool", Hb, 0.0, ["Hb"])
            proj_tm(wp, kp, 0, 512, T - 1, 17, 2)
            P.cp("act", SHO, banks[2][0:17, 0:512], [bk(2)], [kSHO])
            P.dma(shift_o[:, 0:4096].rearrange("t (q f) -> t q f", q=4)[:, :, p * 128:(p + 1) * 128],
                  v3(SHO, 128), [kSHO], [], is_out=True)
            run_overlapped([], elem_gen(p, wp, kp, 0, OUT[0]))
            flush_prefetch()
            for si in range(4):
                s0, W = SEGS[si]
                O = OUT[si % 2]
                On = OUT[(si + 1) % 2]
                run_overlapped([(c, si, O) for c in range(8)], elem_gen(p, wp, kp, si + 1, On))
                P.cp("act", YS[:, 0:W], banks[3][:, 0:W], [bk(3)], [kYS])
                if si == 3:
                    P.tr(banks[2][0:64, 64:192], Hf, ID, ["Hf", "CM"], [bk(2)])
                    P.cp("act", T2[0:64, :], banks[2][0:64, 64:128], [bk(2)], ["T2"])
                    P.cp("act", TJ[0:64, :], banks[2][0:64, 128:192], [bk(2)], ["TJ"])
                    P.dma(rwkv_p[2 * p], T2[0:64, :], ["T2"], [], is_out=True)
                    P.dma(rwkv_p[2 * p + 1], TJ[0:64, :], ["TJ"], [], is_out=True)
                gn_post(p, s0, W, O)
            s0, W = SEGS[4]
            O = OUT[0]
            P.dma(SR, srwkv[:, 2 * p:2 * p + 2].rearrange("s h v k -> (h v) s k"), [], kSR)
            for s2 in range(8):
                hs = slice(s2 * 2, s2 * 2 + 2)
                P.add("dve", lambda e, hs=hs: e.tensor_tensor(
                    out=DG, in0=ISTK.unsqueeze(1).unsqueeze(1).to_broadcast([128, 2, 5, 64]),
                    in1=XQ[:, hs, :].unsqueeze(3).to_broadcast([128, 2, 5, 64]), op=ALU.mult),
                    ["CM", "XQ"], kDG)
                for j in range(2):
                    s = s2 * 2 + j
                    bi = 4 + j
                    bb = banks[bi]
                    dgs = DG[:, j].rearrange("p q k -> p (q k)")
                    for hh in range(2):
                        ps = slice(hh * 64, (hh + 1) * 64)
                        P.mm(bb[ps, 0:320], ONES[ps, 0:64], dgs[ps, :], True, True, ["CM"] + kDG, [bk(bi)])
                    P.stt(TJ, SR[:, s, :], 1.0, bb[:, 0:64], ALU.mult, ALU.mult, kSR + [bk(bi)], ["TJ", "SA"],
                          accum=SA[:, 0:1])
                    P.tt("dve", T2, SR[:, s, :], bb[:, 64:128], ALU.mult, kSR + [bk(bi)], ["T2"])
                    P.stt(T2, bb[:, 128:192], SA[:, 0:1], T2, ALU.mult, ALU.add, [bk(bi), "SA", "T2"], ["T2"])
                    P.stt(SRN[:, s, :], bb[:, 192:256], Vf[:, s:s + 1], T2, ALU.mult, ALU.add,
                          [bk(bi), "Vf", "T2"], kSRN)
                    P.stt(TJ, SRN[:, s, :], 1.0, bb[:, 256:320], ALU.mult, ALU.mult, kSRN + [bk(bi)],
                          ["TJ", kYS], accum=YS[:, s:s + 1])
            P.dma(rwkv_s[:, 2 * p:2 * p + 2].rearrange("s h v k -> (h v) s k"), SRN, kSRN, [], is_out=True)
            gn_post(p, s0, W, O)
        ar.release()
        P.fence()

        P.enabled = "F" in phases
        ar.mark()
        MT = v3(ar.bf16(KC * TA), TA)
        FT = [(ar.bf16(512), ar.bf16(512), ar.f32(512)) for _ in range(2)]
        fcnt = 0
        for dt in range(8):
            wf, kf = load_wgroup(defer=True)
            for (s0, W) in SEGS:
                if s0 == 512:
                    flush_prefetch()
                par = fcnt % 2
                fcnt += 1
                SGA, SGBt, M1 = FT[par]
                kA, kB, kM = f"SGA{par}", f"SGBt{par}", f"M1{par}"
                b0, b1, b2, b3 = [4 * par + i for i in range(4)]
                proj_fm(wf, kf, 0, 128, s0, W, b0)
                proj_fm(wf, kf, 128, 128, s0, W, b1)
                for kc in range(KC):
                    P.mm(banks[b2][:, 0:W], wf[:, kc, 256:384], OgT[:, kc, s0:s0 + W], kc == 0, kc == KC - 1,
                         [kf, "OgT"], [bk(b2)])
                for kc in range(KC):
                    P.mm(banks[b3][:, 0:W], wf[:, kc, 384:512], OrT[:, kc, s0:s0 + W], kc == 0, kc == KC - 1,
                         [kf, "OrT"], [bk(b3)])
                P.act(SGA[:, 0:W], banks[b0][:, 0:W], AF.Sigmoid, [bk(b0)], [kA])
                P.act(SGBt[:, 0:W], banks[b1][:, 0:W], AF.Sigmoid, [bk(b1)], [kB])
                P.tt("dve", M1[:, 0:W], banks[b2][:, 0:W], SGA[:, 0:W], ALU.mult, [bk(b2), kA], [kM])
                P.tt("dve", MT[:, dt, s0:s0 + W], banks[b3][:, 0:W], SGBt[:, 0:W], ALU.mult, [bk(b3), kB], ["MT"])
                P.tt("pool", MT[:, dt, s0:s0 + W], MT[:, dt, s0:s0 + W], M1[:, 0:W], ALU.add, ["MT", kM], ["MT"])
        WO = v3(ar.bf16(KC * 1024), 1024)
        for g in range(2):
            wo, ko = load_wgroup([w_out[:, g * 512:(g + 1) * 512]])
            P.cp("pool", WO[:, :, g * 512:(g + 1) * 512], wo, [ko], ["WO"])
        XA = xT.rearrange("p a b -> p (a b)").bitcast(F32)
        LNGB = XA[:, 0:2048]
        XR = [XA[:, 2048 + i * 1024:2048 + (i + 1) * 1024] for i in range(2)]
        ZT = [XA[:, 4096 + i * 1024:4096 + (i + 1) * 1024] for i in range(2)]
        P.dma(LNGB, lngb, [], ["LNGB", "xT"])
        ST = ar.f32(8)
        alpha = 2.0 ** 0.25
        for tt in range(17):
            rows = 128 if tt < 16 else NS
            t0 = tt * 128
            xr = XR[tt % 2]; xk = f"XR{tt % 2}"
            zt = ZT[tt % 2]; zk = f"ZT{tt % 2}"
            if tt == 0:
                P.dma(xr[0:rows, :], x[t0:t0 + rows, :], [], [xk, "xT"])
            if tt + 1 < 17:
                rn_ = 128 if tt + 1 < 16 else NS
                P.dma(XR[(tt + 1) % 2][0:rn_, :], x[(tt + 1) * 128:(tt + 1) * 128 + rn_, :], [],
                      [f"XR{(tt + 1) % 2}", "xT"])
            for eh in range(2):
                bi = 2 * (tt % 2) + eh
                for kc in range(KC):
                    P.mm(banks[bi][0:rows, 0:512], MT[:, kc, t0:t0 + rows], WO[:, kc, eh * 512:(eh + 1) * 512],
                         kc == 0, kc == KC - 1, ["MT", "WO"], [bk(bi)])
                P.stt(zt[0:rows, eh * 512:(eh + 1) * 512], xr[0:rows, eh * 512:(eh + 1) * 512], alpha,
                      banks[bi][0:rows, 0:512], ALU.mult, ALU.add, [xk, bk(bi)], [zk, "xT"])
            R_ = slice(0, rows)
            P.act(xr[R_, :], zt[R_, :], AF.Copy, [zk], [xk, "ST"], accum=ST[R_, 0:1])
            P.act(xr[R_, :], zt[R_, :], AF.Square, [zk], [xk, "ST"], accum=ST[R_, 1:2])
            P.ts("pool", ST[R_, 2:3], ST[R_, 0:1], 1.0 / D, None, ALU.mult, ALU.bypass, ["ST"], ["ST"])
            P.tt("pool", ST[R_, 3:4], ST[R_, 2:3], ST[R_, 2:3], ALU.mult, ["ST"], ["ST"])
            P.stt(ST[R_, 4:5], ST[R_, 1:2], 1.0 / D, ST[R_, 3:4], ALU.mult, ALU.subtract, ["ST"], ["ST"])
            P.act(ST[R_, 5:6], ST[R_, 4:5], AF.Ln, ["ST"], ["ST"], bias=EPSC[R_, 0:1])
            P.act(ST[R_, 5:6], ST[R_, 5:6], AF.Exp, ["ST"], ["ST"], scale=-0.5)
            P.ts("dve", zt[R_, :], zt[R_, :], ST[R_, 2:3], ST[R_, 5:6], ALU.subtract, ALU.mult, [zk, "ST"], [zk])
            P.tt("pool", zt[R_, :], zt[R_, :], LNGB[R_, 0:1024], ALU.mult, [zk, "LNGB"], [zk])
            P.tt("dve", zt[R_, :], zt[R_, :], LNGB[R_, 1024:2048], ALU.add, [zk, "LNGB"], [zk])
            P.dma(y[t0:t0 + rows, :], zt[R_, :], [zk], [], is_out=True)
        ar.release()
        P.fence()

        P.enabled = True
        P.counting = False
        P.finish()
        P.emit(nc, st)
    return nc


def _consts():
    cm = np.zeros((128, NCM), np.float32)
    cm[:, CM_ID:CM_ID + 128] = np.eye(128, dtype=np.float32)
    s = np.arange(128)[:, None] % 64
    t = np.arange(64)[None, :]
    strict = (s < t).astype(np.float32)
    incl = (s <= t).astype(np.float32)
    lower = (t < s).astype(np.float32)
    cm[:, CM_MASK5:CM_MASK5 + 320] = np.concatenate([strict, incl, strict, incl, lower], axis=1)
    j = np.arange(128)[:, None]
    i = np.arange(128)[None, :]
    cm[:, CM_MASKU:CM_MASKU + 128] = (j <= i).astype(np.float32)
    cm[:, CM_ISTK:CM_ISTK + 64] = (s == t).astype(np.float32)
    blk = (np.arange(128)[:, None] // 64 == np.arange(128)[None, :] // 64).astype(np.float32)
    cm[:, CM_BONES:CM_BONES + 128] = blk
    cm[:, CM_BAVG:CM_BAVG + 128] = blk / 64.0
    cm[:, CM_ONES:CM_ONES + 128] = 1.0
    smg = np.ones(512, np.float32); smg[::128] = 0
    smr = np.ones(512, np.float32); smr[::64] = 0
    cm[:, CM_SMG:CM_SMG + 512] = smg[None, :]
    cm[:, CM_SMR:CM_SMR + 512] = smr[None, :]
    cm[0:16, CM_ID16:CM_ID16 + 16] = np.eye(16, dtype=np.float32)
    return cm


def _cols(vec):
    return np.ascontiguousarray(np.asarray(vec, np.float32).reshape(-1, 128).T)


_NC_CACHE = {}


def kernel(x_prompt, x_sample, state_gla, state_rwkv, state_rwkv_shift, w_in, gla_alpha_w2,
           gla_alpha_b, gla_norm_w, rwkv_mu, rwkv_w0, rwkv_w2, rwkv_a0, rwkv_a2, rwkv_k_k,
           rwkv_k_a, rwkv_r_k, rwkv_lnx_w, rwkv_lnx_b, w_up_gla, w_up_rwkv, w_out, ln_g, ln_b):
    f = lambda a: np.ascontiguousarray(np.asarray(a, dtype=np.float32))
    x_prompt, x_sample = f(x_prompt), f(x_sample)
    cvec = np.concatenate([_cols(rwkv_mu[0]), _cols(gla_alpha_b[0]), _cols(gla_norm_w[0]), _cols(rwkv_w0[0]),
                           _cols(rwkv_a0[0]), _cols(rwkv_k_k[0]), _cols(rwkv_k_a[0]),
                           _cols(np.asarray(rwkv_r_k[0]).reshape(-1)), _cols(rwkv_lnx_w[0]), _cols(rwkv_lnx_b[0])],
                          axis=1)
    assert cvec.shape == (128, NCV)
    cmat = _consts()
    w2a2 = np.concatenate([f(rwkv_w2[0]), f(rwkv_a2[0])], axis=0)
    lngb = np.concatenate([np.broadcast_to(f(ln_g[0])[None, :], (128, D)),
                           np.broadcast_to(f(ln_b[0])[None, :], (128, D))], axis=1)
    lngb = np.ascontiguousarray(lngb)
    common = dict(w_in=f(w_in[0]), alpha_w2=f(gla_alpha_w2[0]), w2a2=w2a2, w_up_gla=f(w_up_gla[0]),
                  w_up_rwkv=f(w_up_rwkv[0]), w_out=f(w_out[0]), lngb=lngb, cvec=f(cvec), cmat=cmat)
    in_maps = []
    for c in range(8):
        m = dict(common)
        m["x"] = np.ascontiguousarray(np.concatenate([x_prompt[c], x_sample[NS * c:NS * (c + 1), 0]], axis=0))
        m["sgla"] = f(state_gla[0, NS * c:NS * (c + 1)])
        m["srwkv"] = f(state_rwkv[0, NS * c:NS * (c + 1)])
        m["sshift"] = f(state_rwkv_shift[0, NS * c:NS * (c + 1)])
        in_maps.append(m)
    if "nc" not in _NC_CACHE:
        _NC_CACHE["nc"] = build_nc()
    nc = _NC_CACHE["nc"]
    res = run_bass_kernel_spmd(nc, in_maps, core_ids=list(range(8)))
    rs = res.results
    y_prompt = np.stack([rs[c]["y"][0:T] for c in range(8)], axis=0)
    y_sample = np.concatenate([rs[c]["y"][T:TA] for c in range(8)], axis=0)[:, None, :]
    gla_p = np.stack([rs[c]["gla_p"] for c in range(8)], axis=0)[None]
    rwkv_p = np.stack([rs[c]["rwkv_p"] for c in range(8)], axis=0)[None]
    shift_p = np.stack([rs[c]["shift_o"][0] for c in range(8)], axis=0)[None]
    gla_s = np.concatenate([rs[c]["gla_s"] for c in range(8)], axis=0)[None]
    rwkv_s = np.concatenate([rs[c]["rwkv_s"] for c in range(8)], axis=0)[None]
    shift_s = np.concatenate([rs[c]["shift_o"][1:17] for c in range(8)], axis=0)[None]
    outs = (y_prompt, y_sample, gla_p, rwkv_p, shift_p, gla_s, rwkv_s, shift_s)
    return tuple(np.ascontiguousarray(o, dtype=np.float32) for o in outs)
```

```python
import contextlib
import numpy as np
import concourse.bass as bass
import concourse.mybir as mybir
from concourse.bass_utils import run_bass_kernel_spmd

F32 = mybir.dt.float32
BF16 = mybir.dt.bfloat16
AF = mybir.ActivationFunctionType
ALU = mybir.AluOpType

ENGS = ["pe", "act", "dve", "pool", "sp"]

T = 2048
NS = 16
TA = T + NS
D = 1024
KC = 8
NIN = 9360
GQ, GK, GV, GG, GA = 0, 512, 1024, 2048, 3072
R0 = 3088
RR, RK, RV, RG, RW = R0, R0 + 1024, R0 + 2048, R0 + 3072, R0 + 4096
G0 = R0 + 4224
SEGS = [(0, 512), (512, 512), (1024, 512), (1536, 512), (2048, 16)]
DECAY = 0.606531
RLEVEL = 99
RPAIRS = 8
RSTOP = None
RLOG = []

CV_MU, CV_AB, CV_GNW, CV_W0, CV_A0, CV_KK, CV_KA, CV_RK, CV_LW, CV_LB = 0, 33, 37, 39, 47, 55, 63, 71, 79, 87
NCV = 95
CM_ID, CM_MASK5, CM_MASKU, CM_ISTK, CM_BONES, CM_BAVG, CM_ONES, CM_SMG, CM_SMR, CM_ID16 = (
    0, 128, 448, 576, 640, 768, 896, 1024, 1536, 2048)
NCM = 2064


class Prog:
    EPOCH = 8192
    NDMA = 24

    def __init__(self):
        self.ops = {e: [] for e in ENGS}
        self.cnt = {e: 0 for e in ENGS}
        self.ndma = 0
        self.dma_events = []
        self.last_w = {}
        self.readers = {}
        self.waited = {e: {} for e in ENGS}
        self.semkeys = set()
        self.out_events = []
        self.enabled = True
        self.pending = {e: [] for e in ENGS}
        self.last_ev = {}
        self.know = {e: {} for e in ENGS}
        self.evclock = {}
        self.evidx = {}
        self.nev = 0

    def fence(self):
        evs = list(self.last_ev.values()) + list(self.dma_events[-self.NDMA:])
        for e in ENGS:
            self.pending[e] = list(evs)

    def _resolve(self, eng, cands):
        know = self.know[eng]
        waits = []
        for ev in sorted(cands, key=lambda e: -self.evidx[e]):
            sk, val = ev
            if eng == "pe" and sk[0] == "pe":
                continue
            if know.get(sk, 0) >= val:
                continue
            waits.append(ev)
            for k2, v2 in self.evclock[ev].items():
                if know.get(k2, 0) < v2:
                    know[k2] = v2
        return waits

    def add(self, eng, fn, r=(), w=(), dma=False, is_out=False):
        if not self.enabled:
            return None
        if RSTOP is not None and getattr(self, "counting", False):
            self.nops = getattr(self, "nops", 0) + 1
            if self.nops > RSTOP:
                return None
        xb = [k for k in r if isinstance(k, str) and k.startswith("bank")]
        if xb:
            r = [k for k in r if k not in xb]
            w = list(w) + [k for k in xb if k not in w]
        cands = set()
        if self.pending[eng]:
            cands.update(self.pending[eng])
            self.pending[eng] = []
        for k in r:
            cands.add(self.last_w.get(k))
        for k in w:
            cands.add(self.last_w.get(k))
            cands.update(self.readers.get(k, ()))
        if dma and self.ndma >= self.NDMA:
            cands.add(self.dma_events[self.ndma - self.NDMA])
        cands.discard(None)
        waits = self._resolve(eng, cands)
        clk = dict(self.know[eng])
        if dma:
            j = self.ndma
            self.ndma += 1
            sk = ("dma", j % self.NDMA)
            val = 16 * (j // self.NDMA + 1)
            ev = (sk, val)
            self.dma_events.append(ev)
            inc = 16
            if is_out:
                self.out_events.append(ev)
        else:
            i = self.cnt[eng]
            self.cnt[eng] += 1
            ep = i // self.EPOCH
            sk = (eng, ep)
            ev = (sk, i % self.EPOCH + 1)
            inc = 1
            self.last_ev[eng] = ev
            for e2 in range(ep):
                clk[(eng, e2)] = self.EPOCH
        clk[sk] = max(clk.get(sk, 0), ev[1])
        self.evclock[ev] = clk
        self.evidx[ev] = self.nev
        self.nev += 1
        self.semkeys.add(sk)
        self.ops[eng].append((waits, fn, ev, inc))
        for k in r:
            self.readers.setdefault(k, []).append(ev)
        for k in w:
            self.last_w[k] = ev
            self.readers[k] = []
        return ev

    def finish(self):
        cands = set(self.dma_events[-self.NDMA:]) | set(self.out_events)
        waits = self._resolve("sp", cands)
        self.ops["sp"].append((waits, None, None, 0))

    def emit(self, nc, stack):
        targets = set()
        for e in ENGS:
            for waits, fn, ev, inc in self.ops[e]:
                targets.update(waits)
        real = {}
        used = set()
        for e in ENGS:
            cnt = {}
            for waits, fn, ev, inc in self.ops[e]:
                if ev is None:
                    continue
                if ev[0][0] == "dma":
                    real[ev] = ev[1]
                    used.add(ev[0])
                elif ev in targets:
                    cnt[ev[0]] = cnt.get(ev[0], 0) + 1
                    real[ev] = cnt[ev[0]]
                    used.add(ev[0])
        sems = {}
        for sk in sorted(used, key=str):
            sems[sk] = stack.enter_context(nc.semaphore("s_" + "_".join(str(x) for x in sk)))
        block = stack.enter_context(nc.Block())
        prog = self

        def run(engname):
            def body(eng):
                for waits, fn, ev, inc in prog.ops[engname]:
                    if fn is None:
                        for w_ in waits:
                            eng.wait_ge(sems[w_[0]], real[w_])
                        continue
                    for w_ in waits[:-1]:
                        eng.wait_ge(sems[w_[0]], real[w_])
                    ins = fn(eng)
                    if waits:
                        ins._wait_ge(sems[waits[-1][0]], real[waits[-1]])
                    if ev in real:
                        ins.then_inc(sems[ev[0]], inc)
            return body

        block.tensor(run("pe"))
        block.scalar(run("act"))
        block.vector(run("dve"))
        block.gpsimd(run("pool"))
        block.sync(run("sp"))

    def act(self, out, in_, func, r, w, bias=0.0, scale=1.0, accum=None):
        if accum is None:
            return self.add("act", lambda e: e.activation(out=out, in_=in_, func=func, bias=bias, scale=scale), r, w)
        return self.add("act", lambda e: e.activation(out=out, in_=in_, func=func, bias=bias, scale=scale,
                                                      accum_out=accum), r, w)

    def ts(self, eng, out, in0, s1, s2, op0, op1, r, w):
        return self.add(eng, lambda e: e.tensor_scalar(out=out, in0=in0, scalar1=s1, scalar2=s2, op0=op0, op1=op1), r, w)

    def stt(self, out, in0, scalar, in1, op0, op1, r, w, accum=None):
        if accum is None:
            return self.add("dve", lambda e: e.scalar_tensor_tensor(out=out, in0=in0, scalar=scalar, in1=in1,
                                                                     op0=op0, op1=op1), r, w)
        return self.add("dve", lambda e: e.scalar_tensor_tensor(out=out, in0=in0, scalar=scalar, in1=in1,
                                                                 op0=op0, op1=op1, accum_out=accum), r, w)

    def tt(self, eng, out, in0, in1, op, r, w):
        return self.add(eng, lambda e: e.tensor_tensor(out=out, in0=in0, in1=in1, op=op), r, w)

    def cp(self, eng, out, in_, r, w):
        if eng == "act":
            return self.add("act", lambda e: e.activation(out=out, in_=in_, func=AF.Copy), r, w)
        return self.add(eng, lambda e: e.tensor_copy(out=out, in_=in_), r, w)

    def memset(self, eng, out, val, w):
        return self.add(eng, lambda e: e.memset(out, val), (), w)

    def mm(self, out, lhsT, rhs, start, stop, r, w):
        return self.add("pe", lambda e: e.matmul(out, lhsT=lhsT, rhs=rhs, start=start, stop=stop), r, w)

    def tr(self, out, in_, ident, r, w):
        return self.add("pe", lambda e: e.transpose(out, in_, ident), r, w)

    def dma(self, out, in_, r, w, is_out=False):
        return self.add("sp", lambda e: e.dma_start(out=out, in_=in_), r, w, dma=True, is_out=is_out)

    def scan(self, out, d0, d1, r, w):
        return self.add("dve", lambda e: e.tensor_tensor_scan(out=out, data0=d0, data1=d1, initial=0.0,
                                                               op0=ALU.mult, op1=ALU.add), r, w)


class Arena:
    def __init__(self, t, words):
        self.t = t
        self.words = words
        self.off = 0
        self.marks = []
        self.peak = 0

    def alloc(self, nwords):
        o = self.off
        self.off += nwords
        self.peak = max(self.peak, self.off)
        assert self.off <= self.words, f"SBUF arena overflow {self.off}>{self.words}"
        return o

    def f32(self, n, parts=128):
        o = self.alloc(n)
        return self.t[0:parts, o:o + n]

    def bf16(self, n, parts=128):
        assert n % 2 == 0
        o = self.alloc(n // 2)
        return self.t[0:parts, o:o + n // 2].bitcast(BF16)

    def mark(self):
        self.marks.append(self.off)

    def release(self):
        self.off = self.marks.pop()


def v3(ap, b):
    return ap.rearrange("p (a b) -> p a b", b=b)


def build_nc(phases="0GRF"):
    nc = bass.Bass("TRN2", target_bir_lowering=False)
    di = lambda n, s: nc.dram_tensor(n, list(s), F32, kind="ExternalInput").ap()
    do = lambda n, s: nc.dram_tensor(n, list(s), F32, kind="ExternalOutput").ap()
    x = di("x", (TA, D))
    w_in = di("w_in", (D, NIN))
    alpha_w2 = di("alpha_w2", (16, 512))
    w2a2 = di("w2a2", (128, 1024))
    w_up_gla = di("w_up_gla", (D, D))
    w_up_rwkv = di("w_up_rwkv", (D, D))
    w_out = di("w_out", (D, D))
    lngb = di("lngb", (128, 2048))
    sgla = di("sgla", (NS, 4, 128, 256))
    srwkv = di("srwkv", (NS, 16, 64, 64))
    sshift = di("sshift", (NS, 4224))
    cvec = di("cvec", (128, NCV))
    cmat = di("cmat", (128, NCM))
    y = do("y", (TA, D))
    gla_p = do("gla_p", (4, 128, 256))
    rwkv_p = do("rwkv_p", (16, 64, 64))
    shift_o = do("shift_o", (17, 4224))
    gla_s = do("gla_s", (NS, 4, 128, 256))
    rwkv_s = do("rwkv_s", (NS, 16, 64, 64))

    P = Prog()
    with contextlib.ExitStack() as st:
        WORDS = 53184
        sb = st.enter_context(nc.sbuf_tensor("arena", [128, WORDS], F32))
        ar = Arena(sb, WORDS)
        banks = [st.enter_context(nc.psum_tensor(f"ps{i}", [128, 512], F32)) for i in range(8)]
        bk = lambda i: f"bank{i}"

        CV = ar.f32(NCV)
        CM = ar.f32(NCM)
        P.dma(CV, cvec, [], ["CV"])
        P.dma(CM, cmat, [], ["CM"])
        ID = CM[:, CM_ID:CM_ID + 128]
        MASK5 = CM[:, CM_MASK5:CM_MASK5 + 320]
        MASKU = CM[:, CM_MASKU:CM_MASKU + 128]
        ISTK = CM[:, CM_ISTK:CM_ISTK + 64]
        BONES = CM[:, CM_BONES:CM_BONES + 128]
        BAVG = CM[:, CM_BAVG:CM_BAVG + 128]
        ONES = CM[:, CM_ONES:CM_ONES + 128]
        SMG = CM[:, CM_SMG:CM_SMG + 512]
        SMR = CM[:, CM_SMR:CM_SMR + 512]
        ID16 = CM[0:16, CM_ID16:CM_ID16 + 16]
        IDb = ar.bf16(128)
        ISTKb = ar.bf16(64)
        BONESb = ar.bf16(128)
        ONESb = ar.bf16(128)
        NAB = ar.f32(4)
        EPSC = ar.f32(2)
        P.cp("pool", IDb, ID, ["CM"], ["IDb"])
        P.cp("pool", ISTKb, ISTK, ["CM"], ["ISTKb"])
        P.cp("pool", BONESb, BONES, ["CM"], ["BONESb"])
        P.cp("pool", ONESb, ONES, ["CM"], ["ONESb"])
        P.ts("pool", NAB, CV[:, CV_AB:CV_AB + 4], -1.0, None, ALU.mult, ALU.bypass, ["CV"], ["NAB"])
        P.memset("pool", EPSC[:, 0:1], 1e-5, ["EPSC"])
        P.memset("pool", EPSC[:, 1:2], 64e-5, ["EPSC"])
        cvc = lambda base, j: CV[:, base + j:base + j + 1]

        W2A2 = ar.f32(1024)
        P.dma(W2A2, w2a2, [], ["W2A2"])

        xT = v3(ar.bf16(KC * TA), TA)
        OgT = v3(ar.bf16(KC * TA), TA)
        OrT = v3(ar.bf16(KC * TA), TA)

        WST = v3(ar.f32(4 * 512), 512)
        WBF = [v3(ar.bf16(KC * 512), 512) for _ in range(3)]
        wstate = {"n": 0, "pre": None}
        wqueue = []

        def _issue_load(srcs):
            en = P.enabled
            P.enabled = True
            par = wstate["n"] % 3
            wstate["n"] += 1
            key = f"WBF{par}"
            for half in range(2):
                off = 0
                for s_ in srcs:
                    n = s_.shape[1]
                    P.dma(WST[:, :, off:off + n],
                          s_.rearrange("(kc p) n -> p kc n", p=128)[:, 4 * half:4 * half + 4, :], [], ["WST"])
                    off += n
                for k2 in range(2):
                    P.cp("pool", WBF[par][:, 4 * half + 2 * k2:4 * half + 2 * k2 + 2, 0:off],
                         WST[:, 2 * k2:2 * k2 + 2, 0:off], ["WST"], [key])
            P.enabled = en
            return WBF[par], key

        def load_wgroup(srcs=None, defer=False):
            flush_prefetch()
            if wstate["pre"] is None:
                wstate["pre"] = _issue_load(wqueue.pop(0))
            cur = wstate["pre"]
            wstate["pre"] = None
            wstate["pend"] = bool(wqueue)
            if not defer:
                flush_prefetch()
            return cur

        def flush_prefetch():
            if wstate.get("pend"):
                wstate["pend"] = False
                wstate["pre"] = _issue_load(wqueue.pop(0))

        wqueue.append([w_in[:, GA:GA + 16]])
        for h_ in range(4):
            wqueue.append([w_in[:, GQ + h_ * 128:GQ + (h_ + 1) * 128], w_in[:, GK + h_ * 128:GK + (h_ + 1) * 128],
                           w_in[:, GV + h_ * 256:GV + (h_ + 1) * 256]])
            wqueue.append([w_in[:, GG + h_ * 256:GG + (h_ + 1) * 256]])
        wqueue.append([w_in[:, RW:RW + 128]])
        for p_ in range(8):
            wqueue.append([w_in[:, RR + p_ * 128:RR + (p_ + 1) * 128], w_in[:, RK + p_ * 128:RK + (p_ + 1) * 128],
                           w_in[:, RV + p_ * 128:RV + (p_ + 1) * 128], w_in[:, RG + p_ * 128:RG + (p_ + 1) * 128]])
        for dt_ in range(8):
            wqueue.append([w_in[:, G0 + dt_ * 128:G0 + (dt_ + 1) * 128],
                           w_in[:, G0 + 1024 + dt_ * 128:G0 + 1024 + (dt_ + 1) * 128],
                           w_up_gla[:, dt_ * 128:(dt_ + 1) * 128], w_up_rwkv[:, dt_ * 128:(dt_ + 1) * 128]])
        for g_ in range(2):
            wqueue.append([w_out[:, g_ * 512:(g_ + 1) * 512]])

        def proj_fm(wb, wkey, coff, ncols, s0, W, bank_i, c0=0):
            for kc in range(KC):
                P.mm(banks[bank_i][0:ncols, c0:c0 + W], wb[:, kc, coff:coff + ncols], xT[:, kc, s0:s0 + W],
                     kc == 0, kc == KC - 1, [wkey, "xT"], [bk(bank_i)])

        def proj_tm(wb, wkey, coff, ncols, t0, M, bank_i, c0=0):
            for kc in range(KC):
                P.mm(banks[bank_i][0:M, c0:c0 + ncols], xT[:, kc, t0:t0 + M], wb[:, kc, coff:coff + ncols],
                     kc == 0, kc == KC - 1, [wkey, "xT"], [bk(bank_i)])

        wstate["pre"] = _issue_load(wqueue.pop(0))
        P.enabled = "0" in phases
        ar.mark()
        XS = [ar.f32(1024) for _ in range(6)]
        for tt in range(17):
            rows = 128 if tt < 16 else NS
            xs = XS[tt % 6]
            xk = f"XS{tt % 6}"
            P.dma(xs[0:rows, :], x[tt * 128:tt * 128 + rows, :], [], [xk])
            for half in range(2):
                b = banks[half]
                for j in range(4):
                    kc = half * 4 + j
                    P.tr(b[:, j * 128:j * 128 + rows], xs[0:rows, kc * 128:(kc + 1) * 128], ID[0:rows, 0:rows],
                         [xk, "CM"], [bk(half)])
                src = v3(b[:, 0:512], 128)[:, :, 0:rows]
                dst = xT[:, half * 4:half * 4 + 4, tt * 128:tt * 128 + rows]
                P.cp("act" if half == 0 else "dve", dst, src, [bk(half)], ["xT"])
        ar.release()
        P.fence()

        P.enabled = "G" in phases
        ar.mark()
        AW2 = ar.f32(512, parts=16)
        P.dma(AW2, alpha_w2, [], ["AW2"])
        ALR = ar.f32(TA, parts=16)
        SP = ar.f32(512); CSP = ar.f32(512); EB = ar.f32(512); EINV = ar.f32(512)
        QT = ar.bf16(512); KT = ar.bf16(512); QS = ar.f32(16)
        KH = ar.bf16(128); KHT = ar.bf16(128)
        VTK = v3(ar.bf16(4 * 256), 256)
        ATS = ar.bf16(128)
        SG = [ar.f32(256) for _ in range(2)]
        SGB = ar.bf16(256)
        GS = v3(ar.bf16(2 * 512), 512)
        SQ = v3(ar.bf16(2 * 512), 512)
        RSTD = ar.f32(512)
        ON = ar.f32(512)
        KTOK = ar.f32(128, parts=16); VTOK = ar.f32(256, parts=16)
        KM = v3(ar.f32(16 * 128, parts=16), 128)
        SSb = [v3(ar.f32(4 * 256), 256) for _ in range(2)]
        SNb = [v3(ar.f32(4 * 256), 256) for _ in range(2)]

        wb, wk = load_wgroup([w_in[:, GA:GA + 16]])
        for (s0, W) in SEGS:
            proj_fm(wb, wk, 0, 16, s0, W, 0)
            P.cp("act", ALR[:, s0:s0 + W], banks[0][0:16, 0:W], [bk(0)], ["ALR"])

        for h in range(4):
            wqkv, kqkv = load_wgroup([w_in[:, GQ + h * 128:GQ + (h + 1) * 128],
                                      w_in[:, GK + h * 128:GK + (h + 1) * 128],
                                      w_in[:, GV + h * 256:GV + (h + 1) * 256]])
            wg, kg = load_wgroup(defer=True)
            cur = 0
            P.memset("pool", SG[0], 0.0, ["SG0"])
            P.memset("pool", SGB, 0.0, ["SGB"])
            for si, (s0, W) in enumerate(SEGS):
                samp = si == 4
                if si == 1:
                    flush_prefetch()
                P.mm(banks[2][:, 0:W], AW2[:, h * 128:(h + 1) * 128], ALR[:, s0:s0 + W], True, True,
                     ["AW2", "ALR"], [bk(2)])
                P.act(SP[:, 0:W], banks[2][:, 0:W], AF.Exp, [bk(2), "NAB"], ["SP"], bias=NAB[:, h:h + 1], scale=-1.0)
                P.act(SP[:, 0:W], SP[:, 0:W], AF.Ln, ["SP"], ["SP"], bias=1.0)
                if samp:
                    csp = SP
                else:
                    P.scan(CSP[:, 0:W], SMG[:, 0:W], SP[:, 0:W], ["CM", "SP"], ["CSP"])
                    csp = CSP
                ck = "SP" if samp else "CSP"
                P.act(EB[:, 0:W], csp[:, 0:W], AF.Exp, [ck], ["EB"], scale=-1.0 / 16.0)
                P.act(EINV[:, 0:W], csp[:, 0:W], AF.Exp, [ck], ["EINV"], scale=1.0 / 16.0)
                proj_fm(wqkv, kqkv, 0, 128, s0, W, 0)
                if samp:
                    P.ts("dve", QS[:, 0:W], banks[0][:, 0:W], 128.0 ** -0.5, None, ALU.mult, ALU.bypass,
                         [bk(0)], ["QS"])
                else:
                    P.stt(QT[:, 0:W], banks[0][:, 0:W], 128.0 ** -0.5, EB[:, 0:W], ALU.mult, ALU.mult,
                          [bk(0), "EB"], ["QT"])
                    proj_fm(wqkv, kqkv, 128, 128, s0, W, 1)
                    P.tt("dve", KT[:, 0:W], banks[1][:, 0:W], EINV[:, 0:W], ALU.mult, [bk(1), "EINV"], ["KT"])
                if not samp:
                    for t4 in range(4):
                        bi = t4 % 2
                        proj_tm(wqkv, kqkv, 256, 256, s0 + t4 * 128, 128, bi)
                        P.cp("act", VTK[:, t4, :], banks[bi][:, 0:256], [bk(bi)], ["VTK"])
                    for vh in range(2):
                        proj_fm(wg, kg, vh * 128, 128, s0, W, vh)
                        P.act(GS[:, vh, 0:W], banks[vh][:, 0:W], AF.Silu, [bk(vh)], ["GS"])
                    for c in range(4):
                        cs = slice(c * 128, (c + 1) * 128)
                        ecol = EB[:, c * 128 + 127:c * 128 + 128]
                        first = (si == 0 and c == 0)
                        P.mm(banks[4][:, 0:128], KT[:, cs], QT[:, cs], True, True, ["KT", "QT"], [bk(4)])
                        P.tt("dve", ATS, banks[4][:, 0:128], MASKU, ALU.mult, [bk(4), "CM"], ["ATS"])
                        P.ts("pool", KH, KT[:, cs], ecol, None, ALU.mult, ALU.bypass, ["KT", "EB"], ["KH"])
                        b4b = banks[4].bitcast(BF16)
                        P.tr(b4b[:, 512:640], KH, IDb, ["KH", "IDb"], [bk(4)])
                        P.cp("act", KHT, b4b[:, 512:640], [bk(4)], ["KHT"])
                        for vh in range(2):
                            vs = slice(vh * 128, (vh + 1) * 128)
                            P.mm(banks[5][:, vh * 128:(vh + 1) * 128], VTK[:, c, vs], ATS, True, first,
                                 ["VTK", "ATS"], [bk(5)])
                            if not first:
                                P.mm(banks[5][:, vh * 128:(vh + 1) * 128], SGB[:, vs], QT[:, cs], False, True,
                                     ["SGB", "QT"], [bk(5)])
                        P.cp("act", OgT[:, 2 * h:2 * h + 2, s0 + c * 128:s0 + (c + 1) * 128],
                             v3(banks[5][:, 0:256], 128), [bk(5)], ["OgT"])
                        P.mm(banks[6][:, 0:256], KHT, VTK[:, c, :], True, True, ["KHT", "VTK"], [bk(6)])
                        nxt = 1 - cur
                        P.stt(SG[nxt], SG[cur], ecol, banks[6][:, 0:256], ALU.mult, ALU.add,
                              [f"SG{cur}", "EB", bk(6)], [f"SG{nxt}"])
                        P.cp("act", SGB, SG[nxt], [f"SG{nxt}"], ["SGB"])
                        cur = nxt
                    if si == 3:
                        P.dma(gla_p[h], SG[cur], [f"SG{cur}"], [], is_out=True)
                else:
                    for vh in range(2):
                        proj_fm(wg, kg, vh * 128, 128, s0, W, vh)
                        P.act(GS[:, vh, 0:W], banks[vh][:, 0:W], AF.Silu, [bk(vh)], ["GS"])
                    proj_tm(wqkv, kqkv, 128, 128, T, NS, 4)
                    P.cp("act", KTOK, banks[4][0:16, 0:128], [bk(4)], ["KTOK"])
                    proj_tm(wqkv, kqkv, 256, 256, T, NS, 4)
                    P.cp("act", VTOK, banks[4][0:16, 0:256], [bk(4)], ["VTOK"])
                    P.add("dve", lambda e: e.tensor_tensor(
                        out=KM, in0=KTOK.unsqueeze(1).to_broadcast([16, 16, 128]),
                        in1=ID16.unsqueeze(2).to_broadcast([16, 16, 128]), op=ALU.mult),
                        ["KTOK", "CM"], ["KM"])
                    P.dma(SSb[0], sgla[0:4, h].rearrange("s d v -> d s v"), [], ["SS0"])
                    for sg in range(4):
                        SS, SN = SSb[sg % 2], SNb[sg % 2]
                        kSS, kSN = f"SS{sg % 2}", f"SN{sg % 2}"
                        if sg + 1 < 4:
                            P.dma(SSb[(sg + 1) % 2], sgla[(sg + 1) * 4:(sg + 2) * 4, h].rearrange("s d v -> d s v"),
                                  [], [f"SS{(sg + 1) % 2}"])
                        for j in range(4):
                            s = sg * 4 + j
                            P.add("pe", lambda e, s=s: e.matmul(banks[6][:, 0:256], lhsT=KM[:, s, :], rhs=VTOK,
                                                                start=True, stop=True), ["KM", "VTOK"], [bk(6)])
                            P.stt(SN[:, j, :], SS[:, j, :], EB[:, s:s + 1], banks[6][:, 0:256], ALU.mult, ALU.add,
                                  [kSS, "EB", bk(6)], [kSN])
                            for vh in range(2):
                                P.mm(banks[5][:, vh * 16 + s:vh * 16 + s + 1], SN[:, j, vh * 128:(vh + 1) * 128],
                                     QS[:, s:s + 1], True, True, [kSN, "QS"], [bk(5)])
                        P.dma(gla_s[sg * 4:(sg + 1) * 4, h].rearrange("s d v -> d s v"), SN, [kSN], [], is_out=True)
                    P.cp("act", OgT[:, 2 * h:2 * h + 2, T:TA], v3(banks[5][:, 0:32], 16), [bk(5)], ["OgT"])
                og = OgT[:, 2 * h:2 * h + 2, s0:s0 + W]
                P.act(SQ[:, :, 0:W], og, AF.Square, ["OgT"], ["SQ"])
                for vh in range(2):
                    P.mm(banks[3][:, 0:W], ONESb, SQ[:, vh, 0:W], vh == 0, vh == 1, ["ONESb", "SQ"], [bk(3)])
                P.act(RSTD[:, 0:W], banks[3][:, 0:W], AF.Ln, [bk(3), "EPSC"], ["RSTD"], bias=EPSC[:, 0:1], scale=1.0 / 256.0)
                P.act(RSTD[:, 0:W], RSTD[:, 0:W], AF.Exp, ["RSTD"], ["RSTD"], scale=-0.5)
                for vh in range(2):
                    P.stt(ON[:, 0:W], OgT[:, 2 * h + vh, s0:s0 + W], cvc(CV_GNW, vh), RSTD[:, 0:W],
                          ALU.mult, ALU.mult, ["OgT", "CV", "RSTD"], ["ON"])
                    P.tt("pool", OgT[:, 2 * h + vh, s0:s0 + W], ON[:, 0:W], GS[:, vh, 0:W], ALU.mult,
                         ["ON", "GS"], ["OgT"])
        ar.release()
        P.fence()

        P.enabled = "R" in phases
        ar.mark()
        SHT = v3(ar.f32(33 * 16), 16)
        WSTf = WST.rearrange("p a b -> p (a b)")
        for tb in (0, 11, 22):
            P.dma(WSTf[0:16, 0:11 * 128], sshift[:, tb * 128:(tb + 11) * 128], [], ["WST"])
            for j in range(11):
                P.tr(banks[2][:, j * 16:(j + 1) * 16], WSTf[0:16, j * 128:(j + 1) * 128], ID[0:16, 0:16],
                     ["WST", "CM"], [bk(2)])
            P.cp("act", SHT[:, tb:tb + 11, :], v3(banks[2][:, 0:176], 16), [bk(2)], ["SHT"])

        LORA = ar.f32(TA)
        RAW = [ar.f32(513) for _ in range(4)]
        DQ = ar.f32(512)
        BB = ar.f32(9 * 512)
        Bp = lambda i: BB[:, i * 512:(i + 1) * 512]
        Rl, kRl = Bp(0), "B0"
        Kl, kKl = Bp(1), "B1"
        Gl, kGl = Bp(2), "B2"
        SIG, kSIG = Bp(3), "B3"
        Aa, kAa = Bp(4), "B4"
        KKr, kKKr = Bp(5), "B5"
        RN, kRN = Bp(6), "B6"
        KKn, kKKn = Bp(7), "B7"
        T1, kT1 = Bp(2), "B2"
        Kp, kKp = Bp(8), "B8"
        RKf, kRKf = Bp(5), "B5"
        BA, kBA = Bp(1), "B1"
        CS, kCS = Bp(4), "B4"
        GAM, kGAM = Bp(5), "B5"
        GIN, kGIN = Bp(6), "B6"
        CSX, kCSX = Bp(2), "B2"
        GEX, kGEX = Bp(3), "B3"
        YS, kYS = Bp(0), "B0"
        YSQ, kYSQ = Bp(1), "B1"
        MU2, kMU2 = Bp(2), "B2"
        VAR, kVAR = Bp(3), "B3"
        YD, kYD = Bp(4), "B4"
        SHO, kSHO = Bp(0)[0:17, :], "B0"
        SR, kSR = v3(BB[:, 5 * 512:7 * 512], 64), ["B5", "B6"]
        SRN, kSRN = v3(BB[:, 2 * 512:4 * 512], 64), ["B2", "B3"]
        SQb = ar.bf16(512)
        OUT = []
        for o_ in range(2):
            arkb = ar.bf16(2048)
            OUT.append(dict(
                ARKB=arkb, AR=arkb[:, 0:1024], AR4=arkb[:, 0:1024].rearrange("p (c q t) -> p c q t", q=2, t=64),
                Kt=arkb[:, 1024:1536], Bt=arkb[:, 1536:2048],
                Vb=ar.bf16(512), Gs=ar.bf16(512), BON=ar.bf16(512), GAMC=ar.f32(8),
                kAR=f"AR{o_}", kKt=f"Kt{o_}", kBt=f"Bt{o_}", kVb=f"Vb{o_}", kGs=f"Gs{o_}", kBON=f"BON{o_}",
                kGAMC=f"GAMC{o_}"))
        DG = OUT[0]["ARKB"].bitcast(F32)[:, 0:640].rearrange("p (s q k) -> p s q k", q=5, k=64)
        kDG = ["AR0", "Kt0"]
        Vf = ar.f32(16)
        CB = [(ar.bf16(320), ar.bf16(320), [ar.bf16(256) for _ in range(2)], ar.bf16(64), ar.f32(64), ar.bf16(64))
              for _ in range(3)]
        Hb = ar.bf16(64); Hf = ar.f32(64)
        XQ = v3(ar.f32(16 * 5), 5)
        TJ = ar.f32(64); T2 = ar.f32(64); SA = ar.f32(2)
        B4K = [bk(4), "b4tm", "b4s"]
        B5K = [bk(5), "b5z", "b5g", "b5r", "b5zv"]

        def lerp(q, ftile, src_bank, W, samp, out_ap, okey):
            raw = RAW[q]
            rk = f"RAW{q}"
            P.cp("act", raw[:, 1:1 + W], banks[src_bank][:, 0:W], [bk(src_bank)], [rk])
            if samp:
                P.tt("pool", DQ[:, 0:W], SHT[:, ftile, :], raw[:, 1:1 + W], ALU.subtract, ["SHT", rk], ["DQ"])
            else:
                P.tt("pool", DQ[:, 0:W], raw[:, 0:W], raw[:, 1:1 + W], ALU.subtract, [rk], ["DQ"])
            P.stt(out_ap, DQ[:, 0:W], cvc(CV_MU, ftile), raw[:, 1:1 + W], ALU.mult, ALU.add, ["DQ", "CV", rk], [okey])
            if not samp:
                P.cp("pool", raw[:, 0:1], raw[:, W:W + 1], [rk], [rk])

        wl, kl = load_wgroup([w_in[:, RW:RW + 128]])
        P.memset("pool", RAW[0][:, 0:1], 0.0, ["RAW0"])
        for si, (s0, W) in enumerate(SEGS):
            proj_fm(wl, kl, 0, 128, s0, W, 0)
            lerp(0, 32, 0, W, si == 4, LORA[:, s0:s0 + W], "LORA")
        P.act(LORA[0:64, :], LORA[0:64, :], AF.Tanh, ["LORA"], ["LORA"])
        proj_tm(wl, kl, 0, 128, T - 1, 17, 2)
        P.cp("act", SHO[:, 0:128], banks[2][0:17, 0:128], [bk(2)], [kSHO])
        P.dma(shift_o[:, 4096:4224], SHO[:, 0:128], [kSHO], [], is_out=True)

        def gn_post(p, s0, W, O):
            P.act(YSQ[:, 0:W], YS[:, 0:W], AF.Square, [kYS], [kYSQ])
            P.mm(banks[2][:, 0:W], BAVG, YS[:, 0:W], True, True, ["CM", kYS], [bk(2)])
            P.mm(banks[3][:, 0:W], BAVG, YSQ[:, 0:W], True, True, ["CM", kYSQ], [bk(3)])
            P.act(MU2[:, 0:W], banks[2][:, 0:W], AF.Square, [bk(2)], [kMU2])
            P.tt("dve", VAR[:, 0:W], banks[3][:, 0:W], MU2[:, 0:W], ALU.subtract, [bk(3), kMU2], [kVAR])
            P.act(VAR[:, 0:W], VAR[:, 0:W], AF.Ln, [kVAR], [kVAR], bias=EPSC[:, 1:2])
            P.act(VAR[:, 0:W], VAR[:, 0:W], AF.Exp, [kVAR], [kVAR], scale=-0.5)
            P.tt("dve", YD[:, 0:W], YS[:, 0:W], banks[2][:, 0:W], ALU.subtract, [kYS, bk(2)], [kYD])
            P.tt("pool", YD[:, 0:W], YD[:, 0:W], VAR[:, 0:W], ALU.mult, [kYD, kVAR], [kYD])
            P.ts("dve", YD[:, 0:W], YD[:, 0:W], cvc(CV_LW, p), cvc(CV_LB, p), ALU.mult, ALU.add, [kYD, "CV"], [kYD])
            P.tt("pool", YD[:, 0:W], YD[:, 0:W], O["BON"][:, 0:W], ALU.add, [kYD, O["kBON"]], [kYD])
            P.tt("dve", OrT[:, p, s0:s0 + W], YD[:, 0:W], O["Gs"][:, 0:W], ALU.mult, [kYD, O["kGs"]], ["OrT"])

        def mm2(out, lhsT, rhs, n0, n1, l0, l1, r0, r1, start, stop, r, w):
            for hh in range(2):
                ps = slice(hh * 64, (hh + 1) * 64)
                P.mm(out[ps, n0:n1], lhsT[ps, l0:l1], rhs[ps, r0:r1], start, stop, r, w)

        if RLEVEL < 2:
            P.enabled = False
        P.counting = True
        def elem_gen(p, wp, kp, si, O):
            s0, W = SEGS[si]
            samp = si == 4
            Vb, Gs, BON = O["Vb"], O["Gs"], O["BON"]
            AR4, Kt, Bt = O["AR4"], O["Kt"], O["Bt"]
            proj_fm(wp, kp, 0, 128, s0, W, 2)
            lerp(0, p, 2, W, samp, Rl[:, 0:W], kRl)
            yield
            proj_fm(wp, kp, 128, 128, s0, W, 2)
            lerp(1, 8 + p, 2, W, samp, Kl[:, 0:W], kKl)
            yield
            proj_fm(wp, kp, 256, 128, s0, W, 2)
            lerp(2, 16 + p, 2, W, samp, Vb[:, 0:W], O["kVb"])
            if samp:
                P.stt(Vf[:, 0:W], DQ[:, 0:W], cvc(CV_MU, 16 + p), RAW[2][:, 1:1 + W], ALU.mult, ALU.add,
                      ["DQ", "CV", "RAW2"], ["Vf"])
            yield
            proj_fm(wp, kp, 384, 128, s0, W, 2)
            lerp(3, 24 + p, 2, W, samp, Gl[:, 0:W], kGl)
            P.act(Gs[:, 0:W], Gl[:, 0:W], AF.Silu, [kGl], [O["kGs"]])
            yield
            P.mm(banks[2][:, 0:W], W2A2[0:64, p * 128:(p + 1) * 128], LORA[0:64, s0:s0 + W], True, True,
                 ["W2A2", "LORA"], [bk(2)])
            P.act(SIG[:, 0:W], banks[2][:, 0:W], AF.Sigmoid, [bk(2), "CV"], [kSIG], bias=cvc(CV_W0, p))
            yield
            P.mm(banks[2][:, 0:W], W2A2[64:128, p * 128:(p + 1) * 128], LORA[64:128, s0:s0 + W], True, True,
                 ["W2A2", "LORA"], [bk(2)])
            P.act(Aa[:, 0:W], banks[2][:, 0:W], AF.Sigmoid, [bk(2), "CV"], [kAa], bias=cvc(CV_A0, p))
            yield
            P.ts("pool", KKr[:, 0:W], Kl[:, 0:W], cvc(CV_KK, p), None, ALU.mult, ALU.bypass, [kKl, "CV"], [kKKr])
            P.act(SQb[:, 0:W], KKr[:, 0:W], AF.Square, [kKKr], ["SQb"])
            P.mm(banks[2][:, 0:W], BONESb, SQb[:, 0:W], True, True, ["BONESb", "SQb"], [bk(2)])
            yield
            P.act(RN[:, 0:W], banks[2][:, 0:W], AF.Ln, [bk(2)], [kRN])
            P.act(RN[:, 0:W], RN[:, 0:W], AF.Exp, [kRN], [kRN], scale=-0.5)
            yield
            P.tt("pool", KKn[:, 0:W], KKr[:, 0:W], RN[:, 0:W], ALU.mult, [kKKr, kRN], [kKKn])
            P.ts("dve", T1[:, 0:W], Aa[:, 0:W], -1.0, cvc(CV_KA, p), ALU.add, ALU.mult, [kAa, "CV"], [kT1])
            P.stt(Kp[:, 0:W], T1[:, 0:W], 1.0, Kl[:, 0:W], ALU.add, ALU.mult, [kT1, kKl], [kKp])
            yield
            P.tt("pool", RKf[:, 0:W], Rl[:, 0:W], Kp[:, 0:W], ALU.mult, [kRl, kKp], [kRKf])
            P.ts("pool", RKf[:, 0:W], RKf[:, 0:W], cvc(CV_RK, p), None, ALU.mult, ALU.bypass, [kRKf, "CV"], [kRKf])
            yield
            P.mm(banks[2][:, 0:W], BONES, RKf[:, 0:W], True, True, ["CM", kRKf], [bk(2)])
            P.tt("dve", BON[:, 0:W], banks[2][:, 0:W], Vb[:, 0:W], ALU.mult, [bk(2), O["kVb"]], [O["kBON"]])
            P.tt("pool", BA[:, 0:W], KKn[:, 0:W], Aa[:, 0:W], ALU.mult, [kKKn, kAa], [kBA])
            yield
            if not samp:
                P.scan(CS[:, 0:W], SMR[:, 0:W], SIG[:, 0:W], ["CM", kSIG], [kCS])
                P.act(GAM[:, 0:W], CS[:, 0:W], AF.Exp, [kCS], [kGAM], scale=-DECAY)
                P.act(GIN[:, 0:W], CS[:, 0:W], AF.Exp, [kCS], [kGIN], scale=DECAY)
                yield
                P.tt("pool", CSX[:, 0:W], CS[:, 0:W], SIG[:, 0:W], ALU.subtract, [kCS, kSIG], [kCSX])
                P.act(GEX[:, 0:W], CSX[:, 0:W], AF.Exp, [kCSX], [kGEX], scale=-DECAY)
                P.cp("pool", O["GAMC"], v3(GAM[:, 0:W], 64)[:, :, 63], [kGAM], [O["kGAMC"]])
                yield
                v64 = lambda a_: v3(a_[:, 0:W], 64)
                P.stt(AR4[:, :, 0, :], v64(KKn), -1.0, v64(GEX), ALU.mult, ALU.mult, [kKKn, kGEX], [O["kAR"]])
                P.tt("dve", AR4[:, :, 1, :], v64(Rl), v64(GAM), ALU.mult, [kRl, kGAM], [O["kAR"]])
                yield
                P.tt("pool", Kt[:, 0:W], Kp[:, 0:W], GIN[:, 0:W], ALU.mult, [kKp, kGIN], [O["kKt"]])
                P.tt("pool", Bt[:, 0:W], BA[:, 0:W], GIN[:, 0:W], ALU.mult, [kBA, kGIN], [O["kBt"]])
            else:
                P.ts("pool", XQ[:, :, 0], KKn[:, 0:W], -1.0, None, ALU.mult, ALU.bypass, [kKKn], ["XQ"])
                P.act(XQ[:, :, 1], SIG[:, 0:W], AF.Exp, [kSIG], ["XQ"], scale=-DECAY)
                P.cp("pool", XQ[:, :, 2], BA[:, 0:W], [kBA], ["XQ"])
                P.cp("pool", XQ[:, :, 3], Kp[:, 0:W], [kKp], ["XQ"])
                P.cp("pool", XQ[:, :, 4], Rl[:, 0:W], [kRl], ["XQ"])

        def chunk_gen(c, par, si, O):
            tb, db = [(4, 5), (6, 7), (0, 1)][par]
            bT, bD = banks[tb], banks[db]
            bTb = bT.bitcast(BF16)
            kT, kD = bk(tb), bk(db)
            TMp, SCp, ZPQp, GA1p, GVGp, RHp = CB[par]
            kTM, kSC, kGA1, kGVG, kRH = f"TM{par}", f"SC{par}", f"GA1{par}", f"GVG{par}", f"RH{par}"
            AR, AR4, Kt, Bt, Vb = O["AR"], O["AR4"], O["Kt"], O["Bt"], O["Vb"]
            kAR, kKt, kBt, kVb = O["kAR"], O["kKt"], O["kBt"], O["kVb"]
            cs = slice(c * 64, (c + 1) * 64)
            last = (si == 3 and c == 7)
            arc = AR[:, c * 128:(c + 1) * 128]
            for hh in range(2):
                ps = slice(hh * 64, (hh + 1) * 64)
                idb = IDb[ps, ps]
                P.tr(bTb[ps, 64:128], AR4[ps, c, 0, :], idb, [kAR, "IDb"], [kT])
                P.tr(bTb[ps, 128:192], Bt[ps, cs], idb, [kBt, "IDb"], [kT])
                P.tr(bTb[ps, 192:256], Kt[ps, cs], idb, [kKt, "IDb"], [kT])
                P.tr(bTb[ps, 256:320], Vb[ps, cs], idb, [kVb, "IDb"], [kT])
            P.cp("act", TMp[:, 64:320], bTb[:, 64:320], [kT], [kTM])
            mm2(bD, Bt, arc, 0, 128, c * 64, (c + 1) * 64, 0, 128, True, True, [kBt, kAR], [kD])
            mm2(bD, Kt, arc, 128, 256, c * 64, (c + 1) * 64, 0, 128, True, True, [kKt, kAR], [kD])
            mm2(bD, arc, Bt, 256, 320, 0, 64, c * 64, (c + 1) * 64, True, True, [kBt, kAR], [kD])
            P.tt("dve", SCp, bD[:, 0:320], MASK5, ALU.mult, [kD, "CM"], [kSC])
            yield
            mm2(bD, SCp, TMp, 448, 512, 128, 192, 256, 320, True, True, [kSC, kTM], [kD])
            P.cp("act", TMp[:, 0:64], bD[:, 448:512], [kD], [kTM])
            yield
            zsrc, zk = TMp, kTM
            psrc, pk, pc = SCp, kSC, 0
            qsrc, qk, qc = SCp, kSC, 256
            for lvl in range(6):
                mm2(bD, psrc, zsrc, 0, 128, pc, pc + 64, 0, 128, True, True, [pk, zk], [kD])
                if lvl < 5:
                    mm2(bT, qsrc, psrc, 0, 64, qc, qc + 64, pc, pc + 64, True, True, [pk, qk], [kT])
                    mm2(bT, psrc, qsrc, 64, 128, pc, pc + 64, qc, qc + 64, True, True, [pk, qk], [kT])
                dst = ZPQp[lvl % 2]
                dzk = f"Z{par}{lvl % 2}"
                dpk = f"PQ{par}{lvl % 2}"
                P.tt("dve", dst[:, 0:128], bD[:, 0:128], zsrc[:, 0:128], ALU.add, [kD, zk], [dzk])
                if lvl < 5:
                    P.cp("act", dst[:, 128:256], bT[:, 0:128], [kT], [dpk])
                zsrc, zk = dst, dzk
                psrc, pk, pc = dst, dpk, 128
                qsrc, qk, qc = dst, dpk, 192
                yield
            Wm, wk_ = zsrc, zk
            gam_col = O["GAMC"][:, c:c + 1]
            kGC = O["kGAMC"]
            mm2(bD, Wm, TMp, 256, 320, 64, 128, 128, 192, True, True, [wk_, kTM], [kD])
            mm2(bD, TMp, Wm, 320, 384, 128, 192, 0, 64, True, False, [wk_, kTM], [kD])
            mm2(bD, TMp, TMp, 320, 384, 192, 256, 256, 320, False, True, [kTM], [kD])
            mm2(bT, ISTKb, arc, 384, 448, 0, 64, 64, 128, True, False, ["ISTKb", kAR], [kT])
            mm2(bT, Wm, SCp, 384, 448, 64, 128, 64, 128, False, True, [wk_, kSC], [kT])
            P.tt("dve", GA1p, bD[:, 256:320], ISTK, ALU.add, [kD, "CM"], [kGA1])
            P.ts("dve", GVGp, bD[:, 320:384], gam_col, None, ALU.mult, ALU.bypass, [kD, kGC], [kGVG])
            P.cp("act", RHp, bT[:, 384:448], [kT], [kRH])
            yield
            b3 = banks[3]
            mm2(b3, Wm, SCp, c * 64, (c + 1) * 64, 0, 64, 64, 128, True, False, [wk_, kSC], [bk(3)])
            mm2(b3, TMp, SCp, c * 64, (c + 1) * 64, 256, 320, 192, 256, False, False, [kTM, kSC], [bk(3)])
            mm2(b3, Hb, RHp, c * 64, (c + 1) * 64, 0, 64, 0, 64, False, True, ["Hb", kRH], [bk(3)])
            mm2(banks[2], GA1p, Hb, 0, 64, 0, 64, 0, 64, True, True, [kGA1, "Hb"], [bk(2)])
            if last:
                P.stt(Hf, banks[2][:, 0:64], gam_col, GVGp, ALU.mult, ALU.add, [bk(2), kGC, kGVG], ["Hf"])
            P.stt(Hb, banks[2][:, 0:64], gam_col, GVGp, ALU.mult, ALU.add, [bk(2), kGC, kGVG], ["Hb"])

        def run_overlapped(chunk_args, extra):
            active = []
            nxt = 0
            free_par = [0, 1, 2]
            since = 99
            ex = extra
            while nxt < len(chunk_args) or active or ex is not None:
                if nxt < len(chunk_args) and free_par and (since >= 3 or not active):
                    pr_ = free_par.pop(0)
                    c_, si_, O_ = chunk_args[nxt]
                    active.append((chunk_gen(c_, pr_, si_, O_), pr_))
                    nxt += 1
                    since = 0
                since += 1
                for (g_, pr_) in list(active):
                    try:
                        next(g_)
                    except StopIteration:
                        active.remove((g_, pr_))
                        free_par.append(pr_)
                if ex is not None:
                    try:
                        next(ex)
                    except StopIteration:
                        ex = None

        for p in range(8):
            wp, kp = load_wgroup(defer=True)
            for q in range(4):
                P.memset("pool", RAW[q][:, 0:1], 0.0, [f"RAW{q}"])
            P.memset("pool", Hb, 0.0, ["Hb"])
            proj_tm(wp, kp, 0, 512, T - 1, 17, 2)
            P.cp("act", SHO, banks[2][0:17, 0:512], [bk(2)], [kSHO])
            P.dma(shift_o[:, 0:4096].rearrange("t (q f) -> t q f", q=4)[:, :, p * 128:(p + 1) * 128],
                  v3(SHO, 128), [kSHO], [], is_out=True)
            run_overlapped([], elem_gen(p, wp, kp, 0, OUT[0]))
            flush_prefetch()
            for si in range(4):
                s0, W = SEGS[si]
                O = OUT[si % 2]
                On = OUT[(si + 1) % 2]
                run_overlapped([(c, si, O) for c in range(8)], elem_gen(p, wp, kp, si + 1, On))
                P.cp("act", YS[:, 0:W], banks[3][:, 0:W], [bk(3)], [kYS])
                if si == 3:
                    P.tr(banks[2][0:64, 64:192], Hf, ID, ["Hf", "CM"], [bk(2)])
                    P.cp("act", T2[0:64, :], banks[2][0:64, 64:128], [bk(2)], ["T2"])
                    P.cp("act", TJ[0:64, :], banks[2][0:64, 128:192], [bk(2)], ["TJ"])
                    P.dma(rwkv_p[2 * p], T2[0:64, :], ["T2"], [], is_out=True)
                    P.dma(rwkv_p[2 * p + 1], TJ[0:64, :], ["TJ"], [], is_out=True)
                gn_post(p, s0, W, O)
            s0, W = SEGS[4]
            O = OUT[0]
            P.dma(SR, srwkv[:, 2 * p:2 * p + 2].rearrange("s h v k -> (h v) s k"), [], kSR)
            for s2 in range(8):
                hs = slice(s2 * 2, s2 * 2 + 2)
                P.add("dve", lambda e, hs=hs: e.tensor_tensor(
                    out=DG, in0=ISTK.unsqueeze(1).unsqueeze(1).to_broadcast([128, 2, 5, 64]),
                    in1=XQ[:, hs, :].unsqueeze(3).to_broadcast([128, 2, 5, 64]), op=ALU.mult),
                    ["CM", "XQ"], kDG)
                for j in range(2):
                    s = s2 * 2 + j
                    bi = 4 + j
                    bb = banks[bi]
                    dgs = DG[:, j].rearrange("p q k -> p (q k)")
                    for hh in range(2):
                        ps = slice(hh * 64, (hh + 1) * 64)
                        P.mm(bb[ps, 0:320], ONES[ps, 0:64], dgs[ps, :], True, True, ["CM"] + kDG, [bk(bi)])
                    P.stt(TJ, SR[:, s, :], 1.0, bb[:, 0:64], ALU.mult, ALU.mult, kSR + [bk(bi)], ["TJ", "SA"],
                          accum=SA[:, 0:1])
                    P.tt("dve", T2, SR[:, s, :], bb[:, 64:128], ALU.mult, kSR + [bk(bi)], ["T2"])
                    P.stt(T2, bb[:, 128:192], SA[:, 0:1], T2, ALU.mult, ALU.add, [bk(bi), "SA", "T2"], ["T2"])
                    P.stt(SRN[:, s, :], bb[:, 192:256], Vf[:, s:s + 1], T2, ALU.mult, ALU.add,
                          [bk(bi), "Vf", "T2"], kSRN)
                    P.stt(TJ, SRN[:, s, :], 1.0, bb[:, 256:320], ALU.mult, ALU.mult, kSRN + [bk(bi)],
                          ["TJ", kYS], accum=YS[:, s:s + 1])
            P.dma(rwkv_s[:, 2 * p:2 * p + 2].rearrange("s h v k -> (h v) s k"), SRN, kSRN, [], is_out=True)
            gn_post(p, s0, W, O)
        ar.release()
        P.fence()

        P.enabled = "F" in phases
        ar.mark()
        MT = v3(ar.bf16(KC * TA), TA)
        FT = [(ar.bf16(512), ar.bf16(512), ar.f32(512)) for _ in range(2)]
        fcnt = 0
        for dt in range(8):
            wf, kf = load_wgroup(defer=True)
            for (s0, W) in SEGS:
                if s0 == 512:
                    flush_prefetch()
                par = fcnt % 2
                fcnt += 1
                SGA, SGBt, M1 = FT[par]
                kA, kB, kM = f"SGA{par}", f"SGBt{par}", f"M1{par}"
                b0, b1, b2, b3 = [4 * par + i for i in range(4)]
                proj_fm(wf, kf, 0, 128, s0, W, b0)
                proj_fm(wf, kf, 128, 128, s0, W, b1)
                for kc in range(KC):
                    P.mm(banks[b2][:, 0:W], wf[:, kc, 256:384], OgT[:, kc, s0:s0 + W], kc == 0, kc == KC - 1,
                         [kf, "OgT"], [bk(b2)])
                for kc in range(KC):
                    P.mm(banks[b3][:, 0:W], wf[:, kc, 384:512], OrT[:, kc, s0:s0 + W], kc == 0, kc == KC - 1,
                         [kf, "OrT"], [bk(b3)])
                P.act(SGA[:, 0:W], banks[b0][:, 0:W], AF.Sigmoid, [bk(b0)], [kA])
                P.act(SGBt[:, 0:W], banks[b1][:, 0:W], AF.Sigmoid, [bk(b1)], [kB])
                P.tt("dve", M1[:, 0:W], banks[b2][:, 0:W], SGA[:, 0:W], ALU.mult, [bk(b2), kA], [kM])
                P.tt("dve", MT[:, dt, s0:s0 + W], banks[b3][:, 0:W], SGBt[:, 0:W], ALU.mult, [bk(b3), kB], ["MT"])
                P.tt("pool", MT[:, dt, s0:s0 + W], MT[:, dt, s0:s0 + W], M1[:, 0:W], ALU.add, ["MT", kM], ["MT"])
        WO = v3(ar.bf16(KC * 1024), 1024)
        for g in range(2):
            wo, ko = load_wgroup([w_out[:, g * 512:(g + 1) * 512]])
            P.cp("pool", WO[:, :, g * 512:(g + 1) * 512], wo, [ko], ["WO"])
        XA = xT.rearrange("p a b -> p (a b)").bitcast(F32)
        LNGB = XA[:, 0:2048]
        XR = [XA[:, 2048 + i * 1024:2048 + (i + 1) * 1024] for i in range(2)]
        ZT = [XA[:, 4096 + i * 1024:4096 + (i + 1) * 1024] for i in range(2)]
        P.dma(LNGB, lngb, [], ["LNGB", "xT"])
        xt_seen = set()

        def xk_(k):
            if k in xt_seen:
                return [k]
            xt_seen.add(k)
            return [k, "xT"]
        ST = ar.f32(8)
        alpha = 2.0 ** 0.25
        for tt in range(17):
            rows = 128 if tt < 16 else NS
            t0 = tt * 128
            xr = XR[tt % 2]; xk = f"XR{tt % 2}"
            zt = ZT[tt % 2]; zk = f"ZT{tt % 2}"
            if tt == 0:
                P.dma(xr[0:rows, :], x[t0:t0 + rows, :], [], xk_(xk))
            if tt + 1 < 17:
                rn_ = 128 if tt + 1 < 16 else NS
                P.dma(XR[(tt + 1) % 2][0:rn_, :], x[(tt + 1) * 128:(tt + 1) * 128 + rn_, :], [],
                      xk_(f"XR{(tt + 1) % 2}"))
            for eh in range(2):
                bi = 2 * (tt % 2) + eh
                for kc in range(KC):
                    P.mm(banks[bi][0:rows, 0:512], MT[:, kc, t0:t0 + rows], WO[:, kc, eh * 512:(eh + 1) * 512],
                         kc == 0, kc == KC - 1, ["MT", "WO"], [bk(bi)])
                P.stt(zt[0:rows, eh * 512:(eh + 1) * 512], xr[0:rows, eh * 512:(eh + 1) * 512], alpha,
                      banks[bi][0:rows, 0:512], ALU.mult, ALU.add, [xk, bk(bi)], xk_(zk))
            R_ = slice(0, rows)
            P.act(xr[R_, :], zt[R_, :], AF.Copy, [zk], [xk, "ST"], accum=ST[R_, 0:1])
            P.act(xr[R_, :], zt[R_, :], AF.Square, [zk], [xk, "ST"], accum=ST[R_, 1:2])
            P.ts("pool", ST[R_, 2:3], ST[R_, 0:1], 1.0 / D, None, ALU.mult, ALU.bypass, ["ST"], ["ST"])
            P.tt("pool", ST[R_, 3:4], ST[R_, 2:3], ST[R_, 2:3], ALU.mult, ["ST"], ["ST"])
            P.stt(ST[R_, 4:5], ST[R_, 1:2], 1.0 / D, ST[R_, 3:4], ALU.mult, ALU.subtract, ["ST"], ["ST"])
            P.act(ST[R_, 5:6], ST[R_, 4:5], AF.Ln, ["ST"], ["ST"], bias=EPSC[R_, 0:1])
            P.act(ST[R_, 5:6], ST[R_, 5:6], AF.Exp, ["ST"], ["ST"], scale=-0.5)
            P.ts("dve", zt[R_, :], zt[R_, :], ST[R_, 2:3], ST[R_, 5:6], ALU.subtract, ALU.mult, [zk, "ST"], [zk])
            P.tt("pool", zt[R_, :], zt[R_, :], LNGB[R_, 0:1024], ALU.mult, [zk, "LNGB"], [zk])
            P.tt("dve", zt[R_, :], zt[R_, :], LNGB[R_, 1024:2048], ALU.add, [zk, "LNGB"], [zk])
            P.dma(y[t0:t0 + rows, :], zt[R_, :], [zk], [], is_out=True)
        ar.release()
        P.fence()

        P.enabled = True
        P.counting = False
        P.finish()
        P.emit(nc, st)
    return nc


def _consts():
    cm = np.zeros((128, NCM), np.float32)
    cm[:, CM_ID:CM_ID + 128] = np.eye(128, dtype=np.float32)
    s = np.arange(128)[:, None] % 64
    t = np.arange(64)[None, :]
    strict = (s < t).astype(np.float32)
    incl = (s <= t).astype(np.float32)
    lower = (t < s).astype(np.float32)
    cm[:, CM_MASK5:CM_MASK5 + 320] = np.concatenate([strict, incl, strict, incl, lower], axis=1)
    j = np.arange(128)[:, None]
    i = np.arange(128)[None, :]
    cm[:, CM_MASKU:CM_MASKU + 128] = (j <= i).astype(np.float32)
    cm[:, CM_ISTK:CM_ISTK + 64] = (s == t).astype(np.float32)
    blk = (np.arange(128)[:, None] // 64 == np.arange(128)[None, :] // 64).astype(np.float32)
    cm[:, CM_BONES:CM_BONES + 128] = blk
    cm[:, CM_BAVG:CM_BAVG + 128] = blk / 64.0
    cm[:, CM_ONES:CM_ONES + 128] = 1.0
    smg = np.ones(512, np.float32); smg[::128] = 0
    smr = np.ones(512, np.float32); smr[::64] = 0
    cm[:, CM_SMG:CM_SMG + 512] = smg[None, :]
    cm[:, CM_SMR:CM_SMR + 512] = smr[None, :]
    cm[0:16, CM_ID16:CM_ID16 + 16] = np.eye(16, dtype=np.float32)
    return cm


def _cols(vec):
    return np.ascontiguousarray(np.asarray(vec, np.float32).reshape(-1, 128).T)


_NC_CACHE = {}


def kernel(x_prompt, x_sample, state_gla, state_rwkv, state_rwkv_shift, w_in, gla_alpha_w2,
           gla_alpha_b, gla_norm_w, rwkv_mu, rwkv_w0, rwkv_w2, rwkv_a0, rwkv_a2, rwkv_k_k,
           rwkv_k_a, rwkv_r_k, rwkv_lnx_w, rwkv_lnx_b, w_up_gla, w_up_rwkv, w_out, ln_g, ln_b):
    f = lambda a: np.ascontiguousarray(np.asarray(a, dtype=np.float32))
    x_prompt, x_sample = f(x_prompt), f(x_sample)
    cvec = np.concatenate([_cols(rwkv_mu[0]), _cols(gla_alpha_b[0]), _cols(gla_norm_w[0]), _cols(rwkv_w0[0]),
                           _cols(rwkv_a0[0]), _cols(rwkv_k_k[0]), _cols(rwkv_k_a[0]),
                           _cols(np.asarray(rwkv_r_k[0]).reshape(-1)), _cols(rwkv_lnx_w[0]), _cols(rwkv_lnx_b[0])],
                          axis=1)
    assert cvec.shape == (128, NCV)
    cmat = _consts()
    w2a2 = np.concatenate([f(rwkv_w2[0]), f(rwkv_a2[0])], axis=0)
    lngb = np.concatenate([np.broadcast_to(f(ln_g[0])[None, :], (128, D)),
                           np.broadcast_to(f(ln_b[0])[None, :], (128, D))], axis=1)
    lngb = np.ascontiguousarray(lngb)
    common = dict(w_in=f(w_in[0]), alpha_w2=f(gla_alpha_w2[0]), w2a2=w2a2, w_up_gla=f(w_up_gla[0]),
                  w_up_rwkv=f(w_up_rwkv[0]), w_out=f(w_out[0]), lngb=lngb, cvec=f(cvec), cmat=cmat)
    in_maps = []
    for c in range(8):
        m = dict(common)
        m["x"] = np.ascontiguousarray(np.concatenate([x_prompt[c], x_sample[NS * c:NS * (c + 1), 0]], axis=0))
        m["sgla"] = f(state_gla[0, NS * c:NS * (c + 1)])
        m["srwkv"] = f(state_rwkv[0, NS * c:NS * (c + 1)])
        m["sshift"] = f(state_rwkv_shift[0, NS * c:NS * (c + 1)])
        in_maps.append(m)
    if "nc" not in _NC_CACHE:
        _NC_CACHE["nc"] = build_nc()
    nc = _NC_CACHE["nc"]
    res = run_bass_kernel_spmd(nc, in_maps, core_ids=list(range(8)))
    rs = res.results
    y_prompt = np.stack([rs[c]["y"][0:T] for c in range(8)], axis=0)
    y_sample = np.concatenate([rs[c]["y"][T:TA] for c in range(8)], axis=0)[:, None, :]
    gla_p = np.stack([rs[c]["gla_p"] for c in range(8)], axis=0)[None]
    rwkv_p = np.stack([rs[c]["rwkv_p"] for c in range(8)], axis=0)[None]
    shift_p = np.stack([rs[c]["shift_o"][0] for c in range(8)], axis=0)[None]
    gla_s = np.concatenate([rs[c]["gla_s"] for c in range(8)], axis=0)[None]
    rwkv_s = np.concatenate([rs[c]["rwkv_s"] for c in range(8)], axis=0)[None]
    shift_s = np.concatenate([rs[c]["shift_o"][1:17] for c in range(8)], axis=0)[None]
    outs = (y_prompt, y_sample, gla_p, rwkv_p, shift_p, gla_s, rwkv_s, shift_s)
    return tuple(np.ascontiguousarray(o, dtype=np.float32) for o in outs)
```

```python
import contextlib
import numpy as np
import concourse.bass as bass
import concourse.mybir as mybir
from concourse.bass_utils import run_bass_kernel_spmd

F32 = mybir.dt.float32
BF16 = mybir.dt.bfloat16
AF = mybir.ActivationFunctionType
ALU = mybir.AluOpType

ENGS = ["pe", "act", "dve", "pool", "sp"]

T = 2048
NS = 16
TA = T + NS
D = 1024
KC = 8
NIN = 9360
GQ, GK, GV, GG, GA = 0, 512, 1024, 2048, 3072
R0 = 3088
RR, RK, RV, RG, RW = R0, R0 + 1024, R0 + 2048, R0 + 3072, R0 + 4096
G0 = R0 + 4224
SEGS = [(0, 512), (512, 512), (1024, 512), (1536, 512), (2048, 16)]
DECAY = 0.606531
RLEVEL = 99
RPAIRS = 8
RSTOP = None
RLOG = []

CV_MU, CV_AB, CV_GNW, CV_W0, CV_A0, CV_KK, CV_KA, CV_RK, CV_LW, CV_LB = 0, 33, 37, 39, 47, 55, 63, 71, 79, 87
NCV = 95
CM_ID, CM_MASK5, CM_MASKU, CM_ISTK, CM_BONES, CM_BAVG, CM_ONES, CM_SMG, CM_SMR, CM_ID16 = (
    0, 128, 448, 576, 640, 768, 896, 1024, 1536, 2048)
NCM = 2064


class Prog:
    EPOCH = 8192
    NDMA = 14

    def __init__(self):
        self.ops = {e: [] for e in ENGS}
        self.cnt = {e: 0 for e in ENGS}
        self.ndma = 0
        self.dma_events = []
        self.last_w = {}
        self.readers = {}
        self.waited = {e: {} for e in ENGS}
        self.semkeys = set()
        self.out_events = []
        self.enabled = True
        self.pending = {e: [] for e in ENGS}
        self.last_ev = {}
        self.know = {e: {} for e in ENGS}
        self.evclock = {}
        self.evidx = {}
        self.nev = 0

    def fence(self):
        evs = list(self.last_ev.values()) + list(self.dma_events[-self.NDMA:])
        for e in ENGS:
            self.pending[e] = list(evs)

    def _resolve(self, eng, cands):
        know = self.know[eng]
        waits = []
        for ev in sorted(cands, key=lambda e: -self.evidx[e]):
            sk, val = ev
            if eng == "pe" and sk[0] == "pe":
                continue
            if know.get(sk, 0) >= val:
                continue
            waits.append(ev)
            for k2, v2 in self.evclock[ev].items():
                if know.get(k2, 0) < v2:
                    know[k2] = v2
        return waits

    def add(self, eng, fn, r=(), w=(), dma=False, is_out=False):
        if not self.enabled:
            return None
        if RSTOP is not None and getattr(self, "counting", False):
            self.nops = getattr(self, "nops", 0) + 1
            if self.nops > RSTOP:
                return None
        xb = [k for k in r if isinstance(k, str) and k.startswith("bank")]
        if xb:
            r = [k for k in r if k not in xb]
            w = list(w) + [k for k in xb if k not in w]
        cands = set()
        if self.pending[eng]:
            cands.update(self.pending[eng])
            self.pending[eng] = []
        for k in r:
            cands.add(self.last_w.get(k))
        for k in w:
            cands.add(self.last_w.get(k))
            cands.update(self.readers.get(k, ()))
        if dma and self.ndma >= self.NDMA:
            cands.add(self.dma_events[self.ndma - self.NDMA])
        cands.discard(None)
        waits = self._resolve(eng, cands)
        clk = dict(self.know[eng])
        if dma:
            j = self.ndma
            self.ndma += 1
            sk = ("dma", j % self.NDMA)
            val = 16 * (j // self.NDMA + 1)
            ev = (sk, val)
            self.dma_events.append(ev)
            inc = 16
            if is_out:
                self.out_events.append(ev)
        else:
            i = self.cnt[eng]
            self.cnt[eng] += 1
            ep = i // self.EPOCH
            sk = (eng, ep)
            ev = (sk, i % self.EPOCH + 1)
            inc = 1
            self.last_ev[eng] = ev
            for e2 in range(ep):
                clk[(eng, e2)] = self.EPOCH
        clk[sk] = max(clk.get(sk, 0), ev[1])
        self.evclock[ev] = clk
        self.evidx[ev] = self.nev
        self.nev += 1
        self.semkeys.add(sk)
        self.ops[eng].append((waits, fn, ev, inc))
        for k in r:
            self.readers.setdefault(k, []).append(ev)
        for k in w:
            self.last_w[k] = ev
            self.readers[k] = []
        return ev

    def finish(self):
        cands = set(self.dma_events[-self.NDMA:]) | set(self.out_events)
        waits = self._resolve("sp", cands)
        self.ops["sp"].append((waits, None, None, 0))

    def emit(self, nc, stack):
        targets = set()
        for e in ENGS:
            for waits, fn, ev, inc in self.ops[e]:
                targets.update(waits)
        real = {}
        used = set()
        for e in ENGS:
            cnt = {}
            for waits, fn, ev, inc in self.ops[e]:
                if ev is None:
                    continue
                if ev[0][0] == "dma":
                    real[ev] = ev[1]
                    used.add(ev[0])
                elif ev in targets:
                    cnt[ev[0]] = cnt.get(ev[0], 0) + 1
                    real[ev] = cnt[ev[0]]
                    used.add(ev[0])
        sems = {}
        for sk in sorted(used, key=str):
            sems[sk] = stack.enter_context(nc.semaphore("s_" + "_".join(str(x) for x in sk)))
        block = stack.enter_context(nc.Block())
        prog = self

        def run(engname):
            def body(eng):
                for waits, fn, ev, inc in prog.ops[engname]:
                    if fn is None:
                        for w_ in waits:
                            eng.wait_ge(sems[w_[0]], real[w_])
                        continue
                    for w_ in waits[:-1]:
                        eng.wait_ge(sems[w_[0]], real[w_])
                    ins = fn(eng)
                    if waits:
                        ins._wait_ge(sems[waits[-1][0]], real[waits[-1]])
                    if ev in real:
                        ins.then_inc(sems[ev[0]], inc)
            return body

        block.tensor(run("pe"))
        block.scalar(run("act"))
        block.vector(run("dve"))
        block.gpsimd(run("pool"))
        block.sync(run("sp"))

    def act(self, out, in_, func, r, w, bias=0.0, scale=1.0, accum=None):
        if accum is None:
            return self.add("act", lambda e: e.activation(out=out, in_=in_, func=func, bias=bias, scale=scale), r, w)
        return self.add("act", lambda e: e.activation(out=out, in_=in_, func=func, bias=bias, scale=scale,
                                                      accum_out=accum), r, w)

    def ts(self, eng, out, in0, s1, s2, op0, op1, r, w):
        return self.add(eng, lambda e: e.tensor_scalar(out=out, in0=in0, scalar1=s1, scalar2=s2, op0=op0, op1=op1), r, w)

    def stt(self, out, in0, scalar, in1, op0, op1, r, w, accum=None):
        if accum is None:
            return self.add("dve", lambda e: e.scalar_tensor_tensor(out=out, in0=in0, scalar=scalar, in1=in1,
                                                                     op0=op0, op1=op1), r, w)
        return self.add("dve", lambda e: e.scalar_tensor_tensor(out=out, in0=in0, scalar=scalar, in1=in1,
                                                                 op0=op0, op1=op1, accum_out=accum), r, w)

    def tt(self, eng, out, in0, in1, op, r, w):
        return self.add(eng, lambda e: e.tensor_tensor(out=out, in0=in0, in1=in1, op=op), r, w)

    def cp(self, eng, out, in_, r, w):
        if eng == "act":
            return self.add("act", lambda e: e.activation(out=out, in_=in_, func=AF.Copy), r, w)
        return self.add(eng, lambda e: e.tensor_copy(out=out, in_=in_), r, w)

    def memset(self, eng, out, val, w):
        return self.add(eng, lambda e: e.memset(out, val), (), w)

    def mm(self, out, lhsT, rhs, start, stop, r, w):
        return self.add("pe", lambda e: e.matmul(out, lhsT=lhsT, rhs=rhs, start=start, stop=stop), r, w)

    def tr(self, out, in_, ident, r, w):
        return self.add("pe", lambda e: e.transpose(out, in_, ident), r, w)

    def dma(self, out, in_, r, w, is_out=False):
        return self.add("sp", lambda e: e.dma_start(out=out, in_=in_), r, w, dma=True, is_out=is_out)

    def scan(self, out, d0, d1, r, w):
        return self.add("dve", lambda e: e.tensor_tensor_scan(out=out, data0=d0, data1=d1, initial=0.0,
                                                               op0=ALU.mult, op1=ALU.add), r, w)


class Arena:
    def __init__(self, t, words):
        self.t = t
        self.words = words
        self.off = 0
        self.marks = []
        self.peak = 0

    def alloc(self, nwords):
        o = self.off
        self.off += nwords
        self.peak = max(self.peak, self.off)
        assert self.off <= self.words, f"SBUF arena overflow {self.off}>{self.words}"
        return o

    def f32(self, n, parts=128):
        o = self.alloc(n)
        return self.t[0:parts, o:o + n]

    def bf16(self, n, parts=128):
        assert n % 2 == 0
        o = self.alloc(n // 2)
        return self.t[0:parts, o:o + n // 2].bitcast(BF16)

    def mark(self):
        self.marks.append(self.off)

    def release(self):
        self.off = self.marks.pop()


def v3(ap, b):
    return ap.rearrange("p (a b) -> p a b", b=b)


def build_nc(phases="0GRF"):
    nc = bass.Bass("TRN2", target_bir_lowering=False)
    di = lambda n, s: nc.dram_tensor(n, list(s), F32, kind="ExternalInput").ap()
    do = lambda n, s: nc.dram_tensor(n, list(s), F32, kind="ExternalOutput").ap()
    x = di("x", (TA, D))
    w_in = di("w_in", (D, NIN))
    alpha_w2 = di("alpha_w2", (16, 512))
    w2a2 = di("w2a2", (128, 1024))
    w_up_gla = di("w_up_gla", (D, D))
    w_up_rwkv = di("w_up_rwkv", (D, D))
    w_out = di("w_out", (D, D))
    lngb = di("lngb", (128, 2048))
    sgla = di("sgla", (NS, 4, 128, 256))
    srwkv = di("srwkv", (NS, 16, 64, 64))
    sshift = di("sshift", (NS, 4224))
    cvec = di("cvec", (128, NCV))
    cmat = di("cmat", (128, NCM))
    y = do("y", (TA, D))
    gla_p = do("gla_p", (4, 128, 256))
    rwkv_p = do("rwkv_p", (16, 64, 64))
    shift_o = do("shift_o", (17, 4224))
    gla_s = do("gla_s", (NS, 4, 128, 256))
    rwkv_s = do("rwkv_s", (NS, 16, 64, 64))

    P = Prog()
    with contextlib.ExitStack() as st:
        WORDS = 53184
        sb = st.enter_context(nc.sbuf_tensor("arena", [128, WORDS], F32))
        ar = Arena(sb, WORDS)
        banks = [st.enter_context(nc.psum_tensor(f"ps{i}", [128, 512], F32)) for i in range(8)]
        bk = lambda i: f"bank{i}"

        CV = ar.f32(NCV)
        CM = ar.f32(NCM)
        P.dma(CV, cvec, [], ["CV"])
        P.dma(CM, cmat, [], ["CM"])
        ID = CM[:, CM_ID:CM_ID + 128]
        MASK5 = CM[:, CM_MASK5:CM_MASK5 + 320]
        MASKU = CM[:, CM_MASKU:CM_MASKU + 128]
        ISTK = CM[:, CM_ISTK:CM_ISTK + 64]
        BONES = CM[:, CM_BONES:CM_BONES + 128]
        BAVG = CM[:, CM_BAVG:CM_BAVG + 128]
        ONES = CM[:, CM_ONES:CM_ONES + 128]
        SMG = CM[:, CM_SMG:CM_SMG + 512]
        SMR = CM[:, CM_SMR:CM_SMR + 512]
        ID16 = CM[0:16, CM_ID16:CM_ID16 + 16]
        IDb = ar.bf16(128)
        ISTKb = ar.bf16(64)
        BONESb = ar.bf16(128)
        ONESb = ar.bf16(128)
        NAB = ar.f32(4)
        EPSC = ar.f32(2)
        P.cp("pool", IDb, ID, ["CM"], ["IDb"])
        P.cp("pool", ISTKb, ISTK, ["CM"], ["ISTKb"])
        P.cp("pool", BONESb, BONES, ["CM"], ["BONESb"])
        P.cp("pool", ONESb, ONES, ["CM"], ["ONESb"])
        P.ts("pool", NAB, CV[:, CV_AB:CV_AB + 4], -1.0, None, ALU.mult, ALU.bypass, ["CV"], ["NAB"])
        P.memset("pool", EPSC[:, 0:1], 1e-5, ["EPSC"])
        P.memset("pool", EPSC[:, 1:2], 64e-5, ["EPSC"])
        cvc = lambda base, j: CV[:, base + j:base + j + 1]

        W2A2 = ar.f32(1024)
        P.dma(W2A2, w2a2, [], ["W2A2"])

        xT = v3(ar.bf16(KC * TA), TA)
        OgT = v3(ar.bf16(KC * TA), TA)
        OrT = v3(ar.bf16(KC * TA), TA)

        WST = v3(ar.f32(4 * 512), 512)
        WBF = [v3(ar.bf16(KC * 512), 512) for _ in range(3)]
        wstate = {"n": 0, "pre": None}
        wqueue = []

        def _issue_load(srcs):
            en = P.enabled
            P.enabled = True
            par = wstate["n"] % 3
            wstate["n"] += 1
            key = f"WBF{par}"
            for half in range(2):
                off = 0
                for s_ in srcs:
                    n = s_.shape[1]
                    P.dma(WST[:, :, off:off + n],
                          s_.rearrange("(kc p) n -> p kc n", p=128)[:, 4 * half:4 * half + 4, :], [], ["WST"])
                    off += n
                for k2 in range(2):
                    P.cp("pool", WBF[par][:, 4 * half + 2 * k2:4 * half + 2 * k2 + 2, 0:off],
                         WST[:, 2 * k2:2 * k2 + 2, 0:off], ["WST"], [key])
            P.enabled = en
            return WBF[par], key

        def load_wgroup(srcs=None, defer=False):
            flush_prefetch()
            if wstate["pre"] is None:
                wstate["pre"] = _issue_load(wqueue.pop(0))
            cur = wstate["pre"]
            wstate["pre"] = None
            wstate["pend"] = bool(wqueue)
            if not defer:
                flush_prefetch()
            return cur

        def flush_prefetch():
            if wstate.get("pend"):
                wstate["pend"] = False
                wstate["pre"] = _issue_load(wqueue.pop(0))

        wqueue.append([w_in[:, GA:GA + 16]])
        for h_ in range(4):
            wqueue.append([w_in[:, GQ + h_ * 128:GQ + (h_ + 1) * 128], w_in[:, GK + h_ * 128:GK + (h_ + 1) * 128],
                           w_in[:, GV + h_ * 256:GV + (h_ + 1) * 256]])
            wqueue.append([w_in[:, GG + h_ * 256:GG + (h_ + 1) * 256]])
        wqueue.append([w_in[:, RW:RW + 128]])
        for p_ in range(8):
            wqueue.append([w_in[:, RR + p_ * 128:RR + (p_ + 1) * 128], w_in[:, RK + p_ * 128:RK + (p_ + 1) * 128],
                           w_in[:, RV + p_ * 128:RV + (p_ + 1) * 128], w_in[:, RG + p_ * 128:RG + (p_ + 1) * 128]])
        for dt_ in range(8):
            wqueue.append([w_in[:, G0 + dt_ * 128:G0 + (dt_ + 1) * 128],
                           w_in[:, G0 + 1024 + dt_ * 128:G0 + 1024 + (dt_ + 1) * 128],
                           w_up_gla[:, dt_ * 128:(dt_ + 1) * 128], w_up_rwkv[:, dt_ * 128:(dt_ + 1) * 128]])
        for g_ in range(2):
            wqueue.append([w_out[:, g_ * 512:(g_ + 1) * 512]])

        def proj_fm(wb, wkey, coff, ncols, s0, W, bank_i, c0=0):
            for kc in range(KC):
                P.mm(banks[bank_i][0:ncols, c0:c0 + W], wb[:, kc, coff:coff + ncols], xT[:, kc, s0:s0 + W],
                     kc == 0, kc == KC - 1, [wkey, "xT"], [bk(bank_i)])

        def proj_tm(wb, wkey, coff, ncols, t0, M, bank_i, c0=0):
            for kc in range(KC):
                P.mm(banks[bank_i][0:M, c0:c0 + ncols], xT[:, kc, t0:t0 + M], wb[:, kc, coff:coff + ncols],
                     kc == 0, kc == KC - 1, [wkey, "xT"], [bk(bank_i)])

        wstate["pre"] = _issue_load(wqueue.pop(0))
        P.enabled = "0" in phases
        ar.mark()
        XS = [ar.f32(1024) for _ in range(4)]
        for tt in range(17):
            rows = 128 if tt < 16 else NS
            xs = XS[tt % 4]
            xk = f"XS{tt % 4}"
            P.dma(xs[0:rows, :], x[tt * 128:tt * 128 + rows, :], [], [xk])
            for half in range(2):
                b = banks[half]
                for j in range(4):
                    kc = half * 4 + j
                    P.tr(b[:, j * 128:j * 128 + rows], xs[0:rows, kc * 128:(kc + 1) * 128], ID[0:rows, 0:rows],
                         [xk, "CM"], [bk(half)])
                src = v3(b[:, 0:512], 128)[:, :, 0:rows]
                dst = xT[:, half * 4:half * 4 + 4, tt * 128:tt * 128 + rows]
                P.cp("act" if half == 0 else "dve", dst, src, [bk(half)], ["xT"])
        ar.release()
        P.fence()

        P.enabled = "G" in phases
        ar.mark()
        AW2 = ar.f32(512, parts=16)
        P.dma(AW2, alpha_w2, [], ["AW2"])
        ALR = ar.f32(TA, parts=16)
        SP = ar.f32(512); CSP = ar.f32(512); EB = ar.f32(512); EINV = ar.f32(512)
        QT = ar.bf16(512); KT = ar.bf16(512); QS = ar.f32(16)
        KH = ar.bf16(128); KHT = ar.bf16(128)
        VTK = v3(ar.bf16(4 * 256), 256)
        ATS = ar.bf16(128)
        SG = [ar.f32(256) for _ in range(2)]
        SGB = ar.bf16(256)
        GS = v3(ar.bf16(2 * 512), 512)
        SQ = v3(ar.bf16(2 * 512), 512)
        RSTD = ar.f32(512)
        ON = ar.f32(512)
        KTOK = ar.f32(128, parts=16); VTOK = ar.f32(256, parts=16)
        KM = v3(ar.f32(16 * 128, parts=16), 128)
        SSb = [v3(ar.f32(4 * 256), 256) for _ in range(2)]
        SNb = [v3(ar.f32(4 * 256), 256) for _ in range(2)]

        wb, wk = load_wgroup([w_in[:, GA:GA + 16]])
        for (s0, W) in SEGS:
            proj_fm(wb, wk, 0, 16, s0, W, 0)
            P.cp("act", ALR[:, s0:s0 + W], banks[0][0:16, 0:W], [bk(0)], ["ALR"])

        for h in range(4):
            wqkv, kqkv = load_wgroup([w_in[:, GQ + h * 128:GQ + (h + 1) * 128],
                                      w_in[:, GK + h * 128:GK + (h + 1) * 128],
                                      w_in[:, GV + h * 256:GV + (h + 1) * 256]])
            wg, kg = load_wgroup(defer=True)
            cur = 0
            P.memset("pool", SG[0], 0.0, ["SG0"])
            P.memset("pool", SGB, 0.0, ["SGB"])
            for si, (s0, W) in enumerate(SEGS):
                samp = si == 4
                if si == 1:
                    flush_prefetch()
                P.mm(banks[2][:, 0:W], AW2[:, h * 128:(h + 1) * 128], ALR[:, s0:s0 + W], True, True,
                     ["AW2", "ALR"], [bk(2)])
                P.act(SP[:, 0:W], banks[2][:, 0:W], AF.Exp, [bk(2), "NAB"], ["SP"], bias=NAB[:, h:h + 1], scale=-1.0)
                P.act(SP[:, 0:W], SP[:, 0:W], AF.Ln, ["SP"], ["SP"], bias=1.0)
                if samp:
                    csp = SP
                else:
                    P.scan(CSP[:, 0:W], SMG[:, 0:W], SP[:, 0:W], ["CM", "SP"], ["CSP"])
                    csp = CSP
                ck = "SP" if samp else "CSP"
                P.act(EB[:, 0:W], csp[:, 0:W], AF.Exp, [ck], ["EB"], scale=-1.0 / 16.0)
                P.act(EINV[:, 0:W], csp[:, 0:W], AF.Exp, [ck], ["EINV"], scale=1.0 / 16.0)
                proj_fm(wqkv, kqkv, 0, 128, s0, W, 0)
                if samp:
                    P.ts("dve", QS[:, 0:W], banks[0][:, 0:W], 128.0 ** -0.5, None, ALU.mult, ALU.bypass,
                         [bk(0)], ["QS"])
                else:
                    P.stt(QT[:, 0:W], banks[0][:, 0:W], 128.0 ** -0.5, EB[:, 0:W], ALU.mult, ALU.mult,
                          [bk(0), "EB"], ["QT"])
                    proj_fm(wqkv, kqkv, 128, 128, s0, W, 1)
                    P.tt("dve", KT[:, 0:W], banks[1][:, 0:W], EINV[:, 0:W], ALU.mult, [bk(1), "EINV"], ["KT"])
                if not samp:
                    for t4 in range(4):
                        bi = t4 % 2
                        proj_tm(wqkv, kqkv, 256, 256, s0 + t4 * 128, 128, bi)
                        P.cp("act", VTK[:, t4, :], banks[bi][:, 0:256], [bk(bi)], ["VTK"])
                    for vh in range(2):
                        proj_fm(wg, kg, vh * 128, 128, s0, W, vh)
                        P.act(GS[:, vh, 0:W], banks[vh][:, 0:W], AF.Silu, [bk(vh)], ["GS"])
                    for c in range(4):
                        cs = slice(c * 128, (c + 1) * 128)
                        ecol = EB[:, c * 128 + 127:c * 128 + 128]
                        first = (si == 0 and c == 0)
                        P.mm(banks[4][:, 0:128], KT[:, cs], QT[:, cs], True, True, ["KT", "QT"], [bk(4)])
                        P.tt("dve", ATS, banks[4][:, 0:128], MASKU, ALU.mult, [bk(4), "CM"], ["ATS"])
                        P.ts("pool", KH, KT[:, cs], ecol, None, ALU.mult, ALU.bypass, ["KT", "EB"], ["KH"])
                        b4b = banks[4].bitcast(BF16)
                        P.tr(b4b[:, 512:640], KH, IDb, ["KH", "IDb"], [bk(4)])
                        P.cp("act", KHT, b4b[:, 512:640], [bk(4)], ["KHT"])
                        for vh in range(2):
                            vs = slice(vh * 128, (vh + 1) * 128)
                            P.mm(banks[5][:, vh * 128:(vh + 1) * 128], VTK[:, c, vs], ATS, True, first,
                                 ["VTK", "ATS"], [bk(5)])
                            if not first:
                                P.mm(banks[5][:, vh * 128:(vh + 1) * 128], SGB[:, vs], QT[:, cs], False, True,
                                     ["SGB", "QT"], [bk(5)])
                        P.cp("act", OgT[:, 2 * h:2 * h + 2, s0 + c * 128:s0 + (c + 1) * 128],
                             v3(banks[5][:, 0:256], 128), [bk(5)], ["OgT"])
                        P.mm(banks[6][:, 0:256], KHT, VTK[:, c, :], True, True, ["KHT", "VTK"], [bk(6)])
                        nxt = 1 - cur
                        P.stt(SG[nxt], SG[cur], ecol, banks[6][:, 0:256], ALU.mult, ALU.add,
                              [f"SG{cur}", "EB", bk(6)], [f"SG{nxt}"])
                        P.cp("act", SGB, SG[nxt], [f"SG{nxt}"], ["SGB"])
                        cur = nxt
                    if si == 3:
                        P.dma(gla_p[h], SG[cur], [f"SG{cur}"], [], is_out=True)
                else:
                    for vh in range(2):
                        proj_fm(wg, kg, vh * 128, 128, s0, W, vh)
                        P.act(GS[:, vh, 0:W], banks[vh][:, 0:W], AF.Silu, [bk(vh)], ["GS"])
                    proj_tm(wqkv, kqkv, 128, 128, T, NS, 4)
                    P.cp("act", KTOK, banks[4][0:16, 0:128], [bk(4)], ["KTOK"])
                    proj_tm(wqkv, kqkv, 256, 256, T, NS, 4)
                    P.cp("act", VTOK, banks[4][0:16, 0:256], [bk(4)], ["VTOK"])
                    P.add("dve", lambda e: e.tensor_tensor(
                        out=KM, in0=KTOK.unsqueeze(1).to_broadcast([16, 16, 128]),
                        in1=ID16.unsqueeze(2).to_broadcast([16, 16, 128]), op=ALU.mult),
                        ["KTOK", "CM"], ["KM"])
                    P.dma(SSb[0], sgla[0:4, h].rearrange("s d v -> d s v"), [], ["SS0"])
                    for sg in range(4):
                        SS, SN = SSb[sg % 2], SNb[sg % 2]
                        kSS, kSN = f"SS{sg % 2}", f"SN{sg % 2}"
                        if sg + 1 < 4:
                            P.dma(SSb[(sg + 1) % 2], sgla[(sg + 1) * 4:(sg + 2) * 4, h].rearrange("s d v -> d s v"),
                                  [], [f"SS{(sg + 1) % 2}"])
                        for j in range(4):
                            s = sg * 4 + j
                            P.add("pe", lambda e, s=s: e.matmul(banks[6][:, 0:256], lhsT=KM[:, s, :], rhs=VTOK,
                                                                start=True, stop=True), ["KM", "VTOK"], [bk(6)])
                            P.stt(SN[:, j, :], SS[:, j, :], EB[:, s:s + 1], banks[6][:, 0:256], ALU.mult, ALU.add,
                                  [kSS, "EB", bk(6)], [kSN])
                            for vh in range(2):
                                P.mm(banks[5][:, vh * 16 + s:vh * 16 + s + 1], SN[:, j, vh * 128:(vh + 1) * 128],
                                     QS[:, s:s + 1], True, True, [kSN, "QS"], [bk(5)])
                        P.dma(gla_s[sg * 4:(sg + 1) * 4, h].rearrange("s d v -> d s v"), SN, [kSN], [], is_out=True)
                    P.cp("act", OgT[:, 2 * h:2 * h + 2, T:TA], v3(banks[5][:, 0:32], 16), [bk(5)], ["OgT"])
                og = OgT[:, 2 * h:2 * h + 2, s0:s0 + W]
                P.act(SQ[:, :, 0:W], og, AF.Square, ["OgT"], ["SQ"])
                for vh in range(2):
                    P.mm(banks[3][:, 0:W], ONESb, SQ[:, vh, 0:W], vh == 0, vh == 1, ["ONESb", "SQ"], [bk(3)])
                P.act(RSTD[:, 0:W], banks[3][:, 0:W], AF.Ln, [bk(3), "EPSC"], ["RSTD"], bias=EPSC[:, 0:1], scale=1.0 / 256.0)
                P.act(RSTD[:, 0:W], RSTD[:, 0:W], AF.Exp, ["RSTD"], ["RSTD"], scale=-0.5)
                for vh in range(2):
                    P.stt(ON[:, 0:W], OgT[:, 2 * h + vh, s0:s0 + W], cvc(CV_GNW, vh), RSTD[:, 0:W],
                          ALU.mult, ALU.mult, ["OgT", "CV", "RSTD"], ["ON"])
                    P.tt("pool", OgT[:, 2 * h + vh, s0:s0 + W], ON[:, 0:W], GS[:, vh, 0:W], ALU.mult,
                         ["ON", "GS"], ["OgT"])
        ar.release()
        P.fence()

        P.enabled = "R" in phases
        ar.mark()
        SHT = v3(ar.f32(33 * 16), 16)
        WSTf = WST.rearrange("p a b -> p (a b)")
        for tb in (0, 11, 22):
            P.dma(WSTf[0:16, 0:11 * 128], sshift[:, tb * 128:(tb + 11) * 128], [], ["WST"])
            for j in range(11):
                P.tr(banks[2][:, j * 16:(j + 1) * 16], WSTf[0:16, j * 128:(j + 1) * 128], ID[0:16, 0:16],
                     ["WST", "CM"], [bk(2)])
            P.cp("act", SHT[:, tb:tb + 11, :], v3(banks[2][:, 0:176], 16), [bk(2)], ["SHT"])

        LORA = ar.f32(TA)
        RAW = [ar.f32(513) for _ in range(4)]
        DQ = ar.f32(512)
        BB = ar.f32(9 * 512)
        Bp = lambda i: BB[:, i * 512:(i + 1) * 512]
        Rl, kRl = Bp(0), "B0"
        Kl, kKl = Bp(1), "B1"
        Gl, kGl = Bp(2), "B2"
        SIG, kSIG = Bp(3), "B3"
        Aa, kAa = Bp(4), "B4"
        KKr, kKKr = Bp(5), "B5"
        RN, kRN = Bp(6), "B6"
        KKn, kKKn = Bp(7), "B7"
        T1, kT1 = Bp(2), "B2"
        Kp, kKp = Bp(8), "B8"
        RKf, kRKf = Bp(5), "B5"
        BA, kBA = Bp(1), "B1"
        CS, kCS = Bp(4), "B4"
        GAM, kGAM = Bp(5), "B5"
        GIN, kGIN = Bp(6), "B6"
        CSX, kCSX = Bp(2), "B2"
        GEX, kGEX = Bp(3), "B3"
        YS, kYS = Bp(0), "B0"
        YSQ, kYSQ = Bp(1), "B1"
        MU2, kMU2 = Bp(2), "B2"
        VAR, kVAR = Bp(3), "B3"
        YD, kYD = Bp(4), "B4"
        SHO, kSHO = Bp(0)[0:17, :], "B0"
        SR, kSR = v3(BB[:, 5 * 512:7 * 512], 64), ["B5", "B6"]
        SRN, kSRN = v3(BB[:, 2 * 512:4 * 512], 64), ["B2", "B3"]
        SQb = ar.bf16(512)
        OUT = []
        for o_ in range(2):
            arkb = ar.bf16(2048)
            OUT.append(dict(
                ARKB=arkb, AR=arkb[:, 0:1024], AR4=arkb[:, 0:1024].rearrange("p (c q t) -> p c q t", q=2, t=64),
                Kt=arkb[:, 1024:1536], Bt=arkb[:, 1536:2048],
                Vb=ar.bf16(512), Gs=ar.bf16(512), BON=ar.bf16(512), GAMC=ar.f32(8),
                kAR=f"AR{o_}", kKt=f"Kt{o_}", kBt=f"Bt{o_}", kVb=f"Vb{o_}", kGs=f"Gs{o_}", kBON=f"BON{o_}",
                kGAMC=f"GAMC{o_}"))
        DG = OUT[0]["ARKB"].bitcast(F32)[:, 0:640].rearrange("p (s q k) -> p s q k", q=5, k=64)
        kDG = ["AR0", "Kt0"]
        Vf = ar.f32(16)
        CB = [(ar.bf16(320), ar.bf16(320), [ar.bf16(256) for _ in range(2)], ar.bf16(64), ar.f32(64), ar.bf16(64))
              for _ in range(3)]
        Hb = ar.bf16(64); Hf = ar.f32(64)
        XQ = v3(ar.f32(16 * 5), 5)
        TJ = ar.f32(64); T2 = ar.f32(64); SA = ar.f32(2)
        B4K = [bk(4), "b4tm", "b4s"]
        B5K = [bk(5), "b5z", "b5g", "b5r", "b5zv"]

        OMU = ar.f32(33)
        P.ts("pool", OMU, CV[:, CV_MU:CV_MU + 33], -1.0, 1.0, ALU.mult, ALU.add, ["CV"], ["OMU"])

        def lerp(q, ftile, src_bank, W, samp, out_ap, okey):
            raw = RAW[q]
            rk = f"RAW{q}"
            pb = banks[src_bank]
            omu = OMU[:, ftile:ftile + 1]
            if samp:
                P.ts("pool", DQ[:, 0:W], SHT[:, ftile, :], cvc(CV_MU, ftile), None, ALU.mult, ALU.bypass,
                     ["SHT", "CV"], ["DQ"])
                P.stt(out_ap, pb[:, 0:W], omu, DQ[:, 0:W], ALU.mult, ALU.add, [bk(src_bank), "OMU", "DQ"], [okey])
            else:
                P.act(raw[:, 1:1 + W], pb[:, 0:W], AF.Copy, [bk(src_bank), "CV"], [rk], scale=cvc(CV_MU, ftile))
                P.stt(out_ap, pb[:, 0:W], omu, raw[:, 0:W], ALU.mult, ALU.add, [bk(src_bank), "OMU", rk], [okey])
                P.cp("pool", raw[:, 0:1], raw[:, W:W + 1], [rk], [rk])

        wl, kl = load_wgroup([w_in[:, RW:RW + 128]])
        P.memset("pool", RAW[0][:, 0:1], 0.0, ["RAW0"])
        for si, (s0, W) in enumerate(SEGS):
            proj_fm(wl, kl, 0, 128, s0, W, 0)
            lerp(0, 32, 0, W, si == 4, LORA[:, s0:s0 + W], "LORA")
        P.act(LORA[0:64, :], LORA[0:64, :], AF.Tanh, ["LORA"], ["LORA"])
        proj_tm(wl, kl, 0, 128, T - 1, 17, 2)
        P.cp("act", SHO[:, 0:128], banks[2][0:17, 0:128], [bk(2)], [kSHO])
        P.dma(shift_o[:, 4096:4224], SHO[:, 0:128], [kSHO], [], is_out=True)

        def gn_post(p, s0, W, O):
            P.act(YSQ[:, 0:W], YS[:, 0:W], AF.Square, [kYS], [kYSQ])
            P.mm(banks[2][:, 0:W], BAVG, YS[:, 0:W], True, True, ["CM", kYS], [bk(2)])
            P.mm(banks[3][:, 0:W], BAVG, YSQ[:, 0:W], True, True, ["CM", kYSQ], [bk(3)])
            P.act(MU2[:, 0:W], banks[2][:, 0:W], AF.Square, [bk(2)], [kMU2])
            P.tt("dve", VAR[:, 0:W], banks[3][:, 0:W], MU2[:, 0:W], ALU.subtract, [bk(3), kMU2], [kVAR])
            P.act(VAR[:, 0:W], VAR[:, 0:W], AF.Ln, [kVAR], [kVAR], bias=EPSC[:, 1:2])
            P.act(VAR[:, 0:W], VAR[:, 0:W], AF.Exp, [kVAR], [kVAR], scale=-0.5)
            P.tt("dve", YD[:, 0:W], YS[:, 0:W], banks[2][:, 0:W], ALU.subtract, [kYS, bk(2)], [kYD])
            P.tt("pool", YD[:, 0:W], YD[:, 0:W], VAR[:, 0:W], ALU.mult, [kYD, kVAR], [kYD])
            P.ts("dve", YD[:, 0:W], YD[:, 0:W], cvc(CV_LW, p), cvc(CV_LB, p), ALU.mult, ALU.add, [kYD, "CV"], [kYD])
            P.tt("pool", YD[:, 0:W], YD[:, 0:W], O["BON"][:, 0:W], ALU.add, [kYD, O["kBON"]], [kYD])
            P.tt("dve", OrT[:, p, s0:s0 + W], YD[:, 0:W], O["Gs"][:, 0:W], ALU.mult, [kYD, O["kGs"]], ["OrT"])

        def mm2(out, lhsT, rhs, n0, n1, l0, l1, r0, r1, start, stop, r, w):
            for hh in range(2):
                ps = slice(hh * 64, (hh + 1) * 64)
                P.mm(out[ps, n0:n1], lhsT[ps, l0:l1], rhs[ps, r0:r1], start, stop, r, w)

        if RLEVEL < 2:
            P.enabled = False
        P.counting = True
        def elem_gen(p, wp, kp, si, O):
            s0, W = SEGS[si]
            samp = si == 4
            Vb, Gs, BON = O["Vb"], O["Gs"], O["BON"]
            AR4, Kt, Bt = O["AR4"], O["Kt"], O["Bt"]
            proj_fm(wp, kp, 0, 128, s0, W, 2)
            lerp(0, p, 2, W, samp, Rl[:, 0:W], kRl)
            yield
            proj_fm(wp, kp, 128, 128, s0, W, 2)
            lerp(1, 8 + p, 2, W, samp, Kl[:, 0:W], kKl)
            yield
            proj_fm(wp, kp, 256, 128, s0, W, 2)
            lerp(2, 16 + p, 2, W, samp, Vb[:, 0:W], O["kVb"])
            if samp:
                P.stt(Vf[:, 0:W], banks[2][:, 0:W], OMU[:, 16 + p:17 + p], DQ[:, 0:W], ALU.mult, ALU.add,
                      [bk(2), "OMU", "DQ"], ["Vf"])
            yield
            proj_fm(wp, kp, 384, 128, s0, W, 2)
            lerp(3, 24 + p, 2, W, samp, Gl[:, 0:W], kGl)
            P.act(Gs[:, 0:W], Gl[:, 0:W], AF.Silu, [kGl], [O["kGs"]])
            yield
            P.mm(banks[2][:, 0:W], W2A2[0:64, p * 128:(p + 1) * 128], LORA[0:64, s0:s0 + W], True, True,
                 ["W2A2", "LORA"], [bk(2)])
            P.act(SIG[:, 0:W], banks[2][:, 0:W], AF.Sigmoid, [bk(2), "CV"], [kSIG], bias=cvc(CV_W0, p))
            yield
            P.mm(banks[2][:, 0:W], W2A2[64:128, p * 128:(p + 1) * 128], LORA[64:128, s0:s0 + W], True, True,
                 ["W2A2", "LORA"], [bk(2)])
            P.act(Aa[:, 0:W], banks[2][:, 0:W], AF.Sigmoid, [bk(2), "CV"], [kAa], bias=cvc(CV_A0, p))
            yield
            P.ts("pool", KKr[:, 0:W], Kl[:, 0:W], cvc(CV_KK, p), None, ALU.mult, ALU.bypass, [kKl, "CV"], [kKKr])
            P.act(SQb[:, 0:W], KKr[:, 0:W], AF.Square, [kKKr], ["SQb"])
            P.mm(banks[2][:, 0:W], BONESb, SQb[:, 0:W], True, True, ["BONESb", "SQb"], [bk(2)])
            yield
            P.act(RN[:, 0:W], banks[2][:, 0:W], AF.Ln, [bk(2)], [kRN])
            P.act(RN[:, 0:W], RN[:, 0:W], AF.Exp, [kRN], [kRN], scale=-0.5)
            yield
            P.tt("pool", KKn[:, 0:W], KKr[:, 0:W], RN[:, 0:W], ALU.mult, [kKKr, kRN], [kKKn])
            P.ts("dve", T1[:, 0:W], Aa[:, 0:W], -1.0, cvc(CV_KA, p), ALU.add, ALU.mult, [kAa, "CV"], [kT1])
            P.stt(Kp[:, 0:W], T1[:, 0:W], 1.0, Kl[:, 0:W], ALU.add, ALU.mult, [kT1, kKl], [kKp])
            yield
            P.tt("pool", RKf[:, 0:W], Rl[:, 0:W], Kp[:, 0:W], ALU.mult, [kRl, kKp], [kRKf])
            P.ts("pool", RKf[:, 0:W], RKf[:, 0:W], cvc(CV_RK, p), None, ALU.mult, ALU.bypass, [kRKf, "CV"], [kRKf])
            yield
            P.mm(banks[2][:, 0:W], BONES, RKf[:, 0:W], True, True, ["CM", kRKf], [bk(2)])
            P.tt("dve", BON[:, 0:W], banks[2][:, 0:W], Vb[:, 0:W], ALU.mult, [bk(2), O["kVb"]], [O["kBON"]])
            P.tt("pool", BA[:, 0:W], KKn[:, 0:W], Aa[:, 0:W], ALU.mult, [kKKn, kAa], [kBA])
            yield
            if not samp:
                P.scan(CS[:, 0:W], SMR[:, 0:W], SIG[:, 0:W], ["CM", kSIG], [kCS])
                P.act(GAM[:, 0:W], CS[:, 0:W], AF.Exp, [kCS], [kGAM], scale=-DECAY)
                P.act(GIN[:, 0:W], CS[:, 0:W], AF.Exp, [kCS], [kGIN], scale=DECAY)
                yield
                P.tt("pool", CSX[:, 0:W], CS[:, 0:W], SIG[:, 0:W], ALU.subtract, [kCS, kSIG], [kCSX])
                P.act(GEX[:, 0:W], CSX[:, 0:W], AF.Exp, [kCSX], [kGEX], scale=-DECAY)
                P.cp("pool", O["GAMC"], v3(GAM[:, 0:W], 64)[:, :, 63], [kGAM], [O["kGAMC"]])
                yield
                v64 = lambda a_: v3(a_[:, 0:W], 64)
                P.stt(AR4[:, :, 0, :], v64(KKn), -1.0, v64(GEX), ALU.mult, ALU.mult, [kKKn, kGEX], [O["kAR"]])
                P.tt("dve", AR4[:, :, 1, :], v64(Rl), v64(GAM), ALU.mult, [kRl, kGAM], [O["kAR"]])
                yield
                P.tt("pool", Kt[:, 0:W], Kp[:, 0:W], GIN[:, 0:W], ALU.mult, [kKp, kGIN], [O["kKt"]])
                P.tt("pool", Bt[:, 0:W], BA[:, 0:W], GIN[:, 0:W], ALU.mult, [kBA, kGIN], [O["kBt"]])
            else:
                P.ts("pool", XQ[:, :, 0], KKn[:, 0:W], -1.0, None, ALU.mult, ALU.bypass, [kKKn], ["XQ"])
                P.act(XQ[:, :, 1], SIG[:, 0:W], AF.Exp, [kSIG], ["XQ"], scale=-DECAY)
                P.cp("pool", XQ[:, :, 2], BA[:, 0:W], [kBA], ["XQ"])
                P.cp("pool", XQ[:, :, 3], Kp[:, 0:W], [kKp], ["XQ"])
                P.cp("pool", XQ[:, :, 4], Rl[:, 0:W], [kRl], ["XQ"])

        def chunk_gen(c, par, si, O):
            tb, db = [(4, 5), (6, 7), (0, 1)][par]
            bT, bD = banks[tb], banks[db]
            bTb = bT.bitcast(BF16)
            kT, kD = bk(tb), bk(db)
            TMp, SCp, ZPQp, GA1p, GVGp, RHp = CB[par]
            kTM, kSC, kGA1, kGVG, kRH = f"TM{par}", f"SC{par}", f"GA1{par}", f"GVG{par}", f"RH{par}"
            AR, AR4, Kt, Bt, Vb = O["AR"], O["AR4"], O["Kt"], O["Bt"], O["Vb"]
            kAR, kKt, kBt, kVb = O["kAR"], O["kKt"], O["kBt"], O["kVb"]
            cs = slice(c * 64, (c + 1) * 64)
            last = (si == 3 and c == 7)
            arc = AR[:, c * 128:(c + 1) * 128]
            for hh in range(2):
                ps = slice(hh * 64, (hh + 1) * 64)
                idb = IDb[ps, ps]
                P.tr(bTb[ps, 64:128], AR4[ps, c, 0, :], idb, [kAR, "IDb"], [kT])
                P.tr(bTb[ps, 128:192], Bt[ps, cs], idb, [kBt, "IDb"], [kT])
                P.tr(bTb[ps, 192:256], Kt[ps, cs], idb, [kKt, "IDb"], [kT])
                P.tr(bTb[ps, 256:320], Vb[ps, cs], idb, [kVb, "IDb"], [kT])
            P.cp("act", TMp[:, 64:320], bTb[:, 64:320], [kT], [kTM])
            mm2(bD, Bt, arc, 0, 128, c * 64, (c + 1) * 64, 0, 128, True, True, [kBt, kAR], [kD])
            mm2(bD, Kt, arc, 128, 256, c * 64, (c + 1) * 64, 0, 128, True, True, [kKt, kAR], [kD])
            mm2(bD, arc, Bt, 256, 320, 0, 64, c * 64, (c + 1) * 64, True, True, [kBt, kAR], [kD])
            P.tt("dve", SCp, bD[:, 0:320], MASK5, ALU.mult, [kD, "CM"], [kSC])
            yield
            mm2(bD, SCp, TMp, 448, 512, 128, 192, 256, 320, True, True, [kSC, kTM], [kD])
            P.cp("act", TMp[:, 0:64], bD[:, 448:512], [kD], [kTM])
            yield
            zsrc, zk = TMp, kTM
            psrc, pk, pc = SCp, kSC, 0
            qsrc, qk, qc = SCp, kSC, 256
            for lvl in range(6):
                mm2(bD, psrc, zsrc, 0, 128, pc, pc + 64, 0, 128, True, True, [pk, zk], [kD])
                if lvl < 5:
                    mm2(bT, qsrc, psrc, 0, 64, qc, qc + 64, pc, pc + 64, True, True, [pk, qk], [kT])
                    mm2(bT, psrc, qsrc, 64, 128, pc, pc + 64, qc, qc + 64, True, True, [pk, qk], [kT])
                dst = ZPQp[lvl % 2]
                dzk = f"Z{par}{lvl % 2}"
                dpk = f"PQ{par}{lvl % 2}"
                P.tt("dve", dst[:, 0:128], bD[:, 0:128], zsrc[:, 0:128], ALU.add, [kD, zk], [dzk])
                if lvl < 5:
                    P.cp("act", dst[:, 128:256], bT[:, 0:128], [kT], [dpk])
                zsrc, zk = dst, dzk
                psrc, pk, pc = dst, dpk, 128
                qsrc, qk, qc = dst, dpk, 192
                yield
            Wm, wk_ = zsrc, zk
            gam_col = O["GAMC"][:, c:c + 1]
            kGC = O["kGAMC"]
            mm2(bD, Wm, TMp, 256, 320, 64, 128, 128, 192, True, True, [wk_, kTM], [kD])
            mm2(bD, TMp, Wm, 320, 384, 128, 192, 0, 64, True, False, [wk_, kTM], [kD])
            mm2(bD, TMp, TMp, 320, 384, 192, 256, 256, 320, False, True, [kTM], [kD])
            mm2(bT, ISTKb, arc, 384, 448, 0, 64, 64, 128, True, False, ["ISTKb", kAR], [kT])
            mm2(bT, Wm, SCp, 384, 448, 64, 128, 64, 128, False, True, [wk_, kSC], [kT])
            P.tt("dve", GA1p, bD[:, 256:320], ISTK, ALU.add, [kD, "CM"], [kGA1])
            P.ts("dve", GVGp, bD[:, 320:384], gam_col, None, ALU.mult, ALU.bypass, [kD, kGC], [kGVG])
            P.cp("act", RHp, bT[:, 384:448], [kT], [kRH])
            yield
            b3 = banks[3]
            mm2(b3, Wm, SCp, c * 64, (c + 1) * 64, 0, 64, 64, 128, True, False, [wk_, kSC], [bk(3)])
            mm2(b3, TMp, SCp, c * 64, (c + 1) * 64, 256, 320, 192, 256, False, False, [kTM, kSC], [bk(3)])
            mm2(b3, Hb, RHp, c * 64, (c + 1) * 64, 0, 64, 0, 64, False, True, ["Hb", kRH], [bk(3)])
            mm2(banks[2], GA1p, Hb, 0, 64, 0, 64, 0, 64, True, True, [kGA1, "Hb"], [bk(2)])
            if last:
                P.stt(Hf, banks[2][:, 0:64], gam_col, GVGp, ALU.mult, ALU.add, [bk(2), kGC, kGVG], ["Hf"])
            P.stt(Hb, banks[2][:, 0:64], gam_col, GVGp, ALU.mult, ALU.add, [bk(2), kGC, kGVG], ["Hb"])

        def run_overlapped(chunk_args, extra):
            active = []
            nxt = 0
            free_par = [0, 1, 2]
            since = 99
            ex = extra
            while nxt < len(chunk_args) or active or ex is not None:
                if nxt < len(chunk_args) and free_par and (since >= 3 or not active):
                    pr_ = free_par.pop(0)
                    c_, si_, O_ = chunk_args[nxt]
                    active.append((chunk_gen(c_, pr_, si_, O_), pr_))
                    nxt += 1
                    since = 0
                since += 1
                for (g_, pr_) in list(active):
                    try:
                        next(g_)
                    except StopIteration:
                        active.remove((g_, pr_))
                        free_par.append(pr_)
                if ex is not None:
                    try:
                        next(ex)
                    except StopIteration:
                        ex = None

        for p in range(8):
            wp, kp = load_wgroup(defer=True)
            for q in range(4):
                P.memset("pool", RAW[q][:, 0:1], 0.0, [f"RAW{q}"])
            P.memset("pool", Hb, 0.0, ["Hb"])
            proj_tm(wp, kp, 0, 512, T - 1, 17, 2)
            P.cp("act", SHO, banks[2][0:17, 0:512], [bk(2)], [kSHO])
            P.dma(shift_o[:, 0:4096].rearrange("t (q f) -> t q f", q=4)[:, :, p * 128:(p + 1) * 128],
                  v3(SHO, 128), [kSHO], [], is_out=True)
            run_overlapped([], elem_gen(p, wp, kp, 0, OUT[0]))
            flush_prefetch()
            for si in range(4):
                s0, W = SEGS[si]
                O = OUT[si % 2]
                On = OUT[(si + 1) % 2]
                run_overlapped([(c, si, O) for c in range(8)], elem_gen(p, wp, kp, si + 1, On))
                P.cp("act", YS[:, 0:W], banks[3][:, 0:W], [bk(3)], [kYS])
                if si == 3:
                    P.tr(banks[2][0:64, 64:192], Hf, ID, ["Hf", "CM"], [bk(2)])
                    P.cp("act", T2[0:64, :], banks[2][0:64, 64:128], [bk(2)], ["T2"])
                    P.cp("act", TJ[0:64, :], banks[2][0:64, 128:192], [bk(2)], ["TJ"])
                    P.dma(rwkv_p[2 * p], T2[0:64, :], ["T2"], [], is_out=True)
                    P.dma(rwkv_p[2 * p + 1], TJ[0:64, :], ["TJ"], [], is_out=True)
                gn_post(p, s0, W, O)
            s0, W = SEGS[4]
            O = OUT[0]
            P.dma(SR, srwkv[:, 2 * p:2 * p + 2].rearrange("s h v k -> (h v) s k"), [], kSR)
            for s2 in range(8):
                hs = slice(s2 * 2, s2 * 2 + 2)
                P.add("dve", lambda e, hs=hs: e.tensor_tensor(
                    out=DG, in0=ISTK.unsqueeze(1).unsqueeze(1).to_broadcast([128, 2, 5, 64]),
                    in1=XQ[:, hs, :].unsqueeze(3).to_broadcast([128, 2, 5, 64]), op=ALU.mult),
                    ["CM", "XQ"], kDG)
                for j in range(2):
                    s = s2 * 2 + j
                    bi = 4 + j
                    bb = banks[bi]
                    dgs = DG[:, j].rearrange("p q k -> p (q k)")
                    for hh in range(2):
                        ps = slice(hh * 64, (hh + 1) * 64)
                        P.mm(bb[ps, 0:320], ONES[ps, 0:64], dgs[ps, :], True, True, ["CM"] + kDG, [bk(bi)])
                    P.stt(TJ, SR[:, s, :], 1.0, bb[:, 0:64], ALU.mult, ALU.mult, kSR + [bk(bi)], ["TJ", "SA"],
                          accum=SA[:, 0:1])
                    P.tt("dve", T2, SR[:, s, :], bb[:, 64:128], ALU.mult, kSR + [bk(bi)], ["T2"])
                    P.stt(T2, bb[:, 128:192], SA[:, 0:1], T2, ALU.mult, ALU.add, [bk(bi), "SA", "T2"], ["T2"])
                    P.stt(SRN[:, s, :], bb[:, 192:256], Vf[:, s:s + 1], T2, ALU.mult, ALU.add,
                          [bk(bi), "Vf", "T2"], kSRN)
                    P.stt(TJ, SRN[:, s, :], 1.0, bb[:, 256:320], ALU.mult, ALU.mult, kSRN + [bk(bi)],
                          ["TJ", kYS], accum=YS[:, s:s + 1])
            P.dma(rwkv_s[:, 2 * p:2 * p + 2].rearrange("s h v k -> (h v) s k"), SRN, kSRN, [], is_out=True)
            gn_post(p, s0, W, O)
        ar.release()
        P.fence()

        P.enabled = "F" in phases
        ar.mark()
        MT = v3(ar.bf16(KC * TA), TA)
        FT = [(ar.bf16(512), ar.bf16(512), ar.f32(512)) for _ in range(2)]
        fcnt = 0
        for dt in range(8):
            wf, kf = load_wgroup(defer=True)
            for (s0, W) in SEGS:
                if s0 == 512:
                    flush_prefetch()
                par = fcnt % 2
                fcnt += 1
                SGA, SGBt, M1 = FT[par]
                kA, kB, kM = f"SGA{par}", f"SGBt{par}", f"M1{par}"
                b0, b1, b2, b3 = [4 * par + i for i in range(4)]
                proj_fm(wf, kf, 0, 128, s0, W, b0)
                proj_fm(wf, kf, 128, 128, s0, W, b1)
                for kc in range(KC):
                    P.mm(banks[b2][:, 0:W], wf[:, kc, 256:384], OgT[:, kc, s0:s0 + W], kc == 0, kc == KC - 1,
                         [kf, "OgT"], [bk(b2)])
                for kc in range(KC):
                    P.mm(banks[b3][:, 0:W], wf[:, kc, 384:512], OrT[:, kc, s0:s0 + W], kc == 0, kc == KC - 1,
                         [kf, "OrT"], [bk(b3)])
                P.act(SGA[:, 0:W], banks[b0][:, 0:W], AF.Sigmoid, [bk(b0)], [kA])
                P.act(SGBt[:, 0:W], banks[b1][:, 0:W], AF.Sigmoid, [bk(b1)], [kB])
                P.tt("dve", M1[:, 0:W], banks[b2][:, 0:W], SGA[:, 0:W], ALU.mult, [bk(b2), kA], [kM])
                P.tt("dve", MT[:, dt, s0:s0 + W], banks[b3][:, 0:W], SGBt[:, 0:W], ALU.mult, [bk(b3), kB], ["MT"])
                P.tt("pool", MT[:, dt, s0:s0 + W], MT[:, dt, s0:s0 + W], M1[:, 0:W], ALU.add, ["MT", kM], ["MT"])
        WO = v3(ar.bf16(KC * 1024), 1024)
        for g in range(2):
            wo, ko = load_wgroup([w_out[:, g * 512:(g + 1) * 512]])
            P.cp("pool", WO[:, :, g * 512:(g + 1) * 512], wo, [ko], ["WO"])
        XA = xT.rearrange("p a b -> p (a b)").bitcast(F32)
        LNGB = XA[:, 0:2048]
        XR = [XA[:, 2048 + i * 1024:2048 + (i + 1) * 1024] for i in range(2)]
        ZT = [XA[:, 4096 + i * 1024:4096 + (i + 1) * 1024] for i in range(2)]
        P.dma(LNGB, lngb, [], ["LNGB", "xT"])
        xt_seen = set()

        def xk_(k):
            if k in xt_seen:
                return [k]
            xt_seen.add(k)
            return [k, "xT"]
        ST = ar.f32(8)
        alpha = 2.0 ** 0.25
        for tt in range(17):
            rows = 128 if tt < 16 else NS
            t0 = tt * 128
            xr = XR[tt % 2]; xk = f"XR{tt % 2}"
            zt = ZT[tt % 2]; zk = f"ZT{tt % 2}"
            if tt == 0:
                P.dma(xr[0:rows, :], x[t0:t0 + rows, :], [], xk_(xk))
            if tt + 1 < 17:
                rn_ = 128 if tt + 1 < 16 else NS
                P.dma(XR[(tt + 1) % 2][0:rn_, :], x[(tt + 1) * 128:(tt + 1) * 128 + rn_, :], [],
                      xk_(f"XR{(tt + 1) % 2}"))
            for eh in range(2):
                bi = 2 * (tt % 2) + eh
                for kc in range(KC):
                    P.mm(banks[bi][0:rows, 0:512], MT[:, kc, t0:t0 + rows], WO[:, kc, eh * 512:(eh + 1) * 512],
                         kc == 0, kc == KC - 1, ["MT", "WO"], [bk(bi)])
                P.stt(zt[0:rows, eh * 512:(eh + 1) * 512], xr[0:rows, eh * 512:(eh + 1) * 512], alpha,
                      banks[bi][0:rows, 0:512], ALU.mult, ALU.add, [xk, bk(bi)], xk_(zk))
            R_ = slice(0, rows)
            P.act(xr[R_, :], zt[R_, :], AF.Copy, [zk], [xk, "ST"], accum=ST[R_, 0:1])
            P.act(xr[R_, :], zt[R_, :], AF.Square, [zk], [xk, "ST"], accum=ST[R_, 1:2])
            P.ts("pool", ST[R_, 2:3], ST[R_, 0:1], 1.0 / D, None, ALU.mult, ALU.bypass, ["ST"], ["ST"])
            P.tt("pool", ST[R_, 3:4], ST[R_, 2:3], ST[R_, 2:3], ALU.mult, ["ST"], ["ST"])
            P.stt(ST[R_, 4:5], ST[R_, 1:2], 1.0 / D, ST[R_, 3:4], ALU.mult, ALU.subtract, ["ST"], ["ST"])
            P.act(ST[R_, 5:6], ST[R_, 4:5], AF.Ln, ["ST"], ["ST"], bias=EPSC[R_, 0:1])
            P.act(ST[R_, 5:6], ST[R_, 5:6], AF.Exp, ["ST"], ["ST"], scale=-0.5)
            P.ts("dve", zt[R_, :], zt[R_, :], ST[R_, 2:3], ST[R_, 5:6], ALU.subtract, ALU.mult, [zk, "ST"], [zk])
            P.tt("pool", zt[R_, :], zt[R_, :], LNGB[R_, 0:1024], ALU.mult, [zk, "LNGB"], [zk])
            P.tt("dve", zt[R_, :], zt[R_, :], LNGB[R_, 1024:2048], ALU.add, [zk, "LNGB"], [zk])
            P.dma(y[t0:t0 + rows, :], zt[R_, :], [zk], [], is_out=True)
        ar.release()
        P.fence()

        P.enabled = True
        P.counting = False
        P.finish()
        P.emit(nc, st)
    return nc


def _consts():
    cm = np.zeros((128, NCM), np.float32)
    cm[:, CM_ID:CM_ID + 128] = np.eye(128, dtype=np.float32)
    s = np.arange(128)[:, None] % 64
    t = np.arange(64)[None, :]
    strict = (s < t).astype(np.float32)
    incl = (s <= t).astype(np.float32)
    lower = (t < s).astype(np.float32)
    cm[:, CM_MASK5:CM_MASK5 + 320] = np.concatenate([strict, incl, strict, incl, lower], axis=1)
    j = np.arange(128)[:, None]
    i = np.arange(128)[None, :]
    cm[:, CM_MASKU:CM_MASKU + 128] = (j <= i).astype(np.float32)
    cm[:, CM_ISTK:CM_ISTK + 64] = (s == t).astype(np.float32)
    blk = (np.arange(128)[:, None] // 64 == np.arange(128)[None, :] // 64).astype(np.float32)
    cm[:, CM_BONES:CM_BONES + 128] = blk
    cm[:, CM_BAVG:CM_BAVG + 128] = blk / 64.0
    cm[:, CM_ONES:CM_ONES + 128] = 1.0
    smg = np.ones(512, np.float32); smg[::128] = 0
    smr = np.ones(512, np.float32); smr[::64] = 0
    cm[:, CM_SMG:CM_SMG + 512] = smg[None, :]
    cm[:, CM_SMR:CM_SMR + 512] = smr[None, :]
    cm[0:16, CM_ID16:CM_ID16 + 16] = np.eye(16, dtype=np.float32)
    return cm


def _cols(vec):
    return np.ascontiguousarray(np.asarray(vec, np.float32).reshape(-1, 128).T)


_NC_CACHE = {}


def kernel(x_prompt, x_sample, state_gla, state_rwkv, state_rwkv_shift, w_in, gla_alpha_w2,
           gla_alpha_b, gla_norm_w, rwkv_mu, rwkv_w0, rwkv_w2, rwkv_a0, rwkv_a2, rwkv_k_k,
           rwkv_k_a, rwkv_r_k, rwkv_lnx_w, rwkv_lnx_b, w_up_gla, w_up_rwkv, w_out, ln_g, ln_b):
    f = lambda a: np.ascontiguousarray(np.asarray(a, dtype=np.float32))
    x_prompt, x_sample = f(x_prompt), f(x_sample)
    cvec = np.concatenate([_cols(rwkv_mu[0]), _cols(gla_alpha_b[0]), _cols(gla_norm_w[0]), _cols(rwkv_w0[0]),
                           _cols(rwkv_a0[0]), _cols(rwkv_k_k[0]), _cols(rwkv_k_a[0]),
                           _cols(np.asarray(rwkv_r_k[0]).reshape(-1)), _cols(rwkv_lnx_w[0]), _cols(rwkv_lnx_b[0])],
                          axis=1)
    assert cvec.shape == (128, NCV)
    cmat = _consts()
    w2a2 = np.concatenate([f(rwkv_w2[0]), f(rwkv_a2[0])], axis=0)
    lngb = np.concatenate([np.broadcast_to(f(ln_g[0])[None, :], (128, D)),
                           np.broadcast_to(f(ln_b[0])[None, :], (128, D))], axis=1)
    lngb = np.ascontiguousarray(lngb)
    common = dict(w_in=f(w_in[0]), alpha_w2=f(gla_alpha_w2[0]), w2a2=w2a2, w_up_gla=f(w_up_gla[0]),
                  w_up_rwkv=f(w_up_rwkv[0]), w_out=f(w_out[0]), lngb=lngb, cvec=f(cvec), cmat=cmat)
    in_maps = []
    for c in range(8):
        m = dict(common)
        m["x"] = np.ascontiguousarray(np.concatenate([x_prompt[c], x_sample[NS * c:NS * (c + 1), 0]], axis=0))
        m["sgla"] = f(state_gla[0, NS * c:NS * (c + 1)])
        m["srwkv"] = f(state_rwkv[0, NS * c:NS * (c + 1)])
        m["sshift"] = f(state_rwkv_shift[0, NS * c:NS * (c + 1)])
        in_maps.append(m)
    if "nc" not in _NC_CACHE:
        _NC_CACHE["nc"] = build_nc()
    nc = _NC_CACHE["nc"]
    res = run_bass_kernel_spmd(nc, in_maps, core_ids=list(range(8)))
    rs = res.results
    y_prompt = np.stack([rs[c]["y"][0:T] for c in range(8)], axis=0)
    y_sample = np.concatenate([rs[c]["y"][T:TA] for c in range(8)], axis=0)[:, None, :]
    gla_p = np.stack([rs[c]["gla_p"] for c in range(8)], axis=0)[None]
    rwkv_p = np.stack([rs[c]["rwkv_p"] for c in range(8)], axis=0)[None]
    shift_p = np.stack([rs[c]["shift_o"][0] for c in range(8)], axis=0)[None]
    gla_s = np.concatenate([rs[c]["gla_s"] for c in range(8)], axis=0)[None]
    rwkv_s = np.concatenate([rs[c]["rwkv_s"] for c in range(8)], axis=0)[None]
    shift_s = np.concatenate([rs[c]["shift_o"][1:17] for c in range(8)], axis=0)[None]
    outs = (y_prompt, y_sample, gla_p, rwkv_p, shift_p, gla_s, rwkv_s, shift_s)
    return tuple(np.ascontiguousarray(o, dtype=np.float32) for o in outs)
```
